# Optimizing a Trainium2 kernel written in Bass

```python
import math
import jax, jax.numpy as jnp
from jax import lax
import numpy as np

D_MODEL = 1024
BATCH = 4
SEQ = 4096
DEPTH = 1
DEC_BATCH = 128
DEC_SEQ = 1
PAST_LEN = 8192
PAGE_SIZE = 128

HEAD_DIM = 64
ATT_WIDTH = D_MODEL // 2
RWKV_WIDTH = D_MODEL - ATT_WIDTH
N_Q_HEADS = ATT_WIDTH // HEAD_DIM
N_KV_HEADS = 2
KV_WIDTH = N_KV_HEADS * HEAD_DIM
N_IDX_HEADS = 8
IDX_DIM = 64
TOPK_MAX = 256
ROPE_THETA = 500000.0
ROPE_DIMS = HEAD_DIM // 4
N_RWKV_HEADS = RWKV_WIDTH // HEAD_DIM
DECAY_LORA = 64
ICLR_LORA = 64
Q_BLOCK = 128
NORM_EPS = 1e-6
GN_EPS = 64e-5

COL_SIZES = (ATT_WIDTH, KV_WIDTH, KV_WIDTH, N_IDX_HEADS * IDX_DIM, N_IDX_HEADS, IDX_DIM, ATT_WIDTH,
             RWKV_WIDTH, RWKV_WIDTH, RWKV_WIDTH, DECAY_LORA, ICLR_LORA, RWKV_WIDTH)
D_IN = sum(COL_SIZES)
SHIFT_W = 3 * RWKV_WIDTH + DECAY_LORA + ICLR_LORA

kernel_name = 'hymba_rwkv7_dsa_decode_step'


def rms_norm(x, w, eps=NORM_EPS):
    xf = x.astype(jnp.float32)
    y = xf * lax.rsqrt(jnp.mean(xf * xf, axis=-1, keepdims=True) + eps)
    return (y * w.astype(jnp.float32)).astype(x.dtype)


def partial_rope(x, pos):
    half = ROPE_DIMS // 2
    inv = jnp.power(ROPE_THETA, -jnp.arange(half, dtype=jnp.float32) / half)
    ang = pos.astype(jnp.float32)[:, None] * inv[None, :]
    cos = jnp.cos(ang)[None, :, None, :]
    sin = jnp.sin(ang)[None, :, None, :]
    xr = x[..., :ROPE_DIMS].astype(jnp.float32)
    x1, x2 = xr[..., :half], xr[..., half:]
    rot = jnp.concatenate([x1 * cos - x2 * sin, x2 * cos + x1 * sin], axis=-1)
    return jnp.concatenate([rot.astype(x.dtype), x[..., ROPE_DIMS:]], axis=-1)


def split_cols(p):
    offs = [int(o) for o in np.cumsum(COL_SIZES)[:-1]]
    return jnp.split(p, offs, axis=-1)


def adaln_in(x, c, norm_w, w_ada, b_ada):
    mod = jax.nn.silu(c) @ w_ada + b_ada
    shift, scale, gate = jnp.split(mod, 3, axis=-1)
    h = rms_norm(x, norm_w) * (1.0 + scale[:, None, :]) + shift[:, None, :]
    return h, gate


def attn_inputs(cols, pos, q_norm_w, k_norm_w):
    q, k, v, qi, wi, ki = cols
    B, T = q.shape[:2]
    q = partial_rope(rms_norm(q.reshape(B, T, N_Q_HEADS, HEAD_DIM), q_norm_w), pos)
    k = partial_rope(rms_norm(k.reshape(B, T, N_KV_HEADS, HEAD_DIM), k_norm_w), pos)
    v = v.reshape(B, T, N_KV_HEADS, HEAD_DIM)
    qi = partial_rope(qi.reshape(B, T, N_IDX_HEADS, IDX_DIM), pos)
    ki = partial_rope(ki[:, :, None, :], pos)[:, :, 0, :]
    wi = wi * (N_IDX_HEADS ** -0.5 * IDX_DIM ** -0.5)
    return q, k, v, qi, wi, ki


def sparse_attend(q, qi, wi, qpos, kidx, gather_kv, topk):
    B, Q = q.shape[:2]
    L = kidx.shape[1]
    s = jax.nn.relu(jnp.einsum('bqhd,bsd->bqhs', qi, kidx))
    score = jnp.einsum('bqhs,bqh->bqs', s, wi).astype(jnp.float32)
    admissible = jnp.arange(L)[None, :] <= qpos[:, None]
    score = jnp.where(admissible[None], score, -jnp.inf)
    _, sel = lax.top_k(score, topk)
    valid = sel <= qpos[None, :, None]
    k_sel, v_sel = gather_kv(sel)
    qg = q.reshape(B, Q, N_KV_HEADS, N_Q_HEADS // N_KV_HEADS, HEAD_DIM)
    logits = jnp.einsum('bqgrd,bqkgd->bqgrk', qg, k_sel).astype(jnp.float32) * (HEAD_DIM ** -0.5)
    logits = jnp.where(valid[:, :, None, None, :], logits, -jnp.inf)
    p = jax.nn.softmax(logits, axis=-1).astype(v_sel.dtype)
    o = jnp.einsum('bqgrk,bqkgd->bqgrd', p, v_sel)
    return o.reshape(B, Q, N_Q_HEADS * HEAD_DIM)


def wkv_scan(S0, r, w, k, v, a, b):
    def step(S, inp):
        r_t, w_t, k_t, v_t, a_t, b_t = inp
        sa = jnp.einsum('bhij,bhj->bhi', S, a_t)
        S = S * w_t[:, :, None, :] + sa[..., None] * b_t[:, :, None, :] + v_t[..., None] * k_t[:, :, None, :]
        y = jnp.einsum('bhij,bhj->bhi', S, r_t)
        return S, y
    xs = tuple(jnp.moveaxis(t.astype(jnp.float32), 1, 0) for t in (r, w, k, v, a, b))
    S, ys = lax.scan(step, S0.astype(jnp.float32), xs)
    return S, jnp.moveaxis(ys, 0, 1)


def rwkv_branch(xs, prev_row, S0, gate, mu_shift, w0, w_up, a0, a_up, k_k, k_a, r_k, ln_x_w, ln_x_b):
    f32 = jnp.float32
    B, T = xs.shape[:2]
    prev = jnp.concatenate([prev_row.astype(xs.dtype), xs[:, :-1]], axis=1)
    xm = xs + mu_shift * (prev - xs)
    r, k, v, wd, ad = jnp.split(xm, [RWKV_WIDTH, 2 * RWKV_WIDTH, 3 * RWKV_WIDTH, 3 * RWKV_WIDTH + DECAY_LORA], axis=-1)
    heads = lambda t: t.reshape(B, T, N_RWKV_HEADS, HEAD_DIM)
    wlog = -jax.nn.softplus(-(w0 + jnp.tanh(wd) @ w_up).astype(f32)) - 0.5
    decay = jnp.exp(-jnp.exp(wlog))
    a = jax.nn.sigmoid((a0 + ad @ a_up).astype(f32))
    kf = k.astype(f32)
    kk = heads(kf * k_k.astype(f32))
    kk = kk / jnp.maximum(jnp.sqrt(jnp.sum(kk * kk, axis=-1, keepdims=True)), 1e-12)
    k_mod = heads(kf * (1.0 + (a - 1.0) * k_a.astype(f32)))
    a_h = heads(a)
    rf, vf = heads(r.astype(f32)), heads(v.astype(f32))
    S_new, y = wkv_scan(S0, rf, heads(decay), k_mod, vf, -kk, kk * a_h)
    mean = jnp.mean(y, axis=-1, keepdims=True)
    var = jnp.mean(jnp.square(y - mean), axis=-1, keepdims=True)
    yn = ((y - mean) * lax.rsqrt(var + GN_EPS)).reshape(B, T, RWKV_WIDTH)
    yn = yn * ln_x_w.astype(f32) + ln_x_b.astype(f32)
    bonus = jnp.sum(rf * k_mod * r_k.astype(f32), axis=-1, keepdims=True) * vf
    out = (yn + bonus.reshape(B, T, RWKV_WIDTH)) * jax.nn.silu(gate.astype(f32))
    return out.astype(xs.dtype), S_new, xs[:, -1:]


def setup_inputs(seed: int = 0) -> dict:
    key = jax.random.key(seed)
    ks = jax.random.split(key, 32)
    nrm = jax.random.normal
    n_pages = PAST_LEN // PAGE_SIZE
    n_used = DEC_BATCH * n_pages
    n_phys = (n_used * 5 + 3) // 4
    page_table = jax.random.permutation(ks[0], n_phys)[:n_used].reshape(DEC_BATCH, n_pages).astype(jnp.int32)
    return {
        'x_prompt': nrm(ks[1], (BATCH, SEQ, D_MODEL), jnp.float32),
        'x_sample': nrm(ks[2], (DEC_BATCH, DEC_SEQ, D_MODEL), jnp.float32),
        'cache_k': nrm(ks[3], (n_phys, PAGE_SIZE, N_KV_HEADS, HEAD_DIM), jnp.float32),
        'cache_v': nrm(ks[4], (n_phys, PAGE_SIZE, N_KV_HEADS, HEAD_DIM), jnp.float32),
        'cache_kidx': nrm(ks[5], (n_phys, PAGE_SIZE, IDX_DIM), jnp.float32),
        'state_wkv': 0.5 * nrm(ks[6], (DEC_BATCH, N_RWKV_HEADS, HEAD_DIM, HEAD_DIM), jnp.float32),
        'state_shift': nrm(ks[7], (DEC_BATCH, 1, SHIFT_W), jnp.float32),
        'page_table': page_table,
        'c_prompt': nrm(ks[8], (BATCH, D_MODEL), jnp.float32),
        'c_sample': nrm(ks[9], (DEC_BATCH, D_MODEL), jnp.float32),
        'norm_w': 1.0 + 0.1 * nrm(ks[10], (D_MODEL,), jnp.float32),
        'w_ada': 0.2 * D_MODEL ** -0.5 * nrm(ks[11], (D_MODEL, 3 * D_MODEL), jnp.float32),
        'b_ada': 0.02 * nrm(ks[12], (3 * D_MODEL,), jnp.float32),
        'w_in': D_MODEL ** -0.5 * nrm(ks[13], (D_MODEL, D_IN), jnp.float32),
        'q_norm_w': 1.0 + 0.1 * nrm(ks[14], (HEAD_DIM,), jnp.float32),
        'k_norm_w': 1.0 + 0.1 * nrm(ks[15], (HEAD_DIM,), jnp.float32),
        'mu_shift': jax.random.uniform(ks[16], (SHIFT_W,), jnp.float32),
        'w0': 0.5 + 0.5 * nrm(ks[17], (RWKV_WIDTH,), jnp.float32),
        'w_up': 0.1 * DECAY_LORA ** -0.5 * nrm(ks[18], (DECAY_LORA, RWKV_WIDTH), jnp.float32),
        'a0': 0.1 * nrm(ks[19], (RWKV_WIDTH,), jnp.float32),
        'a_up': 0.1 * ICLR_LORA ** -0.5 * nrm(ks[20], (ICLR_LORA, RWKV_WIDTH), jnp.float32),
        'k_k': 0.85 + 0.05 * nrm(ks[21], (RWKV_WIDTH,), jnp.float32),
        'k_a': 1.0 + 0.05 * nrm(ks[22], (RWKV_WIDTH,), jnp.float32),
        'r_k': 0.1 * nrm(ks[23], (N_RWKV_HEADS, HEAD_DIM), jnp.float32),
        'ln_x_w': 1.0 + 0.1 * nrm(ks[24], (RWKV_WIDTH,), jnp.float32),
        'ln_x_b': 0.02 * nrm(ks[25], (RWKV_WIDTH,), jnp.float32),
        'w_out': D_MODEL ** -0.5 * nrm(ks[26], (D_MODEL, D_MODEL), jnp.float32),
    }


def reference(x_prompt, x_sample, cache_k, cache_v, cache_kidx, state_wkv, state_shift, page_table,
              c_prompt, c_sample, norm_w, w_ada, b_ada, w_in, q_norm_w, k_norm_w, mu_shift, w0, w_up,
              a0, a_up, k_k, k_a, r_k, ln_x_w, ln_x_b, w_out):
    rw = (mu_shift, w0, w_up, a0, a_up, k_k, k_a, r_k, ln_x_w, ln_x_b)
    sdt = state_wkv.dtype

    x = x_prompt
    for _ in range(DEPTH):
        B, T, _ = x.shape
        h, gate_c = adaln_in(x, c_prompt, norm_w, w_ada, b_ada)
        cols = split_cols(h @ w_in)
        pos = jnp.arange(T)
        q, k_p, v_p, qi, wi, ki_p = attn_inputs(cols[:6], pos, q_norm_w, k_norm_w)
        bidx = jnp.arange(B)[:, None, None]
        topk = min(TOPK_MAX, T // 4)

        def prompt_block(i):
            start = i * Q_BLOCK
            sl = lambda t: lax.dynamic_slice_in_dim(t, start, Q_BLOCK, axis=1)
            return sparse_attend(sl(q), sl(qi), sl(wi), start + jnp.arange(Q_BLOCK), ki_p,
                                 lambda sel: (k_p[bidx, sel], v_p[bidx, sel]), topk)

        att = lax.map(prompt_block, jnp.arange(T // Q_BLOCK))
        att = jnp.moveaxis(att, 0, 1).reshape(B, T, ATT_WIDTH) * jax.nn.silu(cols[6])
        rw_out, S_p, shift_p = rwkv_branch(jnp.concatenate(cols[7:12], axis=-1),
                                           jnp.zeros((B, 1, SHIFT_W), x.dtype),
                                           jnp.zeros((B, N_RWKV_HEADS, HEAD_DIM, HEAD_DIM), jnp.float32),
                                           cols[12], *rw)
        mix = jnp.concatenate([att, rw_out], axis=-1) @ w_out
        x = x + gate_c[:, None, :] * mix
    y_prompt = x

    x = x_sample
    for _ in range(DEPTH):
        Bd, Qn, _ = x.shape
        past_len = page_table.shape[1] * PAGE_SIZE
        h, gate_c = adaln_in(x, c_sample, norm_w, w_ada, b_ada)
        cols = split_cols(h @ w_in)
        qpos = past_len + jnp.arange(Qn)
        q, k_s, v_s, qi, wi, ki_s = attn_inputs(cols[:6], qpos, q_norm_w, k_norm_w)
        bidx = jnp.arange(Bd)[:, None, None]
        kidx_all = jnp.concatenate([cache_kidx[page_table].reshape(Bd, past_len, IDX_DIM),
                                    ki_s.astype(cache_kidx.dtype)], axis=1)
        k_flat = cache_k.reshape(-1, N_KV_HEADS, HEAD_DIM)
        v_flat = cache_v.reshape(-1, N_KV_HEADS, HEAD_DIM)

        def gather_sample(sel):
            in_past = (sel < past_len)[..., None, None]
            sp = jnp.minimum(sel, past_len - 1)
            rows = page_table[bidx, sp // PAGE_SIZE] * PAGE_SIZE + sp % PAGE_SIZE
            j = jnp.clip(sel - past_len, 0, Qn - 1)
            k_sel = jnp.where(in_past, k_flat[rows], k_s[bidx, j].astype(k_flat.dtype))
            v_sel = jnp.where(in_past, v_flat[rows], v_s[bidx, j].astype(v_flat.dtype))
            return k_sel, v_sel

        topk = min(TOPK_MAX, (past_len + Qn) // 4)
        att = sparse_attend(q, qi, wi, qpos, kidx_all, gather_sample, topk)
        att = att.astype(x.dtype) * jax.nn.silu(cols[6])
        rw_out, S_s, shift_s = rwkv_branch(jnp.concatenate(cols[7:12], axis=-1), state_shift, state_wkv,
                                           cols[12], *rw)
        mix = jnp.concatenate([att, rw_out.astype(att.dtype)], axis=-1) @ w_out
        x = x + gate_c[:, None, :] * mix
    y_sample = x

    return (y_prompt, y_sample, k_p, v_p, ki_p, S_p.astype(sdt), shift_p.astype(state_shift.dtype),
            k_s, v_s, ki_s, S_s.astype(sdt), shift_s.astype(state_shift.dtype))
```

```python
import os
import numpy as np
from contextlib import ExitStack
import concourse.bass as bass
import concourse.mybir as mybir
from concourse.bass_utils import run_bass_kernel_spmd

F32 = mybir.dt.float32
BF16 = mybir.dt.bfloat16
I32 = mybir.dt.int32
AF = mybir.ActivationFunctionType
ALU = mybir.AluOpType
AX = mybir.AxisListType

ENGS = ("pe", "act", "dve", "pool", "sp")
NDMA = 32
NSW = 8

D = 1024
HD = 64
DIN = 4040
C_Q, C_K, C_V, C_QI, C_WI, C_KI, C_GA = 0, 512, 640, 768, 1280, 1288, 1352
C_R, C_RK, C_RV, C_WD, C_AD, C_GR = 1864, 2376, 2888, 3400, 3464, 3528
SHW = 1664
NORM_EPS = 1e-6
GN_EPS = 64e-5
ROPE_THETA = 500000.0


PSUM_RES = {"PTb", "F2", "R2", "K2", "V2"}


class Res:
    __slots__ = ("w", "r")

    def __init__(self):
        self.w = None
        self.r = []


class Bld:
    def __init__(self, nc, es):
        self.nc = nc
        self.es = es
        self.sem = {e: es.enter_context(nc.semaphore("s_" + e)) for e in ENGS}
        self.dsem = [es.enter_context(nc.semaphore("d%d" % i)) for i in range(NDMA)]
        self.dval = [0] * NDMA
        self.dnext = 0
        self.dnext_sw = 0
        self.cnt = {e: 0 for e in ENGS}
        self.waited = {e: {} for e in ENGS}
        self.ops = {e: [] for e in ENGS}
        self.res = {}

    def sb(self, name, shape, dt=F32):
        return self.es.enter_context(self.nc.sbuf_tensor("sb_" + name, list(shape), dt))

    def ps(self, name, shape, dt=F32):
        return self.es.enter_context(self.nc.psum_tensor("ps_" + name, list(shape), dt))

    def _r(self, key):
        r = self.res.get(key)
        if r is None:
            r = self.res[key] = Res()
        return r

    def _need(self, e, tok, waits):
        if tok is None:
            return
        key, val = tok
        if key == "pe" and e == "pe":
            return
        if self.waited[e].get(key, 0) >= val:
            return
        self.waited[e][key] = val
        waits.append((key, val))

    def op(self, e, fn, reads=(), writes=(), dma=False):
        if e == "pool" and not dma:
            e = "dve"
        pr = [k for k in reads if k in PSUM_RES]
        if pr:
            reads = [k for k in reads if k not in PSUM_RES]
            writes = list(writes) + pr
        waits = []
        for k in reads:
            self._need(e, self._r(k).w, waits)
        for k in writes:
            r = self._r(k)
            self._need(e, r.w, waits)
            for t in r.r:
                self._need(e, t, waits)
        if dma:
            if e == "pool":
                i = NDMA - NSW + self.dnext_sw
                self.dnext_sw = (self.dnext_sw + 1) % NSW
            else:
                i = self.dnext
                self.dnext = (self.dnext + 1) % (NDMA - NSW)
            if self.dval[i] > 0:
                self._need(e, (("d", i), self.dval[i]), waits)
            self.dval[i] += 16
            tok = (("d", i), self.dval[i])
            inc = (self.dsem[i], 16)
        else:
            self.cnt[e] += 1
            tok = (e, self.cnt[e])
            inc = (self.sem[e], 1)
        self.ops[e].append((waits, fn, inc))
        for k in reads:
            self._r(k).r.append(tok)
        for k in writes:
            r = self._r(k)
            r.w = tok
            r.r = []
        return tok

    def dma(self, e, out, in_, reads=(), writes=()):
        return self.op(e, lambda g: g.dma_start(out=out, in_=in_), reads, writes, dma=True)

    def barrier(self):
        for e in ENGS:
            waits = []
            for e2 in ENGS:
                if e2 != e and self.cnt[e2] > 0:
                    self._need(e, (e2, self.cnt[e2]), waits)
            for i in range(NDMA):
                if self.dval[i] > 0:
                    self._need(e, (("d", i), self.dval[i]), waits)
            self.ops[e].append((waits, None, None))

    def emit(self):
        nc = self.nc
        with nc.Block() as block:
            def mk(e):
                def body(g):
                    for waits, fn, inc in self.ops[e]:
                        for key, val in waits:
                            s = self.dsem[key[1]] if isinstance(key, tuple) else self.sem[key]
                            g.wait_ge(s, val)
                        if fn is not None:
                            fn(g).then_inc(inc[0], inc[1])
                return body
            block.tensor(mk("pe"))
            block.scalar(mk("act"))
            block.vector(mk("dve"))
            block.gpsimd(mk("pool"))
            block.sync(mk("sp"))


def build(T=4096, NS=16, NPG=64, NPHYS=10240, topk_p=256, topk_s=256, n_bis=24, do_sample=True, AW=34500, stop_after=None, nt_lim=None, no_lim=None):
    NT = T // 128
    NO = NT // 2
    NTILES = NT + NO + 1
    nc = bass.Bass("TRN2", target_bir_lowering=False)
    es = ExitStack()
    b = Bld(nc, es)

    def din(name, shape, dt=F32):
        return nc.dram_tensor(name, list(shape), dt, kind="ExternalInput").ap()

    def dout(name, shape, dt=F32):
        return nc.dram_tensor(name, list(shape), dt, kind="ExternalOutput").ap()

    xall = din("xall", [NTILES * 128, D])
    call = din("call", [17, D])
    w_in = din("w_in", [D, DIN])
    w_ada = din("w_ada", [D, 3 * D])
    b_ada = din("b_ada", [3 * D])
    norm_w = din("norm_w", [D])
    w_out = din("w_out", [D, D])
    qnw = din("qnw", [HD])
    knw = din("knw", [HD])
    mu = din("mu", [SHW])
    pw0 = din("w0", [512]); pa0 = din("a0", [512]); pkk = din("k_k", [512]); pka = din("k_a", [512])
    prk = din("r_k", [512]); plnw = din("ln_x_w", [512]); plnb = din("ln_x_b", [512])
    w_up = din("w_up", [64, 512]); a_up = din("a_up", [64, 512])
    identf_d = din("identf", [128, 128])
    cs_all = din("cs_all", [NTILES * 128, 16])
    parsel_d = din("parsel", [128, 2])
    qrel_d = din("qrel", [128, 1])
    ownidx_d = din("ownidx", [128, NO], I32)
    iota_d = din("iota256", [128, 256])
    maskT_d = din("maskT", [64, 256])
    maskL_d = din("maskL", [64, 64])
    reset_d = din("resetm", [64, 1024])
    sel16_d = din("sel16", [17, 128])
    ones64_d = din("ones64", [64, 64])

    swkv_d = din("swkv", [128, 4096]); sshift_d = din("sshift", [16, SHW]); ptab_d = din("ptab", [128, 8], I32)
    cache_k = din("cache_k", [NPHYS * 128, 128]); cache_v = din("cache_v", [NPHYS * 128, 128])
    cache_ki = din("cache_kidx", [NPHYS, 8192])
    rep_d = din("rep", [16, 8 * 128]); repT_d = din("repT", [128, 8 * 16]); blk_d = din("blk", [128, 128]); oh0_d = din("oh0", [128, 1])
    y_s = dout("y_s", [16, D]); k_s = dout("k_s", [16, 128]); v_s = dout("v_s", [16, 128]); ki_s = dout("ki_s", [16, 64])
    wkv_s = dout("wkv_s", [128, 4096]); shift_s = dout("shift_s", [16, SHW])
    gscr = nc.dram_tensor("gscr", [16, D], F32, kind="Internal").ap()
    scr1 = nc.dram_tensor("scr1", [16, 3072], F32, kind="Internal").ap()
    scr2 = nc.dram_tensor("scr2", [128, 64], F32, kind="Internal").ap()
    scr3 = nc.dram_tensor("scr3", [16, 512], F32, kind="Internal").ap()
    y_own = dout("y_own", [NO * 128, D])
    k_nat = dout("k_nat", [T, 128]); v_nat = dout("v_nat", [T, 128]); ki_nat = dout("ki_nat", [T, 64])
    wkv_p = dout("wkv_p", [8, 64, 64]); shift_p = dout("shift_p", [SHW])
    rwscr = nc.dram_tensor("rwscr", [T, 512], F32, kind="Internal").ap()

    PTb = b.ps("PTb", [128, 1024], BF16)
    F2 = b.ps("F2", [128, 512])
    R2 = b.ps("R2", [128, 1024])
    K2 = b.ps("K2", [128, 1024])
    V2 = b.ps("V2", [128, 1024])

    identf = b.sb("identf", [128, 128]); identb = b.sb("identb", [128, 128], BF16)
    cst = b.sb("cst", [128, 4])
    kT_all = b.sb("kT_all", [64, 2, T], BF16)
    kiT_all = b.sb("kiT_all", [64, T], BF16)
    Vaug = b.sb("Vaug", [128, NT, 2, 65], BF16)
    modT = b.sb("modT", [128, 24, 17])
    g1 = b.sb("g1", [128, 8, 17])
    nwT = b.sb("nwT", [128, 8]); badaT = b.sb("badaT", [128, 24])
    gate_bc = b.sb("gate_bc", [128, D])
    lnw_bc = b.sb("lnw_bc", [64, 512]); lnb_bc = b.sb("lnb_bc", [64, 512])
    qnw_bc = b.sb("qnw_bc", [128, 64]); knw_bc = b.sb("knw_bc", [128, 64])
    sel16 = b.sb("sel16", [17, 128]); ones64 = b.sb("ones64", [64, 64])
    maskT = b.sb("maskT", [64, 256]); maskL = b.sb("maskL", [64, 64]); resetm = b.sb("resetm", [64, 1024])
    iota256 = b.sb("iota256", [128, 256]); qrel = b.sb("qrel", [128, 1]); parsel = b.sb("parsel", [128, 2]); ownidx = b.sb("ownidx", [128, NO], I32)
    fp = {}
    for nm in ("w0", "a0", "kk", "ka", "rk"):
        fp[nm] = b.sb("fp_" + nm, [64, 8])
    muT = b.sb("muT", [64, 26]); wupS = b.sb("wupS", [64, 512]); aupS = b.sb("aupS", [64, 512])
    xt0 = b.sb("xt0", [128, D]); xt = [xt0, xt0]
    xn = b.sb("xn", [128, D], BF16)
    hT0 = b.sb("hT0", [128, 8, 128], BF16); hT = [hT0, hT0]
    hTs = b.sb("hTs", [128, 8, 128], BF16)
    hlast = b.sb("hlast", [128, 8, 1], BF16)
    cs_t = b.sb("cs_t", [128, 16])
    sm = b.sb("sm", [128, 64])
    ARENA = b.sb("ARENA", [128, AW])
    csT = b.sb("csT", [128, 8, 17])

    def TT(e, out, in0, in1, op, R, W):
        b.op(e, lambda g: g.tensor_tensor(out=out, in0=in0, in1=in1, op=op), R, W)

    def TS(e, out, in0, s1, s2, op0, op1, R, W, accum=None):
        if op1 is None:
            b.op(e, lambda g: g.tensor_scalar(out=out, in0=in0, scalar1=s1, scalar2=None, op0=op0), R, W)
        elif accum is None:
            b.op(e, lambda g: g.tensor_scalar(out=out, in0=in0, scalar1=s1, scalar2=s2, op0=op0, op1=op1), R, W)
        else:
            b.op(e, lambda g: g.tensor_scalar(out=out, in0=in0, scalar1=s1, scalar2=s2, op0=op0, op1=op1,
                                              accum_out=accum), R, W)

    def STT(out, in0, scalar, in1, op0, op1, R, W):
        b.op("dve", lambda g: g.scalar_tensor_tensor(out=out, in0=in0, scalar=scalar, in1=in1, op0=op0, op1=op1), R, W)

    def ACT(out, in_, func, R, W, scale=1.0, bias=None, accum=None):
        kw = {}
        if bias is not None:
            kw["bias"] = bias
        if accum is not None:
            kw["accum_out"] = accum
        b.op("act", lambda g: g.activation(out=out, in_=in_, func=func, scale=scale, **kw), R, W)

    def MM(out, lhsT, rhs, start, stop, R, W):
        b.op("pe", lambda g: g.matmul(out=out, lhsT=lhsT, rhs=rhs, start=start, stop=stop), R, W)

    def TR(out, in_, ident, R, W):
        b.op("pe", lambda g: g.transpose(out=out, in_=in_, identity=ident), R, W)

    def CP(e, out, in_, R, W):
        if e == "act":
            b.op(e, lambda g: g.copy(out=out, in_=in_), R, W)
        else:
            b.op(e, lambda g: g.tensor_copy(out=out, in_=in_), R, W)

    def RED(out, in_, op, R, W, axis=AX.X):
        b.op("dve", lambda g: g.tensor_reduce(out=out, in_=in_, axis=axis, op=op), R, W)

    def MS(e, ap, val, W):
        b.op(e, lambda g: g.memset(ap, val), (), W)

    def bc(ap, shape):
        return ap.to_broadcast(list(shape))

    ncd = nc.allow_non_contiguous_dma(reason="small parameter layouts")
    ncd.__enter__()

    b.dma("sp", identf[:], identf_d[:, :], writes=["identf"])
    CP("dve", identb[:], identf[:], ["identf"], ["identb"])
    MS("dve", cst[:, 0:1], NORM_EPS, ["cst"]); MS("dve", cst[:, 1:2], GN_EPS, ["cst"]); MS("dve", cst[:, 2:3], 1e-24, ["cst"])
    for (t_, d_, nm) in ((sel16, sel16_d, "sel16"), (ones64, ones64_d, "ones64"), (maskT, maskT_d, "maskT"),
                         (maskL, maskL_d, "maskL"), (resetm, reset_d, "resetm"), (iota256, iota_d, "iota256"),
                         (qrel, qrel_d, "qrel"), (parsel, parsel_d, "parsel"), (ownidx, ownidx_d, "ownidx"), (wupS, w_up, "wupS"), (aupS, a_up, "aupS")):
        b.dma("sp", t_[:], d_[:, :], writes=[nm])
    for nm, src in (("w0", pw0), ("a0", pa0), ("kk", pkk), ("ka", pka), ("rk", prk)):
        b.dma("sp", fp[nm][:], src.rearrange("(h j) -> j h", j=64), writes=["fp_" + nm])
    b.dma("sp", muT[:], mu.rearrange("(c j) -> j c", j=64), writes=["muT"])
    b.dma("sp", nwT[:], norm_w.rearrange("(k p) -> p k", p=128), writes=["nwT"])
    b.dma("sp", badaT[:], b_ada.rearrange("(t p) -> p t", p=128), writes=["badaT"])
    b.dma("sp", lnw_bc[:], plnw.partition_broadcast(64), writes=["lnw_bc"])
    b.dma("sp", lnb_bc[:], plnb.partition_broadcast(64), writes=["lnb_bc"])
    b.dma("sp", qnw_bc[:], qnw.partition_broadcast(128), writes=["qnw_bc"])
    b.dma("sp", knw_bc[:], knw.partition_broadcast(128), writes=["knw_bc"])
    MS("pool", Vaug[:, :, :, 64:65], 1.0, ["Vaug"])

    def carve(off, shape, dt=F32):
        n = int(np.prod(shape[1:]))
        words = n if dt in (F32, I32) else (n + 1) // 2
        v = ARENA[0:shape[0], off:off + words]
        if dt != F32:
            v = v.bitcast(dt)
        if len(shape) == 3:
            v = v.rearrange("p (a b) -> p a b", a=shape[1])
        elif len(shape) == 4:
            v = v.rearrange("p (a b c) -> p a b c", a=shape[1], b=shape[2])
        return v, off + words

    off = 0
    Wn, off = carve(off, [128, 8, 832], BF16)
    Wm, off = carve(off, [128, 8, SHW], BF16)
    Wom, off = carve(off, [128, 8, SHW], BF16)
    W_end = off
    stg, off = carve(off, [128, 8, 512])
    mu_bc, off = carve(off, [128, SHW])
    omu_bc, off = carve(off, [128, SHW])
    gtok, off = carve(off, [17, D])
    bgate, off = carve(off, [17, D])
    csall_sil, off = carve(off, [17, D])

    b.dma("sp", mu_bc, mu.partition_broadcast(128), writes=["mu_bc"])
    b.dma("sp", bgate, b_ada[2 * D:3 * D].partition_broadcast(17), writes=["bgate"])
    TS("dve", omu_bc, mu_bc, -1.0, 1.0, ALU.mult, ALU.add, ["mu_bc"], ["omu_bc"])
    w_in_v = w_in.rearrange("(k p) c -> p k c", p=128)

    def load_cols(dst, dcol, c0, n, scale_bc=None, scale_off=0, tag=""):
        done = 0
        while done < n:
            w = min(512, n - done)
            b.dma("sp", stg[:, :, 0:w], w_in_v[:, :, c0 + done:c0 + done + w], writes=["stg"])
            if scale_bc is None:
                CP("pool", dst[:, :, dcol + done:dcol + done + w], stg[:, :, 0:w], ["stg"], [tag])
            else:
                for sname, sbcv, d2 in scale_bc:
                    TT("dve", d2[:, :, dcol + done:dcol + done + w], stg[:, :, 0:w],
                       bc(sbcv[:, scale_off + done:scale_off + done + w].unsqueeze(1), [128, 8, w]),
                       ALU.mult, ["stg", sname], [tag])
            done += w

    load_cols(Wn, 0, C_K, 256, tag="Wn")
    load_cols(Wn, 256, C_KI, 64, tag="Wn")
    load_cols(Wn, 320, C_GR, 512, tag="Wn")
    load_cols(None, 0, C_R, SHW, scale_bc=[("mu_bc", mu_bc, Wm), ("omu_bc", omu_bc, Wom)], tag="Wm")
    calt = sm
    b.dma("sp", csall_sil, call[:, :], writes=["csil"])
    ACT(csall_sil, csall_sil, AF.Silu, ["csil"], ["csil"])
    for k in range(8):
        TR(F2[:, k * 17:(k + 1) * 17], csall_sil[:, k * 128:(k + 1) * 128], identf[0:17, 0:17], ["csil", "identf"], ["F2"])
    CP("dve", csT[:], F2[:, 0:136].rearrange("p (k m) -> p k m", k=8), ["F2"], ["csT"])
    w_ada_v = w_ada.rearrange("(k p) c -> p k c", p=128)
    for ch in range(6):
        b.dma("sp", stg[:, :, :], w_ada_v[:, :, ch * 512:(ch + 1) * 512], writes=["stg"])
        for ct in range(4):
            for k in range(8):
                MM(R2[:, ct * 17:(ct + 1) * 17], stg[:, k, ct * 128:(ct + 1) * 128], csT[:, k, :], k == 0, k == 7,
                   ["stg", "csT"], ["R2"])
        TT("dve", modT[:, ch * 4:(ch + 1) * 4, :], R2[:, 0:68].rearrange("p (c m) -> p c m", c=4),
           bc(badaT[:, ch * 4:(ch + 1) * 4].unsqueeze(2), [128, 4, 17]), ALU.add, ["R2", "badaT"], ["modT"])
        if ch >= 4:
            for k in range(8):
                MM(K2[0:17, 0:512], csT[:, k, :], stg[:, k, :], k == 0, k == 7, ["stg", "csT"], ["K2"])
            TT("dve", gtok[:, (ch - 4) * 512:(ch - 3) * 512], K2[0:17, 0:512], bgate[:, (ch - 4) * 512:(ch - 3) * 512],
               ALU.add, ["K2", "bgate"], ["gtok"])
    b.dma("sp", gscr[:, :], gtok[0:16, :], reads=["gtok"], writes=["gscr"])
    STT(g1[:], modT[:, 8:16, :], 1.0, bc(nwT[:].unsqueeze(2), [128, 8, 17]), ALU.add, ALU.mult, ["modT", "nwT"], ["g1"])
    for hh in range(2):
        MM(R2[:, 0:512], sel16[:, :], gtok[:, hh * 512:(hh + 1) * 512], True, True, ["sel16", "gtok"], ["R2"])
        CP("dve", gate_bc[:, hh * 512:(hh + 1) * 512], R2[:, 0:512], ["R2"], ["gate_bc"])

    def front(ti, par, m_prompt=True, ntok=128):
        x_ = xt[par]
        h_ = hT[par]
        xr, hr = "xt0", "hT0"
        b.dma("sp", x_[:], xall[ti * 128:(ti + 1) * 128, :], writes=[xr])
        b.dma("sp", cs_t[:], cs_all[ti * 128:(ti + 1) * 128, :], writes=["cs_t"])
        ACT(xn[:], x_[:], AF.Square, [xr], ["xn", "sm"], accum=sm[:, 0:1])
        ACT(sm[:, 1:2], sm[:, 0:1], AF.Sqrt, ["sm", "cst"], ["sm"], scale=1.0 / D, bias=cst[:, 0:1])
        b.op("dve", lambda g: g.reciprocal(out=sm[:, 2:3], in_=sm[:, 1:2]), ["sm"], ["sm"])
        TS("dve", xn[:], x_[:], sm[:, 2:3], None, ALU.mult, None, [xr, "sm"], ["xn"])
        for k in range(8):
            TR(PTb[:, k * 128:(k + 1) * 128], xn[:, k * 128:(k + 1) * 128], identb[:], ["xn", "identb"], ["PTb"])
        pv = PTb[:, :].rearrange("p (k t) -> p k t", k=8)
        if m_prompt:
            TT("dve", h_[:], pv, bc(g1[:, :, 16:17], [128, 8, 128]), ALU.mult, ["PTb", "g1"], [hr])
            TT("pool", h_[:], h_[:], bc(modT[:, 0:8, 16:17], [128, 8, 128]), ALU.add, [hr, "modT"], [hr])
        else:
            TT("dve", h_[:, :, 0:ntok], pv[:, :, 0:ntok], g1[:, :, 0:ntok], ALU.mult, ["PTb", "g1"], [hr])
            TT("pool", h_[:, :, 0:ntok], h_[:, :, 0:ntok], modT[:, 0:8, 0:ntok], ALU.add, [hr, "modT"], [hr])
        return x_, h_, xr, hr

    def rope(e, buf, nh, hd_stride_view, R, nrows=128):
        x1 = buf[:, :, 0:8]
        x2 = buf[:, :, 8:16]
        cosb = bc(cs_t[0:nrows, 0:8].unsqueeze(1), [nrows, nh, 8])
        sinb = bc(cs_t[0:nrows, 8:16].unsqueeze(1), [nrows, nh, 8])
        t = ropet[0:nrows, 0:4 * nh * 8].rearrange("p (a h d) -> p a h d", a=4, h=nh)
        TT(e, t[:, 0], x1, cosb, ALU.mult, R + ["cs_t"], ["ropet"])
        TT(e, t[:, 1], x2, sinb, ALU.mult, R + ["cs_t"], ["ropet"])
        TT(e, t[:, 2], x2, cosb, ALU.mult, R + ["cs_t"], ["ropet"])
        TT(e, t[:, 3], x1, sinb, ALU.mult, R + ["cs_t"], ["ropet"])
        TT(e, x1, t[:, 0], t[:, 1], ALU.subtract, ["ropet"], R)
        TT(e, x2, t[:, 2], t[:, 3], ALU.add, ["ropet"], R)

    ropet = b.sb("ropet", [128, 256])

    def qknorm(src_ps, dst, nh, wbc, extra_scale, Rsrc, Wdst, nrows=128):
        sq = nrm_t[0:nrows, 0:nh * 64].rearrange("p (h d) -> p h d", h=nh)
        ACT(sq, src_ps, AF.Square, Rsrc, ["nrm_t"])
        RED(sm[0:nrows, 8:8 + nh], sq, ALU.add, ["nrm_t"], ["sm"])
        ACT(sm[0:nrows, 16:16 + nh], sm[0:nrows, 8:8 + nh], AF.Sqrt, ["sm", "cst"], ["sm"], scale=1.0 / 64, bias=cst[0:nrows, 0:1])
        b.op("dve", lambda g: g.reciprocal(out=sm[0:nrows, 24:24 + nh], in_=sm[0:nrows, 16:16 + nh]), ["sm"], ["sm"])
        TT("dve", dst, src_ps, bc(sm[0:nrows, 24:24 + nh].unsqueeze(2), [nrows, nh, 64]), ALU.mult, Rsrc + ["sm"], Wdst)
        STT(dst, dst, float(extra_scale), bc(wbc[0:nrows, :].unsqueeze(1), [nrows, nh, 64]), ALU.mult, ALU.mult, Wdst + ["qnw_bc", "knw_bc"], Wdst)

    nrm_t = b.sb("nrm_t", [128, 512])
    kfin = b.sb("kfin", [128, 128]); vfin = b.sb("vfin", [128, 128]); kifin = b.sb("kifin", [128, 64])
    gr_s = b.sb("gr_s", [128, 512])

    off = W_end
    rw = {}
    for nm in ("tw", "adc"):
        rw[nm], off = carve(off, [64, 128])
    for nm in ("sg", "asig", "L", "g", "ginv", "gprev", "kkn", "kmod", "t1"):
        rw[nm], off = carve(off, [64, 8, 128])
    rw["sg"] = rw["sg"]
    QTt, off = carve(off, [64, 8, 2, 128], BF16)
    KTt, off = carve(off, [64, 8, 2, 128], BF16)
    AM, off = carve(off, [64, 4, 256], BF16)
    Lm = [None, None]; Nm = [None, None]; Pm = [None, None]
    for i in range(2):
        Lm[i], off = carve(off, [64, 4, 64], BF16)
        Nm[i], off = carve(off, [64, 4, 64], BF16)
        Pm[i], off = carve(off, [64, 4, 64], BF16)
    BKtok, off = carve(off, [64, 4, 2, 64], BF16)
    Vc, off = carve(off, [64, 2, 8, 64], BF16)
    P0s, off = carve(off, [64, 4, 64], BF16)
    Us, off = carve(off, [64, 4, 64], BF16)
    H32, off = carve(off, [64, 8, 64])
    Hb, off = carve(off, [64, 8, 64], BF16)
    ych, off = carve(off, [64, 8, 64])
    yt1, off = carve(off, [64, 8, 64])
    bon, off = carve(off, [128, 8])
    rawl, off = carve(off, [64, 26])
    st8, off = carve(off, [64, 64])
    assert off <= AW, off
    identb64 = identb[0:64, 0:64]

    KR = int(os.environ.get('KR', '9'))
    KQ = int(os.environ.get('KQ', '9'))

    def rwkv_tile(ti, h_, hr):
        CP("pool", hTs[:, :, 1:128], h_[:, :, 0:127], [hr], ["hTs"])
        CP("pool", hTs[:, :, 0:1], hlast[:], ["hlast"], ["hTs"])
        CP("pool", hlast[:], h_[:, :, 127:128], [hr], ["hlast"])
        if KQ < 1:
            return
        for c in range(2):
            col = 1536 + c * 64
            for k in range(8):
                MM(F2[0:64, c * 128:(c + 1) * 128], Wom[:, k, col:col + 64], h_[:, k, :], k == 0, False, ["Wm", hr], ["F2"])
            for k in range(8):
                MM(F2[0:64, c * 128:(c + 1) * 128], Wm[:, k, col:col + 64], hTs[:, k, :], False, k == 7, ["Wm", "hTs"], ["F2"])
        if KQ < 2:
            return
        KW = int(os.environ.get('KW', '3'))
        if KW & 1:
            ACT(rw["tw"], F2[0:64, 0:128], AF.Tanh, ["F2"], ["tw"])
        if KW & 2:
            CP("dve", rw["adc"], F2[0:64, 128:256], ["F2"], ["adc"])
        if KR < 1:
            return
        R2v = R2[0:64, :].rearrange("p (h t) -> p h t", h=8)
        K2v = K2[0:64, :].rearrange("p (h t) -> p h t", h=8)
        V2v = V2[0:64, :].rearrange("p (h t) -> p h t", h=8)
        for h in range(8):
            MM(R2v[:, h, :], wupS[:, h * 64:(h + 1) * 64], rw["tw"], True, True, ["wupS", "tw"], ["R2"])
            MM(K2v[:, h, :], aupS[:, h * 64:(h + 1) * 64], rw["adc"], True, True, ["aupS", "adc"], ["K2"])
        TT("dve", rw["sg"], R2v, bc(fp["w0"][:].unsqueeze(2), [64, 8, 128]), ALU.add, ["R2", "fp_w0"], ["sg"])
        ACT(rw["sg"], rw["sg"], AF.Sigmoid, ["sg"], ["sg"])
        TT("dve", rw["asig"], K2v, bc(fp["a0"][:].unsqueeze(2), [64, 8, 128]), ALU.add, ["K2", "fp_a0"], ["asig"])
        ACT(rw["asig"], rw["asig"], AF.Sigmoid, ["asig"], ["asig"])
        TS("dve", rw["sg"], rw["sg"], -0.6065306597126334, None, ALU.mult, None, ["sg"], ["sg"])
        b.op("dve", lambda g: g.tensor_tensor_scan(out=rw["L"].rearrange("p h t -> p (h t)"), data0=resetm[:, :],
                                                   data1=rw["sg"].rearrange("p h t -> p (h t)"), initial=0.0,
                                                   op0=ALU.mult, op1=ALU.add), ["sg", "resetm"], ["L"])
        ACT(rw["g"], rw["L"], AF.Exp, ["L"], ["g"])
        ACT(rw["ginv"], rw["L"], AF.Exp, ["L"], ["ginv"], scale=-1.0)
        TT("pool", rw["gprev"], rw["L"], rw["sg"], ALU.subtract, ["L", "sg"], ["gprev"])
        ACT(rw["gprev"], rw["gprev"], AF.Exp, ["gprev"], ["gprev"])
        if KR < 2:
            return
        for (dstv, c0, nm) in ((R2v, 0, "R2"), (K2v, 512, "K2")):
            for h in range(8):
                col = c0 + h * 64
                for k in range(8):
                    MM(dstv[:, h, :], Wom[:, k, col:col + 64], h_[:, k, :], k == 0, False, ["Wm", hr], [nm])
                for k in range(8):
                    MM(dstv[:, h, :], Wm[:, k, col:col + 64], hTs[:, k, :], False, k == 7, ["Wm", "hTs"], [nm])
        TT("dve", rw["L"], K2v, bc(fp["kk"][:].unsqueeze(2), [64, 8, 128]), ALU.mult, ["K2", "fp_kk"], ["L"])
        ACT(rw["t1"], rw["L"], AF.Square, ["L"], ["t1"])
        t1f = rw["t1"].rearrange("p h t -> p (h t)")
        for hh in range(2):
            MM(V2[0:64, hh * 512:(hh + 1) * 512], ones64[:, :], t1f[:, hh * 512:(hh + 1) * 512], True, True, ["ones64", "t1"], ["V2"])
        ACT(rw["t1"], V2v, AF.Sqrt, ["V2", "cst"], ["t1"], bias=cst[0:64, 2:3])
        b.op("dve", lambda g: g.reciprocal(out=rw["t1"], in_=rw["t1"]), ["t1"], ["t1"])
        TT("dve", rw["kkn"], rw["L"], rw["t1"], ALU.mult, ["L", "t1"], ["kkn"])
        STT(rw["t1"], rw["asig"], -1.0, bc(fp["ka"][:].unsqueeze(2), [64, 8, 128]), ALU.add, ALU.mult, ["asig", "fp_ka"], ["t1"])
        STT(rw["kmod"], rw["t1"], 1.0, K2v, ALU.add, ALU.mult, ["t1", "K2"], ["kmod"])
        if KR < 3:
            return
        QTv = QTt.rearrange("p h c (q t) -> p h c q t", q=2)
        KTv = KTt.rearrange("p h c (q t) -> p h c q t", q=2)

        def ch(v):
            return v.rearrange("p h (c t) -> p h c t", c=2)
        STT(QTv[:, :, :, 0, :], ch(rw["kkn"]), -1.0, ch(rw["gprev"]), ALU.mult, ALU.mult, ["kkn", "gprev"], ["QTt"])
        TT("dve", QTv[:, :, :, 1, :], ch(R2v), ch(rw["g"]), ALU.mult, ["R2", "g"], ["QTt"])
        TT("pool", rw["t1"], rw["kkn"], rw["asig"], ALU.mult, ["kkn", "asig"], ["t1"])
        TT("pool", KTv[:, :, :, 0, :], ch(rw["t1"]), ch(rw["ginv"]), ALU.mult, ["t1", "ginv"], ["KTt"])
        TT("pool", KTv[:, :, :, 1, :], ch(rw["kmod"]), ch(rw["ginv"]), ALU.mult, ["kmod", "ginv"], ["KTt"])
        TT("dve", rw["L"], R2v, bc(fp["rk"][:].unsqueeze(2), [64, 8, 128]), ALU.mult, ["R2", "fp_rk"], ["L"])
        TT("dve", rw["L"], rw["L"], rw["kmod"], ALU.mult, ["L", "kmod"], ["L"])
        for h in range(8):
            MM(F2[:, 256 + h:257 + h], rw["L"][:, h, :], ones64[:, 0:1], True, True, ["L", "ones64"], ["F2"])
        CP("dve", bon, F2[:, 256:264], ["F2"], ["bon"])
        if KR < 4:
            return
        for h in range(8):
            col = 1024 + h * 64
            for k in range(8):
                MM(V2v[:, h, :], Wom[:, k, col:col + 64], h_[:, k, :], k == 0, False, ["Wm", hr], ["V2"])
            for k in range(8):
                MM(V2v[:, h, :], Wm[:, k, col:col + 64], hTs[:, k, :], False, k == 7, ["Wm", "hTs"], ["V2"])
        CP("act", rw["sg"], V2v, ["V2"], ["sg"])
        V2c = V2[0:64, :].rearrange("p (c h i) -> p c h i", c=2, h=8)
        for c in range(2):
            for h in range(8):
                TR(V2c[:, c, h, :], rw["sg"][:, h, c * 64:(c + 1) * 64], identf[0:64, 0:64], ["sg", "identf"], ["V2"])
        CP("act", Vc, V2c, ["V2"], ["Vc"])
        if int(os.environ.get("KLVL", "9")) < 3:
            return
        for c in range(2):
            for hg in range(2):
                AMp = K2[0:64, :].rearrange("p (h x) -> p h x", h=4)
                for hd in range(4):
                    h = hg * 4 + hd
                    MM(AMp[:, hd, 0:128], KTt[:, h, c, 0:64], QTt[:, h, c, :], True, True, ["KTt", "QTt"], ["K2"])
                    MM(AMp[:, hd, 128:256], KTt[:, h, c, 64:128], QTt[:, h, c, :], True, True, ["KTt", "QTt"], ["K2"])
                TT("dve", AM, AMp, bc(maskT[:].unsqueeze(1), [64, 4, 256]), ALU.mult, ["K2", "maskT"], ["AM"])
                Lp = R2[0:64, 0:256].rearrange("p (h x) -> p h x", h=4)
                for hd in range(4):
                    h = hg * 4 + hd
                    MM(Lp[:, hd, :], QTt[:, h, c, 0:64], KTt[:, h, c, 0:64], True, True, ["KTt", "QTt"], ["R2"])
                TT("dve", Lm[0], Lp, bc(maskL[:].unsqueeze(1), [64, 4, 64]), ALU.mult, ["R2", "maskL"], ["Lm0"])
                CP("pool", Nm[0], AM[:, :, 0:64], ["AM"], ["Nm0"])
                TT("pool", Pm[0], AM[:, :, 0:64], bc(identb64.unsqueeze(1), [64, 4, 64]), ALU.add, ["AM", "identb"], ["Pm0"])
                cur = 0
                for lvl in range(1, 6):
                    nx = 1 - cur
                    Np = R2[0:64, 0:256].rearrange("p (h x) -> p h x", h=4)
                    Lpp = R2[0:64, 256:512].rearrange("p (h x) -> p h x", h=4)
                    PPp = R2[0:64, 512:768].rearrange("p (h x) -> p h x", h=4)
                    for hd in range(4):
                        if lvl < 5:
                            MM(Np[:, hd, :], Lm[cur][:, hd, :], Nm[cur][:, hd, :], True, True, ["Lm%d" % cur, "Nm%d" % cur], ["R2"])
                        MM(Lpp[:, hd, :], Nm[cur][:, hd, :], Lm[cur][:, hd, :], True, True, ["Lm%d" % cur, "Nm%d" % cur], ["R2"])
                    if lvl < 5:
                        CP("act", Nm[nx], Np, ["R2"], ["Nm%d" % nx])
                    CP("dve", Lm[nx], Lpp, ["R2"], ["Lm%d" % nx])
                    for hd in range(4):
                        MM(PPp[:, hd, :], Lm[nx][:, hd, :], Pm[cur][:, hd, :], True, True, ["Lm%d" % nx, "Pm%d" % cur], ["R2"])
                    TT("dve", Pm[nx], PPp, Pm[cur], ALU.add, ["R2", "Pm%d" % cur], ["Pm%d" % nx])
                    cur = nx
                P6 = Pm[cur]
                P6n = "Pm%d" % cur
                BKp = PTb[0:64, 0:512].rearrange("p (h q j) -> p h q j", h=4, q=2)
                for hd in range(4):
                    h = hg * 4 + hd
                    TR(BKp[:, hd, 0, :], KTt[:, h, c, 0:64], identb64, ["KTt", "identb"], ["PTb"])
                    TR(BKp[:, hd, 1, :], KTt[:, h, c, 64:128], identb64, ["KTt", "identb"], ["PTb"])
                CP("act", BKtok, BKp, ["PTb"], ["BKtok"])
                P0p = F2[0:64, 0:256].rearrange("p (h i) -> p h i", h=4)
                Up = F2[0:64, 256:512].rearrange("p (h i) -> p h i", h=4)
                Yp = V2[0:64, 0:256].rearrange("p (h i) -> p h i", h=4)
                Hp = V2[0:64, 512:768].rearrange("p (h i) -> p h i", h=4)
                for hd in range(4):
                    h = hg * 4 + hd
                    MM(P0p[:, hd, :], QTt[:, h, c, 0:64], Hb[:, h, :], True, False, ["QTt", "Hb"], ["F2"])
                    MM(P0p[:, hd, :], AM[:, hd, 128:192], Vc[:, c, h, :], False, True, ["AM", "Vc"], ["F2"])
                CP("act", P0s, P0p, ["F2"], ["P0s"])
                for hd in range(4):
                    MM(Up[:, hd, :], P6[:, hd, :], P0s[:, hd, :], True, True, [P6n, "P0s"], ["F2"])
                CP("act", Us, Up, ["F2"], ["Us"])
                for hd in range(4):
                    h = hg * 4 + hd
                    MM(Yp[:, hd, :], QTt[:, h, c, 64:128], Hb[:, h, :], True, False, ["QTt", "Hb"], ["V2"])
                    MM(Yp[:, hd, :], AM[:, hd, 64:128], Us[:, hd, :], False, False, ["AM", "Us"], ["V2"])
                    MM(Yp[:, hd, :], AM[:, hd, 192:256], Vc[:, c, h, :], False, True, ["AM", "Vc"], ["V2"])
                    MM(Hp[:, hd, :], BKtok[:, hd, 0, :], Us[:, hd, :], True, False, ["BKtok", "Us"], ["V2"])
                    MM(Hp[:, hd, :], BKtok[:, hd, 1, :], Vc[:, c, h, :], False, True, ["BKtok", "Vc"], ["V2"])
                CP("act", ych[:, hg * 4:(hg + 1) * 4, :], Yp, ["V2"], ["ych"])
                TT("dve", H32[:, hg * 4:(hg + 1) * 4, :], H32[:, hg * 4:(hg + 1) * 4, :], Hp, ALU.add, ["H32", "V2"], ["H32"])
                TT("dve", H32[:, hg * 4:(hg + 1) * 4, :], H32[:, hg * 4:(hg + 1) * 4, :],
                   bc(rw["g"][:, hg * 4:(hg + 1) * 4, c * 64 + 63:c * 64 + 64], [64, 4, 64]), ALU.mult, ["H32", "g"], ["H32"])
                CP("act", Hb[:, hg * 4:(hg + 1) * 4, :], H32[:, hg * 4:(hg + 1) * 4, :], ["H32"], ["Hb"])
            RED(st8[:, 0:8], ych, ALU.add, ["ych"], ["st8"])
            TT("dve", yt1, ych, ych, ALU.mult, ["ych"], ["yt1"])
            RED(st8[:, 8:16], yt1, ALU.add, ["yt1"], ["st8"])
            TS("dve", st8[:, 0:16], st8[:, 0:16], 1.0 / 64, None, ALU.mult, None, ["st8"], ["st8"])
            TT("dve", st8[:, 16:24], st8[:, 0:8], st8[:, 0:8], ALU.mult, ["st8"], ["st8"])
            TT("dve", st8[:, 24:32], st8[:, 8:16], st8[:, 16:24], ALU.subtract, ["st8"], ["st8"])
            ACT(st8[:, 32:40], st8[:, 24:32], AF.Sqrt, ["st8", "cst"], ["st8"], bias=cst[0:64, 1:2])
            b.op("dve", lambda g: g.reciprocal(out=st8[:, 40:48], in_=st8[:, 32:40]), ["st8"], ["st8"])
            TT("dve", yt1, ych, bc(st8[:, 0:8].unsqueeze(2), [64, 8, 64]), ALU.subtract, ["ych", "st8"], ["yt1"])
            TT("dve", yt1, yt1, bc(st8[:, 40:48].unsqueeze(2), [64, 8, 64]), ALU.mult, ["yt1", "st8"], ["yt1"])
            lnwv = lnw_bc[:].rearrange("p (h i) -> p h i", h=8)
            lnbv = lnb_bc[:].rearrange("p (h i) -> p h i", h=8)
            TT("dve", yt1, yt1, lnwv, ALU.mult, ["yt1", "lnw_bc"], ["yt1"])
            TT("pool", yt1, yt1, lnbv, ALU.add, ["yt1", "lnb_bc"], ["yt1"])
            MM(F2[0:64, 264:272], identf[:, c * 64:(c + 1) * 64], bon, True, True, ["identf", "bon"], ["F2"])
            CP("act", st8[:, 48:56], F2[0:64, 264:272], ["F2"], ["st8"])
            TT("dve", ych, Vc[:, c], bc(st8[:, 48:56].unsqueeze(2), [64, 8, 64]), ALU.mult, ["Vc", "st8"], ["ych"])
            TT("pool", yt1, yt1, ych, ALU.add, ["yt1", "ych"], ["yt1"])
            MM(R2[0:64, 0:512], identf[:, c * 64:(c + 1) * 64], gr_s[:, :], True, True, ["identf", "gr_s"], ["R2"])
            TT("dve", yt1.rearrange("p h i -> p (h i)"), yt1.rearrange("p h i -> p (h i)"), R2[0:64, 0:512], ALU.mult, ["yt1", "R2"], ["yt1"])
            b.dma("sp", rwscr[ti * 128 + c * 64: ti * 128 + (c + 1) * 64, :], yt1.rearrange("p h i -> p (h i)"), reads=["yt1"], writes=["rwscr"])

    if stop_after == "A":
        b.barrier(); b.emit(); ncd.__exit__(None, None, None); es.close()
        return nc
    b.barrier()
    MS("dve", H32, 0.0, ["H32"]); MS("dve", Hb, 0.0, ["Hb"]); MS("pool", hlast[:], 0.0, ["hlast"])
    KSUB = int(os.environ.get('KSUB', '9'))
    V2a = V2[:, 0:512]
    V2b = V2[:, 512:1024]
    for ti in range(NT if nt_lim is None else nt_lim):
        par = ti % 2
        x_, h_, xr, hr = front(ti, par)
        if KSUB >= 1:
            for (c0, n, dst) in ((0, 256, V2a[:, 0:256]), (256, 64, V2a[:, 256:320]), (320, 512, V2b)):
                for k in range(8):
                    MM(dst, h_[:, k, :], Wn[:, k, c0:c0 + n], k == 0, k == 7, [hr, "Wn"], ["V2"])
        if KSUB >= 2:
            kv3 = kfin[:].rearrange("p (g d) -> p g d", g=2)
            qknorm(V2a[:, 0:128].rearrange("p (g d) -> p g d", g=2), kv3, 2, knw_bc, 1.0, ["V2"], ["kfin"])
            rope("dve", kv3, 2, None, ["kfin"])
            CP("act", vfin[:], V2a[:, 128:256], ["V2"], ["vfin"])
            CP("act", Vaug[:, ti, :, 0:64], V2a[:, 128:256].rearrange("p (g d) -> p g d", g=2), ["V2"], ["Vaug"])
            CP("act", kifin[:], V2a[:, 256:320], ["V2"], ["kifin"])
            rope("pool", kifin[:].unsqueeze(1), 1, None, ["kifin"])
            ACT(gr_s[:], V2b, AF.Silu, ["V2"], ["gr_s"])
        if KSUB >= 3:
            b.dma("sp", k_nat[ti * 128:(ti + 1) * 128, :], kfin[:], reads=["kfin"])
            b.dma("sp", v_nat[ti * 128:(ti + 1) * 128, :], vfin[:], reads=["vfin"])
            b.dma("sp", ki_nat[ti * 128:(ti + 1) * 128, :], kifin[:], reads=["kifin"])
        if KSUB >= 4:
            for g_ in range(2):
                TR(F2[0:64, g_ * 128:(g_ + 1) * 128], kfin[:, g_ * 64:(g_ + 1) * 64], identf[:], ["kfin", "identf"], ["F2"])
            TR(F2[0:64, 256:384], kifin[:, :], identf[:], ["kifin", "identf"], ["F2"])
            if KSUB >= 5:
                CP("act", kT_all[:, :, ti * 128:(ti + 1) * 128], F2[0:64, 0:256].rearrange("p (g t) -> p g t", g=2), ["F2"], ["kT_all"])
            if KSUB >= 6:
                if os.environ.get("KV") == "1":
                    CP("act", nrm_t[0:64, 0:128], F2[0:64, 256:384], ["F2"], ["nrm_t"])
                elif os.environ.get("KV") == "2":
                    CP("act", kiT_all[:, ti * 128:(ti + 1) * 128], F2[0:64, 0:128], ["F2"], ["kiT_all"])
                else:
                    CP("act", kiT_all[:, ti * 128:(ti + 1) * 128], F2[0:64, 256:384], ["F2"], ["kiT_all"])

        if int(os.environ.get("KLVL", "9")) >= 2:
            rwkv_tile(ti, h_, hr)
        if ti == NT - 1:
            for cc in range(26):
                col = cc * 64
                for k in range(8):
                    MM(F2[0:64, 300 + cc:301 + cc], Wom[:, k, col:col + 64], h_[:, k, 127:128], k == 0, False, ["Wm", hr], ["F2"])
                for k in range(8):
                    MM(F2[0:64, 300 + cc:301 + cc], Wm[:, k, col:col + 64], h_[:, k, 127:128], False, k == 7, ["Wm", hr], ["F2"])
            CP("dve", rawl, F2[0:64, 300:326], ["F2"], ["rawl"])
            b.dma("sp", shift_p.rearrange("(c j) -> j c", j=64), rawl, reads=["rawl"])
    for h in range(8):
        TR(F2[0:64, h * 64:(h + 1) * 64], H32[:, h, :], identf[0:64, 0:64], ["H32", "identf"], ["F2"])
    CP("dve", ych, F2[0:64, 0:512].rearrange("p (h j) -> p h j", h=8), ["F2"], ["ych"])
    b.dma("sp", wkv_p.rearrange("h i j -> i h j"), ych, reads=["ych"])

    if stop_after == "B":
        b.barrier(); b.emit(); ncd.__exit__(None, None, None); es.close()
        return nc
    b.barrier()
    off = 0
    Wq, off = carve(off, [128, 8, 1544], BF16)
    stg, off = carve(off, [128, 8, 512])
    score, off = carve(off, [128, T])
    selm, off = carve(off, [128, T], BF16)
    selT, off = carve(off, [128, NT, 128], BF16)
    rl, off = carve(off, [128, 512])
    qfin, off = carve(off, [128, 512])
    qifin, off = carve(off, [128, 512])
    qT, off = carve(off, [64, 8, 128], BF16)
    qiT, off = carve(off, [64, 8, 128], BF16)
    ga, off = carve(off, [128, 512])
    eT, off = carve(off, [128, 4, 128], BF16)
    pTt, off = carve(off, [128, 4, 128], BF16)
    cat, off = carve(off, [128, D], BF16)
    catT, off = carve(off, [128, 8, 128], BF16)
    rwo, off = carve(off, [128, 512])
    rwo2, off = carve(off, [128, 512])
    att, off = carve(off, [128, 8, 64])
    ybuf, off = carve(off, [128, D])
    bs, off = carve(off, [128, 16])
    wis, off = carve(off, [128, 8])
    oacc, off = carve(off, [128, 2, 4, 65])
    Wout, off = carve(off, [128, 8, D], BF16)
    assert off <= AW, off
    load_cols(Wq, 0, C_Q, 512, tag="Wq")
    load_cols(Wq, 512, C_QI, 520, tag="Wq")
    load_cols(Wq, 1032, C_GA, 512, tag="Wq")
    w_out_v = w_out.rearrange("(k p) c -> p k c", p=128)
    for hh in range(2):
        b.dma("sp", stg[:, :, :], w_out_v[:, :, hh * 512:(hh + 1) * 512], writes=["stg"])
        CP("pool", Wout[:, :, hh * 512:(hh + 1) * 512], stg[:, :, :], ["stg"], ["Wout"])


    for j in range(NO if no_lim is None else no_lim):
        ti = NT + j
        x_, h_, xr, hr = front(ti, 0)
        NKT = 2 * (j + 1)
        NK = NKT * 128
        for (c0, n, dst, nm) in ((0, 512, R2[:, 0:512], "R2"), (512, 512, R2[:, 512:1024], "R2"),
                                 (1024, 8, F2[:, 0:8], "F2"), (1032, 512, K2[:, 0:512], "K2")):
            for k in range(8):
                MM(dst, h_[:, k, :], Wq[:, k, c0:c0 + n], k == 0, k == 7, [hr, "Wq"], [nm])
        q3 = qfin.rearrange("p (h d) -> p h d", h=8)
        qknorm(R2[:, 0:512].rearrange("p (h d) -> p h d", h=8), q3, 8, qnw_bc, 0.125, ["R2"], ["qfin"])
        rope("dve", q3, 8, None, ["qfin"])
        qi3 = qifin.rearrange("p (h d) -> p h d", h=8)
        CP("act", qifin, R2[:, 512:1024], ["R2"], ["qifin"])
        rope("pool", qi3, 8, None, ["qifin"])
        TS("dve", wis, F2[:, 0:8], 0.044194173824159216, None, ALU.mult, None, ["F2"], ["wis"])
        ACT(ga, K2[:, 0:512], AF.Silu, ["K2"], ["ga"])
        for (src, srcn, dstT, dn) in ((qfin, "qfin", qT, "qT"), (qifin, "qifin", qiT, "qiT")):
            pv = K2[0:64, :].rearrange("p (h t) -> p h t", h=8)
            for h in range(8):
                TR(pv[:, h, :], src[:, h * 64:(h + 1) * 64], identf[:], [srcn, "identf"], ["K2"])
            CP("act", dstT, pv, ["K2"], [dn])
        nchk = (NK + 511) // 512
        ib = 0
        for kc in range(nchk):
            w = min(512, NK - kc * 512)
            for h in range(8):
                pb = (R2[:, 0:512], R2[:, 512:1024])[ib % 2]
                ib += 1
                MM(pb[:, 0:w], qiT[:, h, :], kiT_all[:, kc * 512:kc * 512 + w], True, True, ["qiT", "kiT_all"], ["R2"])
                ACT(rl[:, 0:w], pb[:, 0:w], AF.Relu, ["R2"], ["rl"])
                sc = score[:, kc * 512:kc * 512 + w]
                if h == 0:
                    TS("dve", sc, rl[:, 0:w], wis[:, 0:1], None, ALU.mult, None, ["rl", "wis"], ["score"])
                else:
                    STT(sc, rl[:, 0:w], wis[:, h:h + 1], sc, ALU.mult, ALU.add, ["rl", "wis", "score"], ["score"])
        RED(bs[:, 0:1], score[:, 0:NK], ALU.max, ["score"], ["bs"])
        RED(bs[:, 1:2], score[:, 0:NK], ALU.min, ["score"], ["bs"])
        TS("dve", rl[:, 0:256], iota256[:, :], qrel[:, 0:1], -1e30, ALU.is_gt, ALU.mult, ["iota256", "qrel"], ["rl"])
        TT("dve", score[:, NK - 256:NK], score[:, NK - 256:NK], rl[:, 0:256], ALU.add, ["score", "rl"], ["score"])
        TS("dve", bs[:, 2:3], bs[:, 1:2], -1.0, None, ALU.add, None, ["bs"], ["bs"])
        STT(bs[:, 3:4], bs[:, 0:1], 2.0, bs[:, 1:2], ALU.add, ALU.subtract, ["bs"], ["bs"])
        for it in range(1, n_bis + 1):
            sc_ = float(2.0 ** (-it))
            STT(bs[:, 4:5], bs[:, 3:4], sc_, bs[:, 2:3], ALU.mult, ALU.add, ["bs"], ["bs"])
            TS("dve", selm[:, 0:NK], score[:, 0:NK], bs[:, 4:5], 0.0, ALU.is_gt, ALU.add, ["score", "bs"], ["selm", "bs"], accum=bs[:, 5:6])
            TS("dve", bs[:, 6:7], bs[:, 5:6], float(topk_p) - 0.5, bs[:, 3:4], ALU.is_gt, ALU.mult, ["bs"], ["bs"])
            STT(bs[:, 2:3], bs[:, 6:7], sc_, bs[:, 2:3], ALU.mult, ALU.add, ["bs"], ["bs"])
        TS("dve", selm[:, 0:NK], score[:, 0:NK], bs[:, 2:3], None, ALU.is_gt, None, ["score", "bs"], ["selm"])
        for kt in range(NKT):
            TR(PTb[:, (kt % 8) * 128:(kt % 8 + 1) * 128], selm[:, kt * 128:(kt + 1) * 128], identb[:], ["selm", "identb"], ["PTb"])
            if kt % 8 == 7 or kt == NKT - 1:
                k0 = (kt // 8) * 8
                n_ = kt - k0 + 1
                CP("act", selT[:, k0:k0 + n_, :], PTb[:, 0:n_ * 128].rearrange("p (a t) -> p a t", a=n_), ["PTb"], ["selT"])
        po = [V2[:, 0:260].rearrange("p (h e) -> p h e", h=4), V2[:, 512:772].rearrange("p (h e) -> p h e", h=4)]
        for kt in range(NKT):
            for g_ in range(2):
                lp = K2[:, g_ * 512:(g_ + 1) * 512]
                MM(lp, kT_all[:, g_, kt * 128:(kt + 1) * 128], qT[:, g_ * 4:(g_ + 1) * 4, :].rearrange("p h t -> p (h t)"),
                   True, True, ["kT_all", "qT"], ["K2"])
                ACT(eT, lp.rearrange("p (h t) -> p h t", h=4), AF.Exp, ["K2"], ["eT"])
                TT("pool" if g_ else "dve", pTt, eT, bc(selT[:, kt, :].unsqueeze(1), [128, 4, 128]), ALU.mult, ["eT", "selT"], ["pTt"])
                for hh in range(4):
                    MM(po[g_][:, hh, :], pTt[:, hh, :], Vaug[:, kt, g_, :], True, True, ["pTt", "Vaug"], ["V2"])
                if kt == 0:
                    CP("dve", oacc[:, g_], po[g_], ["V2"], ["oacc"])
                else:
                    TT("dve", oacc[:, g_], oacc[:, g_], po[g_], ALU.add, ["V2", "oacc"], ["oacc"])
        for g_ in range(2):
            b.op("dve", lambda g, g_=g_: g.reciprocal(out=bs[:, 8 + g_ * 4:12 + g_ * 4], in_=oacc[:, g_, :, 64]), ["oacc"], ["bs"])
            TT("dve", att[:, g_ * 4:(g_ + 1) * 4, :], oacc[:, g_, :, 0:64], bc(bs[:, 8 + g_ * 4:12 + g_ * 4].unsqueeze(2), [128, 4, 64]),
               ALU.mult, ["oacc", "bs"], ["att"])
        TT("dve", cat[:, 0:512], att.rearrange("p h d -> p (h d)"), ga, ALU.mult, ["att", "ga"], ["cat"])
        b.dma("sp", rwo, rwscr[(2 * j) * 128:(2 * j + 1) * 128, :], reads=["rwscr"], writes=["rwo"])
        b.dma("sp", rwo2, rwscr[(2 * j + 1) * 128:(2 * j + 2) * 128, :], reads=["rwscr"], writes=["rwo2"])
        TS("dve", rwo, rwo, parsel[:, 1:2], None, ALU.mult, None, ["rwo", "parsel"], ["rwo"])
        STT(rwo, rwo2, parsel[:, 0:1], rwo, ALU.mult, ALU.add, ["rwo2", "parsel", "rwo"], ["rwo"])
        CP("act", cat[:, 512:1024], rwo, ["rwo"], ["cat"])
        for k in range(8):
            TR(PTb[:, k * 128:(k + 1) * 128], cat[:, k * 128:(k + 1) * 128], identb[:], ["cat", "identb"], ["PTb"])
        CP("act", catT, PTb[:, :].rearrange("p (k t) -> p k t", k=8), ["PTb"], ["catT"])
        for hh in range(2):
            for k in range(8):
                MM(R2[:, hh * 512:(hh + 1) * 512], catT[:, k, :], Wout[:, k, hh * 512:(hh + 1) * 512], k == 0, k == 7, ["catT", "Wout"], ["R2"])
        TT("dve", ybuf, R2[:, :], gate_bc[:, :], ALU.mult, ["R2", "gate_bc"], ["ybuf"])
        TT("pool", ybuf, ybuf, x_[:], ALU.add, ["ybuf", xr], ["ybuf"])
        b.dma("sp", y_own[j * 128:(j + 1) * 128, :], ybuf, reads=["ybuf"])


    if do_sample:
        b.barrier()
        PW = 10500
        off = 0
        proj, off = carve(off, [16, DIN])
        tk = {}
        for nm in ("qs", "ga", "grs", "ta", "tb"):
            tk[nm], off = carve(off, [16, 512])
        ks_, off = carve(off, [16, 128])
        s16, off = carve(off, [16, 64])
        tokd, off = carve(off, [16, 1040])
        ysb, off = carve(off, [16, D])
        cats, off = carve(off, [16, D], BF16)
        catTs, off = carve(off, [128, 8, 16], BF16)
        assert off <= PW, off
        off = PW
        stg, off = carve(off, [128, 8, 512])
        wbf, off = carve(off, [128, 8, 512], BF16)
        sshift_t, off = carve(off, [16, SHW])
        mu16, off = carve(off, [16, SHW])
        X1 = off
        xm, off = carve(off, [16, SHW])
        prm, off = carve(off, [16, 5, 512])
        vecs, off = carve(off, [16, 8, 6, 64])
        for nm in ("dec", "asg", "kkv", "kkn", "kmod"):
            tk[nm], off = carve(off, [16, 512])
        wdt, off = carve(off, [16, 128])
        wdT, off = carve(off, [64, 32])
        assert off <= AW, off
        NPAIR = NS // 2

        x_, h_, xr, hr = front(NT + NO, 0, m_prompt=False, ntok=16)
        for ch in range(8):
            c0 = ch * 505
            b.dma("sp", stg[:, :, 0:505], w_in_v[:, :, c0:c0 + 505], writes=["stg"])
            CP("dve", wbf[:, :, 0:505], stg[:, :, 0:505], ["stg"], ["wbf"])
            for k in range(8):
                MM(R2[0:16, 0:505], h_[:, k, 0:16], wbf[:, k, 0:505], k == 0, k == 7, [hr, "wbf"], ["R2"])
            CP("act", proj[:, c0:c0 + 505], R2[0:16, 0:505], ["R2"], ["proj"])
        b.dma("sp", sshift_t, sshift_d[:, :], writes=["sshift"])
        b.dma("sp", mu16, mu.partition_broadcast(16), writes=["mu16"])
        for i_, src in enumerate((pw0, pa0, pkk, pka, prk)):
            b.dma("sp", prm[:, i_, :], src.partition_broadcast(16), writes=["prm"])
        qs3 = tk["qs"].rearrange("p (h d) -> p h d", h=8)
        qknorm(proj[:, 0:512].rearrange("p (h d) -> p h d", h=8), qs3, 8, qnw_bc, 0.125, ["proj"], ["qs"], nrows=16)
        rope("dve", qs3, 8, None, ["qs"], nrows=16)
        ks3 = ks_.rearrange("p (g d) -> p g d", g=2)
        qknorm(proj[:, 512:640].rearrange("p (g d) -> p g d", g=2), ks3, 2, knw_bc, 1.0, ["proj"], ["ks"], nrows=16)
        rope("dve", ks3, 2, None, ["ks"], nrows=16)
        b.dma("sp", k_s[:, :], ks_, reads=["ks"])
        b.dma("sp", v_s[:, :], proj[:, 640:768], reads=["proj"])
        b.dma("sp", shift_s[:, :], proj[:, C_R:C_R + SHW], reads=["proj"])
        rope("dve", proj[:, 768:1280].rearrange("p (h d) -> p h d", h=8), 8, None, ["proj"], nrows=16)
        rope("dve", proj[:, 1288:1352].unsqueeze(1), 1, None, ["proj"], nrows=16)
        b.dma("sp", ki_s[:, :], proj[:, 1288:1352], reads=["proj"])
        ACT(tk["ga"], proj[:, C_GA:C_GA + 512], AF.Silu, ["proj"], ["ga"])
        ACT(tk["grs"], proj[:, C_GR:C_GR + 512], AF.Silu, ["proj"], ["grs"])
        xs_ = proj[:, C_R:C_R + SHW]
        TT("dve", xm, sshift_t, xs_, ALU.subtract, ["sshift", "proj"], ["xm"])
        TT("dve", xm, xm, mu16, ALU.mult, ["xm", "mu16"], ["xm"])
        TT("dve", xm, xm, xs_, ALU.add, ["xm", "proj"], ["xm"])
        ACT(wdt[:, 0:64], xm[:, 1536:1600], AF.Tanh, ["xm"], ["wdt"])
        CP("dve", wdt[:, 64:128], xm[:, 1600:1664], ["xm"], ["wdt"])
        TR(F2[0:64, 0:16], wdt[:, 0:64], identf[0:16, 0:16], ["wdt", "identf"], ["F2"])
        TR(F2[0:64, 16:32], wdt[:, 64:128], identf[0:16, 0:16], ["wdt", "identf"], ["F2"])
        CP("dve", wdT, F2[0:64, 0:32], ["F2"], ["wdT"])
        MM(R2[0:16, 0:512], wdT[:, 0:16], wupS[:, :], True, True, ["wdT", "wupS"], ["R2"])
        MM(R2[0:16, 512:1024], wdT[:, 16:32], aupS[:, :], True, True, ["wdT", "aupS"], ["R2"])
        TT("dve", tk["dec"], R2[0:16, 0:512], prm[:, 0, :], ALU.add, ["R2", "prm"], ["dec"])
        ACT(tk["dec"], tk["dec"], AF.Sigmoid, ["dec"], ["dec"])
        ACT(tk["dec"], tk["dec"], AF.Exp, ["dec"], ["dec"], scale=-0.6065306597126334)
        TT("dve", tk["asg"], R2[0:16, 512:1024], prm[:, 1, :], ALU.add, ["R2", "prm"], ["asg"])
        ACT(tk["asg"], tk["asg"], AF.Sigmoid, ["asg"], ["asg"])
        xr_, xk_, xv_ = xm[:, 0:512], xm[:, 512:1024], xm[:, 1024:1536]
        TT("dve", tk["kkv"], xk_, prm[:, 2, :], ALU.mult, ["xm", "prm"], ["kkv"])
        ACT(tk["ta"], tk["kkv"], AF.Square, ["kkv"], ["ta"])
        RED(s16[:, 0:8], tk["ta"].rearrange("p (h d) -> p h d", h=8), ALU.add, ["ta"], ["s16"])
        ACT(s16[:, 8:16], s16[:, 0:8], AF.Sqrt, ["s16", "cst"], ["s16"], bias=cst[0:16, 2:3])
        b.op("dve", lambda g: g.reciprocal(out=s16[:, 16:24], in_=s16[:, 8:16]), ["s16"], ["s16"])
        TT("dve", tk["kkn"].rearrange("p (h d) -> p h d", h=8), tk["kkv"].rearrange("p (h d) -> p h d", h=8),
           bc(s16[:, 16:24].unsqueeze(2), [16, 8, 64]), ALU.mult, ["kkv", "s16"], ["kkn"])
        STT(tk["ta"], tk["asg"], -1.0, prm[:, 3, :], ALU.add, ALU.mult, ["asg", "prm"], ["ta"])
        STT(tk["kmod"], tk["ta"], 1.0, xk_, ALU.add, ALU.mult, ["ta", "xm"], ["kmod"])

        def v8(ap):
            return ap.rearrange("p (h d) -> p h d", h=8)
        CP("dve", vecs[:, :, 0, :], v8(tk["dec"]), ["dec"], ["vecs"])
        TS("dve", vecs[:, :, 1, :], v8(tk["kkn"]), -1.0, None, ALU.mult, None, ["kkn"], ["vecs"])
        TT("dve", vecs[:, :, 2, :], v8(tk["kkn"]), v8(tk["asg"]), ALU.mult, ["kkn", "asg"], ["vecs"])
        CP("dve", vecs[:, :, 3, :], v8(tk["kmod"]), ["kmod"], ["vecs"])
        CP("dve", vecs[:, :, 4, :], v8(xr_), ["xm"], ["vecs"])
        CP("dve", vecs[:, :, 5, :], v8(xv_), ["xm"], ["vecs"])
        TT("dve", tk["ta"], xr_, prm[:, 4, :], ALU.mult, ["xm", "prm"], ["ta"])
        TT("dve", tk["ta"], tk["ta"], tk["kmod"], ALU.mult, ["ta", "kmod"], ["ta"])
        RED(s16[:, 24:32], v8(tk["ta"]), ALU.add, ["ta"], ["s16"])
        b.dma("sp", scr1[:, :], vecs.rearrange("p h v j -> p (h v j)"), reads=["vecs"], writes=["scr1"])
        b.barrier()
        off = PW
        S_, off = carve(off, [128, 4096])
        tmpS, off = carve(off, [128, 4096])
        vsh, off = carve(off, [128, 384])
        ysh, off = carve(off, [128, 128])
        assert off <= X1
        b.dma("sp", S_, swkv_d[:, :], writes=["S"])
        b.dma("sp", vsh, scr1.rearrange("s (h x) -> (s h) x", h=8), reads=["scr1"], writes=["vsh"])
        S3 = S_.rearrange("p (i j) -> p i j", i=64)
        T3 = tmpS.rearrange("p (i j) -> p i j", i=64)

        def jb(vi):
            return bc(vsh[:, vi * 64:(vi + 1) * 64].unsqueeze(1), [128, 64, 64])

        def ib(ap):
            return bc(ap.unsqueeze(2), [128, 64, 64])
        TT("dve", T3, S3, jb(1), ALU.mult, ["S", "vsh"], ["tmpS"])
        RED(ysh[:, 0:64], T3, ALU.add, ["tmpS"], ["ysh"])
        TT("dve", S3, S3, jb(0), ALU.mult, ["S", "vsh"], ["S"])
        TT("dve", T3, jb(2), ib(ysh[:, 0:64]), ALU.mult, ["vsh", "ysh"], ["tmpS"])
        TT("dve", S3, S3, T3, ALU.add, ["S", "tmpS"], ["S"])
        TT("dve", T3, jb(3), ib(vsh[:, 320:384]), ALU.mult, ["vsh"], ["tmpS"])
        TT("dve", S3, S3, T3, ALU.add, ["S", "tmpS"], ["S"])
        b.dma("sp", wkv_s[:, :], S_, reads=["S"])
        TT("dve", T3, S3, jb(4), ALU.mult, ["S", "vsh"], ["tmpS"])
        RED(ysh[:, 64:128], T3, ALU.add, ["tmpS"], ["ysh"])
        b.dma("sp", scr2[:, :], ysh[:, 64:128], reads=["ysh"], writes=["scr2"])
        yS = tk["tb"]
        b.dma("sp", yS, scr2.rearrange("(s h) i -> s (h i)", h=8), reads=["scr2"], writes=["tb"])
        y3 = v8(yS)
        RED(s16[:, 32:40], y3, ALU.add, ["tb"], ["s16"])
        ACT(tk["ta"], yS, AF.Square, ["tb"], ["ta"])
        RED(s16[:, 40:48], v8(tk["ta"]), ALU.add, ["ta"], ["s16"])
        TS("dve", s16[:, 32:48], s16[:, 32:48], 1.0 / 64, None, ALU.mult, None, ["s16"], ["s16"])
        TT("dve", s16[:, 48:56], s16[:, 32:40], s16[:, 32:40], ALU.mult, ["s16"], ["s16"])
        TT("dve", s16[:, 48:56], s16[:, 40:48], s16[:, 48:56], ALU.subtract, ["s16"], ["s16"])
        ACT(s16[:, 56:64], s16[:, 48:56], AF.Sqrt, ["s16", "cst"], ["s16"], bias=cst[0:16, 1:2])
        b.op("dve", lambda g: g.reciprocal(out=s16[:, 56:64], in_=s16[:, 56:64]), ["s16"], ["s16"])
        TT("dve", y3, y3, bc(s16[:, 32:40].unsqueeze(2), [16, 8, 64]), ALU.subtract, ["tb", "s16"], ["tb"])
        TT("dve", y3, y3, bc(s16[:, 56:64].unsqueeze(2), [16, 8, 64]), ALU.mult, ["tb", "s16"], ["tb"])
        TT("dve", yS, yS, lnw_bc[0:16, :], ALU.mult, ["tb", "lnw_bc"], ["tb"])
        TT("dve", yS, yS, lnb_bc[0:16, :], ALU.add, ["tb", "lnb_bc"], ["tb"])
        TT("dve", v8(tk["ta"]), v8(xv_), bc(s16[:, 24:32].unsqueeze(2), [16, 8, 64]), ALU.mult, ["xm", "s16"], ["ta"])
        TT("dve", yS, yS, tk["ta"], ALU.add, ["tb", "ta"], ["tb"])
        TT("dve", cats[:, 512:1024], yS, tk["grs"], ALU.mult, ["tb", "grs"], ["cats"])
        b.barrier()
        off = PW
        Gi, off = carve(off, [128, 8192])
        tmpG, off = carve(off, [128, 64, 64])
        Kc, off = carve(off, [128, 24, 128])
        Vcd, off = carve(off, [128, 24, 128])
        tmpc, off = carve(off, [128, 24, 64])
        repd, off = carve(off, [128, 1040])
        opd, off = carve(off, [128, 520])
        repS, off = carve(off, [16, 1024])
        repTS, off = carve(off, [128, 128])
        blkS, off = carve(off, [128, 128])
        sc, off = carve(off, [128, 132])
        msc, off = carve(off, [128, 132])
        sh_, off = carve(off, [128, 128])
        cv, off = carve(off, [128, 24])
        ci, off = carve(off, [128, 24], I32)
        cif, off = carve(off, [128, 24])
        rowi, off = carve(off, [128, 24], I32)
        lg, off = carve(off, [128, 8, 24])
        b2, off = carve(off, [128, 16])
        ptab, off = carve(off, [128, 8], I32)
        ptf, off = carve(off, [128, 8])
        oh0, off = carve(off, [128, 1])
        assert off <= AW, off
        Gi3 = Gi.rearrange("p (t d) -> p t d", t=128)
        for (t_, d_, nm) in ((ptab, ptab_d, "ptab"), (repS, rep_d, "repS"), (repTS, repT_d, "repTS"), (blkS, blk_d, "blkS"), (oh0, oh0_d, "oh0")):
            b.dma("sp", t_, d_[:, :], writes=[nm])
        CP("dve", tokd[:, 0:512], proj[:, 768:1280], ["proj"], ["tokd"])
        TS("dve", tokd[:, 512:520], proj[:, 1280:1288], 0.044194173824159216, None, ALU.mult, None, ["proj"], ["tokd"])
        CP("dve", tokd[:, 520:1032], tk["qs"], ["qs"], ["tokd"])
        TT("dve", v8(tk["ta"]), v8(tokd[:, 0:512]), bc(proj[:, 1288:1352].unsqueeze(1), [16, 8, 64]), ALU.mult, ["tokd", "proj"], ["ta"])
        RED(s16[:, 0:8], v8(tk["ta"]), ALU.add, ["ta"], ["s16"])
        TS("dve", s16[:, 0:8], s16[:, 0:8], 0.0, None, ALU.max, None, ["s16"], ["s16"])
        TT("dve", s16[:, 0:8], s16[:, 0:8], tokd[:, 512:520], ALU.mult, ["s16", "tokd"], ["s16"])
        RED(tokd[:, 1032:1033], s16[:, 0:8], ALU.add, ["s16"], ["tokd"])
        CP("dve", ptf, ptab, ["ptab"], ["ptf"])
        TS("dve", ptf, ptf, 128.0, None, ALU.mult, None, ["ptf"], ["ptf"])
        ck_rows = cache_k
        cv_rows = cache_v
        for sp in range(NPAIR):
            b.op("pool", lambda g, sp=sp: g.indirect_dma_start(out=Gi, out_offset=None, in_=cache_ki[:, :],
                                                                in_offset=bass.IndirectOffsetOnAxis(ap=ptab[:, sp:sp + 1], axis=0)),
                 ["ptab"], ["Gi"], dma=True)
            for (c0, n) in ((0, 512), (512, 512), (1024, 9)):
                MM(K2[:, 0:n], repS[:, sp * 128:(sp + 1) * 128], tokd[:, c0:c0 + n], True, True, ["repS", "tokd"], ["K2"])
                CP("act", repd[:, c0:c0 + n], K2[:, 0:n], ["K2"], ["repd"])
            for h in range(8):
                for hf in range(2):
                    TT("dve", tmpG, Gi3[:, hf * 64:(hf + 1) * 64, :], bc(repd[:, h * 64:(h + 1) * 64].unsqueeze(1), [128, 64, 64]), ALU.mult, ["Gi", "repd"], ["tmpG"])
                    RED(sh_[:, hf * 64:(hf + 1) * 64], tmpG, ALU.add, ["tmpG"], ["sh"])
                if h == 0:
                    TS("dve", sc[:, 0:128], sh_, 0.0, repd[:, 512:513], ALU.max, ALU.mult, ["sh", "repd"], ["sc"])
                else:
                    TS("dve", sh_, sh_, 0.0, repd[:, 512 + h:513 + h], ALU.max, ALU.mult, ["sh", "repd"], ["sh"])
                    TT("dve", sc[:, 0:128], sc[:, 0:128], sh_, ALU.add, ["sc", "sh"], ["sc"])
            TS("dve", b2[:, 0:1], oh0, 1e30, -1e30, ALU.mult, ALU.add, ["oh0"], ["b2"])
            STT(sc[:, 128:129], repd[:, 1032:1033], oh0[:, 0:1], b2[:, 0:1], ALU.mult, ALU.add, ["repd", "oh0", "b2"], ["sc"])
            b.op("dve", lambda g: g.tensor_reduce(out=b2[:, 1:2], in_=sc[:, 0:128], axis=AX.X, op=ALU.max, apply_absolute_value=True), ["sc"], ["b2"])
            b.op("dve", lambda g: g.tensor_reduce(out=b2[:, 2:3], in_=repd[:, 1032:1033], axis=AX.X, op=ALU.max, apply_absolute_value=True), ["repd"], ["b2"])
            TT("dve", b2[:, 1:2], b2[:, 1:2], b2[:, 2:3], ALU.max, ["b2"], ["b2"])
            MM(F2[:, 0:1], blkS, b2[:, 1:2], True, True, ["blkS", "b2"], ["F2"])
            TS("dve", b2[:, 3:4], F2[:, 0:1], -1.0, -1.0, ALU.mult, ALU.add, ["F2"], ["b2"])
            TS("dve", b2[:, 4:5], F2[:, 0:1], 2.0, 2.0, ALU.mult, ALU.add, ["F2"], ["b2"])
            for it in range(1, n_bis + 7):
                sc_ = float(2.0 ** (-it))
                STT(b2[:, 5:6], b2[:, 4:5], sc_, b2[:, 3:4], ALU.mult, ALU.add, ["b2"], ["b2"])
                TS("dve", msc[:, 0:129], sc[:, 0:129], b2[:, 5:6], 0.0, ALU.is_gt, ALU.add, ["sc", "b2"], ["msc", "b2"], accum=b2[:, 6:7])
                MM(F2[:, 0:1], blkS, b2[:, 6:7], True, True, ["blkS", "b2"], ["F2"])
                TS("dve", b2[:, 7:8], F2[:, 0:1], float(topk_s) - 0.5, b2[:, 4:5], ALU.is_gt, ALU.mult, ["F2", "b2"], ["b2"])
                STT(b2[:, 3:4], b2[:, 7:8], sc_, b2[:, 3:4], ALU.mult, ALU.add, ["b2"], ["b2"])
            TS("dve", msc[:, 0:129], sc[:, 0:129], b2[:, 3:4], None, ALU.is_gt, None, ["sc", "b2"], ["msc"])
            CP("dve", b2[:, 8:9], msc[:, 128:129], ["msc"], ["b2"])
            TS("dve", sh_, msc[:, 0:128], 1e30, -1e30, ALU.mult, ALU.add, ["msc"], ["sh"])
            TT("dve", msc[:, 0:128], msc[:, 0:128], sc[:, 0:128], ALU.mult, ["msc", "sc"], ["msc"])
            TT("dve", msc[:, 0:128], msc[:, 0:128], sh_, ALU.add, ["msc", "sh"], ["msc"])
            for r_ in range(3):
                b.op("dve", lambda g, r_=r_: g.max(out=cv[:, r_ * 8:(r_ + 1) * 8], in_=msc[:, 0:128]), ["msc"], ["cv"])
                b.op("dve", lambda g, r_=r_: g.max_index(out=ci[:, r_ * 8:(r_ + 1) * 8].bitcast(mybir.dt.uint32), in_max=cv[:, r_ * 8:(r_ + 1) * 8],
                                                         in_values=msc[:, 0:128]), ["msc", "cv"], ["ci"])
                b.op("dve", lambda g, r_=r_: g.match_replace(out=msc[:, 0:128], in_to_replace=cv[:, r_ * 8:(r_ + 1) * 8], in_values=msc[:, 0:128],
                                                             imm_value=-3e30), ["msc", "cv"], ["msc"])
            CP("dve", cif, ci, ["ci"], ["cif"])
            TS("dve", cif, cif, ptf[:, sp:sp + 1], None, ALU.add, None, ["cif", "ptf"], ["cif"])
            CP("dve", rowi, cif, ["cif"], ["rowi"])
            TS("dve", cv, cv, -1e29, None, ALU.is_gt, None, ["cv"], ["cv"])
            for c_ in range(24):
                b.op("pool", lambda g, c_=c_: g.indirect_dma_start(out=Kc[:, c_, :], out_offset=None, in_=ck_rows[:, :],
                                                                  in_offset=bass.IndirectOffsetOnAxis(ap=rowi[:, c_:c_ + 1], axis=0)),
                     ["rowi"], ["Kc"], dma=True)
                b.op("pool", lambda g, c_=c_: g.indirect_dma_start(out=Vcd[:, c_, :], out_offset=None, in_=cv_rows[:, :],
                                                                  in_offset=bass.IndirectOffsetOnAxis(ap=rowi[:, c_:c_ + 1], axis=0)),
                     ["rowi"], ["Vcd"], dma=True)
            Kc4 = Kc.rearrange("p c (g d) -> p c g d", g=2)
            Vc4 = Vcd.rearrange("p c (g d) -> p c g d", g=2)
            for h in range(8):
                TT("dve", tmpc, Kc4[:, :, h // 4, :], bc(repd[:, 520 + h * 64:520 + (h + 1) * 64].unsqueeze(1), [128, 24, 64]), ALU.mult, ["Kc", "repd"], ["tmpc"])
                RED(lg[:, h, :], tmpc, ALU.add, ["tmpc"], ["lg"])
            ACT(lg, lg, AF.Exp, ["lg"], ["lg"])
            TT("dve", lg, lg, bc(cv.unsqueeze(1), [128, 8, 24]), ALU.mult, ["lg", "cv"], ["lg"])
            RED(opd[:, 512:520], lg, ALU.add, ["lg"], ["opd"])
            for h in range(8):
                TT("dve", tmpc, Vc4[:, :, h // 4, :], bc(lg[:, h, :].unsqueeze(2), [128, 24, 64]), ALU.mult, ["Vcd", "lg"], ["tmpc"])
                RED(opd[:, h * 64:(h + 1) * 64], tmpc.rearrange("p c d -> p d c"), ALU.add, ["tmpc"], ["opd"])
            MM(V2[0:16, 0:512], repTS[:, sp * 16:(sp + 1) * 16], opd[:, 0:512], sp == 0, sp == NPAIR - 1, ["repTS", "opd"], ["V2"])
            MM(V2[0:16, 512:520], repTS[:, sp * 16:(sp + 1) * 16], opd[:, 512:520], sp == 0, sp == NPAIR - 1, ["repTS", "opd"], ["V2"])
            MM(F2[0:16, 8 + sp:9 + sp], repTS[:, sp * 16:(sp + 1) * 16], b2[:, 8:9], True, True, ["repTS", "b2"], ["F2"])
            CP("dve", s16[:, 40 + sp:41 + sp], F2[0:16, 8 + sp:9 + sp], ["F2"], ["s16"])
        RED(s16[:, 32:33], s16[:, 40:40 + NPAIR], ALU.add, ["s16"], ["s16"])
        qv = v8(tk["qs"])
        for g_ in range(2):
            TT("dve", v8(tk["ta"])[:, g_ * 4:(g_ + 1) * 4, :], qv[:, g_ * 4:(g_ + 1) * 4, :],
               bc(ks_[:, g_ * 64:(g_ + 1) * 64].unsqueeze(1), [16, 4, 64]), ALU.mult, ["qs", "ks"], ["ta"])
        RED(s16[:, 0:8], v8(tk["ta"]), ALU.add, ["ta"], ["s16"])
        ACT(s16[:, 0:8], s16[:, 0:8], AF.Exp, ["s16"], ["s16"])
        TS("dve", s16[:, 0:8], s16[:, 0:8], s16[:, 32:33], None, ALU.mult, None, ["s16"], ["s16"])
        TT("dve", s16[:, 8:16], V2[0:16, 512:520], s16[:, 0:8], ALU.add, ["V2", "s16"], ["s16"])
        b.op("dve", lambda g: g.reciprocal(out=s16[:, 8:16], in_=s16[:, 8:16]), ["s16"], ["s16"])
        for g_ in range(2):
            TT("dve", v8(tk["ta"])[:, g_ * 4:(g_ + 1) * 4, :], bc(proj[:, 640 + g_ * 64:640 + (g_ + 1) * 64].unsqueeze(1), [16, 4, 64]),
               bc(s16[:, g_ * 4:(g_ + 1) * 4].unsqueeze(2), [16, 4, 64]), ALU.mult, ["proj", "s16"], ["ta"])
        TT("dve", tk["ta"], tk["ta"], V2[0:16, 0:512], ALU.add, ["ta", "V2"], ["ta"])
        TT("dve", v8(tk["ta"]), v8(tk["ta"]), bc(s16[:, 8:16].unsqueeze(2), [16, 8, 64]), ALU.mult, ["ta", "s16"], ["ta"])
        TT("dve", cats[:, 0:512], tk["ta"], tk["ga"], ALU.mult, ["ta", "ga"], ["cats"])
        for k in range(8):
            TR(PTb[:, k * 16:(k + 1) * 16], cats[:, k * 128:(k + 1) * 128], identb[0:16, 0:16], ["cats", "identb"], ["PTb"])
        CP("act", catTs, PTb[:, 0:128].rearrange("p (k t) -> p k t", k=8), ["PTb"], ["catTs"])
        b.barrier()
        off = PW
        stg2, off = carve(off, [128, 8, 512])
        wbf2, off = carve(off, [128, 8, 512], BF16)
        b.dma("sp", ysb, gscr[:, :], reads=["gscr"], writes=["ysb"])
        w_out_v2 = w_out.rearrange("(k p) c -> p k c", p=128)
        for hh in range(2):
            b.dma("sp", stg2, w_out_v2[:, :, hh * 512:(hh + 1) * 512], writes=["stg2"])
            CP("dve", wbf2, stg2, ["stg2"], ["wbf2"])
            for k in range(8):
                MM(R2[0:16, hh * 512:(hh + 1) * 512], catTs[:, k, :], wbf2[:, k, :], k == 0, k == 7, ["catTs", "wbf2"], ["R2"])
        TT("dve", ysb, ysb, R2[0:16, :], ALU.mult, ["ysb", "R2"], ["ysb"])
        TT("dve", ysb, ysb, x_[0:16, :], ALU.add, ["ysb", xr], ["ysb"])
        b.dma("sp", y_s[:, :], ysb, reads=["ysb"])

    b.barrier()
    b.emit()
    ncd.__exit__(None, None, None)
    es.close()
    return nc


def _consts(T):
    NT = T // 128
    NO = NT // 2
    cst = {}
    cst["identf"] = np.eye(128, dtype=np.float32)
    cst["iota256"] = np.tile(np.arange(256, dtype=np.float32)[None, :], (128, 1))
    s = np.arange(64)[:, None]
    t = np.arange(64)[None, :]
    lt = (s < t).astype(np.float32)
    le = (s <= t).astype(np.float32)
    cst["maskT"] = np.concatenate([lt, le, lt, le], axis=1)
    cst["maskL"] = (np.arange(64)[None, :] < np.arange(64)[:, None]).astype(np.float32)
    r = np.ones((64, 1024), np.float32)
    r[:, ::64] = 0.0
    cst["resetm"] = r
    sel = np.zeros((17, 128), np.float32)
    sel[16, :] = 1.0
    cst["sel16"] = sel
    cst["ones64"] = np.ones((64, 64), np.float32)
    return cst


def _rope_table(pos):
    half = 8
    inv = np.power(np.float32(ROPE_THETA), -np.arange(half, dtype=np.float32) / np.float32(half)).astype(np.float32)
    ang = pos.astype(np.float32)[:, None] * inv[None, :]
    return np.concatenate([np.cos(ang), np.sin(ang)], axis=1).astype(np.float32)


def _core_inputs(inp, c, T, NS, past_len):
    NT = T // 128
    NO = NT // 2
    bi, par = c // 2, c % 2
    xp = np.asarray(inp["x_prompt"][bi], np.float32)
    own_tiles = [2 * j + par for j in range(NO)]
    own_rows = np.concatenate([np.arange(t * 128, (t + 1) * 128) for t in own_tiles])
    xs = np.zeros((128, D), np.float32)
    xs[:NS] = np.asarray(inp["x_sample"][c * NS:(c + 1) * NS, 0], np.float32)
    m = {}
    m["xall"] = np.ascontiguousarray(np.concatenate([xp, xp[own_rows], xs], axis=0))
    m["call"] = np.ascontiguousarray(np.concatenate([inp["c_sample"][c * NS:(c + 1) * NS], inp["c_prompt"][bi:bi + 1]], axis=0).astype(np.float32))
    pos = np.concatenate([np.arange(T), own_rows, np.full(128, past_len)])
    m["cs_all"] = _rope_table(pos)
    m["parsel"] = np.tile(np.array([[par, 1 - par]], np.float32), (128, 1))
    m["qrel"] = (par * 128 + np.arange(128, dtype=np.float32)).reshape(128, 1)
    m["ownidx"] = np.ascontiguousarray(own_rows.reshape(NO, 128).T.astype(np.int32))
    for k_, v_ in (("w_in", "w_in"), ("w_ada", "w_ada"), ("b_ada", "b_ada"), ("norm_w", "norm_w"), ("w_out", "w_out"),
                   ("qnw", "q_norm_w"), ("knw", "k_norm_w"), ("mu", "mu_shift"), ("w0", "w0"), ("a0", "a0"),
                   ("k_k", "k_k"), ("k_a", "k_a"), ("ln_x_w", "ln_x_w"), ("ln_x_b", "ln_x_b"), ("w_up", "w_up"), ("a_up", "a_up")):
        m[k_] = np.ascontiguousarray(np.asarray(inp[v_], np.float32))
    m["r_k"] = np.ascontiguousarray(np.asarray(inp["r_k"], np.float32).reshape(512))
    m["swkv"] = np.ascontiguousarray(np.asarray(inp["state_wkv"][c * NS:(c + 1) * NS], np.float32).reshape(NS * 8, 4096))
    m["sshift"] = np.ascontiguousarray(np.asarray(inp["state_shift"][c * NS:(c + 1) * NS, 0], np.float32))
    pt = np.asarray(inp["page_table"][c * NS:(c + 1) * NS], np.int32)
    m["ptab"] = np.ascontiguousarray(pt.reshape(NS // 2, 128).T)
    nphys = inp["cache_k"].shape[0]
    m["cache_k"] = np.asarray(inp["cache_k"], np.float32).reshape(nphys * 128, 128)
    m["cache_v"] = np.asarray(inp["cache_v"], np.float32).reshape(nphys * 128, 128)
    m["cache_kidx"] = np.asarray(inp["cache_kidx"], np.float32).reshape(nphys, 8192)
    rep = np.zeros((16, 8, 128), np.float32)
    for sp in range(8):
        for p in range(128):
            rep[2 * sp + p // 64, sp, p] = 1.0
    m["rep"] = rep.reshape(16, 1024)
    m["repT"] = np.ascontiguousarray(rep.transpose(2, 1, 0).reshape(128, 128))
    blk = np.zeros((128, 128), np.float32)
    blk[:64, :64] = 1.0
    blk[64:, 64:] = 1.0
    m["blk"] = blk
    oh = np.zeros((128, 1), np.float32)
    oh[0, 0] = 1.0
    oh[64, 0] = 1.0
    m["oh0"] = oh
    m.update(_consts(T))
    return m


_NC_CACHE = {}


def kernel(**inp):
    T = 4096
    NS = 16
    past_len = 8192
    inp = {k: np.asarray(v) for k, v in inp.items()}
    if "nc" not in _NC_CACHE:
        _NC_CACHE["nc"] = build(T=T, NPHYS=int(inp["cache_k"].shape[0]))
    nc = _NC_CACHE["nc"]
    in_maps = [_core_inputs(inp, c, T, NS, past_len) for c in range(8)]
    res = run_bass_kernel_spmd(nc, in_maps, core_ids=list(range(8)))
    outs = res.results
    B = 4
    NO = T // 256
    y_p = np.zeros((B, T, D), np.float32)
    for c in range(8):
        bi, par = c // 2, c % 2
        yo = np.asarray(outs[c]["y_own"]).reshape(NO, 128, D)
        y_p[bi].reshape(T // 256, 2, 128, D)[:, par] = yo
    k_p = np.stack([np.asarray(outs[2 * bi]["k_nat"]).reshape(T, 2, 64) for bi in range(B)])
    v_p = np.stack([np.asarray(outs[2 * bi]["v_nat"]).reshape(T, 2, 64) for bi in range(B)])
    ki_p = np.stack([np.asarray(outs[2 * bi]["ki_nat"]).reshape(T, 64) for bi in range(B)])
    wkv_pp = np.stack([np.asarray(outs[2 * bi]["wkv_p"]).reshape(8, 64, 64) for bi in range(B)])
    sh_p = np.stack([np.asarray(outs[2 * bi]["shift_p"]).reshape(1, SHW) for bi in range(B)])
    y_s = np.concatenate([np.asarray(outs[c]["y_s"]) for c in range(8)]).reshape(128, 1, D)
    k_s = np.concatenate([np.asarray(outs[c]["k_s"]) for c in range(8)]).reshape(128, 1, 2, 64)
    v_s = np.concatenate([np.asarray(outs[c]["v_s"]) for c in range(8)]).reshape(128, 1, 2, 64)
    ki_s = np.concatenate([np.asarray(outs[c]["ki_s"]) for c in range(8)]).reshape(128, 1, 64)
    wkv_s = np.concatenate([np.asarray(outs[c]["wkv_s"]) for c in range(8)]).reshape(128, 8, 64, 64)
    sh_s = np.concatenate([np.asarray(outs[c]["shift_s"]) for c in range(8)]).reshape(128, 1, SHW)
    f = lambda a: np.ascontiguousarray(a, dtype=np.float32)
    return (f(y_p), f(y_s), f(k_p), f(v_p), f(ki_p), f(wkv_pp), f(sh_p), f(k_s), f(v_s), f(ki_s), f(wkv_s), f(sh_s))
```

```python
import os
import numpy as np
from contextlib import ExitStack
import concourse.bass as bass
import concourse.mybir as mybir
from concourse.bass_utils import run_bass_kernel_spmd

F32 = mybir.dt.float32
BF16 = mybir.dt.bfloat16
I32 = mybir.dt.int32
AF = mybir.ActivationFunctionType
ALU = mybir.AluOpType
AX = mybir.AxisListType

ENGS = ("pe", "act", "dve", "pool", "sp")
NDMA = 32
NSW = 8

D = 1024
HD = 64
DIN = 4040
C_Q, C_K, C_V, C_QI, C_WI, C_KI, C_GA = 0, 512, 640, 768, 1280, 1288, 1352
C_R, C_RK, C_RV, C_WD, C_AD, C_GR = 1864, 2376, 2888, 3400, 3464, 3528
SHW = 1664
NORM_EPS = 1e-6
GN_EPS = 64e-5
ROPE_THETA = 500000.0


PSUM_RES = {"PTb", "F2", "R2", "K2", "V2", "R2a", "R2b", "K2a", "K2b", "V2a", "V2b"}


class Res:
    __slots__ = ("w", "r")

    def __init__(self):
        self.w = None
        self.r = []


class Bld:
    def __init__(self, nc, es):
        self.nc = nc
        self.es = es
        self.sem = {e: es.enter_context(nc.semaphore("s_" + e)) for e in ENGS}
        self.dsem = [es.enter_context(nc.semaphore("d%d" % i)) for i in range(NDMA)]
        self.dval = [0] * NDMA
        self.dnext = 0
        self.dnext_sw = 0
        self.cnt = {e: 0 for e in ENGS}
        self.waited = {e: {} for e in ENGS}
        self.ops = {e: [] for e in ENGS}
        self.res = {}

    def sb(self, name, shape, dt=F32):
        return self.es.enter_context(self.nc.sbuf_tensor("sb_" + name, list(shape), dt))

    def ps(self, name, shape, dt=F32):
        return self.es.enter_context(self.nc.psum_tensor("ps_" + name, list(shape), dt))

    def _r(self, key):
        r = self.res.get(key)
        if r is None:
            r = self.res[key] = Res()
        return r

    def _need(self, e, tok, waits):
        if tok is None:
            return
        key, val = tok
        if key == "pe" and e == "pe":
            return
        if self.waited[e].get(key, 0) >= val:
            return
        self.waited[e][key] = val
        waits.append((key, val))

    def op(self, e, fn, reads=(), writes=(), dma=False):
        if e == "pool" and not dma:
            e = "dve"
        pr = [k for k in reads if k in PSUM_RES]
        if pr:
            reads = [k for k in reads if k not in PSUM_RES]
            writes = list(writes) + pr
        waits = []
        for k in reads:
            self._need(e, self._r(k).w, waits)
        for k in writes:
            r = self._r(k)
            self._need(e, r.w, waits)
            for t in r.r:
                self._need(e, t, waits)
        if dma:
            if e == "pool":
                i = NDMA - NSW + self.dnext_sw
                self.dnext_sw = (self.dnext_sw + 1) % NSW
            else:
                i = self.dnext
                self.dnext = (self.dnext + 1) % (NDMA - NSW)
            if self.dval[i] > 0:
                self._need(e, (("d", i), self.dval[i]), waits)
            self.dval[i] += 16
            tok = (("d", i), self.dval[i])
            inc = (self.dsem[i], 16)
        else:
            self.cnt[e] += 1
            tok = (e, self.cnt[e])
            inc = (self.sem[e], 1)
        self.ops[e].append((waits, fn, inc))
        for k in reads:
            self._r(k).r.append(tok)
        for k in writes:
            r = self._r(k)
            r.w = tok
            r.r = []
        return tok

    def dma(self, e, out, in_, reads=(), writes=()):
        return self.op(e, lambda g: g.dma_start(out=out, in_=in_), reads, writes, dma=True)

    def barrier(self):
        for e in ENGS:
            waits = []
            for e2 in ENGS:
                if e2 != e and self.cnt[e2] > 0:
                    self._need(e, (e2, self.cnt[e2]), waits)
            for i in range(NDMA):
                if self.dval[i] > 0:
                    self._need(e, (("d", i), self.dval[i]), waits)
            self.ops[e].append((waits, None, None))

    def emit(self):
        nc = self.nc
        with nc.Block() as block:
            def mk(e):
                def body(g):
                    for waits, fn, inc in self.ops[e]:
                        for key, val in waits:
                            s = self.dsem[key[1]] if isinstance(key, tuple) else self.sem[key]
                            g.wait_ge(s, val)
                        if fn is not None:
                            fn(g).then_inc(inc[0], inc[1])
                return body
            block.tensor(mk("pe"))
            block.scalar(mk("act"))
            block.vector(mk("dve"))
            block.gpsimd(mk("pool"))
            block.sync(mk("sp"))


def build(T=4096, NS=16, NPG=64, NPHYS=10240, topk_p=256, topk_s=256, n_bis=16, do_sample=True, AW=36000, stop_after=None, nt_lim=None, no_lim=None):
    NT = T // 128
    NO = NT // 2
    NTILES = NT + NO + 1
    nc = bass.Bass("TRN2", target_bir_lowering=False)
    es = ExitStack()
    b = Bld(nc, es)

    def din(name, shape, dt=F32):
        return nc.dram_tensor(name, list(shape), dt, kind="ExternalInput").ap()

    def dout(name, shape, dt=F32):
        return nc.dram_tensor(name, list(shape), dt, kind="ExternalOutput").ap()

    xall = din("xall", [NTILES * 128, D])
    call = din("call", [17, D])
    w_in = din("w_in", [D, DIN])
    w_ada = din("w_ada", [D, 3 * D])
    b_ada = din("b_ada", [3 * D])
    norm_w = din("norm_w", [D])
    w_out = din("w_out", [D, D])
    qnw = din("qnw", [HD])
    knw = din("knw", [HD])
    mu = din("mu", [SHW])
    pw0 = din("w0", [512]); pa0 = din("a0", [512]); pkk = din("k_k", [512]); pka = din("k_a", [512])
    prk = din("r_k", [512]); plnw = din("ln_x_w", [512]); plnb = din("ln_x_b", [512])
    w_up = din("w_up", [64, 512]); a_up = din("a_up", [64, 512])
    identf_d = din("identf", [128, 128])
    cs_all = din("cs_all", [NTILES * 128, 16])
    parsel_d = din("parsel", [128, 2])
    qrel_d = din("qrel", [128, 1])
    ownidx_d = din("ownidx", [128, NO], I32)
    iota_d = din("iota256", [128, 256])
    maskT_d = din("maskT", [64, 256])
    maskL_d = din("maskL", [64, 64])
    reset_d = din("resetm", [64, 1024])
    sel16_d = din("sel16", [17, 128])
    ones64_d = din("ones64", [64, 64])

    swkv_d = din("swkv", [128, 4096]); sshift_d = din("sshift", [16, SHW]); ptab_d = din("ptab", [128, 8], I32)
    if do_sample:
        cache_k = din("cache_k", [NPHYS * 128, 128]); cache_v = din("cache_v", [NPHYS * 128, 128])
        cache_ki = din("cache_kidx", [NPHYS, 8192])
    rep_d = din("rep", [16, 8 * 128]); repT_d = din("repT", [128, 8 * 16]); blk_d = din("blk", [128, 128]); oh0_d = din("oh0", [128, 1])
    y_s = dout("y_s", [16, D]); k_s = dout("k_s", [16, 128]); v_s = dout("v_s", [16, 128]); ki_s = dout("ki_s", [16, 64])
    wkv_s = dout("wkv_s", [128, 4096]); shift_s = dout("shift_s", [16, SHW])
    gscr = nc.dram_tensor("gscr", [17, D], F32, kind="Internal").ap()
    scr1 = nc.dram_tensor("scr1", [16, 3072], F32, kind="Internal").ap()
    scr2 = nc.dram_tensor("scr2", [128, 64], F32, kind="Internal").ap()
    scr3 = nc.dram_tensor("scr3", [16, 512], F32, kind="Internal").ap()
    scr4 = nc.dram_tensor("scr4", [16, 64, 24], F32, kind="Internal").ap()
    y_own = dout("y_own", [NO * 128, D])
    k_nat = dout("k_nat", [T, 128]); v_nat = dout("v_nat", [T, 128]); ki_nat = dout("ki_nat", [T, 64])
    wkv_p = dout("wkv_p", [8, 64, 64]); shift_p = dout("shift_p", [SHW])
    rwscr = nc.dram_tensor("rwscr", [T, 512], F32, kind="Internal").ap()

    PTb = b.ps("PTb", [128, 1024], BF16)
    F2 = b.ps("F2", [128, 512])
    R2 = b.ps("R2", [128, 1024])
    K2 = b.ps("K2", [128, 1024])
    V2 = b.ps("V2", [128, 1024])

    identf = b.sb("identf", [128, 128]); identb = b.sb("identb", [128, 128], BF16)
    cst = b.sb("cst", [128, 4])
    kT_all = b.sb("kT_all", [64, 2, T], BF16)
    kiT_all = b.sb("kiT_all", [64, T], BF16)
    Vaug = b.sb("Vaug", [128, NT, 2, 65], BF16)
    modT = b.sb("modT", [128, 24, 17])
    g1 = b.sb("g1", [128, 8, 17])
    nwT = b.sb("nwT", [128, 8]); badaT = b.sb("badaT", [128, 24])
    lnw_bc = b.sb("lnw_bc", [64, 512]); lnb_bc = b.sb("lnb_bc", [64, 512])
    qnw_bc = b.sb("qnw_bc", [128, 64]); knw_bc = b.sb("knw_bc", [128, 64])
    sel16 = b.sb("sel16", [17, 128]); ones64 = b.sb("ones64", [64, 64])
    maskT = b.sb("maskT", [64, 256]); maskL = b.sb("maskL", [64, 64]); resetm = b.sb("resetm", [64, 1024])
    qrel = b.sb("qrel", [128, 1]); parsel = b.sb("parsel", [128, 2]); ownidx = b.sb("ownidx", [128, NO], I32)
    fp = {}
    for nm in ("w0", "a0", "kk", "ka", "rk"):
        fp[nm] = b.sb("fp_" + nm, [64, 8])
    muT = b.sb("muT", [64, 26]); wupS = b.sb("wupS", [64, 512]); aupS = b.sb("aupS", [64, 512])
    xt0 = b.sb("xt0", [128, D]); xt = [xt0, xt0]
    xn = b.sb("xn", [128, D], BF16)
    hT0 = b.sb("hT0", [128, 8, 128], BF16); hT = [hT0, hT0]
    hTs = b.sb("hTs", [128, 8, 128], BF16)
    hlast = b.sb("hlast", [128, 8, 1], BF16)
    cs_t = b.sb("cs_t", [128, 16])
    sm = b.sb("sm", [128, 64])
    ARENA = b.sb("ARENA", [128, AW])
    csT = b.sb("csT", [128, 8, 17])

    def TT(e, out, in0, in1, op, R, W):
        b.op(e, lambda g: g.tensor_tensor(out=out, in0=in0, in1=in1, op=op), R, W)

    def TS(e, out, in0, s1, s2, op0, op1, R, W, accum=None):
        if op1 is None:
            b.op(e, lambda g: g.tensor_scalar(out=out, in0=in0, scalar1=s1, scalar2=None, op0=op0), R, W)
        elif accum is None:
            b.op(e, lambda g: g.tensor_scalar(out=out, in0=in0, scalar1=s1, scalar2=s2, op0=op0, op1=op1), R, W)
        else:
            b.op(e, lambda g: g.tensor_scalar(out=out, in0=in0, scalar1=s1, scalar2=s2, op0=op0, op1=op1,
                                              accum_out=accum), R, W)

    def STT(out, in0, scalar, in1, op0, op1, R, W):
        b.op("dve", lambda g: g.scalar_tensor_tensor(out=out, in0=in0, scalar=scalar, in1=in1, op0=op0, op1=op1), R, W)

    def ACT(out, in_, func, R, W, scale=1.0, bias=None, accum=None):
        kw = {}
        if bias is not None:
            kw["bias"] = bias
        if accum is not None:
            kw["accum_out"] = accum
        b.op("act", lambda g: g.activation(out=out, in_=in_, func=func, scale=scale, **kw), R, W)

    def MM(out, lhsT, rhs, start, stop, R, W):
        b.op("pe", lambda g: g.matmul(out=out, lhsT=lhsT, rhs=rhs, start=start, stop=stop), R, W)

    def TR(out, in_, ident, R, W):
        b.op("pe", lambda g: g.transpose(out=out, in_=in_, identity=ident), R, W)

    def CP(e, out, in_, R, W):
        if e == "act":
            b.op(e, lambda g: g.copy(out=out, in_=in_), R, W)
        else:
            b.op(e, lambda g: g.tensor_copy(out=out, in_=in_), R, W)

    def RED(out, in_, op, R, W, axis=AX.X):
        b.op("dve", lambda g: g.tensor_reduce(out=out, in_=in_, axis=axis, op=op), R, W)

    def MS(e, ap, val, W):
        b.op(e, lambda g: g.memset(ap, val), (), W)

    def bc(ap, shape):
        return ap.to_broadcast(list(shape))

    ncd = nc.allow_non_contiguous_dma(reason="small parameter layouts")
    ncd.__enter__()

    b.dma("sp", identf[:], identf_d[:, :], writes=["identf"])
    CP("dve", identb[:], identf[:], ["identf"], ["identb"])
    MS("dve", cst[:, 0:1], NORM_EPS, ["cst"]); MS("dve", cst[:, 1:2], GN_EPS, ["cst"]); MS("dve", cst[:, 2:3], 1e-24, ["cst"])
    for (t_, d_, nm) in ((sel16, sel16_d, "sel16"), (ones64, ones64_d, "ones64"), (maskT, maskT_d, "maskT"),
                         (maskL, maskL_d, "maskL"), (resetm, reset_d, "resetm"),
                         (qrel, qrel_d, "qrel"), (parsel, parsel_d, "parsel"), (ownidx, ownidx_d, "ownidx"), (wupS, w_up, "wupS"), (aupS, a_up, "aupS")):
        b.dma("sp", t_[:], d_[:, :], writes=[nm])
    for nm, src in (("w0", pw0), ("a0", pa0), ("kk", pkk), ("ka", pka), ("rk", prk)):
        b.dma("sp", fp[nm][:], src.rearrange("(h j) -> j h", j=64), writes=["fp_" + nm])
    b.dma("sp", muT[:], mu.rearrange("(c j) -> j c", j=64), writes=["muT"])
    b.dma("sp", nwT[:], norm_w.rearrange("(k p) -> p k", p=128), writes=["nwT"])
    b.dma("sp", badaT[:], b_ada.rearrange("(t p) -> p t", p=128), writes=["badaT"])
    b.dma("sp", lnw_bc[:], plnw.partition_broadcast(64), writes=["lnw_bc"])
    b.dma("sp", lnb_bc[:], plnb.partition_broadcast(64), writes=["lnb_bc"])
    b.dma("sp", qnw_bc[:], qnw.partition_broadcast(128), writes=["qnw_bc"])
    b.dma("sp", knw_bc[:], knw.partition_broadcast(128), writes=["knw_bc"])
    MS("pool", Vaug[:, :, :, 64:65], 1.0, ["Vaug"])

    def carve(off, shape, dt=F32):
        n = int(np.prod(shape[1:]))
        words = n if dt in (F32, I32) else (n + 1) // 2
        v = ARENA[0:shape[0], off:off + words]
        if dt != F32:
            v = v.bitcast(dt)
        if len(shape) == 3:
            v = v.rearrange("p (a b) -> p a b", a=shape[1])
        elif len(shape) == 4:
            v = v.rearrange("p (a b c) -> p a b c", a=shape[1], b=shape[2])
        return v, off + words

    off = 0
    Wn, off = carve(off, [128, 8, 832], BF16)
    Wm, off = carve(off, [128, 8, SHW], BF16)
    Wom, off = carve(off, [128, 8, SHW], BF16)
    W_end = off
    stg, off = carve(off, [128, 8, 512])
    mu_bc, off = carve(off, [128, SHW])
    omu_bc, off = carve(off, [128, SHW])
    gtok, off = carve(off, [17, D])
    bgate, off = carve(off, [17, D])
    csall_sil, off = carve(off, [17, D])

    b.dma("sp", mu_bc, mu.partition_broadcast(128), writes=["mu_bc"])
    b.dma("sp", bgate, b_ada[2 * D:3 * D].partition_broadcast(17), writes=["bgate"])
    TS("dve", omu_bc, mu_bc, -1.0, 1.0, ALU.mult, ALU.add, ["mu_bc"], ["omu_bc"])
    w_in_v = w_in.rearrange("(k p) c -> p k c", p=128)

    def load_cols(dst, dcol, c0, n, scale_bc=None, scale_off=0, tag=""):
        done = 0
        while done < n:
            w = min(512, n - done)
            b.dma("sp", stg[:, :, 0:w], w_in_v[:, :, c0 + done:c0 + done + w], writes=["stg"])
            if scale_bc is None:
                CP("pool", dst[:, :, dcol + done:dcol + done + w], stg[:, :, 0:w], ["stg"], [tag])
            else:
                for sname, sbcv, d2 in scale_bc:
                    TT("dve", d2[:, :, dcol + done:dcol + done + w], stg[:, :, 0:w],
                       bc(sbcv[:, scale_off + done:scale_off + done + w].unsqueeze(1), [128, 8, w]),
                       ALU.mult, ["stg", sname], [tag])
            done += w

    load_cols(Wn, 0, C_K, 256, tag="Wn")
    load_cols(Wn, 256, C_KI, 64, tag="Wn")
    load_cols(Wn, 320, C_GR, 512, tag="Wn")
    load_cols(None, 0, C_R, SHW, scale_bc=[("mu_bc", mu_bc, Wm), ("omu_bc", omu_bc, Wom)], tag="Wm")
    calt = sm
    b.dma("sp", csall_sil, call[:, :], writes=["csil"])
    ACT(csall_sil, csall_sil, AF.Silu, ["csil"], ["csil"])
    for k in range(8):
        TR(F2[:, k * 17:(k + 1) * 17], csall_sil[:, k * 128:(k + 1) * 128], identf[0:17, 0:17], ["csil", "identf"], ["F2"])
    CP("dve", csT[:], F2[:, 0:136].rearrange("p (k m) -> p k m", k=8), ["F2"], ["csT"])
    w_ada_v = w_ada.rearrange("(k p) c -> p k c", p=128)
    for ch in range(6):
        b.dma("sp", stg[:, :, :], w_ada_v[:, :, ch * 512:(ch + 1) * 512], writes=["stg"])
        for ct in range(4):
            for k in range(8):
                MM(R2[:, ct * 17:(ct + 1) * 17], stg[:, k, ct * 128:(ct + 1) * 128], csT[:, k, :], k == 0, k == 7,
                   ["stg", "csT"], ["R2"])
        TT("dve", modT[:, ch * 4:(ch + 1) * 4, :], R2[:, 0:68].rearrange("p (c m) -> p c m", c=4),
           bc(badaT[:, ch * 4:(ch + 1) * 4].unsqueeze(2), [128, 4, 17]), ALU.add, ["R2", "badaT"], ["modT"])
        if ch >= 4:
            for k in range(8):
                MM(K2[0:17, 0:512], csT[:, k, :], stg[:, k, :], k == 0, k == 7, ["stg", "csT"], ["K2"])
            TT("dve", gtok[:, (ch - 4) * 512:(ch - 3) * 512], K2[0:17, 0:512], bgate[:, (ch - 4) * 512:(ch - 3) * 512],
               ALU.add, ["K2", "bgate"], ["gtok"])
    b.dma("sp", gscr[:, :], gtok[0:17, :], reads=["gtok"], writes=["gscr"])
    STT(g1[:], modT[:, 8:16, :], 1.0, bc(nwT[:].unsqueeze(2), [128, 8, 17]), ALU.add, ALU.mult, ["modT", "nwT"], ["g1"])
    def front(ti, par, m_prompt=True, ntok=128):
        x_ = xt[par]
        h_ = hT[par]
        xr, hr = "xt0", "hT0"
        b.dma("sp", x_[:], xall[ti * 128:(ti + 1) * 128, :], writes=[xr])
        b.dma("sp", cs_t[:], cs_all[ti * 128:(ti + 1) * 128, :], writes=["cs_t"])
        ACT(xn[:], x_[:], AF.Square, [xr], ["xn", "sm"], accum=sm[:, 0:1])
        ACT(sm[:, 1:2], sm[:, 0:1], AF.Sqrt, ["sm", "cst"], ["sm"], scale=1.0 / D, bias=cst[:, 0:1])
        b.op("dve", lambda g: g.reciprocal(out=sm[:, 2:3], in_=sm[:, 1:2]), ["sm"], ["sm"])
        TS("dve", xn[:], x_[:], sm[:, 2:3], None, ALU.mult, None, [xr, "sm"], ["xn"])
        for k in range(8):
            TR(PTb[:, k * 128:(k + 1) * 128], xn[:, k * 128:(k + 1) * 128], identb[:], ["xn", "identb"], ["PTb"])
        pv = PTb[:, :].rearrange("p (k t) -> p k t", k=8)
        if m_prompt:
            TT("dve", h_[:], pv, bc(g1[:, :, 16:17], [128, 8, 128]), ALU.mult, ["PTb", "g1"], [hr])
            TT("pool", h_[:], h_[:], bc(modT[:, 0:8, 16:17], [128, 8, 128]), ALU.add, [hr, "modT"], [hr])
        else:
            TT("dve", h_[:, :, 0:ntok], pv[:, :, 0:ntok], g1[:, :, 0:ntok], ALU.mult, ["PTb", "g1"], [hr])
            TT("pool", h_[:, :, 0:ntok], h_[:, :, 0:ntok], modT[:, 0:8, 0:ntok], ALU.add, [hr, "modT"], [hr])
        return x_, h_, xr, hr

    def rope(e, buf, nh, hd_stride_view, R, nrows=128):
        x1 = buf[:, :, 0:8]
        x2 = buf[:, :, 8:16]
        cosb = bc(cs_t[0:nrows, 0:8].unsqueeze(1), [nrows, nh, 8])
        sinb = bc(cs_t[0:nrows, 8:16].unsqueeze(1), [nrows, nh, 8])
        t = ropet[0:nrows, 0:4 * nh * 8].rearrange("p (a h d) -> p a h d", a=4, h=nh)
        TT(e, t[:, 0], x1, cosb, ALU.mult, R + ["cs_t"], ["ropet"])
        TT(e, t[:, 1], x2, sinb, ALU.mult, R + ["cs_t"], ["ropet"])
        TT(e, t[:, 2], x2, cosb, ALU.mult, R + ["cs_t"], ["ropet"])
        TT(e, t[:, 3], x1, sinb, ALU.mult, R + ["cs_t"], ["ropet"])
        TT(e, x1, t[:, 0], t[:, 1], ALU.subtract, ["ropet"], R)
        TT(e, x2, t[:, 2], t[:, 3], ALU.add, ["ropet"], R)

    ropet = b.sb("ropet", [128, 256])

    def qknorm(src_ps, dst, nh, wbc, extra_scale, Rsrc, Wdst, nrows=128):
        sq = nrm_t[0:nrows, 0:nh * 64].rearrange("p (h d) -> p h d", h=nh)
        ACT(sq, src_ps, AF.Square, Rsrc, ["nrm_t"])
        RED(sm[0:nrows, 8:8 + nh], sq, ALU.add, ["nrm_t"], ["sm"])
        ACT(sm[0:nrows, 16:16 + nh], sm[0:nrows, 8:8 + nh], AF.Sqrt, ["sm", "cst"], ["sm"], scale=1.0 / 64, bias=cst[0:nrows, 0:1])
        b.op("dve", lambda g: g.reciprocal(out=sm[0:nrows, 24:24 + nh], in_=sm[0:nrows, 16:16 + nh]), ["sm"], ["sm"])
        TT("dve", dst, src_ps, bc(sm[0:nrows, 24:24 + nh].unsqueeze(2), [nrows, nh, 64]), ALU.mult, Rsrc + ["sm"], Wdst)
        STT(dst, dst, float(extra_scale), bc(wbc[0:nrows, :].unsqueeze(1), [nrows, nh, 64]), ALU.mult, ALU.mult, Wdst + ["qnw_bc", "knw_bc"], Wdst)

    nrm_t = b.sb("nrm_t", [128, 512])
    kfin = b.sb("kfin", [128, 128]); vfin = b.sb("vfin", [128, 128]); kifin = b.sb("kifin", [128, 64])
    gr_s = b.sb("gr_s", [128, 512])

    off = W_end
    rw = {}
    for nm in ("tw", "adc"):
        rw[nm], off = carve(off, [64, 128])
    for nm in ("sg", "L", "g", "t1"):
        rw[nm], off = carve(off, [64, 8, 128])
    blkA, off = carve(off, [64, 2048])
    blkB, off = carve(off, [64, 3072])
    rw["gprev"] = blkA[:, 0:1024].rearrange("p (h t) -> p h t", h=8)
    rw["ginv"] = blkA[:, 1024:2048].rearrange("p (h t) -> p h t", h=8)
    rw["asig"] = blkB[:, 0:1024].rearrange("p (h t) -> p h t", h=8)
    rw["kkn"] = blkB[:, 1024:2048].rearrange("p (h t) -> p h t", h=8)
    rw["kmod"] = blkB[:, 2048:3072].rearrange("p (h t) -> p h t", h=8)
    AMx = blkA.bitcast(BF16).rearrange("p (h x) -> p h x", h=16)
    LNPx = blkB.bitcast(BF16).rearrange("p (a h x) -> p a h x", a=6, h=16)
    rw["sg"] = rw["sg"]
    QTt, off = carve(off, [64, 8, 2, 128], BF16)
    KTt, off = carve(off, [64, 8, 2, 128], BF16)
    def alias(view64, shape):
        return view64.rearrange("p h t -> p (h t)").bitcast(BF16)
    AM = AMx
    Lm = [LNPx[:, 0], LNPx[:, 1]]
    Nm = [LNPx[:, 2], LNPx[:, 3]]
    Pm = [LNPx[:, 4], LNPx[:, 5]]
    BKtok, off = carve(off, [64, 8, 2, 64], BF16)
    Vc, off = carve(off, [64, 2, 8, 64], BF16)
    P0s, off = carve(off, [64, 8, 64], BF16)
    Us, off = carve(off, [64, 8, 64], BF16)
    H32, off = carve(off, [64, 8, 64])
    Hb, off = carve(off, [64, 8, 64], BF16)
    ych, off = carve(off, [64, 8, 64])
    yt1, off = carve(off, [64, 8, 64])
    bon, off = carve(off, [128, 8])
    rawl, off = carve(off, [64, 26])
    st8, off = carve(off, [64, 64])
    assert off <= AW, off
    identb64 = identb[0:64, 0:64]

    KR = int(os.environ.get('KR', '9'))
    KQ = int(os.environ.get('KQ', '9'))

    def rwkv_tile(ti, h_, hr):
        CP("pool", hTs[:, :, 1:128], h_[:, :, 0:127], [hr], ["hTs"])
        CP("pool", hTs[:, :, 0:1], hlast[:], ["hlast"], ["hTs"])
        CP("pool", hlast[:], h_[:, :, 127:128], [hr], ["hlast"])
        if KQ < 1:
            return
        for c in range(2):
            col = 1536 + c * 64
            for k in range(8):
                MM(F2[0:64, c * 128:(c + 1) * 128], Wom[:, k, col:col + 64], h_[:, k, :], k == 0, False, ["Wm", hr], ["F2"])
            for k in range(8):
                MM(F2[0:64, c * 128:(c + 1) * 128], Wm[:, k, col:col + 64], hTs[:, k, :], False, k == 7, ["Wm", "hTs"], ["F2"])
        if KQ < 2:
            return
        KW = int(os.environ.get('KW', '3'))
        if KW & 1:
            ACT(rw["tw"], F2[0:64, 0:128], AF.Tanh, ["F2"], ["tw"])
        if KW & 2:
            CP("dve", rw["adc"], F2[0:64, 128:256], ["F2"], ["adc"])
        if KR < 1:
            return
        R2v = R2[0:64, :].rearrange("p (h t) -> p h t", h=8)
        K2v = K2[0:64, :].rearrange("p (h t) -> p h t", h=8)
        V2v = V2[0:64, :].rearrange("p (h t) -> p h t", h=8)
        for h in range(8):
            MM(R2v[:, h, :], wupS[:, h * 64:(h + 1) * 64], rw["tw"], True, True, ["wupS", "tw"], ["R2"])
            MM(K2v[:, h, :], aupS[:, h * 64:(h + 1) * 64], rw["adc"], True, True, ["aupS", "adc"], ["K2"])
        TT("dve", rw["sg"], R2v, bc(fp["w0"][:].unsqueeze(2), [64, 8, 128]), ALU.add, ["R2", "fp_w0"], ["sg"])
        ACT(rw["sg"], rw["sg"], AF.Sigmoid, ["sg"], ["sg"])
        TT("dve", rw["asig"], K2v, bc(fp["a0"][:].unsqueeze(2), [64, 8, 128]), ALU.add, ["K2", "fp_a0"], ["asig"])
        ACT(rw["asig"], rw["asig"], AF.Sigmoid, ["asig"], ["asig"])
        TS("dve", rw["sg"], rw["sg"], -0.6065306597126334, None, ALU.mult, None, ["sg"], ["sg"])
        b.op("dve", lambda g: g.tensor_tensor_scan(out=rw["L"].rearrange("p h t -> p (h t)"), data0=resetm[:, :],
                                                   data1=rw["sg"].rearrange("p h t -> p (h t)"), initial=0.0,
                                                   op0=ALU.mult, op1=ALU.add), ["sg", "resetm"], ["L"])
        ACT(rw["g"], rw["L"], AF.Exp, ["L"], ["g"])
        ACT(rw["ginv"], rw["L"], AF.Exp, ["L"], ["ginv"], scale=-1.0)
        TT("pool", rw["gprev"], rw["L"], rw["sg"], ALU.subtract, ["L", "sg"], ["gprev"])
        ACT(rw["gprev"], rw["gprev"], AF.Exp, ["gprev"], ["gprev"])
        if KR < 2:
            return
        for (dstv, c0, nm) in ((R2v, 0, "R2"), (K2v, 512, "K2")):
            for h in range(8):
                col = c0 + h * 64
                for k in range(8):
                    MM(dstv[:, h, :], Wom[:, k, col:col + 64], h_[:, k, :], k == 0, False, ["Wm", hr], [nm])
                for k in range(8):
                    MM(dstv[:, h, :], Wm[:, k, col:col + 64], hTs[:, k, :], False, k == 7, ["Wm", "hTs"], [nm])
        TT("dve", rw["L"], K2v, bc(fp["kk"][:].unsqueeze(2), [64, 8, 128]), ALU.mult, ["K2", "fp_kk"], ["L"])
        ACT(rw["t1"], rw["L"], AF.Square, ["L"], ["t1"])
        t1f = rw["t1"].rearrange("p h t -> p (h t)")
        for hh in range(2):
            MM(V2[0:64, hh * 512:(hh + 1) * 512], ones64[:, :], t1f[:, hh * 512:(hh + 1) * 512], True, True, ["ones64", "t1"], ["V2"])
        ACT(rw["t1"], V2v, AF.Sqrt, ["V2", "cst"], ["t1"], bias=cst[0:64, 2:3])
        b.op("dve", lambda g: g.reciprocal(out=rw["t1"], in_=rw["t1"]), ["t1"], ["t1"])
        TT("dve", rw["kkn"], rw["L"], rw["t1"], ALU.mult, ["L", "t1"], ["kkn"])
        STT(rw["t1"], rw["asig"], -1.0, bc(fp["ka"][:].unsqueeze(2), [64, 8, 128]), ALU.add, ALU.mult, ["asig", "fp_ka"], ["t1"])
        STT(rw["kmod"], rw["t1"], 1.0, K2v, ALU.add, ALU.mult, ["t1", "K2"], ["kmod"])
        if KR < 3:
            return
        QTv = QTt.rearrange("p h c (q t) -> p h c q t", q=2)
        KTv = KTt.rearrange("p h c (q t) -> p h c q t", q=2)

        def ch(v):
            return v.rearrange("p h (c t) -> p h c t", c=2)
        STT(QTv[:, :, :, 0, :], ch(rw["kkn"]), -1.0, ch(rw["gprev"]), ALU.mult, ALU.mult, ["kkn", "gprev"], ["QTt"])
        TT("dve", QTv[:, :, :, 1, :], ch(R2v), ch(rw["g"]), ALU.mult, ["R2", "g"], ["QTt"])
        TT("pool", rw["t1"], rw["kkn"], rw["asig"], ALU.mult, ["kkn", "asig"], ["t1"])
        TT("pool", KTv[:, :, :, 0, :], ch(rw["t1"]), ch(rw["ginv"]), ALU.mult, ["t1", "ginv"], ["KTt"])
        TT("pool", KTv[:, :, :, 1, :], ch(rw["kmod"]), ch(rw["ginv"]), ALU.mult, ["kmod", "ginv"], ["KTt"])
        TT("dve", rw["L"], R2v, bc(fp["rk"][:].unsqueeze(2), [64, 8, 128]), ALU.mult, ["R2", "fp_rk"], ["L"])
        TT("dve", rw["L"], rw["L"], rw["kmod"], ALU.mult, ["L", "kmod"], ["L"])
        for h in range(8):
            MM(F2[:, 256 + h:257 + h], rw["L"][:, h, :], ones64[:, 0:1], True, True, ["L", "ones64"], ["F2"])
        CP("dve", bon, F2[:, 256:264], ["F2"], ["bon"])
        if KR < 4:
            return
        for h in range(8):
            col = 1024 + h * 64
            for k in range(8):
                MM(V2v[:, h, :], Wom[:, k, col:col + 64], h_[:, k, :], k == 0, False, ["Wm", hr], ["V2"])
            for k in range(8):
                MM(V2v[:, h, :], Wm[:, k, col:col + 64], hTs[:, k, :], False, k == 7, ["Wm", "hTs"], ["V2"])
        CP("act", rw["sg"], V2v, ["V2"], ["sg"])
        V2c = V2[0:64, :].rearrange("p (c h i) -> p c h i", c=2, h=8)
        for c in range(2):
            for h in range(8):
                TR(V2c[:, c, h, :], rw["sg"][:, h, c * 64:(c + 1) * 64], identf[0:64, 0:64], ["sg", "identf"], ["V2"])
        CP("act", Vc, V2c, ["V2"], ["Vc"])
        if int(os.environ.get("KLVL", "9")) < 3:
            return
        for q in range(4):
            c, hg = q // 2, q % 2
            bk, bkn = (K2, "K2") if q % 2 == 0 else (V2, "V2")
            AMp = bk[0:64, :].rearrange("p (h x) -> p h x", h=4)
            for hd in range(4):
                h = hg * 4 + hd
                MM(AMp[:, hd, 0:128], KTt[:, h, c, 0:64], QTt[:, h, c, :], True, True, ["KTt", "QTt"], [bkn])
                MM(AMp[:, hd, 128:256], KTt[:, h, c, 64:128], QTt[:, h, c, :], True, True, ["KTt", "QTt"], [bkn])
            TT("dve", AM[:, q * 4:(q + 1) * 4, :], AMp, bc(maskT[:].unsqueeze(1), [64, 4, 256]), ALU.mult, [bkn, "maskT"], ["AM"])
        Lp = R2[0:64, :].rearrange("p (h x) -> p h x", h=16)
        for q in range(4):
            c, hg = q // 2, q % 2
            for hd in range(4):
                h = hg * 4 + hd
                MM(Lp[:, q * 4 + hd, :], QTt[:, h, c, 0:64], KTt[:, h, c, 0:64], True, True, ["KTt", "QTt"], ["R2"])
        TT("dve", Lm[0], Lp, bc(maskL[:].unsqueeze(1), [64, 16, 64]), ALU.mult, ["R2", "maskL"], ["Lm0"])
        CP("act", Nm[0], AM[:, :, 0:64], ["AM"], ["Nm0"])
        TT("dve", Pm[0], AM[:, :, 0:64], bc(identb64.unsqueeze(1), [64, 16, 64]), ALU.add, ["AM", "identb"], ["Pm0"])
        cur = 0
        Np = K2[0:64, :].rearrange("p (h x) -> p h x", h=16)
        Lpp = V2[0:64, :].rearrange("p (h x) -> p h x", h=16)
        PPp = R2[0:64, :].rearrange("p (h x) -> p h x", h=16)
        for lvl in range(1, 6):
            nx = 1 - cur
            for i in range(16):
                if lvl < 5:
                    MM(Np[:, i, :], Lm[cur][:, i, :], Nm[cur][:, i, :], True, True, ["Lm%d" % cur, "Nm%d" % cur], ["K2"])
                MM(Lpp[:, i, :], Nm[cur][:, i, :], Lm[cur][:, i, :], True, True, ["Lm%d" % cur, "Nm%d" % cur], ["V2"])
            if lvl < 5:
                CP("act", Nm[nx], Np, ["K2"], ["Nm%d" % nx])
            CP("dve", Lm[nx], Lpp, ["V2"], ["Lm%d" % nx])
            for i in range(16):
                MM(PPp[:, i, :], Lm[nx][:, i, :], Pm[cur][:, i, :], True, True, ["Lm%d" % nx, "Pm%d" % cur], ["R2"])
            TT("dve", Pm[nx], PPp, Pm[cur], ALU.add, ["R2", "Pm%d" % cur], ["Pm%d" % nx])
            cur = nx
        P6 = Pm[cur]
        P6n = "Pm%d" % cur
        for c in range(2):
            BKp = PTb[0:64, :].rearrange("p (h q j) -> p h q j", h=8, q=2)
            for h in range(8):
                TR(BKp[:, h, 0, :], KTt[:, h, c, 0:64], identb64, ["KTt", "identb"], ["PTb"])
                TR(BKp[:, h, 1, :], KTt[:, h, c, 64:128], identb64, ["KTt", "identb"], ["PTb"])
            CP("act", BKtok, BKp, ["PTb"], ["BKtok"])
            P0p = F2[0:64, :].rearrange("p (h i) -> p h i", h=8)
            Up = R2[0:64, 0:512].rearrange("p (h i) -> p h i", h=8)
            Yp = K2[0:64, 0:512].rearrange("p (h i) -> p h i", h=8)
            Hp = V2[0:64, 0:512].rearrange("p (h i) -> p h i", h=8)

            def ai(h):
                return (c * 2 + h // 4) * 4 + h % 4
            for h in range(8):
                MM(P0p[:, h, :], QTt[:, h, c, 0:64], Hb[:, h, :], True, False, ["QTt", "Hb"], ["F2"])
                MM(P0p[:, h, :], AM[:, ai(h), 128:192], Vc[:, c, h, :], False, True, ["AM", "Vc"], ["F2"])
            CP("act", P0s, P0p, ["F2"], ["P0s"])
            for h in range(8):
                MM(Up[:, h, :], P6[:, ai(h), :], P0s[:, h, :], True, True, [P6n, "P0s"], ["R2"])
            CP("act", Us, Up, ["R2"], ["Us"])
            for h in range(8):
                MM(Yp[:, h, :], QTt[:, h, c, 64:128], Hb[:, h, :], True, False, ["QTt", "Hb"], ["K2"])
                MM(Yp[:, h, :], AM[:, ai(h), 64:128], Us[:, h, :], False, False, ["AM", "Us"], ["K2"])
                MM(Yp[:, h, :], AM[:, ai(h), 192:256], Vc[:, c, h, :], False, True, ["AM", "Vc"], ["K2"])
            for h in range(8):
                MM(Hp[:, h, :], BKtok[:, h, 0, :], Us[:, h, :], True, False, ["BKtok", "Us"], ["V2"])
                MM(Hp[:, h, :], BKtok[:, h, 1, :], Vc[:, c, h, :], False, True, ["BKtok", "Vc"], ["V2"])
            CP("act", ych, Yp, ["K2"], ["ych"])
            TT("dve", H32, H32, Hp, ALU.add, ["H32", "V2"], ["H32"])
            TT("dve", H32, H32, bc(rw["g"][:, :, c * 64 + 63:c * 64 + 64], [64, 8, 64]), ALU.mult, ["H32", "g"], ["H32"])
            CP("act", Hb, H32, ["H32"], ["Hb"])
            RED(st8[:, 0:8], ych, ALU.add, ["ych"], ["st8"])
            TT("dve", yt1, ych, ych, ALU.mult, ["ych"], ["yt1"])
            RED(st8[:, 8:16], yt1, ALU.add, ["yt1"], ["st8"])
            TS("dve", st8[:, 0:16], st8[:, 0:16], 1.0 / 64, None, ALU.mult, None, ["st8"], ["st8"])
            TT("dve", st8[:, 16:24], st8[:, 0:8], st8[:, 0:8], ALU.mult, ["st8"], ["st8"])
            TT("dve", st8[:, 24:32], st8[:, 8:16], st8[:, 16:24], ALU.subtract, ["st8"], ["st8"])
            ACT(st8[:, 32:40], st8[:, 24:32], AF.Sqrt, ["st8", "cst"], ["st8"], bias=cst[0:64, 1:2])
            b.op("dve", lambda g: g.reciprocal(out=st8[:, 40:48], in_=st8[:, 32:40]), ["st8"], ["st8"])
            TT("dve", yt1, ych, bc(st8[:, 0:8].unsqueeze(2), [64, 8, 64]), ALU.subtract, ["ych", "st8"], ["yt1"])
            TT("dve", yt1, yt1, bc(st8[:, 40:48].unsqueeze(2), [64, 8, 64]), ALU.mult, ["yt1", "st8"], ["yt1"])
            lnwv = lnw_bc[:].rearrange("p (h i) -> p h i", h=8)
            lnbv = lnb_bc[:].rearrange("p (h i) -> p h i", h=8)
            TT("dve", yt1, yt1, lnwv, ALU.mult, ["yt1", "lnw_bc"], ["yt1"])
            TT("pool", yt1, yt1, lnbv, ALU.add, ["yt1", "lnb_bc"], ["yt1"])
            MM(F2[0:64, 264:272], identf[:, c * 64:(c + 1) * 64], bon, True, True, ["identf", "bon"], ["F2"])
            CP("act", st8[:, 48:56], F2[0:64, 264:272], ["F2"], ["st8"])
            TT("dve", ych, Vc[:, c], bc(st8[:, 48:56].unsqueeze(2), [64, 8, 64]), ALU.mult, ["Vc", "st8"], ["ych"])
            TT("pool", yt1, yt1, ych, ALU.add, ["yt1", "ych"], ["yt1"])
            MM(R2[0:64, 0:512], identf[:, c * 64:(c + 1) * 64], gr_s[:, :], True, True, ["identf", "gr_s"], ["R2"])
            TT("dve", yt1.rearrange("p h i -> p (h i)"), yt1.rearrange("p h i -> p (h i)"), R2[0:64, 0:512], ALU.mult, ["yt1", "R2"], ["yt1"])
            b.dma("sp", rwscr[ti * 128 + c * 64: ti * 128 + (c + 1) * 64, :], yt1.rearrange("p h i -> p (h i)"), reads=["yt1"], writes=["rwscr"])
        b.barrier()

    if stop_after == "A":
        b.barrier(); b.emit(); ncd.__exit__(None, None, None); es.close()
        return nc
    b.barrier()
    MS("dve", H32, 0.0, ["H32"]); MS("dve", Hb, 0.0, ["Hb"]); MS("pool", hlast[:], 0.0, ["hlast"])
    KSUB = int(os.environ.get('KSUB', '9'))
    V2a = V2[:, 0:512]
    V2b = V2[:, 512:1024]
    for ti in range(NT if nt_lim is None else nt_lim):
        par = ti % 2
        x_, h_, xr, hr = front(ti, par)
        if KSUB >= 1:
            for (c0, n, dst) in ((0, 256, V2a[:, 0:256]), (256, 64, V2a[:, 256:320]), (320, 512, V2b)):
                for k in range(8):
                    MM(dst, h_[:, k, :], Wn[:, k, c0:c0 + n], k == 0, k == 7, [hr, "Wn"], ["V2"])
        if KSUB >= 2:
            kv3 = kfin[:].rearrange("p (g d) -> p g d", g=2)
            qknorm(V2a[:, 0:128].rearrange("p (g d) -> p g d", g=2), kv3, 2, knw_bc, 1.0, ["V2"], ["kfin"])
            rope("dve", kv3, 2, None, ["kfin"])
            CP("act", vfin[:], V2a[:, 128:256], ["V2"], ["vfin"])
            CP("act", Vaug[:, ti, :, 0:64], V2a[:, 128:256].rearrange("p (g d) -> p g d", g=2), ["V2"], ["Vaug"])
            CP("act", kifin[:], V2a[:, 256:320], ["V2"], ["kifin"])
            rope("pool", kifin[:].unsqueeze(1), 1, None, ["kifin"])
            ACT(gr_s[:], V2b, AF.Silu, ["V2"], ["gr_s"])
        if KSUB >= 3:
            b.dma("sp", k_nat[ti * 128:(ti + 1) * 128, :], kfin[:], reads=["kfin"])
            b.dma("sp", v_nat[ti * 128:(ti + 1) * 128, :], vfin[:], reads=["vfin"])
            b.dma("sp", ki_nat[ti * 128:(ti + 1) * 128, :], kifin[:], reads=["kifin"])
        if KSUB >= 4:
            for g_ in range(2):
                TR(F2[0:64, g_ * 128:(g_ + 1) * 128], kfin[:, g_ * 64:(g_ + 1) * 64], identf[:], ["kfin", "identf"], ["F2"])
            TR(F2[0:64, 256:384], kifin[:, :], identf[:], ["kifin", "identf"], ["F2"])
            if KSUB >= 5:
                CP("act", kT_all[:, :, ti * 128:(ti + 1) * 128], F2[0:64, 0:256].rearrange("p (g t) -> p g t", g=2), ["F2"], ["kT_all"])
            if KSUB >= 6:
                if os.environ.get("KV") == "1":
                    CP("act", nrm_t[0:64, 0:128], F2[0:64, 256:384], ["F2"], ["nrm_t"])
                elif os.environ.get("KV") == "2":
                    CP("act", kiT_all[:, ti * 128:(ti + 1) * 128], F2[0:64, 0:128], ["F2"], ["kiT_all"])
                else:
                    CP("act", kiT_all[:, ti * 128:(ti + 1) * 128], F2[0:64, 256:384], ["F2"], ["kiT_all"])

        if int(os.environ.get("KLVL", "9")) >= 2:
            rwkv_tile(ti, h_, hr)
        if ti == NT - 1:
            for cc in range(26):
                col = cc * 64
                for k in range(8):
                    MM(F2[0:64, 300 + cc:301 + cc], Wom[:, k, col:col + 64], h_[:, k, 127:128], k == 0, False, ["Wm", hr], ["F2"])
                for k in range(8):
                    MM(F2[0:64, 300 + cc:301 + cc], Wm[:, k, col:col + 64], h_[:, k, 127:128], False, k == 7, ["Wm", hr], ["F2"])
            CP("dve", rawl, F2[0:64, 300:326], ["F2"], ["rawl"])
            b.dma("sp", shift_p.rearrange("(c j) -> j c", j=64), rawl, reads=["rawl"])
    for h in range(8):
        TR(F2[0:64, h * 64:(h + 1) * 64], H32[:, h, :], identf[0:64, 0:64], ["H32", "identf"], ["F2"])
    CP("dve", ych, F2[0:64, 0:512].rearrange("p (h j) -> p h j", h=8), ["F2"], ["ych"])
    b.dma("sp", wkv_p.rearrange("h i j -> i h j"), ych, reads=["ych"])

    if stop_after == "B":
        b.barrier(); b.emit(); ncd.__exit__(None, None, None); es.close()
        return nc
    b.barrier()
    off = 0
    Wq, off = carve(off, [128, 8, 1544], BF16)
    stg, off = carve(off, [128, 8, 512])
    score, off = carve(off, [128, T])
    selm, off = carve(off, [128, T], BF16)
    selT, off = carve(off, [128, NT, 128], BF16)
    rl, off = carve(off, [128, 512])
    rl2, off = carve(off, [128, 512])
    qfin, off = carve(off, [128, 512])
    qifin, off = carve(off, [128, 512])
    qT, off = carve(off, [64, 8, 128], BF16)
    qiT, off = carve(off, [64, 8, 128], BF16)
    ga, off = carve(off, [128, 512])
    eT, off = carve(off, [128, 4, 128], BF16)
    pTt, off = carve(off, [128, 4, 128], BF16)
    eT2, off = carve(off, [128, 4, 128], BF16)
    pTt2, off = carve(off, [128, 4, 128], BF16)
    cat, off = carve(off, [128, D], BF16)
    catT, off = carve(off, [128, 8, 128], BF16)
    rwo, off = carve(off, [128, 512])
    rwo2, off = carve(off, [128, 512])
    att, off = carve(off, [128, 8, 64])
    ybuf, off = carve(off, [128, D])
    bs, off = carve(off, [128, 16])
    wis, off = carve(off, [128, 8])
    oacc, off = carve(off, [128, 2, 4, 65])
    Wout, off = carve(off, [128, 8, D], BF16)
    gate_bc, off = carve(off, [128, D])
    iota256, off = carve(off, [128, 256])
    assert off <= AW, off
    b.dma("sp", gate_bc, gscr[16:17, :].partition_broadcast(128) if False else gscr[16, :].partition_broadcast(128), reads=["gscr"], writes=["gate_bc"])
    b.dma("sp", iota256, iota_d[:, :], writes=["iota256"])
    load_cols(Wq, 0, C_Q, 512, tag="Wq")
    load_cols(Wq, 512, C_QI, 520, tag="Wq")
    load_cols(Wq, 1032, C_GA, 512, tag="Wq")
    w_out_v = w_out.rearrange("(k p) c -> p k c", p=128)
    for hh in range(2):
        b.dma("sp", stg[:, :, :], w_out_v[:, :, hh * 512:(hh + 1) * 512], writes=["stg"])
        CP("pool", Wout[:, :, hh * 512:(hh + 1) * 512], stg[:, :, :], ["stg"], ["Wout"])


    for j in range(NO if no_lim is None else no_lim):
        ti = NT + j
        x_, h_, xr, hr = front(ti, 0)
        NKT = 2 * (j + 1)
        NK = NKT * 128
        for (c0, n, dst, nm) in ((0, 512, R2[:, 0:512], "R2a"), (512, 512, R2[:, 512:1024], "R2b"),
                                 (1024, 8, F2[:, 0:8], "F2"), (1032, 512, K2[:, 0:512], "K2a")):
            for k in range(8):
                MM(dst, h_[:, k, :], Wq[:, k, c0:c0 + n], k == 0, k == 7, [hr, "Wq"], [nm])
        q3 = qfin.rearrange("p (h d) -> p h d", h=8)
        qknorm(R2[:, 0:512].rearrange("p (h d) -> p h d", h=8), q3, 8, qnw_bc, 0.125, ["R2a"], ["qfin"])
        rope("dve", q3, 8, None, ["qfin"])
        qi3 = qifin.rearrange("p (h d) -> p h d", h=8)
        CP("act", qifin, R2[:, 512:1024], ["R2b"], ["qifin"])
        rope("pool", qi3, 8, None, ["qifin"])
        TS("dve", wis, F2[:, 0:8], 0.044194173824159216, None, ALU.mult, None, ["F2"], ["wis"])
        ACT(ga, K2[:, 0:512], AF.Silu, ["K2a"], ["ga"])
        for (src, srcn, dstT, dn) in ((qfin, "qfin", qT, "qT"), (qifin, "qifin", qiT, "qiT")):
            pv = K2[0:64, :].rearrange("p (h t) -> p h t", h=8)
            for h in range(8):
                TR(pv[:, h, :], src[:, h * 64:(h + 1) * 64], identf[:], [srcn, "identf"], ["K2a" if h < 4 else "K2b"])
            CP("act", dstT, pv, ["K2a", "K2b"], [dn])
        nchk = (NK + 511) // 512
        ib = 0
        for kc in range(nchk):
            w = min(512, NK - kc * 512)
            for h in range(8):
                pb = (R2[:, 0:512], R2[:, 512:1024])[ib % 2]
                pbn = ("R2a", "R2b")[ib % 2]
                rlb = (rl, rl2)[ib % 2]
                rln = ("rl", "rl2")[ib % 2]
                ib += 1
                MM(pb[:, 0:w], qiT[:, h, :], kiT_all[:, kc * 512:kc * 512 + w], True, True, ["qiT", "kiT_all"], [pbn])
                ACT(rlb[:, 0:w], pb[:, 0:w], AF.Relu, [pbn], [rln])
                sc = score[:, kc * 512:kc * 512 + w]
                if h == 0:
                    TS("dve", sc, rlb[:, 0:w], wis[:, 0:1], None, ALU.mult, None, [rln, "wis"], ["score"])
                else:
                    STT(sc, rlb[:, 0:w], wis[:, h:h + 1], sc, ALU.mult, ALU.add, [rln, "wis", "score"], ["score"])
        RED(bs[:, 0:1], score[:, 0:NK], ALU.max, ["score"], ["bs"])
        RED(bs[:, 1:2], score[:, 0:NK], ALU.min, ["score"], ["bs"])
        TS("dve", rl[:, 0:256], iota256, qrel[:, 0:1], -1e30, ALU.is_gt, ALU.mult, ["iota256", "qrel"], ["rl"])
        TT("dve", score[:, NK - 256:NK], score[:, NK - 256:NK], rl[:, 0:256], ALU.add, ["score", "rl"], ["score"])
        TS("dve", bs[:, 2:3], bs[:, 1:2], -1.0, None, ALU.add, None, ["bs"], ["bs"])
        STT(bs[:, 3:4], bs[:, 0:1], 2.0, bs[:, 1:2], ALU.add, ALU.subtract, ["bs"], ["bs"])
        for it in range(1, n_bis + 1):
            sc_ = float(2.0 ** (-it))
            STT(bs[:, 4:5], bs[:, 3:4], sc_, bs[:, 2:3], ALU.mult, ALU.add, ["bs"], ["bs"])
            TS("dve", selm[:, 0:NK], score[:, 0:NK], bs[:, 4:5], 0.0, ALU.is_gt, ALU.add, ["score", "bs"], ["selm", "bs"], accum=bs[:, 5:6])
            TS("dve", bs[:, 6:7], bs[:, 5:6], float(topk_p) - 0.5, bs[:, 3:4], ALU.is_gt, ALU.mult, ["bs"], ["bs"])
            STT(bs[:, 2:3], bs[:, 6:7], sc_, bs[:, 2:3], ALU.mult, ALU.add, ["bs"], ["bs"])
        TS("dve", selm[:, 0:NK], score[:, 0:NK], bs[:, 2:3], None, ALU.is_gt, None, ["score", "bs"], ["selm"])
        for kt in range(NKT):
            TR(PTb[:, (kt % 8) * 128:(kt % 8 + 1) * 128], selm[:, kt * 128:(kt + 1) * 128], identb[:], ["selm", "identb"], ["PTb"])
            if kt % 8 == 7 or kt == NKT - 1:
                k0 = (kt // 8) * 8
                n_ = kt - k0 + 1
                CP("act", selT[:, k0:k0 + n_, :], PTb[:, 0:n_ * 128].rearrange("p (a t) -> p a t", a=n_), ["PTb"], ["selT"])
        po = [V2[:, 0:260].rearrange("p (h e) -> p h e", h=4), V2[:, 512:772].rearrange("p (h e) -> p h e", h=4)]
        for kt in range(NKT):
            for g_ in range(2):
                lp = K2[:, g_ * 512:(g_ + 1) * 512]
                kn_ = ("K2a", "K2b")[g_]
                vn_ = ("V2a", "V2b")[g_]
                eTb = (eT, eT2)[g_]
                pTb = (pTt, pTt2)[g_]
                en_ = ("eT", "eT2")[g_]
                pn_ = ("pTt", "pTt2")[g_]
                on_ = ("oacc0", "oacc1")[g_]
                MM(lp, kT_all[:, g_, kt * 128:(kt + 1) * 128], qT[:, g_ * 4:(g_ + 1) * 4, :].rearrange("p h t -> p (h t)"),
                   True, True, ["kT_all", "qT"], [kn_])
                ACT(eTb, lp.rearrange("p (h t) -> p h t", h=4), AF.Exp, [kn_], [en_])
                TT("dve", pTb, eTb, bc(selT[:, kt, :].unsqueeze(1), [128, 4, 128]), ALU.mult, [en_, "selT"], [pn_])
                for hh in range(4):
                    MM(po[g_][:, hh, :], pTb[:, hh, :], Vaug[:, kt, g_, :], True, True, [pn_, "Vaug"], [vn_])
                if kt == 0:
                    CP("act", oacc[:, g_], po[g_], [vn_], [on_])
                else:
                    TT("dve", oacc[:, g_], oacc[:, g_], po[g_], ALU.add, [vn_, on_], [on_])
        for g_ in range(2):
            b.op("dve", lambda g, g_=g_: g.reciprocal(out=bs[:, 8 + g_ * 4:12 + g_ * 4], in_=oacc[:, g_, :, 64]), ["oacc0", "oacc1"], ["bs"])
            TT("dve", att[:, g_ * 4:(g_ + 1) * 4, :], oacc[:, g_, :, 0:64], bc(bs[:, 8 + g_ * 4:12 + g_ * 4].unsqueeze(2), [128, 4, 64]),
               ALU.mult, ["oacc0", "oacc1", "bs"], ["att"])
        TT("dve", cat[:, 0:512], att.rearrange("p h d -> p (h d)"), ga, ALU.mult, ["att", "ga"], ["cat"])
        b.dma("sp", rwo, rwscr[(2 * j) * 128:(2 * j + 1) * 128, :], reads=["rwscr"], writes=["rwo"])
        b.dma("sp", rwo2, rwscr[(2 * j + 1) * 128:(2 * j + 2) * 128, :], reads=["rwscr"], writes=["rwo2"])
        TS("dve", rwo, rwo, parsel[:, 1:2], None, ALU.mult, None, ["rwo", "parsel"], ["rwo"])
        STT(rwo, rwo2, parsel[:, 0:1], rwo, ALU.mult, ALU.add, ["rwo2", "parsel", "rwo"], ["rwo"])
        CP("act", cat[:, 512:1024], rwo, ["rwo"], ["cat"])
        for k in range(8):
            TR(PTb[:, k * 128:(k + 1) * 128], cat[:, k * 128:(k + 1) * 128], identb[:], ["cat", "identb"], ["PTb"])
        CP("act", catT, PTb[:, :].rearrange("p (k t) -> p k t", k=8), ["PTb"], ["catT"])
        for hh in range(2):
            for k in range(8):
                MM(R2[:, hh * 512:(hh + 1) * 512], catT[:, k, :], Wout[:, k, hh * 512:(hh + 1) * 512], k == 0, k == 7, ["catT", "Wout"], [("R2a", "R2b")[hh]])
        TT("dve", ybuf, R2[:, :], gate_bc, ALU.mult, ["R2a", "R2b", "gate_bc"], ["ybuf"])
        TT("pool", ybuf, ybuf, x_[:], ALU.add, ["ybuf", xr], ["ybuf"])
        b.dma("sp", y_own[j * 128:(j + 1) * 128, :], ybuf, reads=["ybuf"])


    if do_sample:
        b.barrier()
        PW = 10500
        off = 0
        proj, off = carve(off, [16, DIN])
        tk = {}
        for nm in ("qs", "ga", "grs", "ta", "tb"):
            tk[nm], off = carve(off, [16, 512])
        ks_, off = carve(off, [16, 128])
        s16, off = carve(off, [16, 64])
        tokd, off = carve(off, [16, 1040])
        ysb, off = carve(off, [16, D])
        cats, off = carve(off, [16, D], BF16)
        catTs, off = carve(off, [128, 8, 16], BF16)
        assert off <= PW, off
        off = PW
        stg, off = carve(off, [128, 8, 512])
        wbf, off = carve(off, [128, 8, 512], BF16)
        sshift_t, off = carve(off, [16, SHW])
        mu16, off = carve(off, [16, SHW])
        X1 = off
        xm, off = carve(off, [16, SHW])
        prm, off = carve(off, [16, 5, 512])
        vecs, off = carve(off, [16, 8, 6, 64])
        for nm in ("dec", "asg", "kkv", "kkn", "kmod"):
            tk[nm], off = carve(off, [16, 512])
        wdt, off = carve(off, [16, 128])
        wdT, off = carve(off, [64, 32])
        assert off <= AW, off
        NPAIR = NS // 2

        x_, h_, xr, hr = front(NT + NO, 0, m_prompt=False, ntok=16)
        for ch in range(8):
            c0 = ch * 505
            b.dma("sp", stg[:, :, 0:505], w_in_v[:, :, c0:c0 + 505], writes=["stg"])
            CP("dve", wbf[:, :, 0:505], stg[:, :, 0:505], ["stg"], ["wbf"])
            for k in range(8):
                MM(R2[0:16, 0:505], h_[:, k, 0:16], wbf[:, k, 0:505], k == 0, k == 7, [hr, "wbf"], ["R2"])
            CP("act", proj[:, c0:c0 + 505], R2[0:16, 0:505], ["R2"], ["proj"])
        b.dma("sp", sshift_t, sshift_d[:, :], writes=["sshift"])
        b.dma("sp", mu16, mu.partition_broadcast(16), writes=["mu16"])
        for i_, src in enumerate((pw0, pa0, pkk, pka, prk)):
            b.dma("sp", prm[:, i_, :], src.partition_broadcast(16), writes=["prm"])
        qs3 = tk["qs"].rearrange("p (h d) -> p h d", h=8)
        qknorm(proj[:, 0:512].rearrange("p (h d) -> p h d", h=8), qs3, 8, qnw_bc, 0.125, ["proj"], ["qs"], nrows=16)
        rope("dve", qs3, 8, None, ["qs"], nrows=16)
        ks3 = ks_.rearrange("p (g d) -> p g d", g=2)
        qknorm(proj[:, 512:640].rearrange("p (g d) -> p g d", g=2), ks3, 2, knw_bc, 1.0, ["proj"], ["ks"], nrows=16)
        rope("dve", ks3, 2, None, ["ks"], nrows=16)
        b.dma("sp", k_s[:, :], ks_, reads=["ks"])
        b.dma("sp", v_s[:, :], proj[:, 640:768], reads=["proj"])
        b.dma("sp", shift_s[:, :], proj[:, C_R:C_R + SHW], reads=["proj"])
        rope("dve", proj[:, 768:1280].rearrange("p (h d) -> p h d", h=8), 8, None, ["proj"], nrows=16)
        rope("dve", proj[:, 1288:1352].unsqueeze(1), 1, None, ["proj"], nrows=16)
        b.dma("sp", ki_s[:, :], proj[:, 1288:1352], reads=["proj"])
        ACT(tk["ga"], proj[:, C_GA:C_GA + 512], AF.Silu, ["proj"], ["ga"])
        ACT(tk["grs"], proj[:, C_GR:C_GR + 512], AF.Silu, ["proj"], ["grs"])
        xs_ = proj[:, C_R:C_R + SHW]
        TT("dve", xm, sshift_t, xs_, ALU.subtract, ["sshift", "proj"], ["xm"])
        TT("dve", xm, xm, mu16, ALU.mult, ["xm", "mu16"], ["xm"])
        TT("dve", xm, xm, xs_, ALU.add, ["xm", "proj"], ["xm"])
        ACT(wdt[:, 0:64], xm[:, 1536:1600], AF.Tanh, ["xm"], ["wdt"])
        CP("dve", wdt[:, 64:128], xm[:, 1600:1664], ["xm"], ["wdt"])
        TR(F2[0:64, 0:16], wdt[:, 0:64], identf[0:16, 0:16], ["wdt", "identf"], ["F2"])
        TR(F2[0:64, 16:32], wdt[:, 64:128], identf[0:16, 0:16], ["wdt", "identf"], ["F2"])
        CP("dve", wdT, F2[0:64, 0:32], ["F2"], ["wdT"])
        MM(R2[0:16, 0:512], wdT[:, 0:16], wupS[:, :], True, True, ["wdT", "wupS"], ["R2"])
        MM(R2[0:16, 512:1024], wdT[:, 16:32], aupS[:, :], True, True, ["wdT", "aupS"], ["R2"])
        TT("dve", tk["dec"], R2[0:16, 0:512], prm[:, 0, :], ALU.add, ["R2", "prm"], ["dec"])
        ACT(tk["dec"], tk["dec"], AF.Sigmoid, ["dec"], ["dec"])
        ACT(tk["dec"], tk["dec"], AF.Exp, ["dec"], ["dec"], scale=-0.6065306597126334)
        TT("dve", tk["asg"], R2[0:16, 512:1024], prm[:, 1, :], ALU.add, ["R2", "prm"], ["asg"])
        ACT(tk["asg"], tk["asg"], AF.Sigmoid, ["asg"], ["asg"])
        xr_, xk_, xv_ = xm[:, 0:512], xm[:, 512:1024], xm[:, 1024:1536]
        TT("dve", tk["kkv"], xk_, prm[:, 2, :], ALU.mult, ["xm", "prm"], ["kkv"])
        ACT(tk["ta"], tk["kkv"], AF.Square, ["kkv"], ["ta"])
        RED(s16[:, 0:8], tk["ta"].rearrange("p (h d) -> p h d", h=8), ALU.add, ["ta"], ["s16"])
        ACT(s16[:, 8:16], s16[:, 0:8], AF.Sqrt, ["s16", "cst"], ["s16"], bias=cst[0:16, 2:3])
        b.op("dve", lambda g: g.reciprocal(out=s16[:, 16:24], in_=s16[:, 8:16]), ["s16"], ["s16"])
        TT("dve", tk["kkn"].rearrange("p (h d) -> p h d", h=8), tk["kkv"].rearrange("p (h d) -> p h d", h=8),
           bc(s16[:, 16:24].unsqueeze(2), [16, 8, 64]), ALU.mult, ["kkv", "s16"], ["kkn"])
        STT(tk["ta"], tk["asg"], -1.0, prm[:, 3, :], ALU.add, ALU.mult, ["asg", "prm"], ["ta"])
        STT(tk["kmod"], tk["ta"], 1.0, xk_, ALU.add, ALU.mult, ["ta", "xm"], ["kmod"])

        def v8(ap):
            return ap.rearrange("p (h d) -> p h d", h=8)
        CP("dve", vecs[:, :, 0, :], v8(tk["dec"]), ["dec"], ["vecs"])
        TS("dve", vecs[:, :, 1, :], v8(tk["kkn"]), -1.0, None, ALU.mult, None, ["kkn"], ["vecs"])
        TT("dve", vecs[:, :, 2, :], v8(tk["kkn"]), v8(tk["asg"]), ALU.mult, ["kkn", "asg"], ["vecs"])
        CP("dve", vecs[:, :, 3, :], v8(tk["kmod"]), ["kmod"], ["vecs"])
        CP("dve", vecs[:, :, 4, :], v8(xr_), ["xm"], ["vecs"])
        CP("dve", vecs[:, :, 5, :], v8(xv_), ["xm"], ["vecs"])
        TT("dve", tk["ta"], xr_, prm[:, 4, :], ALU.mult, ["xm", "prm"], ["ta"])
        TT("dve", tk["ta"], tk["ta"], tk["kmod"], ALU.mult, ["ta", "kmod"], ["ta"])
        RED(s16[:, 24:32], v8(tk["ta"]), ALU.add, ["ta"], ["s16"])
        b.dma("sp", scr1[:, :], vecs.rearrange("p h v j -> p (h v j)"), reads=["vecs"], writes=["scr1"])
        b.barrier()
        off = PW
        S_, off = carve(off, [128, 4096])
        tmpS, off = carve(off, [128, 4096])
        vsh, off = carve(off, [128, 384])
        ysh, off = carve(off, [128, 128])
        assert off <= X1
        b.dma("sp", S_, swkv_d[:, :], writes=["S"])
        b.dma("sp", vsh, scr1.rearrange("s (h x) -> (s h) x", h=8), reads=["scr1"], writes=["vsh"])
        S3 = S_.rearrange("p (i j) -> p i j", i=64)
        T3 = tmpS.rearrange("p (i j) -> p i j", i=64)

        def jb(vi):
            return bc(vsh[:, vi * 64:(vi + 1) * 64].unsqueeze(1), [128, 64, 64])

        def ib(ap):
            return bc(ap.unsqueeze(2), [128, 64, 64])
        TT("dve", T3, S3, jb(1), ALU.mult, ["S", "vsh"], ["tmpS"])
        RED(ysh[:, 0:64], T3, ALU.add, ["tmpS"], ["ysh"])
        TT("dve", S3, S3, jb(0), ALU.mult, ["S", "vsh"], ["S"])
        TT("dve", T3, jb(2), ib(ysh[:, 0:64]), ALU.mult, ["vsh", "ysh"], ["tmpS"])
        TT("dve", S3, S3, T3, ALU.add, ["S", "tmpS"], ["S"])
        TT("dve", T3, jb(3), ib(vsh[:, 320:384]), ALU.mult, ["vsh"], ["tmpS"])
        TT("dve", S3, S3, T3, ALU.add, ["S", "tmpS"], ["S"])
        b.dma("sp", wkv_s[:, :], S_, reads=["S"])
        TT("dve", T3, S3, jb(4), ALU.mult, ["S", "vsh"], ["tmpS"])
        RED(ysh[:, 64:128], T3, ALU.add, ["tmpS"], ["ysh"])
        b.dma("sp", scr2[:, :], ysh[:, 64:128], reads=["ysh"], writes=["scr2"])
        yS = tk["tb"]
        b.dma("sp", yS, scr2.rearrange("(s h) i -> s (h i)", h=8), reads=["scr2"], writes=["tb"])
        y3 = v8(yS)
        RED(s16[:, 32:40], y3, ALU.add, ["tb"], ["s16"])
        ACT(tk["ta"], yS, AF.Square, ["tb"], ["ta"])
        RED(s16[:, 40:48], v8(tk["ta"]), ALU.add, ["ta"], ["s16"])
        TS("dve", s16[:, 32:48], s16[:, 32:48], 1.0 / 64, None, ALU.mult, None, ["s16"], ["s16"])
        TT("dve", s16[:, 48:56], s16[:, 32:40], s16[:, 32:40], ALU.mult, ["s16"], ["s16"])
        TT("dve", s16[:, 48:56], s16[:, 40:48], s16[:, 48:56], ALU.subtract, ["s16"], ["s16"])
        ACT(s16[:, 56:64], s16[:, 48:56], AF.Sqrt, ["s16", "cst"], ["s16"], bias=cst[0:16, 1:2])
        b.op("dve", lambda g: g.reciprocal(out=s16[:, 56:64], in_=s16[:, 56:64]), ["s16"], ["s16"])
        TT("dve", y3, y3, bc(s16[:, 32:40].unsqueeze(2), [16, 8, 64]), ALU.subtract, ["tb", "s16"], ["tb"])
        TT("dve", y3, y3, bc(s16[:, 56:64].unsqueeze(2), [16, 8, 64]), ALU.mult, ["tb", "s16"], ["tb"])
        TT("dve", yS, yS, lnw_bc[0:16, :], ALU.mult, ["tb", "lnw_bc"], ["tb"])
        TT("dve", yS, yS, lnb_bc[0:16, :], ALU.add, ["tb", "lnb_bc"], ["tb"])
        TT("dve", v8(tk["ta"]), v8(xv_), bc(s16[:, 24:32].unsqueeze(2), [16, 8, 64]), ALU.mult, ["xm", "s16"], ["ta"])
        TT("dve", yS, yS, tk["ta"], ALU.add, ["tb", "ta"], ["tb"])
        TT("dve", cats[:, 512:1024], yS, tk["grs"], ALU.mult, ["tb", "grs"], ["cats"])
        b.barrier()
        off = PW
        Gi, off = carve(off, [128, 8192])
        tmpG, off = carve(off, [128, 64, 64])
        Kc, off = carve(off, [128, 24, 128])
        Vcd, off = carve(off, [128, 24, 128])
        tmpc, off = carve(off, [128, 24, 64])
        repd, off = carve(off, [128, 1040])
        opd, off = carve(off, [128, 520])
        repS, off = carve(off, [16, 1024])
        repTS, off = carve(off, [128, 128])
        blkS, off = carve(off, [128, 128])
        sc, off = carve(off, [128, 132])
        msc, off = carve(off, [128, 132])
        sh_, off = carve(off, [128, 128])
        cv, off = carve(off, [128, 24])
        ci, off = carve(off, [128, 24], I32)
        cif, off = carve(off, [128, 24])
        rowi, off = carve(off, [128, 24], I32)
        lg, off = carve(off, [128, 8, 24])
        b2, off = carve(off, [128, 16])
        ptab, off = carve(off, [128, 8], I32)
        ptf, off = carve(off, [128, 8])
        oh0, off = carve(off, [128, 1])
        assert off <= AW, off
        Gi3 = Gi.rearrange("p (t d) -> p t d", t=128)
        for (t_, d_, nm) in ((ptab, ptab_d, "ptab"), (repS, rep_d, "repS"), (repTS, repT_d, "repTS"), (blkS, blk_d, "blkS"), (oh0, oh0_d, "oh0")):
            b.dma("sp", t_, d_[:, :], writes=[nm])
        CP("dve", tokd[:, 0:512], proj[:, 768:1280], ["proj"], ["tokd"])
        TS("dve", tokd[:, 512:520], proj[:, 1280:1288], 0.044194173824159216, None, ALU.mult, None, ["proj"], ["tokd"])
        CP("dve", tokd[:, 520:1032], tk["qs"], ["qs"], ["tokd"])
        TT("dve", v8(tk["ta"]), v8(tokd[:, 0:512]), bc(proj[:, 1288:1352].unsqueeze(1), [16, 8, 64]), ALU.mult, ["tokd", "proj"], ["ta"])
        RED(s16[:, 0:8], v8(tk["ta"]), ALU.add, ["ta"], ["s16"])
        TS("dve", s16[:, 0:8], s16[:, 0:8], 0.0, None, ALU.max, None, ["s16"], ["s16"])
        TT("dve", s16[:, 0:8], s16[:, 0:8], tokd[:, 512:520], ALU.mult, ["s16", "tokd"], ["s16"])
        RED(tokd[:, 1032:1033], s16[:, 0:8], ALU.add, ["s16"], ["tokd"])
        CP("dve", ptf, ptab, ["ptab"], ["ptf"])
        TS("dve", ptf, ptf, 128.0, None, ALU.mult, None, ["ptf"], ["ptf"])
        ck_rows = cache_k
        cv_rows = cache_v
        cvA, off = carve(off, [128, 8, 24])
        ciA, off = carve(off, [128, 8, 24], I32)
        thrA, off = carve(off, [128, 8])
        tmpGf = tmpG.rearrange("p a b -> p (a b)")
        cand16 = tmpGf[0:16, 0:1540]
        candj = tmpGf[0:16, 1540:3080]
        assert off <= AW, off
        for sp in range(NPAIR):
            b.op("pool", lambda g, sp=sp: g.indirect_dma_start(out=Gi, out_offset=None, in_=cache_ki[:, :],
                                                                in_offset=bass.IndirectOffsetOnAxis(ap=ptab[:, sp:sp + 1], axis=0)),
                 ["ptab"], ["Gi"], dma=True)
            for (c0, n) in ((0, 512), (512, 8)):
                MM(K2[:, 0:n], repS[:, sp * 128:(sp + 1) * 128], tokd[:, c0:c0 + n], True, True, ["repS", "tokd"], ["K2"])
                CP("act", repd[:, c0:c0 + n], K2[:, 0:n], ["K2"], ["repd"])
            for h in range(8):
                for hf in range(2):
                    TT("dve", tmpG, Gi3[:, hf * 64:(hf + 1) * 64, :], bc(repd[:, h * 64:(h + 1) * 64].unsqueeze(1), [128, 64, 64]), ALU.mult, ["Gi", "repd"], ["tmpG"])
                    RED(sh_[:, hf * 64:(hf + 1) * 64], tmpG, ALU.add, ["tmpG"], ["sh"])
                if h == 0:
                    TS("dve", sc[:, 0:128], sh_, 0.0, repd[:, 512:513], ALU.max, ALU.mult, ["sh", "repd"], ["sc"])
                else:
                    TS("dve", sh_, sh_, 0.0, repd[:, 512 + h:513 + h], ALU.max, ALU.mult, ["sh", "repd"], ["sh"])
                    TT("dve", sc[:, 0:128], sc[:, 0:128], sh_, ALU.add, ["sc", "sh"], ["sc"])
            for r_ in range(3):
                b.op("dve", lambda g, r_=r_, sp=sp: g.max(out=cvA[:, sp, r_ * 8:(r_ + 1) * 8], in_=sc[:, 0:128]), ["sc"], ["cvA"])
                b.op("dve", lambda g, r_=r_, sp=sp: g.max_index(out=ciA[:, sp, r_ * 8:(r_ + 1) * 8].bitcast(mybir.dt.uint32),
                                                                in_max=cvA[:, sp, r_ * 8:(r_ + 1) * 8], in_values=sc[:, 0:128]), ["sc", "cvA"], ["ciA"])
                if r_ < 2:
                    b.op("dve", lambda g, r_=r_, sp=sp: g.match_replace(out=sc[:, 0:128], in_to_replace=cvA[:, sp, r_ * 8:(r_ + 1) * 8],
                                                                        in_values=sc[:, 0:128], imm_value=-3e30), ["sc", "cvA"], ["sc"])
        b.dma("sp", scr4.rearrange("(sp s2) g c -> (s2 g) sp c", s2=2), cvA, reads=["cvA"], writes=["scr4"])
        b.dma("sp", cand16[:, 0:1536], scr4.rearrange("s g c -> s (g c)"), reads=["scr4"], writes=["tmpG"])
        CP("dve", cand16[:, 1536:1537], tokd[:, 1032:1033], ["tokd"], ["tmpG"])
        cnd = cand16[:, 0:1537]
        RED(s16[:, 40:41], cnd, ALU.max, ["tmpG"], ["s16"])
        RED(s16[:, 41:42], cnd, ALU.min, ["tmpG"], ["s16"])
        TS("dve", s16[:, 42:43], s16[:, 41:42], -1.0, None, ALU.add, None, ["s16"], ["s16"])
        STT(s16[:, 43:44], s16[:, 40:41], 2.0, s16[:, 41:42], ALU.add, ALU.subtract, ["s16"], ["s16"])
        for it in range(1, n_bis + 5):
            sc_ = float(2.0 ** (-it))
            STT(s16[:, 44:45], s16[:, 43:44], sc_, s16[:, 42:43], ALU.mult, ALU.add, ["s16"], ["s16"])
            TS("dve", candj[:, 0:1537], cnd, s16[:, 44:45], 0.0, ALU.is_gt, ALU.add, ["tmpG", "s16"], ["tmpG", "s16"], accum=s16[:, 45:46])
            TS("dve", s16[:, 46:47], s16[:, 45:46], float(topk_s) - 0.5, s16[:, 43:44], ALU.is_gt, ALU.mult, ["s16"], ["s16"])
            STT(s16[:, 42:43], s16[:, 46:47], sc_, s16[:, 42:43], ALU.mult, ALU.add, ["s16"], ["s16"])
        TT("dve", s16[:, 32:33], tokd[:, 1032:1033], s16[:, 42:43], ALU.is_gt, ["tokd", "s16"], ["s16"])
        for sp in range(NPAIR):
            MM(F2[:, sp:sp + 1], repS[:, sp * 128:(sp + 1) * 128], s16[:, 42:43], True, True, ["repS", "s16"], ["F2"])
        CP("dve", thrA, F2[:, 0:8], ["F2"], ["thrA"])
        for sp in range(NPAIR):
            MM(K2[:, 0:512], repS[:, sp * 128:(sp + 1) * 128], tokd[:, 520:1032], True, True, ["repS", "tokd"], ["K2"])
            CP("act", repd[:, 520:1032], K2[:, 0:512], ["K2"], ["repd"])
            CP("dve", cif, ciA[:, sp, :], ["ciA"], ["cif"])
            TS("dve", cif, cif, ptf[:, sp:sp + 1], None, ALU.add, None, ["cif", "ptf"], ["cif"])
            CP("dve", rowi, cif, ["cif"], ["rowi"])
            TS("dve", cv, cvA[:, sp, :], thrA[:, sp:sp + 1], None, ALU.is_gt, None, ["cvA", "thrA"], ["cv"])
            for c_ in range(24):
                b.op("pool", lambda g, c_=c_: g.indirect_dma_start(out=Kc[:, c_, :], out_offset=None, in_=ck_rows[:, :],
                                                                  in_offset=bass.IndirectOffsetOnAxis(ap=rowi[:, c_:c_ + 1], axis=0)),
                     ["rowi"], ["Kc"], dma=True)
                b.op("pool", lambda g, c_=c_: g.indirect_dma_start(out=Vcd[:, c_, :], out_offset=None, in_=cv_rows[:, :],
                                                                  in_offset=bass.IndirectOffsetOnAxis(ap=rowi[:, c_:c_ + 1], axis=0)),
                     ["rowi"], ["Vcd"], dma=True)
            Kc4 = Kc.rearrange("p c (g d) -> p c g d", g=2)
            Vc4 = Vcd.rearrange("p c (g d) -> p c g d", g=2)
            for h in range(8):
                TT("dve", tmpc, Kc4[:, :, h // 4, :], bc(repd[:, 520 + h * 64:520 + (h + 1) * 64].unsqueeze(1), [128, 24, 64]), ALU.mult, ["Kc", "repd"], ["tmpc"])
                RED(lg[:, h, :], tmpc, ALU.add, ["tmpc"], ["lg"])
            ACT(lg, lg, AF.Exp, ["lg"], ["lg"])
            TT("dve", lg, lg, bc(cv.unsqueeze(1), [128, 8, 24]), ALU.mult, ["lg", "cv"], ["lg"])
            RED(opd[:, 512:520], lg, ALU.add, ["lg"], ["opd"])
            for h in range(8):
                TT("dve", tmpc, Vc4[:, :, h // 4, :], bc(lg[:, h, :].unsqueeze(2), [128, 24, 64]), ALU.mult, ["Vcd", "lg"], ["tmpc"])
                RED(opd[:, h * 64:(h + 1) * 64], tmpc.rearrange("p c d -> p d c"), ALU.add, ["tmpc"], ["opd"])
            MM(V2[0:16, 0:512], repTS[:, sp * 16:(sp + 1) * 16], opd[:, 0:512], sp == 0, sp == NPAIR - 1, ["repTS", "opd"], ["V2"])
            MM(V2[0:16, 512:520], repTS[:, sp * 16:(sp + 1) * 16], opd[:, 512:520], sp == 0, sp == NPAIR - 1, ["repTS", "opd"], ["V2"])
        qv = v8(tk["qs"])
        for g_ in range(2):
            TT("dve", v8(tk["ta"])[:, g_ * 4:(g_ + 1) * 4, :], qv[:, g_ * 4:(g_ + 1) * 4, :],
               bc(ks_[:, g_ * 64:(g_ + 1) * 64].unsqueeze(1), [16, 4, 64]), ALU.mult, ["qs", "ks"], ["ta"])
        RED(s16[:, 0:8], v8(tk["ta"]), ALU.add, ["ta"], ["s16"])
        ACT(s16[:, 0:8], s16[:, 0:8], AF.Exp, ["s16"], ["s16"])
        TS("dve", s16[:, 0:8], s16[:, 0:8], s16[:, 32:33], None, ALU.mult, None, ["s16"], ["s16"])
        TT("dve", s16[:, 8:16], V2[0:16, 512:520], s16[:, 0:8], ALU.add, ["V2", "s16"], ["s16"])
        b.op("dve", lambda g: g.reciprocal(out=s16[:, 8:16], in_=s16[:, 8:16]), ["s16"], ["s16"])
        for g_ in range(2):
            TT("dve", v8(tk["ta"])[:, g_ * 4:(g_ + 1) * 4, :], bc(proj[:, 640 + g_ * 64:640 + (g_ + 1) * 64].unsqueeze(1), [16, 4, 64]),
               bc(s16[:, g_ * 4:(g_ + 1) * 4].unsqueeze(2), [16, 4, 64]), ALU.mult, ["proj", "s16"], ["ta"])
        TT("dve", tk["ta"], tk["ta"], V2[0:16, 0:512], ALU.add, ["ta", "V2"], ["ta"])
        TT("dve", v8(tk["ta"]), v8(tk["ta"]), bc(s16[:, 8:16].unsqueeze(2), [16, 8, 64]), ALU.mult, ["ta", "s16"], ["ta"])
        TT("dve", cats[:, 0:512], tk["ta"], tk["ga"], ALU.mult, ["ta", "ga"], ["cats"])
        for k in range(8):
            TR(PTb[:, k * 16:(k + 1) * 16], cats[:, k * 128:(k + 1) * 128], identb[0:16, 0:16], ["cats", "identb"], ["PTb"])
        CP("act", catTs, PTb[:, 0:128].rearrange("p (k t) -> p k t", k=8), ["PTb"], ["catTs"])
        b.barrier()
        off = PW
        stg2, off = carve(off, [128, 8, 512])
        wbf2, off = carve(off, [128, 8, 512], BF16)
        b.dma("sp", ysb, gscr[0:16, :], reads=["gscr"], writes=["ysb"])
        w_out_v2 = w_out.rearrange("(k p) c -> p k c", p=128)
        for hh in range(2):
            b.dma("sp", stg2, w_out_v2[:, :, hh * 512:(hh + 1) * 512], writes=["stg2"])
            CP("dve", wbf2, stg2, ["stg2"], ["wbf2"])
            for k in range(8):
                MM(R2[0:16, hh * 512:(hh + 1) * 512], catTs[:, k, :], wbf2[:, k, :], k == 0, k == 7, ["catTs", "wbf2"], ["R2"])
        TT("dve", ysb, ysb, R2[0:16, :], ALU.mult, ["ysb", "R2"], ["ysb"])
        TT("dve", ysb, ysb, x_[0:16, :], ALU.add, ["ysb", xr], ["ysb"])
        b.dma("sp", y_s[:, :], ysb, reads=["ysb"])

    b.barrier()
    b.emit()
    ncd.__exit__(None, None, None)
    es.close()
    return nc


def _consts(T):
    NT = T // 128
    NO = NT // 2
    cst = {}
    cst["identf"] = np.eye(128, dtype=np.float32)
    cst["iota256"] = np.tile(np.arange(256, dtype=np.float32)[None, :], (128, 1))
    s = np.arange(64)[:, None]
    t = np.arange(64)[None, :]
    lt = (s < t).astype(np.float32)
    le = (s <= t).astype(np.float32)
    cst["maskT"] = np.concatenate([lt, le, lt, le], axis=1)
    cst["maskL"] = (np.arange(64)[None, :] < np.arange(64)[:, None]).astype(np.float32)
    r = np.ones((64, 1024), np.float32)
    r[:, ::64] = 0.0
    cst["resetm"] = r
    sel = np.zeros((17, 128), np.float32)
    sel[16, :] = 1.0
    cst["sel16"] = sel
    cst["ones64"] = np.ones((64, 64), np.float32)
    return cst


def _rope_table(pos):
    half = 8
    inv = np.power(np.float32(ROPE_THETA), -np.arange(half, dtype=np.float32) / np.float32(half)).astype(np.float32)
    ang = pos.astype(np.float32)[:, None] * inv[None, :]
    return np.concatenate([np.cos(ang), np.sin(ang)], axis=1).astype(np.float32)


def _core_inputs(inp, c, T, NS, past_len):
    NT = T // 128
    NO = NT // 2
    bi, par = c // 2, c % 2
    xp = np.asarray(inp["x_prompt"][bi], np.float32)
    own_tiles = [2 * j + par for j in range(NO)]
    own_rows = np.concatenate([np.arange(t * 128, (t + 1) * 128) for t in own_tiles])
    xs = np.zeros((128, D), np.float32)
    xs[:NS] = np.asarray(inp["x_sample"][c * NS:(c + 1) * NS, 0], np.float32)
    m = {}
    m["xall"] = np.ascontiguousarray(np.concatenate([xp, xp[own_rows], xs], axis=0))
    m["call"] = np.ascontiguousarray(np.concatenate([inp["c_sample"][c * NS:(c + 1) * NS], inp["c_prompt"][bi:bi + 1]], axis=0).astype(np.float32))
    pos = np.concatenate([np.arange(T), own_rows, np.full(128, past_len)])
    m["cs_all"] = _rope_table(pos)
    m["parsel"] = np.tile(np.array([[par, 1 - par]], np.float32), (128, 1))
    m["qrel"] = (par * 128 + np.arange(128, dtype=np.float32)).reshape(128, 1)
    m["ownidx"] = np.ascontiguousarray(own_rows.reshape(NO, 128).T.astype(np.int32))
    for k_, v_ in (("w_in", "w_in"), ("w_ada", "w_ada"), ("b_ada", "b_ada"), ("norm_w", "norm_w"), ("w_out", "w_out"),
                   ("qnw", "q_norm_w"), ("knw", "k_norm_w"), ("mu", "mu_shift"), ("w0", "w0"), ("a0", "a0"),
                   ("k_k", "k_k"), ("k_a", "k_a"), ("ln_x_w", "ln_x_w"), ("ln_x_b", "ln_x_b"), ("w_up", "w_up"), ("a_up", "a_up")):
        m[k_] = np.ascontiguousarray(np.asarray(inp[v_], np.float32))
    m["r_k"] = np.ascontiguousarray(np.asarray(inp["r_k"], np.float32).reshape(512))
    m["swkv"] = np.ascontiguousarray(np.asarray(inp["state_wkv"][c * NS:(c + 1) * NS], np.float32).reshape(NS * 8, 4096))
    m["sshift"] = np.ascontiguousarray(np.asarray(inp["state_shift"][c * NS:(c + 1) * NS, 0], np.float32))
    pt = np.asarray(inp["page_table"][c * NS:(c + 1) * NS], np.int32)
    m["ptab"] = np.ascontiguousarray(pt.reshape(NS // 2, 128).T)
    nphys = inp["cache_k"].shape[0]
    m["cache_k"] = np.asarray(inp["cache_k"], np.float32).reshape(nphys * 128, 128)
    m["cache_v"] = np.asarray(inp["cache_v"], np.float32).reshape(nphys * 128, 128)
    m["cache_kidx"] = np.asarray(inp["cache_kidx"], np.float32).reshape(nphys, 8192)
    rep = np.zeros((16, 8, 128), np.float32)
    for sp in range(8):
        for p in range(128):
            rep[2 * sp + p // 64, sp, p] = 1.0
    m["rep"] = rep.reshape(16, 1024)
    m["repT"] = np.ascontiguousarray(rep.transpose(2, 1, 0).reshape(128, 128))
    blk = np.zeros((128, 128), np.float32)
    blk[:64, :64] = 1.0
    blk[64:, 64:] = 1.0
    m["blk"] = blk
    oh = np.zeros((128, 1), np.float32)
    oh[0, 0] = 1.0
    oh[64, 0] = 1.0
    m["oh0"] = oh
    m.update(_consts(T))
    return m


_NC_CACHE = {}


def kernel(**inp):
    T = 4096
    NS = 16
    past_len = 8192
    inp = {k: np.asarray(v) for k, v in inp.items()}
    if "nc" not in _NC_CACHE:
        _NC_CACHE["nc"] = build(T=T, NPHYS=int(inp["cache_k"].shape[0]))
    nc = _NC_CACHE["nc"]
    in_maps = [_core_inputs(inp, c, T, NS, past_len) for c in range(8)]
    res = run_bass_kernel_spmd(nc, in_maps, core_ids=list(range(8)))
    outs = res.results
    B = 4
    NO = T // 256
    y_p = np.zeros((B, T, D), np.float32)
    for c in range(8):
        bi, par = c // 2, c % 2
        yo = np.asarray(outs[c]["y_own"]).reshape(NO, 128, D)
        y_p[bi].reshape(T // 256, 2, 128, D)[:, par] = yo
    k_p = np.stack([np.asarray(outs[2 * bi]["k_nat"]).reshape(T, 2, 64) for bi in range(B)])
    v_p = np.stack([np.asarray(outs[2 * bi]["v_nat"]).reshape(T, 2, 64) for bi in range(B)])
    ki_p = np.stack([np.asarray(outs[2 * bi]["ki_nat"]).reshape(T, 64) for bi in range(B)])
    wkv_pp = np.stack([np.asarray(outs[2 * bi]["wkv_p"]).reshape(8, 64, 64) for bi in range(B)])
    sh_p = np.stack([np.asarray(outs[2 * bi]["shift_p"]).reshape(1, SHW) for bi in range(B)])
    y_s = np.concatenate([np.asarray(outs[c]["y_s"]) for c in range(8)]).reshape(128, 1, D)
    k_s = np.concatenate([np.asarray(outs[c]["k_s"]) for c in range(8)]).reshape(128, 1, 2, 64)
    v_s = np.concatenate([np.asarray(outs[c]["v_s"]) for c in range(8)]).reshape(128, 1, 2, 64)
    ki_s = np.concatenate([np.asarray(outs[c]["ki_s"]) for c in range(8)]).reshape(128, 1, 64)
    wkv_s = np.concatenate([np.asarray(outs[c]["wkv_s"]) for c in range(8)]).reshape(128, 8, 64, 64)
    sh_s = np.concatenate([np.asarray(outs[c]["shift_s"]) for c in range(8)]).reshape(128, 1, SHW)
    f = lambda a: np.ascontiguousarray(a, dtype=np.float32)
    return (f(y_p), f(y_s), f(k_p), f(v_p), f(ki_p), f(wkv_pp), f(sh_p), f(k_s), f(v_s), f(ki_s), f(wkv_s), f(sh_s))
```

```python
import os
import numpy as np
from contextlib import ExitStack
import concourse.bass as bass
import concourse.mybir as mybir
from concourse.bass_utils import run_bass_kernel_spmd

F32 = mybir.dt.float32
BF16 = mybir.dt.bfloat16
I32 = mybir.dt.int32
AF = mybir.ActivationFunctionType
ALU = mybir.AluOpType
AX = mybir.AxisListType

ENGS = ("pe", "act", "dve", "pool", "sp")
NDMA = 32
NSW = 8

D = 1024
HD = 64
DIN = 4040
C_Q, C_K, C_V, C_QI, C_WI, C_KI, C_GA = 0, 512, 640, 768, 1280, 1288, 1352
C_R, C_RK, C_RV, C_WD, C_AD, C_GR = 1864, 2376, 2888, 3400, 3464, 3528
SHW = 1664
NORM_EPS = 1e-6
GN_EPS = 64e-5
ROPE_THETA = 500000.0


USE_POOL = bool(int(os.environ.get('USE_POOL', '0')))
PSUM_RES = {"PTb", "F2", "R2", "K2", "V2", "R2a", "R2b", "K2a", "K2b", "V2a", "V2b"}


class Res:
    __slots__ = ("w", "r")

    def __init__(self):
        self.w = None
        self.r = []


class Bld:
    def __init__(self, nc, es):
        self.nc = nc
        self.es = es
        self.sem = {e: es.enter_context(nc.semaphore("s_" + e)) for e in ENGS}
        self.dsem = [es.enter_context(nc.semaphore("d%d" % i)) for i in range(NDMA)]
        self.dval = [0] * NDMA
        self.dnext = 0
        self.dnext_sw = 0
        self.cnt = {e: 0 for e in ENGS}
        self.waited = {e: {} for e in ENGS}
        self.ops = {e: [] for e in ENGS}
        self.res = {}

    def sb(self, name, shape, dt=F32):
        return self.es.enter_context(self.nc.sbuf_tensor("sb_" + name, list(shape), dt))

    def ps(self, name, shape, dt=F32):
        return self.es.enter_context(self.nc.psum_tensor("ps_" + name, list(shape), dt))

    def _r(self, key):
        r = self.res.get(key)
        if r is None:
            r = self.res[key] = Res()
        return r

    def _need(self, e, tok, waits):
        if tok is None:
            return
        key, val = tok
        if key == "pe" and e == "pe":
            return
        if self.waited[e].get(key, 0) >= val:
            return
        self.waited[e][key] = val
        waits.append((key, val))

    def op(self, e, fn, reads=(), writes=(), dma=False):
        if e == "pool" and not dma and not USE_POOL:
            e = "dve"
        pr = [k for k in reads if k in PSUM_RES]
        if pr:
            reads = [k for k in reads if k not in PSUM_RES]
            writes = list(writes) + pr
        waits = []
        for k in reads:
            self._need(e, self._r(k).w, waits)
        for k in writes:
            r = self._r(k)
            self._need(e, r.w, waits)
            for t in r.r:
                self._need(e, t, waits)
        if dma:
            if e == "pool":
                i = NDMA - NSW + self.dnext_sw
                self.dnext_sw = (self.dnext_sw + 1) % NSW
            else:
                i = self.dnext
                self.dnext = (self.dnext + 1) % (NDMA - NSW)
            if self.dval[i] > 0:
                self._need(e, (("d", i), self.dval[i]), waits)
            self.dval[i] += 16
            tok = (("d", i), self.dval[i])
            inc = (self.dsem[i], 16)
        else:
            self.cnt[e] += 1
            tok = (e, self.cnt[e])
            inc = (self.sem[e], 1)
        self.ops[e].append((waits, fn, inc))
        for k in reads:
            self._r(k).r.append(tok)
        for k in writes:
            r = self._r(k)
            r.w = tok
            r.r = []
        return tok

    def dma(self, e, out, in_, reads=(), writes=()):
        return self.op(e, lambda g: g.dma_start(out=out, in_=in_), reads, writes, dma=True)

    def barrier(self):
        for e in ENGS:
            waits = []
            for e2 in ENGS:
                if e2 != e and self.cnt[e2] > 0:
                    self._need(e, (e2, self.cnt[e2]), waits)
            for i in range(NDMA):
                if self.dval[i] > 0:
                    self._need(e, (("d", i), self.dval[i]), waits)
            self.ops[e].append((waits, None, None))

    def emit(self):
        nc = self.nc
        with nc.Block() as block:
            def mk(e):
                def body(g):
                    for waits, fn, inc in self.ops[e]:
                        for key, val in waits:
                            s = self.dsem[key[1]] if isinstance(key, tuple) else self.sem[key]
                            g.wait_ge(s, val)
                        if fn is not None:
                            fn(g).then_inc(inc[0], inc[1])
                return body
            block.tensor(mk("pe"))
            block.scalar(mk("act"))
            block.vector(mk("dve"))
            block.gpsimd(mk("pool"))
            block.sync(mk("sp"))


def build(T=4096, NS=16, NPG=64, NPHYS=10240, topk_p=256, topk_s=256, n_bis=16, do_sample=True, AW=36000, stop_after=None, nt_lim=None, no_lim=None):
    NT = T // 128
    NO = NT // 2
    NTILES = NT + NO + 1
    nc = bass.Bass("TRN2", target_bir_lowering=False)
    es = ExitStack()
    b = Bld(nc, es)

    def din(name, shape, dt=F32):
        return nc.dram_tensor(name, list(shape), dt, kind="ExternalInput").ap()

    def dout(name, shape, dt=F32):
        return nc.dram_tensor(name, list(shape), dt, kind="ExternalOutput").ap()

    xall = din("xall", [NTILES * 128, D])
    call = din("call", [17, D])
    w_in = din("w_in", [D, DIN])
    w_ada = din("w_ada", [D, 3 * D])
    b_ada = din("b_ada", [3 * D])
    norm_w = din("norm_w", [D])
    w_out = din("w_out", [D, D])
    qnw = din("qnw", [HD])
    knw = din("knw", [HD])
    mu = din("mu", [SHW])
    pw0 = din("w0", [512]); pa0 = din("a0", [512]); pkk = din("k_k", [512]); pka = din("k_a", [512])
    prk = din("r_k", [512]); plnw = din("ln_x_w", [512]); plnb = din("ln_x_b", [512])
    w_up = din("w_up", [64, 512]); a_up = din("a_up", [64, 512])
    identf_d = din("identf", [128, 128])
    cs_all = din("cs_all", [NTILES * 128, 16])
    parsel_d = din("parsel", [128, 2])
    qrel_d = din("qrel", [128, 1])
    ownidx_d = din("ownidx", [128, NO], I32)
    iota_d = din("iota256", [128, 256])
    maskT_d = din("maskT", [64, 256])
    maskL_d = din("maskL", [64, 64])
    reset_d = din("resetm", [64, 1024])
    sel16_d = din("sel16", [17, 128])
    ones64_d = din("ones64", [64, 64])

    swkv_d = din("swkv", [128, 4096]); sshift_d = din("sshift", [16, SHW]); ptab_d = din("ptab", [128, 8], I32)
    if do_sample:
        cache_k = din("cache_k", [NPHYS * 128, 128]); cache_v = din("cache_v", [NPHYS * 128, 128])
        cache_ki = din("cache_kidx", [NPHYS, 8192])
    rep_d = din("rep", [16, 8 * 128]); repT_d = din("repT", [128, 8 * 16]); blk_d = din("blk", [128, 128]); oh0_d = din("oh0", [128, 1])
    y_s = dout("y_s", [16, D]); k_s = dout("k_s", [16, 128]); v_s = dout("v_s", [16, 128]); ki_s = dout("ki_s", [16, 64])
    wkv_s = dout("wkv_s", [128, 4096]); shift_s = dout("shift_s", [16, SHW])
    gscr = nc.dram_tensor("gscr", [17, D], F32, kind="Internal").ap()
    scr1 = nc.dram_tensor("scr1", [16, 3072], F32, kind="Internal").ap()
    scr2 = nc.dram_tensor("scr2", [128, 64], F32, kind="Internal").ap()
    scr3 = nc.dram_tensor("scr3", [16, 512], F32, kind="Internal").ap()
    scr4 = nc.dram_tensor("scr4", [16, 64, 16], F32, kind="Internal").ap()
    y_own = dout("y_own", [NO * 128, D])
    k_nat = dout("k_nat", [T, 128]); v_nat = dout("v_nat", [T, 128]); ki_nat = dout("ki_nat", [T, 64])
    wkv_p = dout("wkv_p", [8, 64, 64]); shift_p = dout("shift_p", [SHW])
    rwscr = nc.dram_tensor("rwscr", [T, 512], F32, kind="Internal").ap()

    PTb = b.ps("PTb", [128, 1024], BF16)
    F2 = b.ps("F2", [128, 512])
    R2 = b.ps("R2", [128, 1024])
    K2 = b.ps("K2", [128, 1024])
    V2 = b.ps("V2", [128, 1024])

    identf = b.sb("identf", [128, 128]); identb = b.sb("identb", [128, 128], BF16)
    cst = b.sb("cst", [128, 4])
    kT_all = b.sb("kT_all", [64, 2, T], BF16)
    kiT_all = b.sb("kiT_all", [64, T], BF16)
    Vaug = b.sb("Vaug", [128, NT, 2, 65], BF16)
    modT = b.sb("modT", [128, 24, 17])
    g1 = b.sb("g1", [128, 8, 17])
    nwT = b.sb("nwT", [128, 8]); badaT = b.sb("badaT", [128, 24])
    lnw_bc = b.sb("lnw_bc", [64, 512]); lnb_bc = b.sb("lnb_bc", [64, 512])
    qnw_bc = b.sb("qnw_bc", [128, 64]); knw_bc = b.sb("knw_bc", [128, 64])
    sel16 = b.sb("sel16", [17, 128]); ones64 = b.sb("ones64", [64, 64])
    maskT = b.sb("maskT", [64, 256]); maskL = b.sb("maskL", [64, 64]); resetm = b.sb("resetm", [64, 1024])
    qrel = b.sb("qrel", [128, 1]); parsel = b.sb("parsel", [128, 2]); ownidx = b.sb("ownidx", [128, NO], I32)
    fp = {}
    for nm in ("w0", "a0", "kk", "ka", "rk"):
        fp[nm] = b.sb("fp_" + nm, [64, 8])
    muT = b.sb("muT", [64, 26]); wupS = b.sb("wupS", [64, 512]); aupS = b.sb("aupS", [64, 512])
    xt0 = b.sb("xt0", [128, D]); xt = [xt0, xt0]
    xn = b.sb("xn", [128, D], BF16)
    hT0 = b.sb("hT0", [128, 8, 128], BF16); hT = [hT0, hT0]
    hTs = b.sb("hTs", [128, 8, 128], BF16)
    hlast = b.sb("hlast", [128, 8, 1], BF16)
    cs_t = b.sb("cs_t", [128, 16])
    sm = b.sb("sm", [128, 64])
    ARENA = b.sb("ARENA", [128, AW])
    csT = b.sb("csT", [128, 8, 17])

    def TT(e, out, in0, in1, op, R, W):
        b.op(e, lambda g: g.tensor_tensor(out=out, in0=in0, in1=in1, op=op), R, W)

    def TS(e, out, in0, s1, s2, op0, op1, R, W, accum=None):
        if op1 is None:
            b.op(e, lambda g: g.tensor_scalar(out=out, in0=in0, scalar1=s1, scalar2=None, op0=op0), R, W)
        elif accum is None:
            b.op(e, lambda g: g.tensor_scalar(out=out, in0=in0, scalar1=s1, scalar2=s2, op0=op0, op1=op1), R, W)
        else:
            b.op(e, lambda g: g.tensor_scalar(out=out, in0=in0, scalar1=s1, scalar2=s2, op0=op0, op1=op1,
                                              accum_out=accum), R, W)

    def STT(out, in0, scalar, in1, op0, op1, R, W):
        b.op("dve", lambda g: g.scalar_tensor_tensor(out=out, in0=in0, scalar=scalar, in1=in1, op0=op0, op1=op1), R, W)

    def ACT(out, in_, func, R, W, scale=1.0, bias=None, accum=None):
        kw = {}
        if bias is not None:
            kw["bias"] = bias
        if accum is not None:
            kw["accum_out"] = accum
        b.op("act", lambda g: g.activation(out=out, in_=in_, func=func, scale=scale, **kw), R, W)

    def MM(out, lhsT, rhs, start, stop, R, W):
        b.op("pe", lambda g: g.matmul(out=out, lhsT=lhsT, rhs=rhs, start=start, stop=stop), R, W)

    def TR(out, in_, ident, R, W):
        b.op("pe", lambda g: g.transpose(out=out, in_=in_, identity=ident), R, W)

    def CP(e, out, in_, R, W):
        if e == "act":
            b.op(e, lambda g: g.copy(out=out, in_=in_), R, W)
        else:
            b.op(e, lambda g: g.tensor_copy(out=out, in_=in_), R, W)

    def RED(out, in_, op, R, W, axis=AX.X):
        b.op("dve", lambda g: g.tensor_reduce(out=out, in_=in_, axis=axis, op=op), R, W)

    def MS(e, ap, val, W):
        b.op(e, lambda g: g.memset(ap, val), (), W)

    def bc(ap, shape):
        return ap.to_broadcast(list(shape))

    ncd = nc.allow_non_contiguous_dma(reason="small parameter layouts")
    ncd.__enter__()

    b.dma("sp", identf[:], identf_d[:, :], writes=["identf"])
    CP("dve", identb[:], identf[:], ["identf"], ["identb"])
    MS("dve", cst[:, 0:1], NORM_EPS, ["cst"]); MS("dve", cst[:, 1:2], GN_EPS, ["cst"]); MS("dve", cst[:, 2:3], 1e-24, ["cst"])
    for (t_, d_, nm) in ((sel16, sel16_d, "sel16"), (ones64, ones64_d, "ones64"), (maskT, maskT_d, "maskT"),
                         (maskL, maskL_d, "maskL"), (resetm, reset_d, "resetm"),
                         (qrel, qrel_d, "qrel"), (parsel, parsel_d, "parsel"), (ownidx, ownidx_d, "ownidx"), (wupS, w_up, "wupS"), (aupS, a_up, "aupS")):
        b.dma("sp", t_[:], d_[:, :], writes=[nm])
    for nm, src in (("w0", pw0), ("a0", pa0), ("kk", pkk), ("ka", pka), ("rk", prk)):
        b.dma("sp", fp[nm][:], src.rearrange("(h j) -> j h", j=64), writes=["fp_" + nm])
    b.dma("sp", muT[:], mu.rearrange("(c j) -> j c", j=64), writes=["muT"])
    b.dma("sp", nwT[:], norm_w.rearrange("(k p) -> p k", p=128), writes=["nwT"])
    b.dma("sp", badaT[:], b_ada.rearrange("(t p) -> p t", p=128), writes=["badaT"])
    b.dma("sp", lnw_bc[:], plnw.partition_broadcast(64), writes=["lnw_bc"])
    b.dma("sp", lnb_bc[:], plnb.partition_broadcast(64), writes=["lnb_bc"])
    b.dma("sp", qnw_bc[:], qnw.partition_broadcast(128), writes=["qnw_bc"])
    b.dma("sp", knw_bc[:], knw.partition_broadcast(128), writes=["knw_bc"])
    MS("pool", Vaug[:, :, :, 64:65], 1.0, ["Vaug"])

    def carve(off, shape, dt=F32):
        n = int(np.prod(shape[1:]))
        words = n if dt in (F32, I32) else (n + 1) // 2
        v = ARENA[0:shape[0], off:off + words]
        if dt != F32:
            v = v.bitcast(dt)
        if len(shape) == 3:
            v = v.rearrange("p (a b) -> p a b", a=shape[1])
        elif len(shape) == 4:
            v = v.rearrange("p (a b c) -> p a b c", a=shape[1], b=shape[2])
        return v, off + words

    off = 0
    Wn, off = carve(off, [128, 8, 832], BF16)
    Wm, off = carve(off, [128, 8, SHW], BF16)
    Wom, off = carve(off, [128, 8, SHW], BF16)
    W_end = off
    stg, off = carve(off, [128, 8, 512])
    mu_bc, off = carve(off, [128, SHW])
    omu_bc, off = carve(off, [128, SHW])
    gtok, off = carve(off, [17, D])
    bgate, off = carve(off, [17, D])
    csall_sil, off = carve(off, [17, D])

    b.dma("sp", mu_bc, mu.partition_broadcast(128), writes=["mu_bc"])
    b.dma("sp", bgate, b_ada[2 * D:3 * D].partition_broadcast(17), writes=["bgate"])
    TS("dve", omu_bc, mu_bc, -1.0, 1.0, ALU.mult, ALU.add, ["mu_bc"], ["omu_bc"])
    w_in_v = w_in.rearrange("(k p) c -> p k c", p=128)

    def load_cols(dst, dcol, c0, n, scale_bc=None, scale_off=0, tag=""):
        done = 0
        while done < n:
            w = min(512, n - done)
            b.dma("sp", stg[:, :, 0:w], w_in_v[:, :, c0 + done:c0 + done + w], writes=["stg"])
            if scale_bc is None:
                CP("pool", dst[:, :, dcol + done:dcol + done + w], stg[:, :, 0:w], ["stg"], [tag])
            else:
                for sname, sbcv, d2 in scale_bc:
                    TT("dve", d2[:, :, dcol + done:dcol + done + w], stg[:, :, 0:w],
                       bc(sbcv[:, scale_off + done:scale_off + done + w].unsqueeze(1), [128, 8, w]),
                       ALU.mult, ["stg", sname], [tag])
            done += w

    load_cols(Wn, 0, C_K, 256, tag="Wn")
    load_cols(Wn, 256, C_KI, 64, tag="Wn")
    load_cols(Wn, 320, C_GR, 512, tag="Wn")
    load_cols(None, 0, C_R, SHW, scale_bc=[("mu_bc", mu_bc, Wm), ("omu_bc", omu_bc, Wom)], tag="Wm")
    calt = sm
    b.dma("sp", csall_sil, call[:, :], writes=["csil"])
    ACT(csall_sil, csall_sil, AF.Silu, ["csil"], ["csil"])
    for k in range(8):
        TR(F2[:, k * 17:(k + 1) * 17], csall_sil[:, k * 128:(k + 1) * 128], identf[0:17, 0:17], ["csil", "identf"], ["F2"])
    CP("dve", csT[:], F2[:, 0:136].rearrange("p (k m) -> p k m", k=8), ["F2"], ["csT"])
    w_ada_v = w_ada.rearrange("(k p) c -> p k c", p=128)
    for ch in range(6):
        b.dma("sp", stg[:, :, :], w_ada_v[:, :, ch * 512:(ch + 1) * 512], writes=["stg"])
        for ct in range(4):
            for k in range(8):
                MM(R2[:, ct * 17:(ct + 1) * 17], stg[:, k, ct * 128:(ct + 1) * 128], csT[:, k, :], k == 0, k == 7,
                   ["stg", "csT"], ["R2"])
        TT("dve", modT[:, ch * 4:(ch + 1) * 4, :], R2[:, 0:68].rearrange("p (c m) -> p c m", c=4),
           bc(badaT[:, ch * 4:(ch + 1) * 4].unsqueeze(2), [128, 4, 17]), ALU.add, ["R2", "badaT"], ["modT"])
        if ch >= 4:
            for k in range(8):
                MM(K2[0:17, 0:512], csT[:, k, :], stg[:, k, :], k == 0, k == 7, ["stg", "csT"], ["K2"])
            TT("dve", gtok[:, (ch - 4) * 512:(ch - 3) * 512], K2[0:17, 0:512], bgate[:, (ch - 4) * 512:(ch - 3) * 512],
               ALU.add, ["K2", "bgate"], ["gtok"])
    b.dma("sp", gscr[:, :], gtok[0:17, :], reads=["gtok"], writes=["gscr"])
    STT(g1[:], modT[:, 8:16, :], 1.0, bc(nwT[:].unsqueeze(2), [128, 8, 17]), ALU.add, ALU.mult, ["modT", "nwT"], ["g1"])
    def front(ti, par, m_prompt=True, ntok=128):
        x_ = xt[par]
        h_ = hT[par]
        xr, hr = "xt0", "hT0"
        b.dma("sp", x_[:], xall[ti * 128:(ti + 1) * 128, :], writes=[xr])
        b.dma("sp", cs_t[:], cs_all[ti * 128:(ti + 1) * 128, :], writes=["cs_t"])
        ACT(xn[:], x_[:], AF.Square, [xr], ["xn", "sm"], accum=sm[:, 0:1])
        ACT(sm[:, 1:2], sm[:, 0:1], AF.Sqrt, ["sm", "cst"], ["sm"], scale=1.0 / D, bias=cst[:, 0:1])
        b.op("dve", lambda g: g.reciprocal(out=sm[:, 2:3], in_=sm[:, 1:2]), ["sm"], ["sm"])
        TS("dve", xn[:], x_[:], sm[:, 2:3], None, ALU.mult, None, [xr, "sm"], ["xn"])
        for k in range(8):
            TR(PTb[:, k * 128:(k + 1) * 128], xn[:, k * 128:(k + 1) * 128], identb[:], ["xn", "identb"], ["PTb"])
        pv = PTb[:, :].rearrange("p (k t) -> p k t", k=8)
        if m_prompt:
            TT("dve", h_[:], pv, bc(g1[:, :, 16:17], [128, 8, 128]), ALU.mult, ["PTb", "g1"], [hr])
            TT("pool", h_[:], h_[:], bc(modT[:, 0:8, 16:17], [128, 8, 128]), ALU.add, [hr, "modT"], [hr])
        else:
            TT("dve", h_[:, :, 0:ntok], pv[:, :, 0:ntok], g1[:, :, 0:ntok], ALU.mult, ["PTb", "g1"], [hr])
            TT("pool", h_[:, :, 0:ntok], h_[:, :, 0:ntok], modT[:, 0:8, 0:ntok], ALU.add, [hr, "modT"], [hr])
        return x_, h_, xr, hr

    def rope(e, buf, nh, hd_stride_view, R, nrows=128):
        x1 = buf[:, :, 0:8]
        x2 = buf[:, :, 8:16]
        cosb = bc(cs_t[0:nrows, 0:8].unsqueeze(1), [nrows, nh, 8])
        sinb = bc(cs_t[0:nrows, 8:16].unsqueeze(1), [nrows, nh, 8])
        t = ropet[0:nrows, 0:4 * nh * 8].rearrange("p (a h d) -> p a h d", a=4, h=nh)
        TT(e, t[:, 0], x1, cosb, ALU.mult, R + ["cs_t"], ["ropet"])
        TT(e, t[:, 1], x2, sinb, ALU.mult, R + ["cs_t"], ["ropet"])
        TT(e, t[:, 2], x2, cosb, ALU.mult, R + ["cs_t"], ["ropet"])
        TT(e, t[:, 3], x1, sinb, ALU.mult, R + ["cs_t"], ["ropet"])
        TT(e, x1, t[:, 0], t[:, 1], ALU.subtract, ["ropet"], R)
        TT(e, x2, t[:, 2], t[:, 3], ALU.add, ["ropet"], R)

    ropet = b.sb("ropet", [128, 256])

    def qknorm(src_ps, dst, nh, wbc, extra_scale, Rsrc, Wdst, nrows=128):
        sq = nrm_t[0:nrows, 0:nh * 64].rearrange("p (h d) -> p h d", h=nh)
        ACT(sq, src_ps, AF.Square, Rsrc, ["nrm_t"])
        RED(sm[0:nrows, 8:8 + nh], sq, ALU.add, ["nrm_t"], ["sm"])
        ACT(sm[0:nrows, 16:16 + nh], sm[0:nrows, 8:8 + nh], AF.Sqrt, ["sm", "cst"], ["sm"], scale=1.0 / 64, bias=cst[0:nrows, 0:1])
        b.op("dve", lambda g: g.reciprocal(out=sm[0:nrows, 24:24 + nh], in_=sm[0:nrows, 16:16 + nh]), ["sm"], ["sm"])
        TT("dve", dst, src_ps, bc(sm[0:nrows, 24:24 + nh].unsqueeze(2), [nrows, nh, 64]), ALU.mult, Rsrc + ["sm"], Wdst)
        STT(dst, dst, float(extra_scale), bc(wbc[0:nrows, :].unsqueeze(1), [nrows, nh, 64]), ALU.mult, ALU.mult, Wdst + ["qnw_bc", "knw_bc"], Wdst)

    nrm_t = b.sb("nrm_t", [128, 512])
    kfin = b.sb("kfin", [128, 128]); vfin = b.sb("vfin", [128, 128]); kifin = b.sb("kifin", [128, 64])
    gr_s = b.sb("gr_s", [128, 512])

    off = W_end
    rw = {}
    for nm in ("tw", "adc"):
        rw[nm], off = carve(off, [64, 128])
    for nm in ("sg", "L", "g", "t1"):
        rw[nm], off = carve(off, [64, 8, 128])
    blkA, off = carve(off, [64, 2048])
    blkB, off = carve(off, [64, 3072])
    rw["gprev"] = blkA[:, 0:1024].rearrange("p (h t) -> p h t", h=8)
    rw["ginv"] = blkA[:, 1024:2048].rearrange("p (h t) -> p h t", h=8)
    rw["asig"] = blkB[:, 0:1024].rearrange("p (h t) -> p h t", h=8)
    rw["kkn"] = blkB[:, 1024:2048].rearrange("p (h t) -> p h t", h=8)
    rw["kmod"] = blkB[:, 2048:3072].rearrange("p (h t) -> p h t", h=8)
    AMx = blkA.bitcast(BF16).rearrange("p (h x) -> p h x", h=16)
    LNPx = blkB.bitcast(BF16).rearrange("p (a h x) -> p a h x", a=6, h=16)
    rw["sg"] = rw["sg"]
    QTt, off = carve(off, [64, 8, 2, 128], BF16)
    KTt, off = carve(off, [64, 8, 2, 128], BF16)
    def alias(view64, shape):
        return view64.rearrange("p h t -> p (h t)").bitcast(BF16)
    AM = AMx
    Lm = [LNPx[:, 0], LNPx[:, 1]]
    Nm = [LNPx[:, 2], LNPx[:, 3]]
    Pm = [LNPx[:, 4], LNPx[:, 5]]
    BKtok, off = carve(off, [64, 8, 2, 64], BF16)
    Vc, off = carve(off, [64, 2, 8, 64], BF16)
    P0s, off = carve(off, [64, 8, 64], BF16)
    Us, off = carve(off, [64, 8, 64], BF16)
    H32, off = carve(off, [64, 8, 64])
    Hb, off = carve(off, [64, 8, 64], BF16)
    ych, off = carve(off, [64, 8, 64])
    yt1, off = carve(off, [64, 8, 64])
    bon, off = carve(off, [128, 8])
    rawl, off = carve(off, [64, 26])
    xmt, off = carve(off, [128, SHW])
    st8, off = carve(off, [64, 64])
    assert off <= AW, off
    identb64 = identb[0:64, 0:64]

    KR = int(os.environ.get('KR', '9'))
    KQ = int(os.environ.get('KQ', '9'))

    def rwkv_tile(ti, h_, hr):
        CP("pool", hTs[:, :, 1:128], h_[:, :, 0:127], [hr], ["hTs"])
        CP("pool", hTs[:, :, 0:1], hlast[:], ["hlast"], ["hTs"])
        CP("pool", hlast[:], h_[:, :, 127:128], [hr], ["hlast"])
        if KQ < 1:
            return
        for gi, (c0, n) in enumerate(((0, 512), (512, 512), (1024, 512), (1536, 128))):
            dst = (R2[:, 0:512], R2[:, 512:1024], K2[:, 0:512], K2[:, 512:640])[gi]
            nm = ("R2", "R2", "K2", "K2")[gi]
            for k in range(8):
                MM(dst, h_[:, k, :], Wom[:, k, c0:c0 + n], k == 0, False, ["Wm", hr], [nm])
            for k in range(8):
                MM(dst, hTs[:, k, :], Wm[:, k, c0:c0 + n], False, k == 7, ["Wm", "hTs"], [nm])
        CP("act", xmt[:, 0:1024], R2[:, :], ["R2"], ["xmt"])
        CP("dve", xmt[:, 1024:1664], K2[:, 0:640], ["K2"], ["xmt"])
        TR(F2[0:64, 0:128], xmt[:, 1536:1600], identf[:], ["xmt", "identf"], ["F2"])
        TR(F2[0:64, 128:256], xmt[:, 1600:1664], identf[:], ["xmt", "identf"], ["F2"])
        KW = int(os.environ.get('KW', '3'))
        if KW & 1:
            ACT(rw["tw"], F2[0:64, 0:128], AF.Tanh, ["F2"], ["tw"])
        if KW & 2:
            CP("dve", rw["adc"], F2[0:64, 128:256], ["F2"], ["adc"])
        if KR < 1:
            return
        R2v = R2[0:64, :].rearrange("p (h t) -> p h t", h=8)
        K2v = K2[0:64, :].rearrange("p (h t) -> p h t", h=8)
        V2v = V2[0:64, :].rearrange("p (h t) -> p h t", h=8)
        for h in range(8):
            MM(R2v[:, h, :], wupS[:, h * 64:(h + 1) * 64], rw["tw"], True, True, ["wupS", "tw"], ["R2"])
            MM(K2v[:, h, :], aupS[:, h * 64:(h + 1) * 64], rw["adc"], True, True, ["aupS", "adc"], ["K2"])
        TT("dve", rw["sg"], R2v, bc(fp["w0"][:].unsqueeze(2), [64, 8, 128]), ALU.add, ["R2", "fp_w0"], ["sg"])
        ACT(rw["sg"], rw["sg"], AF.Sigmoid, ["sg"], ["sg"])
        TT("dve", rw["asig"], K2v, bc(fp["a0"][:].unsqueeze(2), [64, 8, 128]), ALU.add, ["K2", "fp_a0"], ["asig"])
        ACT(rw["asig"], rw["asig"], AF.Sigmoid, ["asig"], ["asig"])
        TS("dve", rw["sg"], rw["sg"], -0.6065306597126334, None, ALU.mult, None, ["sg"], ["sg"])
        b.op("dve", lambda g: g.tensor_tensor_scan(out=rw["L"].rearrange("p h t -> p (h t)"), data0=resetm[:, :],
                                                   data1=rw["sg"].rearrange("p h t -> p (h t)"), initial=0.0,
                                                   op0=ALU.mult, op1=ALU.add), ["sg", "resetm"], ["L"])
        ACT(rw["g"], rw["L"], AF.Exp, ["L"], ["g"])
        ACT(rw["ginv"], rw["L"], AF.Exp, ["L"], ["ginv"], scale=-1.0)
        TT("pool", rw["gprev"], rw["L"], rw["sg"], ALU.subtract, ["L", "sg"], ["gprev"])
        ACT(rw["gprev"], rw["gprev"], AF.Exp, ["gprev"], ["gprev"])
        if KR < 2:
            return
        for h in range(8):
            TR(R2v[:, h, :], xmt[:, h * 64:(h + 1) * 64], identf[:], ["xmt", "identf"], ["R2"])
            TR(K2v[:, h, :], xmt[:, 512 + h * 64:512 + (h + 1) * 64], identf[:], ["xmt", "identf"], ["K2"])
        TT("dve", rw["L"], K2v, bc(fp["kk"][:].unsqueeze(2), [64, 8, 128]), ALU.mult, ["K2", "fp_kk"], ["L"])
        ACT(rw["t1"], rw["L"], AF.Square, ["L"], ["t1"])
        t1f = rw["t1"].rearrange("p h t -> p (h t)")
        for hh in range(2):
            MM(V2[0:64, hh * 512:(hh + 1) * 512], ones64[:, :], t1f[:, hh * 512:(hh + 1) * 512], True, True, ["ones64", "t1"], ["V2"])
        ACT(rw["t1"], V2v, AF.Sqrt, ["V2", "cst"], ["t1"], bias=cst[0:64, 2:3])
        b.op("dve", lambda g: g.reciprocal(out=rw["t1"], in_=rw["t1"]), ["t1"], ["t1"])
        TT("dve", rw["kkn"], rw["L"], rw["t1"], ALU.mult, ["L", "t1"], ["kkn"])
        STT(rw["t1"], rw["asig"], -1.0, bc(fp["ka"][:].unsqueeze(2), [64, 8, 128]), ALU.add, ALU.mult, ["asig", "fp_ka"], ["t1"])
        STT(rw["kmod"], rw["t1"], 1.0, K2v, ALU.add, ALU.mult, ["t1", "K2"], ["kmod"])
        if KR < 3:
            return
        QTv = QTt.rearrange("p h c (q t) -> p h c q t", q=2)
        KTv = KTt.rearrange("p h c (q t) -> p h c q t", q=2)

        def ch(v):
            return v.rearrange("p h (c t) -> p h c t", c=2)
        STT(QTv[:, :, :, 0, :], ch(rw["kkn"]), -1.0, ch(rw["gprev"]), ALU.mult, ALU.mult, ["kkn", "gprev"], ["QTt"])
        TT("dve", QTv[:, :, :, 1, :], ch(R2v), ch(rw["g"]), ALU.mult, ["R2", "g"], ["QTt"])
        TT("pool", rw["t1"], rw["kkn"], rw["asig"], ALU.mult, ["kkn", "asig"], ["t1"])
        TT("pool", KTv[:, :, :, 0, :], ch(rw["t1"]), ch(rw["ginv"]), ALU.mult, ["t1", "ginv"], ["KTt"])
        TT("pool", KTv[:, :, :, 1, :], ch(rw["kmod"]), ch(rw["ginv"]), ALU.mult, ["kmod", "ginv"], ["KTt"])
        TT("dve", rw["L"], R2v, bc(fp["rk"][:].unsqueeze(2), [64, 8, 128]), ALU.mult, ["R2", "fp_rk"], ["L"])
        TT("dve", rw["L"], rw["L"], rw["kmod"], ALU.mult, ["L", "kmod"], ["L"])
        for h in range(8):
            MM(F2[:, 256 + h:257 + h], rw["L"][:, h, :], ones64[:, 0:1], True, True, ["L", "ones64"], ["F2"])
        CP("dve", bon, F2[:, 256:264], ["F2"], ["bon"])
        if KR < 4:
            return
        CP("act", Vc[:, 0], xmt[0:64, 1024:1536].rearrange("p (h i) -> p h i", h=8), ["xmt"], ["Vc"])
        MM(V2[0:64, 0:512], identf[:, 64:128], xmt[:, 1024:1536], True, True, ["identf", "xmt"], ["V2"])
        CP("act", Vc[:, 1], V2[0:64, 0:512].rearrange("p (h i) -> p h i", h=8), ["V2"], ["Vc"])
        if int(os.environ.get("KLVL", "9")) < 3:
            return
        for q in range(4):
            c, hg = q // 2, q % 2
            bk, bkn = (K2, "K2") if q % 2 == 0 else (V2, "V2")
            AMp = bk[0:64, :].rearrange("p (h x) -> p h x", h=4)
            for hd in range(4):
                h = hg * 4 + hd
                MM(AMp[:, hd, 0:128], KTt[:, h, c, 0:64], QTt[:, h, c, :], True, True, ["KTt", "QTt"], [bkn])
                MM(AMp[:, hd, 128:256], KTt[:, h, c, 64:128], QTt[:, h, c, :], True, True, ["KTt", "QTt"], [bkn])
            TT("dve", AM[:, q * 4:(q + 1) * 4, :], AMp, bc(maskT[:].unsqueeze(1), [64, 4, 256]), ALU.mult, [bkn, "maskT"], ["AM"])
        Lp = R2[0:64, :].rearrange("p (h x) -> p h x", h=16)
        for q in range(4):
            c, hg = q // 2, q % 2
            for hd in range(4):
                h = hg * 4 + hd
                MM(Lp[:, q * 4 + hd, :], QTt[:, h, c, 0:64], KTt[:, h, c, 0:64], True, True, ["KTt", "QTt"], ["R2"])
        TT("dve", Lm[0], Lp, bc(maskL[:].unsqueeze(1), [64, 16, 64]), ALU.mult, ["R2", "maskL"], ["Lm0"])
        CP("act", Nm[0], AM[:, :, 0:64], ["AM"], ["Nm0"])
        TT("dve", Pm[0], AM[:, :, 0:64], bc(identb64.unsqueeze(1), [64, 16, 64]), ALU.add, ["AM", "identb"], ["Pm0"])
        cur = 0
        Np = K2[0:64, :].rearrange("p (h x) -> p h x", h=16)
        Lpp = V2[0:64, :].rearrange("p (h x) -> p h x", h=16)
        PPp = R2[0:64, :].rearrange("p (h x) -> p h x", h=16)
        for lvl in range(1, 6):
            nx = 1 - cur
            for i in range(16):
                if lvl < 5:
                    MM(Np[:, i, :], Lm[cur][:, i, :], Nm[cur][:, i, :], True, True, ["Lm%d" % cur, "Nm%d" % cur], ["K2"])
                MM(Lpp[:, i, :], Nm[cur][:, i, :], Lm[cur][:, i, :], True, True, ["Lm%d" % cur, "Nm%d" % cur], ["V2"])
            if lvl < 5:
                CP("act", Nm[nx], Np, ["K2"], ["Nm%d" % nx])
            CP("dve", Lm[nx], Lpp, ["V2"], ["Lm%d" % nx])
            for i in range(16):
                MM(PPp[:, i, :], Lm[nx][:, i, :], Pm[cur][:, i, :], True, True, ["Lm%d" % nx, "Pm%d" % cur], ["R2"])
            TT("dve", Pm[nx], PPp, Pm[cur], ALU.add, ["R2", "Pm%d" % cur], ["Pm%d" % nx])
            cur = nx
        P6 = Pm[cur]
        P6n = "Pm%d" % cur
        for c in range(2):
            BKp = PTb[0:64, :].rearrange("p (h q j) -> p h q j", h=8, q=2)
            for h in range(8):
                TR(BKp[:, h, 0, :], KTt[:, h, c, 0:64], identb64, ["KTt", "identb"], ["PTb"])
                TR(BKp[:, h, 1, :], KTt[:, h, c, 64:128], identb64, ["KTt", "identb"], ["PTb"])
            CP("act", BKtok, BKp, ["PTb"], ["BKtok"])
            P0p = F2[0:64, :].rearrange("p (h i) -> p h i", h=8)
            Up = R2[0:64, 0:512].rearrange("p (h i) -> p h i", h=8)
            Yp = K2[0:64, 0:512].rearrange("p (h i) -> p h i", h=8)
            Hp = V2[0:64, 0:512].rearrange("p (h i) -> p h i", h=8)

            def ai(h):
                return (c * 2 + h // 4) * 4 + h % 4
            for h in range(8):
                MM(P0p[:, h, :], QTt[:, h, c, 0:64], Hb[:, h, :], True, False, ["QTt", "Hb"], ["F2"])
                MM(P0p[:, h, :], AM[:, ai(h), 128:192], Vc[:, c, h, :], False, True, ["AM", "Vc"], ["F2"])
            CP("act", P0s, P0p, ["F2"], ["P0s"])
            for h in range(8):
                MM(Up[:, h, :], P6[:, ai(h), :], P0s[:, h, :], True, True, [P6n, "P0s"], ["R2"])
            CP("act", Us, Up, ["R2"], ["Us"])
            for h in range(8):
                MM(Yp[:, h, :], QTt[:, h, c, 64:128], Hb[:, h, :], True, False, ["QTt", "Hb"], ["K2"])
                MM(Yp[:, h, :], AM[:, ai(h), 64:128], Us[:, h, :], False, False, ["AM", "Us"], ["K2"])
                MM(Yp[:, h, :], AM[:, ai(h), 192:256], Vc[:, c, h, :], False, True, ["AM", "Vc"], ["K2"])
            for h in range(8):
                MM(Hp[:, h, :], BKtok[:, h, 0, :], Us[:, h, :], True, False, ["BKtok", "Us"], ["V2"])
                MM(Hp[:, h, :], BKtok[:, h, 1, :], Vc[:, c, h, :], False, True, ["BKtok", "Vc"], ["V2"])
            CP("act", ych, Yp, ["K2"], ["ych"])
            TT("dve", H32, H32, Hp, ALU.add, ["H32", "V2"], ["H32"])
            TT("dve", H32, H32, bc(rw["g"][:, :, c * 64 + 63:c * 64 + 64], [64, 8, 64]), ALU.mult, ["H32", "g"], ["H32"])
            CP("act", Hb, H32, ["H32"], ["Hb"])
            RED(st8[:, 0:8], ych, ALU.add, ["ych"], ["st8"])
            TT("dve", yt1, ych, ych, ALU.mult, ["ych"], ["yt1"])
            RED(st8[:, 8:16], yt1, ALU.add, ["yt1"], ["st8"])
            TS("dve", st8[:, 0:16], st8[:, 0:16], 1.0 / 64, None, ALU.mult, None, ["st8"], ["st8"])
            TT("dve", st8[:, 16:24], st8[:, 0:8], st8[:, 0:8], ALU.mult, ["st8"], ["st8"])
            TT("dve", st8[:, 24:32], st8[:, 8:16], st8[:, 16:24], ALU.subtract, ["st8"], ["st8"])
            ACT(st8[:, 32:40], st8[:, 24:32], AF.Sqrt, ["st8", "cst"], ["st8"], bias=cst[0:64, 1:2])
            b.op("dve", lambda g: g.reciprocal(out=st8[:, 40:48], in_=st8[:, 32:40]), ["st8"], ["st8"])
            TT("dve", yt1, ych, bc(st8[:, 0:8].unsqueeze(2), [64, 8, 64]), ALU.subtract, ["ych", "st8"], ["yt1"])
            TT("dve", yt1, yt1, bc(st8[:, 40:48].unsqueeze(2), [64, 8, 64]), ALU.mult, ["yt1", "st8"], ["yt1"])
            lnwv = lnw_bc[:].rearrange("p (h i) -> p h i", h=8)
            lnbv = lnb_bc[:].rearrange("p (h i) -> p h i", h=8)
            TT("dve", yt1, yt1, lnwv, ALU.mult, ["yt1", "lnw_bc"], ["yt1"])
            TT("pool", yt1, yt1, lnbv, ALU.add, ["yt1", "lnb_bc"], ["yt1"])
            MM(F2[0:64, 264:272], identf[:, c * 64:(c + 1) * 64], bon, True, True, ["identf", "bon"], ["F2"])
            CP("act", st8[:, 48:56], F2[0:64, 264:272], ["F2"], ["st8"])
            TT("dve", ych, Vc[:, c], bc(st8[:, 48:56].unsqueeze(2), [64, 8, 64]), ALU.mult, ["Vc", "st8"], ["ych"])
            TT("pool", yt1, yt1, ych, ALU.add, ["yt1", "ych"], ["yt1"])
            MM(R2[0:64, 0:512], identf[:, c * 64:(c + 1) * 64], gr_s[:, :], True, True, ["identf", "gr_s"], ["R2"])
            TT("dve", yt1.rearrange("p h i -> p (h i)"), yt1.rearrange("p h i -> p (h i)"), R2[0:64, 0:512], ALU.mult, ["yt1", "R2"], ["yt1"])
            b.dma("sp", rwscr[ti * 128 + c * 64: ti * 128 + (c + 1) * 64, :], yt1.rearrange("p h i -> p (h i)"), reads=["yt1"], writes=["rwscr"])
        b.barrier()

    if stop_after == "A":
        b.barrier(); b.emit(); ncd.__exit__(None, None, None); es.close()
        return nc
    b.barrier()
    MS("dve", H32, 0.0, ["H32"]); MS("dve", Hb, 0.0, ["Hb"]); MS("pool", hlast[:], 0.0, ["hlast"])
    KSUB = int(os.environ.get('KSUB', '9'))
    V2a = V2[:, 0:512]
    V2b = V2[:, 512:1024]
    for ti in range(NT if nt_lim is None else nt_lim):
        par = ti % 2
        x_, h_, xr, hr = front(ti, par)
        if KSUB >= 1:
            for (c0, n, dst) in ((0, 256, V2a[:, 0:256]), (256, 64, V2a[:, 256:320]), (320, 512, V2b)):
                for k in range(8):
                    MM(dst, h_[:, k, :], Wn[:, k, c0:c0 + n], k == 0, k == 7, [hr, "Wn"], ["V2"])
        if KSUB >= 2:
            kv3 = kfin[:].rearrange("p (g d) -> p g d", g=2)
            qknorm(V2a[:, 0:128].rearrange("p (g d) -> p g d", g=2), kv3, 2, knw_bc, 1.0, ["V2"], ["kfin"])
            rope("dve", kv3, 2, None, ["kfin"])
            CP("act", vfin[:], V2a[:, 128:256], ["V2"], ["vfin"])
            CP("act", Vaug[:, ti, :, 0:64], V2a[:, 128:256].rearrange("p (g d) -> p g d", g=2), ["V2"], ["Vaug"])
            CP("act", kifin[:], V2a[:, 256:320], ["V2"], ["kifin"])
            rope("pool", kifin[:].unsqueeze(1), 1, None, ["kifin"])
            ACT(gr_s[:], V2b, AF.Silu, ["V2"], ["gr_s"])
        if KSUB >= 3:
            b.dma("sp", k_nat[ti * 128:(ti + 1) * 128, :], kfin[:], reads=["kfin"])
            b.dma("sp", v_nat[ti * 128:(ti + 1) * 128, :], vfin[:], reads=["vfin"])
            b.dma("sp", ki_nat[ti * 128:(ti + 1) * 128, :], kifin[:], reads=["kifin"])
        if KSUB >= 4:
            for g_ in range(2):
                TR(F2[0:64, g_ * 128:(g_ + 1) * 128], kfin[:, g_ * 64:(g_ + 1) * 64], identf[:], ["kfin", "identf"], ["F2"])
            TR(F2[0:64, 256:384], kifin[:, :], identf[:], ["kifin", "identf"], ["F2"])
            if KSUB >= 5:
                CP("act", kT_all[:, :, ti * 128:(ti + 1) * 128], F2[0:64, 0:256].rearrange("p (g t) -> p g t", g=2), ["F2"], ["kT_all"])
            if KSUB >= 6:
                if os.environ.get("KV") == "1":
                    CP("act", nrm_t[0:64, 0:128], F2[0:64, 256:384], ["F2"], ["nrm_t"])
                elif os.environ.get("KV") == "2":
                    CP("act", kiT_all[:, ti * 128:(ti + 1) * 128], F2[0:64, 0:128], ["F2"], ["kiT_all"])
                else:
                    CP("act", kiT_all[:, ti * 128:(ti + 1) * 128], F2[0:64, 256:384], ["F2"], ["kiT_all"])

        if int(os.environ.get("KLVL", "9")) >= 2:
            rwkv_tile(ti, h_, hr)
        if ti == NT - 1:
            for gi, (c0, n) in enumerate(((0, 512), (512, 512), (1024, 512), (1536, 128))):
                dst = (R2[0:1, 0:512], R2[0:1, 512:1024], K2[0:1, 0:512], K2[0:1, 512:640])[gi]
                nm = ("R2", "R2", "K2", "K2")[gi]
                for k in range(8):
                    MM(dst, h_[:, k, 127:128], Wom[:, k, c0:c0 + n], k == 0, False, ["Wm", hr], [nm])
                for k in range(8):
                    MM(dst, h_[:, k, 127:128], Wm[:, k, c0:c0 + n], False, k == 7, ["Wm", hr], [nm])
            CP("act", xmt[0:1, 0:1024], R2[0:1, :], ["R2"], ["xmt"])
            CP("dve", xmt[0:1, 1024:1664], K2[0:1, 0:640], ["K2"], ["xmt"])
            b.dma("sp", shift_p.rearrange("(a n) -> a n", a=1), xmt[0:1, :], reads=["xmt"])
    for h in range(8):
        TR(F2[0:64, h * 64:(h + 1) * 64], H32[:, h, :], identf[0:64, 0:64], ["H32", "identf"], ["F2"])
    CP("dve", ych, F2[0:64, 0:512].rearrange("p (h j) -> p h j", h=8), ["F2"], ["ych"])
    b.dma("sp", wkv_p.rearrange("h i j -> i h j"), ych, reads=["ych"])

    if stop_after == "B":
        b.barrier(); b.emit(); ncd.__exit__(None, None, None); es.close()
        return nc
    b.barrier()
    off = 0
    Wq, off = carve(off, [128, 8, 1544], BF16)
    stg, off = carve(off, [128, 8, 512])
    score, off = carve(off, [128, T])
    selm, off = carve(off, [128, T], BF16)
    selT, off = carve(off, [128, NT, 128], BF16)
    rl, off = carve(off, [128, 512])
    rl2, off = carve(off, [128, 512])
    qfin, off = carve(off, [128, 512])
    qifin, off = carve(off, [128, 512])
    qT, off = carve(off, [64, 8, 128], BF16)
    qiT, off = carve(off, [64, 8, 128], BF16)
    ga, off = carve(off, [128, 512])
    eT, off = carve(off, [128, 4, 128], BF16)
    pTt, off = carve(off, [128, 4, 128], BF16)
    eT2, off = carve(off, [128, 4, 128], BF16)
    pTt2, off = carve(off, [128, 4, 128], BF16)
    cat, off = carve(off, [128, D], BF16)
    catT, off = carve(off, [128, 8, 128], BF16)
    rwo, off = carve(off, [128, 512])
    rwo2, off = carve(off, [128, 512])
    att, off = carve(off, [128, 8, 64])
    ybuf, off = carve(off, [128, D])
    bs, off = carve(off, [128, 16])
    wis, off = carve(off, [128, 8])
    oacc, off = carve(off, [128, 2, 4, 65])
    Wout, off = carve(off, [128, 8, D], BF16)
    gate_bc, off = carve(off, [128, D])
    iota256, off = carve(off, [128, 256])
    assert off <= AW, off
    b.dma("sp", gate_bc, gscr[16:17, :].partition_broadcast(128) if False else gscr[16, :].partition_broadcast(128), reads=["gscr"], writes=["gate_bc"])
    b.dma("sp", iota256, iota_d[:, :], writes=["iota256"])
    load_cols(Wq, 0, C_Q, 512, tag="Wq")
    load_cols(Wq, 512, C_QI, 520, tag="Wq")
    load_cols(Wq, 1032, C_GA, 512, tag="Wq")
    w_out_v = w_out.rearrange("(k p) c -> p k c", p=128)
    for hh in range(2):
        b.dma("sp", stg[:, :, :], w_out_v[:, :, hh * 512:(hh + 1) * 512], writes=["stg"])
        CP("pool", Wout[:, :, hh * 512:(hh + 1) * 512], stg[:, :, :], ["stg"], ["Wout"])


    for j in range(NO if no_lim is None else no_lim):
        ti = NT + j
        x_, h_, xr, hr = front(ti, 0)
        NKT = 2 * (j + 1)
        NK = NKT * 128
        for (c0, n, dst, nm) in ((0, 512, R2[:, 0:512], "R2a"), (512, 512, R2[:, 512:1024], "R2b"),
                                 (1024, 8, F2[:, 0:8], "F2"), (1032, 512, K2[:, 0:512], "K2a")):
            for k in range(8):
                MM(dst, h_[:, k, :], Wq[:, k, c0:c0 + n], k == 0, k == 7, [hr, "Wq"], [nm])
        q3 = qfin.rearrange("p (h d) -> p h d", h=8)
        qknorm(R2[:, 0:512].rearrange("p (h d) -> p h d", h=8), q3, 8, qnw_bc, 0.125, ["R2a"], ["qfin"])
        rope("dve", q3, 8, None, ["qfin"])
        qi3 = qifin.rearrange("p (h d) -> p h d", h=8)
        CP("act", qifin, R2[:, 512:1024], ["R2b"], ["qifin"])
        rope("pool", qi3, 8, None, ["qifin"])
        TS("dve", wis, F2[:, 0:8], 0.044194173824159216, None, ALU.mult, None, ["F2"], ["wis"])
        ACT(ga, K2[:, 0:512], AF.Silu, ["K2a"], ["ga"])
        for (src, srcn, dstT, dn) in ((qfin, "qfin", qT, "qT"), (qifin, "qifin", qiT, "qiT")):
            pv = K2[0:64, :].rearrange("p (h t) -> p h t", h=8)
            for h in range(8):
                TR(pv[:, h, :], src[:, h * 64:(h + 1) * 64], identf[:], [srcn, "identf"], ["K2a" if h < 4 else "K2b"])
            CP("act", dstT, pv, ["K2a", "K2b"], [dn])
        nchk = (NK + 511) // 512
        ib = 0
        for kc in range(nchk):
            w = min(512, NK - kc * 512)
            for h in range(8):
                pb = (R2[:, 0:512], R2[:, 512:1024])[ib % 2]
                pbn = ("R2a", "R2b")[ib % 2]
                rlb = (rl, rl2)[ib % 2]
                rln = ("rl", "rl2")[ib % 2]
                ib += 1
                MM(pb[:, 0:w], qiT[:, h, :], kiT_all[:, kc * 512:kc * 512 + w], True, True, ["qiT", "kiT_all"], [pbn])
                ACT(rlb[:, 0:w], pb[:, 0:w], AF.Relu, [pbn], [rln])
                sc = score[:, kc * 512:kc * 512 + w]
                if h == 0:
                    TS("dve", sc, rlb[:, 0:w], wis[:, 0:1], None, ALU.mult, None, [rln, "wis"], ["score"])
                else:
                    STT(sc, rlb[:, 0:w], wis[:, h:h + 1], sc, ALU.mult, ALU.add, [rln, "wis", "score"], ["score"])
        RED(bs[:, 0:1], score[:, 0:NK], ALU.max, ["score"], ["bs"])
        RED(bs[:, 1:2], score[:, 0:NK], ALU.min, ["score"], ["bs"])
        TS("dve", rl[:, 0:256], iota256, qrel[:, 0:1], -1e30, ALU.is_gt, ALU.mult, ["iota256", "qrel"], ["rl"])
        TT("dve", score[:, NK - 256:NK], score[:, NK - 256:NK], rl[:, 0:256], ALU.add, ["score", "rl"], ["score"])
        TS("dve", bs[:, 2:3], bs[:, 1:2], -1.0, None, ALU.add, None, ["bs"], ["bs"])
        STT(bs[:, 3:4], bs[:, 0:1], 2.0, bs[:, 1:2], ALU.add, ALU.subtract, ["bs"], ["bs"])
        for it in range(1, n_bis + 1):
            sc_ = float(2.0 ** (-it))
            STT(bs[:, 4:5], bs[:, 3:4], sc_, bs[:, 2:3], ALU.mult, ALU.add, ["bs"], ["bs"])
            TS("dve", selm[:, 0:NK], score[:, 0:NK], bs[:, 4:5], 0.0, ALU.is_gt, ALU.add, ["score", "bs"], ["selm", "bs"], accum=bs[:, 5:6])
            TS("dve", bs[:, 6:7], bs[:, 5:6], float(topk_p) - 0.5, bs[:, 3:4], ALU.is_gt, ALU.mult, ["bs"], ["bs"])
            STT(bs[:, 2:3], bs[:, 6:7], sc_, bs[:, 2:3], ALU.mult, ALU.add, ["bs"], ["bs"])
        TS("dve", selm[:, 0:NK], score[:, 0:NK], bs[:, 2:3], None, ALU.is_gt, None, ["score", "bs"], ["selm"])
        for kt in range(NKT):
            TR(PTb[:, (kt % 8) * 128:(kt % 8 + 1) * 128], selm[:, kt * 128:(kt + 1) * 128], identb[:], ["selm", "identb"], ["PTb"])
            if kt % 8 == 7 or kt == NKT - 1:
                k0 = (kt // 8) * 8
                n_ = kt - k0 + 1
                CP("act", selT[:, k0:k0 + n_, :], PTb[:, 0:n_ * 128].rearrange("p (a t) -> p a t", a=n_), ["PTb"], ["selT"])
        po = [V2[:, 0:260].rearrange("p (h e) -> p h e", h=4), V2[:, 512:772].rearrange("p (h e) -> p h e", h=4)]
        for kt in range(NKT):
            for g_ in range(2):
                lp = K2[:, g_ * 512:(g_ + 1) * 512]
                kn_ = ("K2a", "K2b")[g_]
                vn_ = ("V2a", "V2b")[g_]
                eTb = (eT, eT2)[g_]
                pTb = (pTt, pTt2)[g_]
                en_ = ("eT", "eT2")[g_]
                pn_ = ("pTt", "pTt2")[g_]
                on_ = ("oacc0", "oacc1")[g_]
                MM(lp, kT_all[:, g_, kt * 128:(kt + 1) * 128], qT[:, g_ * 4:(g_ + 1) * 4, :].rearrange("p h t -> p (h t)"),
                   True, True, ["kT_all", "qT"], [kn_])
                ACT(eTb, lp.rearrange("p (h t) -> p h t", h=4), AF.Exp, [kn_], [en_])
                TT("pool" if g_ else "dve", pTb, eTb, bc(selT[:, kt, :].unsqueeze(1), [128, 4, 128]), ALU.mult, [en_, "selT"], [pn_])
                for hh in range(4):
                    MM(po[g_][:, hh, :], pTb[:, hh, :], Vaug[:, kt, g_, :], True, True, [pn_, "Vaug"], [vn_])
                if kt == 0:
                    CP("act", oacc[:, g_], po[g_], [vn_], [on_])
                else:
                    TT("dve", oacc[:, g_], oacc[:, g_], po[g_], ALU.add, [vn_, on_], [on_])
        for g_ in range(2):
            b.op("dve", lambda g, g_=g_: g.reciprocal(out=bs[:, 8 + g_ * 4:12 + g_ * 4], in_=oacc[:, g_, :, 64]), ["oacc0", "oacc1"], ["bs"])
            TT("dve", att[:, g_ * 4:(g_ + 1) * 4, :], oacc[:, g_, :, 0:64], bc(bs[:, 8 + g_ * 4:12 + g_ * 4].unsqueeze(2), [128, 4, 64]),
               ALU.mult, ["oacc0", "oacc1", "bs"], ["att"])
        TT("dve", cat[:, 0:512], att.rearrange("p h d -> p (h d)"), ga, ALU.mult, ["att", "ga"], ["cat"])
        b.dma("sp", rwo, rwscr[(2 * j) * 128:(2 * j + 1) * 128, :], reads=["rwscr"], writes=["rwo"])
        b.dma("sp", rwo2, rwscr[(2 * j + 1) * 128:(2 * j + 2) * 128, :], reads=["rwscr"], writes=["rwo2"])
        TS("dve", rwo, rwo, parsel[:, 1:2], None, ALU.mult, None, ["rwo", "parsel"], ["rwo"])
        STT(rwo, rwo2, parsel[:, 0:1], rwo, ALU.mult, ALU.add, ["rwo2", "parsel", "rwo"], ["rwo"])
        CP("act", cat[:, 512:1024], rwo, ["rwo"], ["cat"])
        for k in range(8):
            TR(PTb[:, k * 128:(k + 1) * 128], cat[:, k * 128:(k + 1) * 128], identb[:], ["cat", "identb"], ["PTb"])
        CP("act", catT, PTb[:, :].rearrange("p (k t) -> p k t", k=8), ["PTb"], ["catT"])
        for hh in range(2):
            for k in range(8):
                MM(R2[:, hh * 512:(hh + 1) * 512], catT[:, k, :], Wout[:, k, hh * 512:(hh + 1) * 512], k == 0, k == 7, ["catT", "Wout"], [("R2a", "R2b")[hh]])
        TT("dve", ybuf, R2[:, :], gate_bc, ALU.mult, ["R2a", "R2b", "gate_bc"], ["ybuf"])
        TT("pool", ybuf, ybuf, x_[:], ALU.add, ["ybuf", xr], ["ybuf"])
        b.dma("sp", y_own[j * 128:(j + 1) * 128, :], ybuf, reads=["ybuf"])


    if do_sample:
        b.barrier()
        PW = 10500
        off = 0
        proj, off = carve(off, [16, DIN])
        tk = {}
        for nm in ("qs", "ga", "grs", "ta", "tb"):
            tk[nm], off = carve(off, [16, 512])
        ks_, off = carve(off, [16, 128])
        s16, off = carve(off, [16, 64])
        tokd, off = carve(off, [16, 1040])
        ysb, off = carve(off, [16, D])
        cats, off = carve(off, [16, D], BF16)
        catTs, off = carve(off, [128, 8, 16], BF16)
        assert off <= PW, off
        off = PW
        stg, off = carve(off, [128, 8, 512])
        wbf, off = carve(off, [128, 8, 512], BF16)
        sshift_t, off = carve(off, [16, SHW])
        mu16, off = carve(off, [16, SHW])
        X1 = off
        xm, off = carve(off, [16, SHW])
        prm, off = carve(off, [16, 5, 512])
        vecs, off = carve(off, [16, 8, 6, 64])
        for nm in ("dec", "asg", "kkv", "kkn", "kmod"):
            tk[nm], off = carve(off, [16, 512])
        wdt, off = carve(off, [16, 128])
        wdT, off = carve(off, [64, 32])
        assert off <= AW, off
        NPAIR = NS // 2

        x_, h_, xr, hr = front(NT + NO, 0, m_prompt=False, ntok=16)
        for ch in range(8):
            c0 = ch * 505
            b.dma("sp", stg[:, :, 0:505], w_in_v[:, :, c0:c0 + 505], writes=["stg"])
            CP("dve", wbf[:, :, 0:505], stg[:, :, 0:505], ["stg"], ["wbf"])
            for k in range(8):
                MM(R2[0:16, 0:505], h_[:, k, 0:16], wbf[:, k, 0:505], k == 0, k == 7, [hr, "wbf"], ["R2"])
            CP("act", proj[:, c0:c0 + 505], R2[0:16, 0:505], ["R2"], ["proj"])
        b.dma("sp", sshift_t, sshift_d[:, :], writes=["sshift"])
        b.dma("sp", mu16, mu.partition_broadcast(16), writes=["mu16"])
        for i_, src in enumerate((pw0, pa0, pkk, pka, prk)):
            b.dma("sp", prm[:, i_, :], src.partition_broadcast(16), writes=["prm"])
        qs3 = tk["qs"].rearrange("p (h d) -> p h d", h=8)
        qknorm(proj[:, 0:512].rearrange("p (h d) -> p h d", h=8), qs3, 8, qnw_bc, 0.125, ["proj"], ["qs"], nrows=16)
        rope("dve", qs3, 8, None, ["qs"], nrows=16)
        ks3 = ks_.rearrange("p (g d) -> p g d", g=2)
        qknorm(proj[:, 512:640].rearrange("p (g d) -> p g d", g=2), ks3, 2, knw_bc, 1.0, ["proj"], ["ks"], nrows=16)
        rope("dve", ks3, 2, None, ["ks"], nrows=16)
        b.dma("sp", k_s[:, :], ks_, reads=["ks"])
        b.dma("sp", v_s[:, :], proj[:, 640:768], reads=["proj"])
        b.dma("sp", shift_s[:, :], proj[:, C_R:C_R + SHW], reads=["proj"])
        rope("dve", proj[:, 768:1280].rearrange("p (h d) -> p h d", h=8), 8, None, ["proj"], nrows=16)
        rope("dve", proj[:, 1288:1352].unsqueeze(1), 1, None, ["proj"], nrows=16)
        b.dma("sp", ki_s[:, :], proj[:, 1288:1352], reads=["proj"])
        ACT(tk["ga"], proj[:, C_GA:C_GA + 512], AF.Silu, ["proj"], ["ga"])
        ACT(tk["grs"], proj[:, C_GR:C_GR + 512], AF.Silu, ["proj"], ["grs"])
        xs_ = proj[:, C_R:C_R + SHW]
        TT("dve", xm, sshift_t, xs_, ALU.subtract, ["sshift", "proj"], ["xm"])
        TT("dve", xm, xm, mu16, ALU.mult, ["xm", "mu16"], ["xm"])
        TT("dve", xm, xm, xs_, ALU.add, ["xm", "proj"], ["xm"])
        ACT(wdt[:, 0:64], xm[:, 1536:1600], AF.Tanh, ["xm"], ["wdt"])
        CP("dve", wdt[:, 64:128], xm[:, 1600:1664], ["xm"], ["wdt"])
        TR(F2[0:64, 0:16], wdt[:, 0:64], identf[0:16, 0:16], ["wdt", "identf"], ["F2"])
        TR(F2[0:64, 16:32], wdt[:, 64:128], identf[0:16, 0:16], ["wdt", "identf"], ["F2"])
        CP("dve", wdT, F2[0:64, 0:32], ["F2"], ["wdT"])
        MM(R2[0:16, 0:512], wdT[:, 0:16], wupS[:, :], True, True, ["wdT", "wupS"], ["R2"])
        MM(R2[0:16, 512:1024], wdT[:, 16:32], aupS[:, :], True, True, ["wdT", "aupS"], ["R2"])
        TT("dve", tk["dec"], R2[0:16, 0:512], prm[:, 0, :], ALU.add, ["R2", "prm"], ["dec"])
        ACT(tk["dec"], tk["dec"], AF.Sigmoid, ["dec"], ["dec"])
        ACT(tk["dec"], tk["dec"], AF.Exp, ["dec"], ["dec"], scale=-0.6065306597126334)
        TT("dve", tk["asg"], R2[0:16, 512:1024], prm[:, 1, :], ALU.add, ["R2", "prm"], ["asg"])
        ACT(tk["asg"], tk["asg"], AF.Sigmoid, ["asg"], ["asg"])
        xr_, xk_, xv_ = xm[:, 0:512], xm[:, 512:1024], xm[:, 1024:1536]
        TT("dve", tk["kkv"], xk_, prm[:, 2, :], ALU.mult, ["xm", "prm"], ["kkv"])
        ACT(tk["ta"], tk["kkv"], AF.Square, ["kkv"], ["ta"])
        RED(s16[:, 0:8], tk["ta"].rearrange("p (h d) -> p h d", h=8), ALU.add, ["ta"], ["s16"])
        ACT(s16[:, 8:16], s16[:, 0:8], AF.Sqrt, ["s16", "cst"], ["s16"], bias=cst[0:16, 2:3])
        b.op("dve", lambda g: g.reciprocal(out=s16[:, 16:24], in_=s16[:, 8:16]), ["s16"], ["s16"])
        TT("dve", tk["kkn"].rearrange("p (h d) -> p h d", h=8), tk["kkv"].rearrange("p (h d) -> p h d", h=8),
           bc(s16[:, 16:24].unsqueeze(2), [16, 8, 64]), ALU.mult, ["kkv", "s16"], ["kkn"])
        STT(tk["ta"], tk["asg"], -1.0, prm[:, 3, :], ALU.add, ALU.mult, ["asg", "prm"], ["ta"])
        STT(tk["kmod"], tk["ta"], 1.0, xk_, ALU.add, ALU.mult, ["ta", "xm"], ["kmod"])

        def v8(ap):
            return ap.rearrange("p (h d) -> p h d", h=8)
        CP("dve", vecs[:, :, 0, :], v8(tk["dec"]), ["dec"], ["vecs"])
        TS("dve", vecs[:, :, 1, :], v8(tk["kkn"]), -1.0, None, ALU.mult, None, ["kkn"], ["vecs"])
        TT("dve", vecs[:, :, 2, :], v8(tk["kkn"]), v8(tk["asg"]), ALU.mult, ["kkn", "asg"], ["vecs"])
        CP("dve", vecs[:, :, 3, :], v8(tk["kmod"]), ["kmod"], ["vecs"])
        CP("dve", vecs[:, :, 4, :], v8(xr_), ["xm"], ["vecs"])
        CP("dve", vecs[:, :, 5, :], v8(xv_), ["xm"], ["vecs"])
        TT("dve", tk["ta"], xr_, prm[:, 4, :], ALU.mult, ["xm", "prm"], ["ta"])
        TT("dve", tk["ta"], tk["ta"], tk["kmod"], ALU.mult, ["ta", "kmod"], ["ta"])
        RED(s16[:, 24:32], v8(tk["ta"]), ALU.add, ["ta"], ["s16"])
        b.dma("sp", scr1[:, :], vecs.rearrange("p h v j -> p (h v j)"), reads=["vecs"], writes=["scr1"])
        b.barrier()
        off = PW
        S_, off = carve(off, [128, 4096])
        tmpS, off = carve(off, [128, 4096])
        vsh, off = carve(off, [128, 384])
        ysh, off = carve(off, [128, 128])
        assert off <= X1
        b.dma("sp", S_, swkv_d[:, :], writes=["S"])
        b.dma("sp", vsh, scr1.rearrange("s (h x) -> (s h) x", h=8), reads=["scr1"], writes=["vsh"])
        S3 = S_.rearrange("p (i j) -> p i j", i=64)
        T3 = tmpS.rearrange("p (i j) -> p i j", i=64)

        def jb(vi):
            return bc(vsh[:, vi * 64:(vi + 1) * 64].unsqueeze(1), [128, 64, 64])

        def ib(ap):
            return bc(ap.unsqueeze(2), [128, 64, 64])
        TT("dve", T3, S3, jb(1), ALU.mult, ["S", "vsh"], ["tmpS"])
        RED(ysh[:, 0:64], T3, ALU.add, ["tmpS"], ["ysh"])
        TT("dve", S3, S3, jb(0), ALU.mult, ["S", "vsh"], ["S"])
        TT("dve", T3, jb(2), ib(ysh[:, 0:64]), ALU.mult, ["vsh", "ysh"], ["tmpS"])
        TT("dve", S3, S3, T3, ALU.add, ["S", "tmpS"], ["S"])
        TT("dve", T3, jb(3), ib(vsh[:, 320:384]), ALU.mult, ["vsh"], ["tmpS"])
        TT("dve", S3, S3, T3, ALU.add, ["S", "tmpS"], ["S"])
        b.dma("sp", wkv_s[:, :], S_, reads=["S"])
        TT("dve", T3, S3, jb(4), ALU.mult, ["S", "vsh"], ["tmpS"])
        RED(ysh[:, 64:128], T3, ALU.add, ["tmpS"], ["ysh"])
        b.dma("sp", scr2[:, :], ysh[:, 64:128], reads=["ysh"], writes=["scr2"])
        yS = tk["tb"]
        b.dma("sp", yS, scr2.rearrange("(s h) i -> s (h i)", h=8), reads=["scr2"], writes=["tb"])
        y3 = v8(yS)
        RED(s16[:, 32:40], y3, ALU.add, ["tb"], ["s16"])
        ACT(tk["ta"], yS, AF.Square, ["tb"], ["ta"])
        RED(s16[:, 40:48], v8(tk["ta"]), ALU.add, ["ta"], ["s16"])
        TS("dve", s16[:, 32:48], s16[:, 32:48], 1.0 / 64, None, ALU.mult, None, ["s16"], ["s16"])
        TT("dve", s16[:, 48:56], s16[:, 32:40], s16[:, 32:40], ALU.mult, ["s16"], ["s16"])
        TT("dve", s16[:, 48:56], s16[:, 40:48], s16[:, 48:56], ALU.subtract, ["s16"], ["s16"])
        ACT(s16[:, 56:64], s16[:, 48:56], AF.Sqrt, ["s16", "cst"], ["s16"], bias=cst[0:16, 1:2])
        b.op("dve", lambda g: g.reciprocal(out=s16[:, 56:64], in_=s16[:, 56:64]), ["s16"], ["s16"])
        TT("dve", y3, y3, bc(s16[:, 32:40].unsqueeze(2), [16, 8, 64]), ALU.subtract, ["tb", "s16"], ["tb"])
        TT("dve", y3, y3, bc(s16[:, 56:64].unsqueeze(2), [16, 8, 64]), ALU.mult, ["tb", "s16"], ["tb"])
        TT("dve", yS, yS, lnw_bc[0:16, :], ALU.mult, ["tb", "lnw_bc"], ["tb"])
        TT("dve", yS, yS, lnb_bc[0:16, :], ALU.add, ["tb", "lnb_bc"], ["tb"])
        TT("dve", v8(tk["ta"]), v8(xv_), bc(s16[:, 24:32].unsqueeze(2), [16, 8, 64]), ALU.mult, ["xm", "s16"], ["ta"])
        TT("dve", yS, yS, tk["ta"], ALU.add, ["tb", "ta"], ["tb"])
        TT("dve", cats[:, 512:1024], yS, tk["grs"], ALU.mult, ["tb", "grs"], ["cats"])
        b.barrier()
        NCAND = 16
        off = PW
        Gi, off = carve(off, [128, 8192])
        tmpG, off = carve(off, [128, 64, 64])
        Kc, off = carve(off, [128, NCAND, 128])
        Vcd, off = carve(off, [128, NCAND, 128])
        tmpc, off = carve(off, [128, NCAND, 64])
        repd, off = carve(off, [128, 1040])
        opd, off = carve(off, [128, 520])
        repS, off = carve(off, [16, 1024])
        repTS, off = carve(off, [128, 128])
        blkS, off = carve(off, [128, 128])
        sc, off = carve(off, [128, 132])
        msc, off = carve(off, [128, 132])
        sh_, off = carve(off, [128, 128])
        cv, off = carve(off, [128, NCAND])
        ci, off = carve(off, [128, NCAND], I32)
        cif, off = carve(off, [128, NCAND])
        rowi, off = carve(off, [128, NCAND], I32)
        lg, off = carve(off, [128, 8, NCAND])
        b2, off = carve(off, [128, 16])
        ptab, off = carve(off, [128, 8], I32)
        ptf, off = carve(off, [128, 8])
        oh0, off = carve(off, [128, 1])
        assert off <= AW, off
        Gi3 = Gi.rearrange("p (t d) -> p t d", t=128)
        for (t_, d_, nm) in ((ptab, ptab_d, "ptab"), (repS, rep_d, "repS"), (repTS, repT_d, "repTS"), (blkS, blk_d, "blkS"), (oh0, oh0_d, "oh0")):
            b.dma("sp", t_, d_[:, :], writes=[nm])
        CP("dve", tokd[:, 0:512], proj[:, 768:1280], ["proj"], ["tokd"])
        TS("dve", tokd[:, 512:520], proj[:, 1280:1288], 0.044194173824159216, None, ALU.mult, None, ["proj"], ["tokd"])
        CP("dve", tokd[:, 520:1032], tk["qs"], ["qs"], ["tokd"])
        TT("dve", v8(tk["ta"]), v8(tokd[:, 0:512]), bc(proj[:, 1288:1352].unsqueeze(1), [16, 8, 64]), ALU.mult, ["tokd", "proj"], ["ta"])
        RED(s16[:, 0:8], v8(tk["ta"]), ALU.add, ["ta"], ["s16"])
        TS("dve", s16[:, 0:8], s16[:, 0:8], 0.0, None, ALU.max, None, ["s16"], ["s16"])
        TT("dve", s16[:, 0:8], s16[:, 0:8], tokd[:, 512:520], ALU.mult, ["s16", "tokd"], ["s16"])
        RED(tokd[:, 1032:1033], s16[:, 0:8], ALU.add, ["s16"], ["tokd"])
        CP("dve", ptf, ptab, ["ptab"], ["ptf"])
        TS("dve", ptf, ptf, 128.0, None, ALU.mult, None, ["ptf"], ["ptf"])
        ck_rows = cache_k
        cv_rows = cache_v
        cvA, off = carve(off, [128, 8, NCAND])
        ciA, off = carve(off, [128, 8, NCAND], I32)
        thrA, off = carve(off, [128, 8])
        tmpGf = tmpG.rearrange("p a b -> p (a b)")
        cand16 = tmpGf[0:16, 0:1540]
        candj = tmpGf[0:16, 1540:3080]
        assert off <= AW, off
        for sp in range(NPAIR):
            b.op("pool", lambda g, sp=sp: g.indirect_dma_start(out=Gi, out_offset=None, in_=cache_ki[:, :],
                                                                in_offset=bass.IndirectOffsetOnAxis(ap=ptab[:, sp:sp + 1], axis=0)),
                 ["ptab"], ["Gi"], dma=True)
            for (c0, n) in ((0, 512), (512, 8)):
                MM(K2[:, 0:n], repS[:, sp * 128:(sp + 1) * 128], tokd[:, c0:c0 + n], True, True, ["repS", "tokd"], ["K2"])
                CP("act", repd[:, c0:c0 + n], K2[:, 0:n], ["K2"], ["repd"])
            for h in range(8):
                for hf in range(2):
                    TT("dve", tmpG, Gi3[:, hf * 64:(hf + 1) * 64, :], bc(repd[:, h * 64:(h + 1) * 64].unsqueeze(1), [128, 64, 64]), ALU.mult, ["Gi", "repd"], ["tmpG"])
                    RED(sh_[:, hf * 64:(hf + 1) * 64], tmpG, ALU.add, ["tmpG"], ["sh"])
                if h == 0:
                    TS("dve", sc[:, 0:128], sh_, 0.0, repd[:, 512:513], ALU.max, ALU.mult, ["sh", "repd"], ["sc"])
                else:
                    TS("dve", sh_, sh_, 0.0, repd[:, 512 + h:513 + h], ALU.max, ALU.mult, ["sh", "repd"], ["sh"])
                    TT("dve", sc[:, 0:128], sc[:, 0:128], sh_, ALU.add, ["sc", "sh"], ["sc"])
            for r_ in range(NCAND // 8):
                b.op("dve", lambda g, r_=r_, sp=sp: g.max(out=cvA[:, sp, r_ * 8:(r_ + 1) * 8], in_=sc[:, 0:128]), ["sc"], ["cvA"])
                b.op("dve", lambda g, r_=r_, sp=sp: g.max_index(out=ciA[:, sp, r_ * 8:(r_ + 1) * 8].bitcast(mybir.dt.uint32),
                                                                in_max=cvA[:, sp, r_ * 8:(r_ + 1) * 8], in_values=sc[:, 0:128]), ["sc", "cvA"], ["ciA"])
                if r_ < NCAND // 8 - 1:
                    b.op("dve", lambda g, r_=r_, sp=sp: g.match_replace(out=sc[:, 0:128], in_to_replace=cvA[:, sp, r_ * 8:(r_ + 1) * 8],
                                                                        in_values=sc[:, 0:128], imm_value=-3e30), ["sc", "cvA"], ["sc"])
        b.dma("sp", scr4.rearrange("(sp s2) g c -> (s2 g) sp c", s2=2), cvA, reads=["cvA"], writes=["scr4"])
        b.dma("sp", cand16[:, 0:64 * NCAND], scr4.rearrange("s g c -> s (g c)"), reads=["scr4"], writes=["tmpG"])
        CP("dve", cand16[:, 64 * NCAND:64 * NCAND + 1], tokd[:, 1032:1033], ["tokd"], ["tmpG"])
        cnd = cand16[:, 0:64 * NCAND + 1]
        RED(s16[:, 40:41], cnd, ALU.max, ["tmpG"], ["s16"])
        RED(s16[:, 41:42], cnd, ALU.min, ["tmpG"], ["s16"])
        TS("dve", s16[:, 42:43], s16[:, 41:42], -1.0, None, ALU.add, None, ["s16"], ["s16"])
        STT(s16[:, 43:44], s16[:, 40:41], 2.0, s16[:, 41:42], ALU.add, ALU.subtract, ["s16"], ["s16"])
        for it in range(1, n_bis + 3):
            sc_ = float(2.0 ** (-it))
            STT(s16[:, 44:45], s16[:, 43:44], sc_, s16[:, 42:43], ALU.mult, ALU.add, ["s16"], ["s16"])
            TS("dve", candj[:, 0:64 * NCAND + 1], cnd, s16[:, 44:45], 0.0, ALU.is_gt, ALU.add, ["tmpG", "s16"], ["tmpG", "s16"], accum=s16[:, 45:46])
            TS("dve", s16[:, 46:47], s16[:, 45:46], float(topk_s) - 0.5, s16[:, 43:44], ALU.is_gt, ALU.mult, ["s16"], ["s16"])
            STT(s16[:, 42:43], s16[:, 46:47], sc_, s16[:, 42:43], ALU.mult, ALU.add, ["s16"], ["s16"])
        TT("dve", s16[:, 32:33], tokd[:, 1032:1033], s16[:, 42:43], ALU.is_gt, ["tokd", "s16"], ["s16"])
        for sp in range(NPAIR):
            MM(F2[:, sp:sp + 1], repS[:, sp * 128:(sp + 1) * 128], s16[:, 42:43], True, True, ["repS", "s16"], ["F2"])
        CP("dve", thrA, F2[:, 0:8], ["F2"], ["thrA"])
        for sp in range(NPAIR):
            MM(K2[:, 0:512], repS[:, sp * 128:(sp + 1) * 128], tokd[:, 520:1032], True, True, ["repS", "tokd"], ["K2"])
            CP("act", repd[:, 520:1032], K2[:, 0:512], ["K2"], ["repd"])
            CP("dve", cif, ciA[:, sp, :], ["ciA"], ["cif"])
            TS("dve", cif, cif, ptf[:, sp:sp + 1], None, ALU.add, None, ["cif", "ptf"], ["cif"])
            CP("dve", rowi, cif, ["cif"], ["rowi"])
            TS("dve", cv, cvA[:, sp, :], thrA[:, sp:sp + 1], None, ALU.is_gt, None, ["cvA", "thrA"], ["cv"])
            for c_ in range(NCAND):
                b.op("pool", lambda g, c_=c_: g.indirect_dma_start(out=Kc[:, c_, :], out_offset=None, in_=ck_rows[:, :],
                                                                  in_offset=bass.IndirectOffsetOnAxis(ap=rowi[:, c_:c_ + 1], axis=0)),
                     ["rowi"], ["Kc"], dma=True)
                b.op("pool", lambda g, c_=c_: g.indirect_dma_start(out=Vcd[:, c_, :], out_offset=None, in_=cv_rows[:, :],
                                                                  in_offset=bass.IndirectOffsetOnAxis(ap=rowi[:, c_:c_ + 1], axis=0)),
                     ["rowi"], ["Vcd"], dma=True)
            Kc4 = Kc.rearrange("p c (g d) -> p c g d", g=2)
            Vc4 = Vcd.rearrange("p c (g d) -> p c g d", g=2)
            for h in range(8):
                TT("dve", tmpc, Kc4[:, :, h // 4, :], bc(repd[:, 520 + h * 64:520 + (h + 1) * 64].unsqueeze(1), [128, NCAND, 64]), ALU.mult, ["Kc", "repd"], ["tmpc"])
                RED(lg[:, h, :], tmpc, ALU.add, ["tmpc"], ["lg"])
            ACT(lg, lg, AF.Exp, ["lg"], ["lg"])
            TT("dve", lg, lg, bc(cv.unsqueeze(1), [128, 8, NCAND]), ALU.mult, ["lg", "cv"], ["lg"])
            RED(opd[:, 512:520], lg, ALU.add, ["lg"], ["opd"])
            for h in range(8):
                TT("dve", tmpc, Vc4[:, :, h // 4, :], bc(lg[:, h, :].unsqueeze(2), [128, NCAND, 64]), ALU.mult, ["Vcd", "lg"], ["tmpc"])
                RED(opd[:, h * 64:(h + 1) * 64], tmpc.rearrange("p c d -> p d c"), ALU.add, ["tmpc"], ["opd"])
            MM(V2[0:16, 0:512], repTS[:, sp * 16:(sp + 1) * 16], opd[:, 0:512], sp == 0, sp == NPAIR - 1, ["repTS", "opd"], ["V2"])
            MM(V2[0:16, 512:520], repTS[:, sp * 16:(sp + 1) * 16], opd[:, 512:520], sp == 0, sp == NPAIR - 1, ["repTS", "opd"], ["V2"])
        qv = v8(tk["qs"])
        for g_ in range(2):
            TT("dve", v8(tk["ta"])[:, g_ * 4:(g_ + 1) * 4, :], qv[:, g_ * 4:(g_ + 1) * 4, :],
               bc(ks_[:, g_ * 64:(g_ + 1) * 64].unsqueeze(1), [16, 4, 64]), ALU.mult, ["qs", "ks"], ["ta"])
        RED(s16[:, 0:8], v8(tk["ta"]), ALU.add, ["ta"], ["s16"])
        ACT(s16[:, 0:8], s16[:, 0:8], AF.Exp, ["s16"], ["s16"])
        TS("dve", s16[:, 0:8], s16[:, 0:8], s16[:, 32:33], None, ALU.mult, None, ["s16"], ["s16"])
        TT("dve", s16[:, 8:16], V2[0:16, 512:520], s16[:, 0:8], ALU.add, ["V2", "s16"], ["s16"])
        b.op("dve", lambda g: g.reciprocal(out=s16[:, 8:16], in_=s16[:, 8:16]), ["s16"], ["s16"])
        for g_ in range(2):
            TT("dve", v8(tk["ta"])[:, g_ * 4:(g_ + 1) * 4, :], bc(proj[:, 640 + g_ * 64:640 + (g_ + 1) * 64].unsqueeze(1), [16, 4, 64]),
               bc(s16[:, g_ * 4:(g_ + 1) * 4].unsqueeze(2), [16, 4, 64]), ALU.mult, ["proj", "s16"], ["ta"])
        TT("dve", tk["ta"], tk["ta"], V2[0:16, 0:512], ALU.add, ["ta", "V2"], ["ta"])
        TT("dve", v8(tk["ta"]), v8(tk["ta"]), bc(s16[:, 8:16].unsqueeze(2), [16, 8, 64]), ALU.mult, ["ta", "s16"], ["ta"])
        TT("dve", cats[:, 0:512], tk["ta"], tk["ga"], ALU.mult, ["ta", "ga"], ["cats"])
        for k in range(8):
            TR(PTb[:, k * 16:(k + 1) * 16], cats[:, k * 128:(k + 1) * 128], identb[0:16, 0:16], ["cats", "identb"], ["PTb"])
        CP("act", catTs, PTb[:, 0:128].rearrange("p (k t) -> p k t", k=8), ["PTb"], ["catTs"])
        b.barrier()
        off = PW
        stg2, off = carve(off, [128, 8, 512])
        wbf2, off = carve(off, [128, 8, 512], BF16)
        b.dma("sp", ysb, gscr[0:16, :], reads=["gscr"], writes=["ysb"])
        w_out_v2 = w_out.rearrange("(k p) c -> p k c", p=128)
        for hh in range(2):
            b.dma("sp", stg2, w_out_v2[:, :, hh * 512:(hh + 1) * 512], writes=["stg2"])
            CP("dve", wbf2, stg2, ["stg2"], ["wbf2"])
            for k in range(8):
                MM(R2[0:16, hh * 512:(hh + 1) * 512], catTs[:, k, :], wbf2[:, k, :], k == 0, k == 7, ["catTs", "wbf2"], ["R2"])
        TT("dve", ysb, ysb, R2[0:16, :], ALU.mult, ["ysb", "R2"], ["ysb"])
        TT("dve", ysb, ysb, x_[0:16, :], ALU.add, ["ysb", xr], ["ysb"])
        b.dma("sp", y_s[:, :], ysb, reads=["ysb"])

    b.barrier()
    b.emit()
    ncd.__exit__(None, None, None)
    es.close()
    return nc


def _consts(T):
    NT = T // 128
    NO = NT // 2
    cst = {}
    cst["identf"] = np.eye(128, dtype=np.float32)
    cst["iota256"] = np.tile(np.arange(256, dtype=np.float32)[None, :], (128, 1))
    s = np.arange(64)[:, None]
    t = np.arange(64)[None, :]
    lt = (s < t).astype(np.float32)
    le = (s <= t).astype(np.float32)
    cst["maskT"] = np.concatenate([lt, le, lt, le], axis=1)
    cst["maskL"] = (np.arange(64)[None, :] < np.arange(64)[:, None]).astype(np.float32)
    r = np.ones((64, 1024), np.float32)
    r[:, ::64] = 0.0
    cst["resetm"] = r
    sel = np.zeros((17, 128), np.float32)
    sel[16, :] = 1.0
    cst["sel16"] = sel
    cst["ones64"] = np.ones((64, 64), np.float32)
    return cst


def _rope_table(pos):
    half = 8
    inv = np.power(np.float32(ROPE_THETA), -np.arange(half, dtype=np.float32) / np.float32(half)).astype(np.float32)
    ang = pos.astype(np.float32)[:, None] * inv[None, :]
    return np.concatenate([np.cos(ang), np.sin(ang)], axis=1).astype(np.float32)


def _core_inputs(inp, c, T, NS, past_len):
    NT = T // 128
    NO = NT // 2
    bi, par = c // 2, c % 2
    xp = np.asarray(inp["x_prompt"][bi], np.float32)
    own_tiles = [2 * j + par for j in range(NO)]
    own_rows = np.concatenate([np.arange(t * 128, (t + 1) * 128) for t in own_tiles])
    xs = np.zeros((128, D), np.float32)
    xs[:NS] = np.asarray(inp["x_sample"][c * NS:(c + 1) * NS, 0], np.float32)
    m = {}
    m["xall"] = np.ascontiguousarray(np.concatenate([xp, xp[own_rows], xs], axis=0))
    m["call"] = np.ascontiguousarray(np.concatenate([inp["c_sample"][c * NS:(c + 1) * NS], inp["c_prompt"][bi:bi + 1]], axis=0).astype(np.float32))
    pos = np.concatenate([np.arange(T), own_rows, np.full(128, past_len)])
    m["cs_all"] = _rope_table(pos)
    m["parsel"] = np.tile(np.array([[par, 1 - par]], np.float32), (128, 1))
    m["qrel"] = (par * 128 + np.arange(128, dtype=np.float32)).reshape(128, 1)
    m["ownidx"] = np.ascontiguousarray(own_rows.reshape(NO, 128).T.astype(np.int32))
    for k_, v_ in (("w_in", "w_in"), ("w_ada", "w_ada"), ("b_ada", "b_ada"), ("norm_w", "norm_w"), ("w_out", "w_out"),
                   ("qnw", "q_norm_w"), ("knw", "k_norm_w"), ("mu", "mu_shift"), ("w0", "w0"), ("a0", "a0"),
                   ("k_k", "k_k"), ("k_a", "k_a"), ("ln_x_w", "ln_x_w"), ("ln_x_b", "ln_x_b"), ("w_up", "w_up"), ("a_up", "a_up")):
        m[k_] = np.ascontiguousarray(np.asarray(inp[v_], np.float32))
    m["r_k"] = np.ascontiguousarray(np.asarray(inp["r_k"], np.float32).reshape(512))
    m["swkv"] = np.ascontiguousarray(np.asarray(inp["state_wkv"][c * NS:(c + 1) * NS], np.float32).reshape(NS * 8, 4096))
    m["sshift"] = np.ascontiguousarray(np.asarray(inp["state_shift"][c * NS:(c + 1) * NS, 0], np.float32))
    pt = np.asarray(inp["page_table"][c * NS:(c + 1) * NS], np.int32)
    m["ptab"] = np.ascontiguousarray(pt.reshape(NS // 2, 128).T)
    nphys = inp["cache_k"].shape[0]
    m["cache_k"] = np.asarray(inp["cache_k"], np.float32).reshape(nphys * 128, 128)
    m["cache_v"] = np.asarray(inp["cache_v"], np.float32).reshape(nphys * 128, 128)
    m["cache_kidx"] = np.asarray(inp["cache_kidx"], np.float32).reshape(nphys, 8192)
    rep = np.zeros((16, 8, 128), np.float32)
    for sp in range(8):
        for p in range(128):
            rep[2 * sp + p // 64, sp, p] = 1.0
    m["rep"] = rep.reshape(16, 1024)
    m["repT"] = np.ascontiguousarray(rep.transpose(2, 1, 0).reshape(128, 128))
    blk = np.zeros((128, 128), np.float32)
    blk[:64, :64] = 1.0
    blk[64:, 64:] = 1.0
    m["blk"] = blk
    oh = np.zeros((128, 1), np.float32)
    oh[0, 0] = 1.0
    oh[64, 0] = 1.0
    m["oh0"] = oh
    m.update(_consts(T))
    return m


_NC_CACHE = {}


def kernel(**inp):
    T = 4096
    NS = 16
    past_len = 8192
    inp = {k: np.asarray(v) for k, v in inp.items()}
    if "nc" not in _NC_CACHE:
        _NC_CACHE["nc"] = build(T=T, NPHYS=int(inp["cache_k"].shape[0]))
    nc = _NC_CACHE["nc"]
    in_maps = [_core_inputs(inp, c, T, NS, past_len) for c in range(8)]
    res = run_bass_kernel_spmd(nc, in_maps, core_ids=list(range(8)))
    outs = res.results
    B = 4
    NO = T // 256
    y_p = np.zeros((B, T, D), np.float32)
    for c in range(8):
        bi, par = c // 2, c % 2
        yo = np.asarray(outs[c]["y_own"]).reshape(NO, 128, D)
        y_p[bi].reshape(T // 256, 2, 128, D)[:, par] = yo
    k_p = np.stack([np.asarray(outs[2 * bi]["k_nat"]).reshape(T, 2, 64) for bi in range(B)])
    v_p = np.stack([np.asarray(outs[2 * bi]["v_nat"]).reshape(T, 2, 64) for bi in range(B)])
    ki_p = np.stack([np.asarray(outs[2 * bi]["ki_nat"]).reshape(T, 64) for bi in range(B)])
    wkv_pp = np.stack([np.asarray(outs[2 * bi]["wkv_p"]).reshape(8, 64, 64) for bi in range(B)])
    sh_p = np.stack([np.asarray(outs[2 * bi]["shift_p"]).reshape(1, SHW) for bi in range(B)])
    y_s = np.concatenate([np.asarray(outs[c]["y_s"]) for c in range(8)]).reshape(128, 1, D)
    k_s = np.concatenate([np.asarray(outs[c]["k_s"]) for c in range(8)]).reshape(128, 1, 2, 64)
    v_s = np.concatenate([np.asarray(outs[c]["v_s"]) for c in range(8)]).reshape(128, 1, 2, 64)
    ki_s = np.concatenate([np.asarray(outs[c]["ki_s"]) for c in range(8)]).reshape(128, 1, 64)
    wkv_s = np.concatenate([np.asarray(outs[c]["wkv_s"]) for c in range(8)]).reshape(128, 8, 64, 64)
    sh_s = np.concatenate([np.asarray(outs[c]["shift_s"]) for c in range(8)]).reshape(128, 1, SHW)
    f = lambda a: np.ascontiguousarray(a, dtype=np.float32)
    return (f(y_p), f(y_s), f(k_p), f(v_p), f(ki_p), f(wkv_pp), f(sh_p), f(k_s), f(v_s), f(ki_s), f(wkv_s), f(sh_s))
```

```python
import os
import numpy as np
from contextlib import ExitStack
import concourse.bass as bass
import concourse.mybir as mybir
from concourse.bass_utils import run_bass_kernel_spmd

F32 = mybir.dt.float32
BF16 = mybir.dt.bfloat16
I32 = mybir.dt.int32
AF = mybir.ActivationFunctionType
ALU = mybir.AluOpType
AX = mybir.AxisListType

ENGS = ("pe", "act", "dve", "pool", "sp")
NDMA = 32
NSW = 8

D = 1024
HD = 64
DIN = 4040
C_Q, C_K, C_V, C_QI, C_WI, C_KI, C_GA = 0, 512, 640, 768, 1280, 1288, 1352
C_R, C_RK, C_RV, C_WD, C_AD, C_GR = 1864, 2376, 2888, 3400, 3464, 3528
SHW = 1664
NORM_EPS = 1e-6
GN_EPS = 64e-5
ROPE_THETA = 500000.0


USE_POOL = bool(int(os.environ.get('USE_POOL', '0')))
PSUM_RES = {"PTb", "F2", "R2", "K2", "V2", "R2a", "R2b", "K2a", "K2b", "V2a", "V2b"}


class Res:
    __slots__ = ("w", "r")

    def __init__(self):
        self.w = None
        self.r = []


class Bld:
    def __init__(self, nc, es):
        self.nc = nc
        self.es = es
        self.sem = {e: es.enter_context(nc.semaphore("s_" + e)) for e in ENGS}
        self.dsem = [es.enter_context(nc.semaphore("d%d" % i)) for i in range(NDMA)]
        self.dval = [0] * NDMA
        self.dnext = 0
        self.dnext_sw = 0
        self.cnt = {e: 0 for e in ENGS}
        self.waited = {e: {} for e in ENGS}
        self.ops = {e: [] for e in ENGS}
        self.res = {}

    def sb(self, name, shape, dt=F32):
        return self.es.enter_context(self.nc.sbuf_tensor("sb_" + name, list(shape), dt))

    def ps(self, name, shape, dt=F32):
        return self.es.enter_context(self.nc.psum_tensor("ps_" + name, list(shape), dt))

    def _r(self, key):
        r = self.res.get(key)
        if r is None:
            r = self.res[key] = Res()
        return r

    def _need(self, e, tok, waits):
        if tok is None:
            return
        key, val = tok
        if key == "pe" and e == "pe":
            return
        if self.waited[e].get(key, 0) >= val:
            return
        self.waited[e][key] = val
        waits.append((key, val))

    def op(self, e, fn, reads=(), writes=(), dma=False):
        if e == "pool" and not dma and not USE_POOL:
            e = "dve"
        pr = [k for k in reads if k in PSUM_RES]
        if pr:
            reads = [k for k in reads if k not in PSUM_RES]
            writes = list(writes) + pr
        waits = []
        for k in reads:
            self._need(e, self._r(k).w, waits)
        for k in writes:
            r = self._r(k)
            self._need(e, r.w, waits)
            for t in r.r:
                self._need(e, t, waits)
        if dma:
            if e == "pool":
                i = NDMA - NSW + self.dnext_sw
                self.dnext_sw = (self.dnext_sw + 1) % NSW
            else:
                i = self.dnext
                self.dnext = (self.dnext + 1) % (NDMA - NSW)
            if self.dval[i] > 0:
                self._need(e, (("d", i), self.dval[i]), waits)
            self.dval[i] += 16
            tok = (("d", i), self.dval[i])
            inc = (self.dsem[i], 16)
        else:
            self.cnt[e] += 1
            tok = (e, self.cnt[e])
            inc = (self.sem[e], 1)
        self.ops[e].append((waits, fn, inc))
        for k in reads:
            self._r(k).r.append(tok)
        for k in writes:
            r = self._r(k)
            r.w = tok
            r.r = []
        return tok

    def dma(self, e, out, in_, reads=(), writes=()):
        return self.op(e, lambda g: g.dma_start(out=out, in_=in_), reads, writes, dma=True)

    def barrier(self):
        for e in ENGS:
            waits = []
            for e2 in ENGS:
                if e2 != e and self.cnt[e2] > 0:
                    self._need(e, (e2, self.cnt[e2]), waits)
            for i in range(NDMA):
                if self.dval[i] > 0:
                    self._need(e, (("d", i), self.dval[i]), waits)
            self.ops[e].append((waits, None, None))

    def emit(self):
        nc = self.nc
        with nc.Block() as block:
            def mk(e):
                def body(g):
                    for waits, fn, inc in self.ops[e]:
                        for key, val in waits:
                            s = self.dsem[key[1]] if isinstance(key, tuple) else self.sem[key]
                            g.wait_ge(s, val)
                        if fn is not None:
                            fn(g).then_inc(inc[0], inc[1])
                return body
            block.tensor(mk("pe"))
            block.scalar(mk("act"))
            block.vector(mk("dve"))
            block.gpsimd(mk("pool"))
            block.sync(mk("sp"))


def build(T=4096, NS=16, NPG=64, NPHYS=10240, topk_p=256, topk_s=256, n_bis=16, do_sample=True, AW=36000, stop_after=None, nt_lim=None, no_lim=None):
    NT = T // 128
    NO = NT // 2
    NTILES = NT + NO + 1
    nc = bass.Bass("TRN2", target_bir_lowering=False)
    es = ExitStack()
    b = Bld(nc, es)

    def din(name, shape, dt=F32):
        return nc.dram_tensor(name, list(shape), dt, kind="ExternalInput").ap()

    def dout(name, shape, dt=F32):
        return nc.dram_tensor(name, list(shape), dt, kind="ExternalOutput").ap()

    xall = din("xall", [NTILES * 128, D])
    call = din("call", [17, D])
    w_in = din("w_in", [D, DIN])
    w_ada = din("w_ada", [D, 3 * D])
    b_ada = din("b_ada", [3 * D])
    norm_w = din("norm_w", [D])
    w_out = din("w_out", [D, D])
    qnw = din("qnw", [HD])
    knw = din("knw", [HD])
    mu = din("mu", [SHW])
    pw0 = din("w0", [512]); pa0 = din("a0", [512]); pkk = din("k_k", [512]); pka = din("k_a", [512])
    prk = din("r_k", [512]); plnw = din("ln_x_w", [512]); plnb = din("ln_x_b", [512])
    w_up = din("w_up", [64, 512]); a_up = din("a_up", [64, 512])
    identf_d = din("identf", [128, 128])
    cs_all = din("cs_all", [NTILES * 128, 16])
    parsel_d = din("parsel", [128, 2])
    qrel_d = din("qrel", [128, 1])
    ownidx_d = din("ownidx", [128, NO], I32)
    iota_d = din("iota256", [128, 256])
    maskT_d = din("maskT", [64, 256])
    maskL_d = din("maskL", [64, 64])
    reset_d = din("resetm", [64, 1024])
    sel16_d = din("sel16", [17, 128])
    ones64_d = din("ones64", [64, 64])

    swkv_d = din("swkv", [128, 4096]); sshift_d = din("sshift", [16, SHW]); ptab_d = din("ptab", [128, 8], I32)
    if do_sample:
        cache_k = din("cache_k", [NPHYS * 128, 128]); cache_v = din("cache_v", [NPHYS * 128, 128])
        cache_ki = din("cache_kidx", [NPHYS, 8192])
    rep_d = din("rep", [16, 8 * 128]); repT_d = din("repT", [128, 8 * 16]); blk_d = din("blk", [128, 128]); oh0_d = din("oh0", [128, 1])
    y_s = dout("y_s", [16, D]); k_s = dout("k_s", [16, 128]); v_s = dout("v_s", [16, 128]); ki_s = dout("ki_s", [16, 64])
    wkv_s = dout("wkv_s", [128, 4096]); shift_s = dout("shift_s", [16, SHW])
    gscr = nc.dram_tensor("gscr", [17, D], F32, kind="Internal").ap()
    scr1 = nc.dram_tensor("scr1", [16, 3072], F32, kind="Internal").ap()
    scr2 = nc.dram_tensor("scr2", [128, 64], F32, kind="Internal").ap()
    scr3 = nc.dram_tensor("scr3", [16, 512], F32, kind="Internal").ap()
    scr4 = nc.dram_tensor("scr4", [16, 64, 16], F32, kind="Internal").ap()
    y_own = dout("y_own", [NO * 128, D])
    k_nat = dout("k_nat", [T, 128]); v_nat = dout("v_nat", [T, 128]); ki_nat = dout("ki_nat", [T, 64])
    wkv_p = dout("wkv_p", [8, 64, 64]); shift_p = dout("shift_p", [SHW])
    rwscr = nc.dram_tensor("rwscr", [T, 512], F32, kind="Internal").ap()

    PTb = b.ps("PTb", [128, 1024], BF16)
    F2 = b.ps("F2", [128, 512])
    R2 = b.ps("R2", [128, 1024])
    K2 = b.ps("K2", [128, 1024])
    V2 = b.ps("V2", [128, 1024])

    identf = b.sb("identf", [128, 128]); identb = b.sb("identb", [128, 128], BF16)
    cst = b.sb("cst", [128, 4])
    kT_all = b.sb("kT_all", [64, 2, T], BF16)
    kiT_all = b.sb("kiT_all", [64, T], BF16)
    Vaug = b.sb("Vaug", [128, NT, 2, 65], BF16)
    modT = b.sb("modT", [128, 24, 17])
    g1 = b.sb("g1", [128, 8, 17])
    nwT = b.sb("nwT", [128, 8]); badaT = b.sb("badaT", [128, 24])
    lnw_bc = b.sb("lnw_bc", [64, 512]); lnb_bc = b.sb("lnb_bc", [64, 512])
    qnw_bc = b.sb("qnw_bc", [128, 64]); knw_bc = b.sb("knw_bc", [128, 64])
    sel16 = b.sb("sel16", [17, 128]); ones64 = b.sb("ones64", [64, 64])
    maskT = b.sb("maskT", [64, 256]); maskL = b.sb("maskL", [64, 64]); resetm = b.sb("resetm", [64, 1024])
    qrel = b.sb("qrel", [128, 1]); parsel = b.sb("parsel", [128, 2]); ownidx = b.sb("ownidx", [128, NO], I32)
    fp = {}
    for nm in ("w0", "a0", "kk", "ka", "rk"):
        fp[nm] = b.sb("fp_" + nm, [64, 8])
    muT = b.sb("muT", [64, 26]); wupS = b.sb("wupS", [64, 512]); aupS = b.sb("aupS", [64, 512])
    xt0 = b.sb("xt0", [128, D]); xt = [xt0, xt0]
    xn = b.sb("xn", [128, D], BF16)
    hT0 = b.sb("hT0", [128, 8, 128], BF16); hT = [hT0, hT0]
    hTs = b.sb("hTs", [128, 8, 128], BF16)
    hlast = b.sb("hlast", [128, 8, 1], BF16)
    cs_t = b.sb("cs_t", [128, 16])
    sm = b.sb("sm", [128, 64])
    ARENA = b.sb("ARENA", [128, AW])
    csT = b.sb("csT", [128, 8, 17])

    def TT(e, out, in0, in1, op, R, W):
        b.op(e, lambda g: g.tensor_tensor(out=out, in0=in0, in1=in1, op=op), R, W)

    def TS(e, out, in0, s1, s2, op0, op1, R, W, accum=None):
        if op1 is None:
            b.op(e, lambda g: g.tensor_scalar(out=out, in0=in0, scalar1=s1, scalar2=None, op0=op0), R, W)
        elif accum is None:
            b.op(e, lambda g: g.tensor_scalar(out=out, in0=in0, scalar1=s1, scalar2=s2, op0=op0, op1=op1), R, W)
        else:
            b.op(e, lambda g: g.tensor_scalar(out=out, in0=in0, scalar1=s1, scalar2=s2, op0=op0, op1=op1,
                                              accum_out=accum), R, W)

    def STT(out, in0, scalar, in1, op0, op1, R, W):
        b.op("dve", lambda g: g.scalar_tensor_tensor(out=out, in0=in0, scalar=scalar, in1=in1, op0=op0, op1=op1), R, W)

    def ACT(out, in_, func, R, W, scale=1.0, bias=None, accum=None):
        kw = {}
        if bias is not None:
            kw["bias"] = bias
        if accum is not None:
            kw["accum_out"] = accum
        b.op("act", lambda g: g.activation(out=out, in_=in_, func=func, scale=scale, **kw), R, W)

    def MM(out, lhsT, rhs, start, stop, R, W):
        b.op("pe", lambda g: g.matmul(out=out, lhsT=lhsT, rhs=rhs, start=start, stop=stop), R, W)

    def TR(out, in_, ident, R, W):
        b.op("pe", lambda g: g.transpose(out=out, in_=in_, identity=ident), R, W)

    def CP(e, out, in_, R, W):
        if e == "act":
            b.op(e, lambda g: g.copy(out=out, in_=in_), R, W)
        else:
            b.op(e, lambda g: g.tensor_copy(out=out, in_=in_), R, W)

    def RED(out, in_, op, R, W, axis=AX.X):
        b.op("dve", lambda g: g.tensor_reduce(out=out, in_=in_, axis=axis, op=op), R, W)

    def MS(e, ap, val, W):
        b.op(e, lambda g: g.memset(ap, val), (), W)

    def bc(ap, shape):
        return ap.to_broadcast(list(shape))

    ncd = nc.allow_non_contiguous_dma(reason="small parameter layouts")
    ncd.__enter__()

    b.dma("sp", identf[:], identf_d[:, :], writes=["identf"])
    CP("dve", identb[:], identf[:], ["identf"], ["identb"])
    MS("dve", cst[:, 0:1], NORM_EPS, ["cst"]); MS("dve", cst[:, 1:2], GN_EPS, ["cst"]); MS("dve", cst[:, 2:3], 1e-24, ["cst"]); MS("dve", cst[:, 3:4], -30000.0, ["cst"])
    for (t_, d_, nm) in ((sel16, sel16_d, "sel16"), (ones64, ones64_d, "ones64"), (maskT, maskT_d, "maskT"),
                         (maskL, maskL_d, "maskL"), (resetm, reset_d, "resetm"),
                         (qrel, qrel_d, "qrel"), (parsel, parsel_d, "parsel"), (ownidx, ownidx_d, "ownidx"), (wupS, w_up, "wupS"), (aupS, a_up, "aupS")):
        b.dma("sp", t_[:], d_[:, :], writes=[nm])
    for nm, src in (("w0", pw0), ("a0", pa0), ("kk", pkk), ("ka", pka), ("rk", prk)):
        b.dma("sp", fp[nm][:], src.rearrange("(h j) -> j h", j=64), writes=["fp_" + nm])
    b.dma("sp", muT[:], mu.rearrange("(c j) -> j c", j=64), writes=["muT"])
    b.dma("sp", nwT[:], norm_w.rearrange("(k p) -> p k", p=128), writes=["nwT"])
    b.dma("sp", badaT[:], b_ada.rearrange("(t p) -> p t", p=128), writes=["badaT"])
    b.dma("sp", lnw_bc[:], plnw.partition_broadcast(64), writes=["lnw_bc"])
    b.dma("sp", lnb_bc[:], plnb.partition_broadcast(64), writes=["lnb_bc"])
    b.dma("sp", qnw_bc[:], qnw.partition_broadcast(128), writes=["qnw_bc"])
    b.dma("sp", knw_bc[:], knw.partition_broadcast(128), writes=["knw_bc"])
    MS("pool", Vaug[:, :, :, 64:65], 1.0, ["Vaug"])

    def carve(off, shape, dt=F32):
        n = int(np.prod(shape[1:]))
        words = n if dt in (F32, I32) else (n + 1) // 2
        v = ARENA[0:shape[0], off:off + words]
        if dt != F32:
            v = v.bitcast(dt)
        if len(shape) == 3:
            v = v.rearrange("p (a b) -> p a b", a=shape[1])
        elif len(shape) == 4:
            v = v.rearrange("p (a b c) -> p a b c", a=shape[1], b=shape[2])
        return v, off + words

    off = 0
    Wn, off = carve(off, [128, 8, 832], BF16)
    Wm, off = carve(off, [128, 8, SHW], BF16)
    Wom, off = carve(off, [128, 8, SHW], BF16)
    W_end = off
    stg, off = carve(off, [128, 8, 512])
    mu_bc, off = carve(off, [128, SHW])
    omu_bc, off = carve(off, [128, SHW])
    gtok, off = carve(off, [17, D])
    bgate, off = carve(off, [17, D])
    csall_sil, off = carve(off, [17, D])

    b.dma("sp", mu_bc, mu.partition_broadcast(128), writes=["mu_bc"])
    b.dma("sp", bgate, b_ada[2 * D:3 * D].partition_broadcast(17), writes=["bgate"])
    TS("dve", omu_bc, mu_bc, -1.0, 1.0, ALU.mult, ALU.add, ["mu_bc"], ["omu_bc"])
    w_in_v = w_in.rearrange("(k p) c -> p k c", p=128)

    def load_cols(dst, dcol, c0, n, scale_bc=None, scale_off=0, tag=""):
        done = 0
        while done < n:
            w = min(512, n - done)
            b.dma("sp", stg[:, :, 0:w], w_in_v[:, :, c0 + done:c0 + done + w], writes=["stg"])
            if scale_bc is None:
                CP("pool", dst[:, :, dcol + done:dcol + done + w], stg[:, :, 0:w], ["stg"], [tag])
            else:
                for sname, sbcv, d2 in scale_bc:
                    TT("dve", d2[:, :, dcol + done:dcol + done + w], stg[:, :, 0:w],
                       bc(sbcv[:, scale_off + done:scale_off + done + w].unsqueeze(1), [128, 8, w]),
                       ALU.mult, ["stg", sname], [tag])
            done += w

    load_cols(Wn, 0, C_K, 256, tag="Wn")
    load_cols(Wn, 256, C_KI, 64, tag="Wn")
    load_cols(Wn, 320, C_GR, 512, tag="Wn")
    load_cols(None, 0, C_R, SHW, scale_bc=[("mu_bc", mu_bc, Wm), ("omu_bc", omu_bc, Wom)], tag="Wm")
    calt = sm
    b.dma("sp", csall_sil, call[:, :], writes=["csil"])
    ACT(csall_sil, csall_sil, AF.Silu, ["csil"], ["csil"])
    for k in range(8):
        TR(F2[:, k * 17:(k + 1) * 17], csall_sil[:, k * 128:(k + 1) * 128], identf[0:17, 0:17], ["csil", "identf"], ["F2"])
    CP("dve", csT[:], F2[:, 0:136].rearrange("p (k m) -> p k m", k=8), ["F2"], ["csT"])
    w_ada_v = w_ada.rearrange("(k p) c -> p k c", p=128)
    for ch in range(6):
        b.dma("sp", stg[:, :, :], w_ada_v[:, :, ch * 512:(ch + 1) * 512], writes=["stg"])
        for ct in range(4):
            for k in range(8):
                MM(R2[:, ct * 17:(ct + 1) * 17], stg[:, k, ct * 128:(ct + 1) * 128], csT[:, k, :], k == 0, k == 7,
                   ["stg", "csT"], ["R2"])
        TT("dve", modT[:, ch * 4:(ch + 1) * 4, :], R2[:, 0:68].rearrange("p (c m) -> p c m", c=4),
           bc(badaT[:, ch * 4:(ch + 1) * 4].unsqueeze(2), [128, 4, 17]), ALU.add, ["R2", "badaT"], ["modT"])
        if ch >= 4:
            for k in range(8):
                MM(K2[0:17, 0:512], csT[:, k, :], stg[:, k, :], k == 0, k == 7, ["stg", "csT"], ["K2"])
            TT("dve", gtok[:, (ch - 4) * 512:(ch - 3) * 512], K2[0:17, 0:512], bgate[:, (ch - 4) * 512:(ch - 3) * 512],
               ALU.add, ["K2", "bgate"], ["gtok"])
    b.dma("sp", gscr[:, :], gtok[0:17, :], reads=["gtok"], writes=["gscr"])
    STT(g1[:], modT[:, 8:16, :], 1.0, bc(nwT[:].unsqueeze(2), [128, 8, 17]), ALU.add, ALU.mult, ["modT", "nwT"], ["g1"])
    def front(ti, par, m_prompt=True, ntok=128):
        x_ = xt[par]
        h_ = hT[par]
        xr, hr = "xt0", "hT0"
        b.dma("sp", x_[:], xall[ti * 128:(ti + 1) * 128, :], writes=[xr])
        b.dma("sp", cs_t[:], cs_all[ti * 128:(ti + 1) * 128, :], writes=["cs_t"])
        ACT(xn[:], x_[:], AF.Square, [xr], ["xn", "sm"], accum=sm[:, 0:1])
        ACT(sm[:, 1:2], sm[:, 0:1], AF.Sqrt, ["sm", "cst"], ["sm"], scale=1.0 / D, bias=cst[:, 0:1])
        b.op("dve", lambda g: g.reciprocal(out=sm[:, 2:3], in_=sm[:, 1:2]), ["sm"], ["sm"])
        TS("dve", xn[:], x_[:], sm[:, 2:3], None, ALU.mult, None, [xr, "sm"], ["xn"])
        for k in range(8):
            TR(PTb[:, k * 128:(k + 1) * 128], xn[:, k * 128:(k + 1) * 128], identb[:], ["xn", "identb"], ["PTb"])
        pv = PTb[:, :].rearrange("p (k t) -> p k t", k=8)
        if m_prompt:
            TT("dve", h_[:], pv, bc(g1[:, :, 16:17], [128, 8, 128]), ALU.mult, ["PTb", "g1"], [hr])
            TT("pool", h_[:], h_[:], bc(modT[:, 0:8, 16:17], [128, 8, 128]), ALU.add, [hr, "modT"], [hr])
        else:
            TT("dve", h_[:, :, 0:ntok], pv[:, :, 0:ntok], g1[:, :, 0:ntok], ALU.mult, ["PTb", "g1"], [hr])
            TT("pool", h_[:, :, 0:ntok], h_[:, :, 0:ntok], modT[:, 0:8, 0:ntok], ALU.add, [hr, "modT"], [hr])
        return x_, h_, xr, hr

    def rope(e, buf, nh, hd_stride_view, R, nrows=128):
        x1 = buf[:, :, 0:8]
        x2 = buf[:, :, 8:16]
        cosb = bc(cs_t[0:nrows, 0:8].unsqueeze(1), [nrows, nh, 8])
        sinb = bc(cs_t[0:nrows, 8:16].unsqueeze(1), [nrows, nh, 8])
        t = ropet[0:nrows, 0:4 * nh * 8].rearrange("p (a h d) -> p a h d", a=4, h=nh)
        TT(e, t[:, 0], x1, cosb, ALU.mult, R + ["cs_t"], ["ropet"])
        TT(e, t[:, 1], x2, sinb, ALU.mult, R + ["cs_t"], ["ropet"])
        TT(e, t[:, 2], x2, cosb, ALU.mult, R + ["cs_t"], ["ropet"])
        TT(e, t[:, 3], x1, sinb, ALU.mult, R + ["cs_t"], ["ropet"])
        TT(e, x1, t[:, 0], t[:, 1], ALU.subtract, ["ropet"], R)
        TT(e, x2, t[:, 2], t[:, 3], ALU.add, ["ropet"], R)

    ropet = b.sb("ropet", [128, 256])

    def qknorm(src_ps, dst, nh, wbc, extra_scale, Rsrc, Wdst, nrows=128):
        sq = nrm_t[0:nrows, 0:nh * 64].rearrange("p (h d) -> p h d", h=nh)
        ACT(sq, src_ps, AF.Square, Rsrc, ["nrm_t"])
        RED(sm[0:nrows, 8:8 + nh], sq, ALU.add, ["nrm_t"], ["sm"])
        ACT(sm[0:nrows, 16:16 + nh], sm[0:nrows, 8:8 + nh], AF.Sqrt, ["sm", "cst"], ["sm"], scale=1.0 / 64, bias=cst[0:nrows, 0:1])
        b.op("dve", lambda g: g.reciprocal(out=sm[0:nrows, 24:24 + nh], in_=sm[0:nrows, 16:16 + nh]), ["sm"], ["sm"])
        TT("dve", dst, src_ps, bc(sm[0:nrows, 24:24 + nh].unsqueeze(2), [nrows, nh, 64]), ALU.mult, Rsrc + ["sm"], Wdst)
        STT(dst, dst, float(extra_scale), bc(wbc[0:nrows, :].unsqueeze(1), [nrows, nh, 64]), ALU.mult, ALU.mult, Wdst + ["qnw_bc", "knw_bc"], Wdst)

    nrm_t = b.sb("nrm_t", [128, 512])
    kfin = b.sb("kfin", [128, 128]); vfin = b.sb("vfin", [128, 128]); kifin = b.sb("kifin", [128, 64])
    gr_s = b.sb("gr_s", [128, 512])

    off = W_end
    rw = {}
    for nm in ("tw", "adc"):
        rw[nm], off = carve(off, [64, 128])
    for nm in ("sg", "L", "g", "t1"):
        rw[nm], off = carve(off, [64, 8, 128])
    blkA, off = carve(off, [64, 2048])
    blkB, off = carve(off, [64, 3072])
    rw["gprev"] = blkA[:, 0:1024].rearrange("p (h t) -> p h t", h=8)
    rw["ginv"] = blkA[:, 1024:2048].rearrange("p (h t) -> p h t", h=8)
    rw["asig"] = blkB[:, 0:1024].rearrange("p (h t) -> p h t", h=8)
    rw["kkn"] = blkB[:, 1024:2048].rearrange("p (h t) -> p h t", h=8)
    rw["kmod"] = blkB[:, 2048:3072].rearrange("p (h t) -> p h t", h=8)
    AMx = blkA.bitcast(BF16).rearrange("p (h x) -> p h x", h=16)
    LNPx = blkB.bitcast(BF16).rearrange("p (a h x) -> p a h x", a=6, h=16)
    rw["sg"] = rw["sg"]
    QTt, off = carve(off, [64, 8, 2, 128], BF16)
    KTt, off = carve(off, [64, 8, 2, 128], BF16)
    def alias(view64, shape):
        return view64.rearrange("p h t -> p (h t)").bitcast(BF16)
    AM = AMx
    Lm = [LNPx[:, 0], LNPx[:, 1]]
    Nm = [LNPx[:, 2], LNPx[:, 3]]
    Pm = [LNPx[:, 4], LNPx[:, 5]]
    BKtok, off = carve(off, [64, 8, 2, 64], BF16)
    Vc, off = carve(off, [64, 2, 8, 64], BF16)
    P0s, off = carve(off, [64, 8, 64], BF16)
    Us, off = carve(off, [64, 8, 64], BF16)
    H32, off = carve(off, [64, 8, 64])
    Hb, off = carve(off, [64, 8, 64], BF16)
    ych, off = carve(off, [64, 8, 64])
    yt1, off = carve(off, [64, 8, 64])
    bon, off = carve(off, [128, 8])
    rawl, off = carve(off, [64, 26])
    xmt, off = carve(off, [128, SHW])
    st8, off = carve(off, [64, 64])
    assert off <= AW, off
    identb64 = identb[0:64, 0:64]

    KR = int(os.environ.get('KR', '9'))
    KQ = int(os.environ.get('KQ', '9'))

    def rwkv_tile(ti, h_, hr):
        CP("pool", hTs[:, :, 1:128], h_[:, :, 0:127], [hr], ["hTs"])
        CP("pool", hTs[:, :, 0:1], hlast[:], ["hlast"], ["hTs"])
        CP("pool", hlast[:], h_[:, :, 127:128], [hr], ["hlast"])
        if KQ < 1:
            return
        for gi, (c0, n) in enumerate(((0, 512), (512, 512), (1024, 512), (1536, 128))):
            dst = (R2[:, 0:512], R2[:, 512:1024], K2[:, 0:512], K2[:, 512:640])[gi]
            nm = ("R2", "R2", "K2", "K2")[gi]
            for k in range(8):
                MM(dst, h_[:, k, :], Wom[:, k, c0:c0 + n], k == 0, False, ["Wm", hr], [nm])
            for k in range(8):
                MM(dst, hTs[:, k, :], Wm[:, k, c0:c0 + n], False, k == 7, ["Wm", "hTs"], [nm])
        CP("act", xmt[:, 0:1024], R2[:, :], ["R2"], ["xmt"])
        CP("dve", xmt[:, 1024:1664], K2[:, 0:640], ["K2"], ["xmt"])
        TR(F2[0:64, 0:128], xmt[:, 1536:1600], identf[:], ["xmt", "identf"], ["F2"])
        TR(F2[0:64, 128:256], xmt[:, 1600:1664], identf[:], ["xmt", "identf"], ["F2"])
        KW = int(os.environ.get('KW', '3'))
        if KW & 1:
            ACT(rw["tw"], F2[0:64, 0:128], AF.Tanh, ["F2"], ["tw"])
        if KW & 2:
            CP("dve", rw["adc"], F2[0:64, 128:256], ["F2"], ["adc"])
        if KR < 1:
            return
        R2v = R2[0:64, :].rearrange("p (h t) -> p h t", h=8)
        K2v = K2[0:64, :].rearrange("p (h t) -> p h t", h=8)
        V2v = V2[0:64, :].rearrange("p (h t) -> p h t", h=8)
        for h in range(8):
            MM(R2v[:, h, :], wupS[:, h * 64:(h + 1) * 64], rw["tw"], True, True, ["wupS", "tw"], ["R2"])
            MM(K2v[:, h, :], aupS[:, h * 64:(h + 1) * 64], rw["adc"], True, True, ["aupS", "adc"], ["K2"])
        TT("dve", rw["sg"], R2v, bc(fp["w0"][:].unsqueeze(2), [64, 8, 128]), ALU.add, ["R2", "fp_w0"], ["sg"])
        ACT(rw["sg"], rw["sg"], AF.Sigmoid, ["sg"], ["sg"])
        TT("dve", rw["asig"], K2v, bc(fp["a0"][:].unsqueeze(2), [64, 8, 128]), ALU.add, ["K2", "fp_a0"], ["asig"])
        ACT(rw["asig"], rw["asig"], AF.Sigmoid, ["asig"], ["asig"])
        TS("dve", rw["sg"], rw["sg"], -0.6065306597126334, None, ALU.mult, None, ["sg"], ["sg"])
        b.op("dve", lambda g: g.tensor_tensor_scan(out=rw["L"].rearrange("p h t -> p (h t)"), data0=resetm[:, :],
                                                   data1=rw["sg"].rearrange("p h t -> p (h t)"), initial=0.0,
                                                   op0=ALU.mult, op1=ALU.add), ["sg", "resetm"], ["L"])
        ACT(rw["g"], rw["L"], AF.Exp, ["L"], ["g"])
        ACT(rw["ginv"], rw["L"], AF.Exp, ["L"], ["ginv"], scale=-1.0)
        TT("pool", rw["gprev"], rw["L"], rw["sg"], ALU.subtract, ["L", "sg"], ["gprev"])
        ACT(rw["gprev"], rw["gprev"], AF.Exp, ["gprev"], ["gprev"])
        if KR < 2:
            return
        for h in range(8):
            TR(R2v[:, h, :], xmt[:, h * 64:(h + 1) * 64], identf[:], ["xmt", "identf"], ["R2"])
            TR(K2v[:, h, :], xmt[:, 512 + h * 64:512 + (h + 1) * 64], identf[:], ["xmt", "identf"], ["K2"])
        TT("dve", rw["L"], K2v, bc(fp["kk"][:].unsqueeze(2), [64, 8, 128]), ALU.mult, ["K2", "fp_kk"], ["L"])
        ACT(rw["t1"], rw["L"], AF.Square, ["L"], ["t1"])
        t1f = rw["t1"].rearrange("p h t -> p (h t)")
        for hh in range(2):
            MM(V2[0:64, hh * 512:(hh + 1) * 512], ones64[:, :], t1f[:, hh * 512:(hh + 1) * 512], True, True, ["ones64", "t1"], ["V2"])
        ACT(rw["t1"], V2v, AF.Sqrt, ["V2", "cst"], ["t1"], bias=cst[0:64, 2:3])
        b.op("dve", lambda g: g.reciprocal(out=rw["t1"], in_=rw["t1"]), ["t1"], ["t1"])
        TT("dve", rw["kkn"], rw["L"], rw["t1"], ALU.mult, ["L", "t1"], ["kkn"])
        STT(rw["t1"], rw["asig"], -1.0, bc(fp["ka"][:].unsqueeze(2), [64, 8, 128]), ALU.add, ALU.mult, ["asig", "fp_ka"], ["t1"])
        STT(rw["kmod"], rw["t1"], 1.0, K2v, ALU.add, ALU.mult, ["t1", "K2"], ["kmod"])
        if KR < 3:
            return
        QTv = QTt.rearrange("p h c (q t) -> p h c q t", q=2)
        KTv = KTt.rearrange("p h c (q t) -> p h c q t", q=2)

        def ch(v):
            return v.rearrange("p h (c t) -> p h c t", c=2)
        STT(QTv[:, :, :, 0, :], ch(rw["kkn"]), -1.0, ch(rw["gprev"]), ALU.mult, ALU.mult, ["kkn", "gprev"], ["QTt"])
        TT("dve", QTv[:, :, :, 1, :], ch(R2v), ch(rw["g"]), ALU.mult, ["R2", "g"], ["QTt"])
        TT("pool", rw["t1"], rw["kkn"], rw["asig"], ALU.mult, ["kkn", "asig"], ["t1"])
        TT("pool", KTv[:, :, :, 0, :], ch(rw["t1"]), ch(rw["ginv"]), ALU.mult, ["t1", "ginv"], ["KTt"])
        TT("pool", KTv[:, :, :, 1, :], ch(rw["kmod"]), ch(rw["ginv"]), ALU.mult, ["kmod", "ginv"], ["KTt"])
        TT("dve", rw["L"], R2v, bc(fp["rk"][:].unsqueeze(2), [64, 8, 128]), ALU.mult, ["R2", "fp_rk"], ["L"])
        TT("dve", rw["L"], rw["L"], rw["kmod"], ALU.mult, ["L", "kmod"], ["L"])
        for h in range(8):
            MM(F2[:, 256 + h:257 + h], rw["L"][:, h, :], ones64[:, 0:1], True, True, ["L", "ones64"], ["F2"])
        CP("dve", bon, F2[:, 256:264], ["F2"], ["bon"])
        if KR < 4:
            return
        CP("act", Vc[:, 0], xmt[0:64, 1024:1536].rearrange("p (h i) -> p h i", h=8), ["xmt"], ["Vc"])
        MM(V2[0:64, 0:512], identf[:, 64:128], xmt[:, 1024:1536], True, True, ["identf", "xmt"], ["V2"])
        CP("act", Vc[:, 1], V2[0:64, 0:512].rearrange("p (h i) -> p h i", h=8), ["V2"], ["Vc"])
        if int(os.environ.get("KLVL", "9")) < 3:
            return
        for q in range(4):
            c, hg = q // 2, q % 2
            bk, bkn = (K2, "K2") if q % 2 == 0 else (V2, "V2")
            AMp = bk[0:64, :].rearrange("p (h x) -> p h x", h=4)
            for hd in range(4):
                h = hg * 4 + hd
                MM(AMp[:, hd, 0:128], KTt[:, h, c, 0:64], QTt[:, h, c, :], True, True, ["KTt", "QTt"], [bkn])
                MM(AMp[:, hd, 128:256], KTt[:, h, c, 64:128], QTt[:, h, c, :], True, True, ["KTt", "QTt"], [bkn])
            TT("dve", AM[:, q * 4:(q + 1) * 4, :], AMp, bc(maskT[:].unsqueeze(1), [64, 4, 256]), ALU.mult, [bkn, "maskT"], ["AM"])
        Lp = R2[0:64, :].rearrange("p (h x) -> p h x", h=16)
        for q in range(4):
            c, hg = q // 2, q % 2
            for hd in range(4):
                h = hg * 4 + hd
                MM(Lp[:, q * 4 + hd, :], QTt[:, h, c, 0:64], KTt[:, h, c, 0:64], True, True, ["KTt", "QTt"], ["R2"])
        TT("dve", Lm[0], Lp, bc(maskL[:].unsqueeze(1), [64, 16, 64]), ALU.mult, ["R2", "maskL"], ["Lm0"])
        CP("act", Nm[0], AM[:, :, 0:64], ["AM"], ["Nm0"])
        TT("dve", Pm[0], AM[:, :, 0:64], bc(identb64.unsqueeze(1), [64, 16, 64]), ALU.add, ["AM", "identb"], ["Pm0"])
        cur = 0
        Np = K2[0:64, :].rearrange("p (h x) -> p h x", h=16)
        Lpp = V2[0:64, :].rearrange("p (h x) -> p h x", h=16)
        PPp = R2[0:64, :].rearrange("p (h x) -> p h x", h=16)
        for lvl in range(1, 6):
            nx = 1 - cur
            for i in range(16):
                if lvl < 5:
                    MM(Np[:, i, :], Lm[cur][:, i, :], Nm[cur][:, i, :], True, True, ["Lm%d" % cur, "Nm%d" % cur], ["K2"])
                MM(Lpp[:, i, :], Nm[cur][:, i, :], Lm[cur][:, i, :], True, True, ["Lm%d" % cur, "Nm%d" % cur], ["V2"])
            if lvl < 5:
                CP("act", Nm[nx], Np, ["K2"], ["Nm%d" % nx])
            CP("dve", Lm[nx], Lpp, ["V2"], ["Lm%d" % nx])
            for i in range(16):
                MM(PPp[:, i, :], Lm[nx][:, i, :], Pm[cur][:, i, :], True, True, ["Lm%d" % nx, "Pm%d" % cur], ["R2"])
            TT("dve", Pm[nx], PPp, Pm[cur], ALU.add, ["R2", "Pm%d" % cur], ["Pm%d" % nx])
            cur = nx
        P6 = Pm[cur]
        P6n = "Pm%d" % cur
        for c in range(2):
            BKp = PTb[0:64, :].rearrange("p (h q j) -> p h q j", h=8, q=2)
            for h in range(8):
                TR(BKp[:, h, 0, :], KTt[:, h, c, 0:64], identb64, ["KTt", "identb"], ["PTb"])
                TR(BKp[:, h, 1, :], KTt[:, h, c, 64:128], identb64, ["KTt", "identb"], ["PTb"])
            CP("act", BKtok, BKp, ["PTb"], ["BKtok"])
            P0p = F2[0:64, :].rearrange("p (h i) -> p h i", h=8)
            Up = R2[0:64, 0:512].rearrange("p (h i) -> p h i", h=8)
            Yp = K2[0:64, 0:512].rearrange("p (h i) -> p h i", h=8)
            Hp = V2[0:64, 0:512].rearrange("p (h i) -> p h i", h=8)

            def ai(h):
                return (c * 2 + h // 4) * 4 + h % 4
            for h in range(8):
                MM(P0p[:, h, :], QTt[:, h, c, 0:64], Hb[:, h, :], True, False, ["QTt", "Hb"], ["F2"])
                MM(P0p[:, h, :], AM[:, ai(h), 128:192], Vc[:, c, h, :], False, True, ["AM", "Vc"], ["F2"])
            CP("act", P0s, P0p, ["F2"], ["P0s"])
            for h in range(8):
                MM(Up[:, h, :], P6[:, ai(h), :], P0s[:, h, :], True, True, [P6n, "P0s"], ["R2"])
            CP("act", Us, Up, ["R2"], ["Us"])
            for h in range(8):
                MM(Yp[:, h, :], QTt[:, h, c, 64:128], Hb[:, h, :], True, False, ["QTt", "Hb"], ["K2"])
                MM(Yp[:, h, :], AM[:, ai(h), 64:128], Us[:, h, :], False, False, ["AM", "Us"], ["K2"])
                MM(Yp[:, h, :], AM[:, ai(h), 192:256], Vc[:, c, h, :], False, True, ["AM", "Vc"], ["K2"])
            for h in range(8):
                MM(Hp[:, h, :], BKtok[:, h, 0, :], Us[:, h, :], True, False, ["BKtok", "Us"], ["V2"])
                MM(Hp[:, h, :], BKtok[:, h, 1, :], Vc[:, c, h, :], False, True, ["BKtok", "Vc"], ["V2"])
            CP("act", ych, Yp, ["K2"], ["ych"])
            TT("dve", H32, H32, Hp, ALU.add, ["H32", "V2"], ["H32"])
            TT("dve", H32, H32, bc(rw["g"][:, :, c * 64 + 63:c * 64 + 64], [64, 8, 64]), ALU.mult, ["H32", "g"], ["H32"])
            CP("act", Hb, H32, ["H32"], ["Hb"])
            RED(st8[:, 0:8], ych, ALU.add, ["ych"], ["st8"])
            TT("dve", yt1, ych, ych, ALU.mult, ["ych"], ["yt1"])
            RED(st8[:, 8:16], yt1, ALU.add, ["yt1"], ["st8"])
            TS("dve", st8[:, 0:16], st8[:, 0:16], 1.0 / 64, None, ALU.mult, None, ["st8"], ["st8"])
            TT("dve", st8[:, 16:24], st8[:, 0:8], st8[:, 0:8], ALU.mult, ["st8"], ["st8"])
            TT("dve", st8[:, 24:32], st8[:, 8:16], st8[:, 16:24], ALU.subtract, ["st8"], ["st8"])
            ACT(st8[:, 32:40], st8[:, 24:32], AF.Sqrt, ["st8", "cst"], ["st8"], bias=cst[0:64, 1:2])
            b.op("dve", lambda g: g.reciprocal(out=st8[:, 40:48], in_=st8[:, 32:40]), ["st8"], ["st8"])
            TT("dve", yt1, ych, bc(st8[:, 0:8].unsqueeze(2), [64, 8, 64]), ALU.subtract, ["ych", "st8"], ["yt1"])
            TT("dve", yt1, yt1, bc(st8[:, 40:48].unsqueeze(2), [64, 8, 64]), ALU.mult, ["yt1", "st8"], ["yt1"])
            lnwv = lnw_bc[:].rearrange("p (h i) -> p h i", h=8)
            lnbv = lnb_bc[:].rearrange("p (h i) -> p h i", h=8)
            TT("dve", yt1, yt1, lnwv, ALU.mult, ["yt1", "lnw_bc"], ["yt1"])
            TT("pool", yt1, yt1, lnbv, ALU.add, ["yt1", "lnb_bc"], ["yt1"])
            MM(F2[0:64, 264:272], identf[:, c * 64:(c + 1) * 64], bon, True, True, ["identf", "bon"], ["F2"])
            CP("act", st8[:, 48:56], F2[0:64, 264:272], ["F2"], ["st8"])
            TT("dve", ych, Vc[:, c], bc(st8[:, 48:56].unsqueeze(2), [64, 8, 64]), ALU.mult, ["Vc", "st8"], ["ych"])
            TT("pool", yt1, yt1, ych, ALU.add, ["yt1", "ych"], ["yt1"])
            MM(R2[0:64, 0:512], identf[:, c * 64:(c + 1) * 64], gr_s[:, :], True, True, ["identf", "gr_s"], ["R2"])
            TT("dve", yt1.rearrange("p h i -> p (h i)"), yt1.rearrange("p h i -> p (h i)"), R2[0:64, 0:512], ALU.mult, ["yt1", "R2"], ["yt1"])
            b.dma("sp", rwscr[ti * 128 + c * 64: ti * 128 + (c + 1) * 64, :], yt1.rearrange("p h i -> p (h i)"), reads=["yt1"], writes=["rwscr"])
        b.barrier()

    if stop_after == "A":
        b.barrier(); b.emit(); ncd.__exit__(None, None, None); es.close()
        return nc
    b.barrier()
    MS("dve", H32, 0.0, ["H32"]); MS("dve", Hb, 0.0, ["Hb"]); MS("pool", hlast[:], 0.0, ["hlast"])
    KSUB = int(os.environ.get('KSUB', '9'))
    V2a = V2[:, 0:512]
    V2b = V2[:, 512:1024]
    for ti in range(NT if nt_lim is None else nt_lim):
        par = ti % 2
        x_, h_, xr, hr = front(ti, par)
        if KSUB >= 1:
            for (c0, n, dst) in ((0, 256, V2a[:, 0:256]), (256, 64, V2a[:, 256:320]), (320, 512, V2b)):
                for k in range(8):
                    MM(dst, h_[:, k, :], Wn[:, k, c0:c0 + n], k == 0, k == 7, [hr, "Wn"], ["V2"])
        if KSUB >= 2:
            kv3 = kfin[:].rearrange("p (g d) -> p g d", g=2)
            qknorm(V2a[:, 0:128].rearrange("p (g d) -> p g d", g=2), kv3, 2, knw_bc, 1.0, ["V2"], ["kfin"])
            rope("dve", kv3, 2, None, ["kfin"])
            CP("act", vfin[:], V2a[:, 128:256], ["V2"], ["vfin"])
            CP("act", Vaug[:, ti, :, 0:64], V2a[:, 128:256].rearrange("p (g d) -> p g d", g=2), ["V2"], ["Vaug"])
            CP("act", kifin[:], V2a[:, 256:320], ["V2"], ["kifin"])
            rope("pool", kifin[:].unsqueeze(1), 1, None, ["kifin"])
            ACT(gr_s[:], V2b, AF.Silu, ["V2"], ["gr_s"])
        if KSUB >= 3:
            b.dma("sp", k_nat[ti * 128:(ti + 1) * 128, :], kfin[:], reads=["kfin"])
            b.dma("sp", v_nat[ti * 128:(ti + 1) * 128, :], vfin[:], reads=["vfin"])
            b.dma("sp", ki_nat[ti * 128:(ti + 1) * 128, :], kifin[:], reads=["kifin"])
        if KSUB >= 4:
            for g_ in range(2):
                TR(F2[0:64, g_ * 128:(g_ + 1) * 128], kfin[:, g_ * 64:(g_ + 1) * 64], identf[:], ["kfin", "identf"], ["F2"])
            TR(F2[0:64, 256:384], kifin[:, :], identf[:], ["kifin", "identf"], ["F2"])
            if KSUB >= 5:
                CP("act", kT_all[:, :, ti * 128:(ti + 1) * 128], F2[0:64, 0:256].rearrange("p (g t) -> p g t", g=2), ["F2"], ["kT_all"])
            if KSUB >= 6:
                if os.environ.get("KV") == "1":
                    CP("act", nrm_t[0:64, 0:128], F2[0:64, 256:384], ["F2"], ["nrm_t"])
                elif os.environ.get("KV") == "2":
                    CP("act", kiT_all[:, ti * 128:(ti + 1) * 128], F2[0:64, 0:128], ["F2"], ["kiT_all"])
                else:
                    CP("act", kiT_all[:, ti * 128:(ti + 1) * 128], F2[0:64, 256:384], ["F2"], ["kiT_all"])

        if int(os.environ.get("KLVL", "9")) >= 2:
            rwkv_tile(ti, h_, hr)
        if ti == NT - 1:
            for gi, (c0, n) in enumerate(((0, 512), (512, 512), (1024, 512), (1536, 128))):
                dst = (R2[0:1, 0:512], R2[0:1, 512:1024], K2[0:1, 0:512], K2[0:1, 512:640])[gi]
                nm = ("R2", "R2", "K2", "K2")[gi]
                for k in range(8):
                    MM(dst, h_[:, k, 127:128], Wom[:, k, c0:c0 + n], k == 0, False, ["Wm", hr], [nm])
                for k in range(8):
                    MM(dst, h_[:, k, 127:128], Wm[:, k, c0:c0 + n], False, k == 7, ["Wm", hr], [nm])
            CP("act", xmt[0:1, 0:1024], R2[0:1, :], ["R2"], ["xmt"])
            CP("dve", xmt[0:1, 1024:1664], K2[0:1, 0:640], ["K2"], ["xmt"])
            b.dma("sp", shift_p.rearrange("(a n) -> a n", a=1), xmt[0:1, :], reads=["xmt"])
    for h in range(8):
        TR(F2[0:64, h * 64:(h + 1) * 64], H32[:, h, :], identf[0:64, 0:64], ["H32", "identf"], ["F2"])
    CP("dve", ych, F2[0:64, 0:512].rearrange("p (h j) -> p h j", h=8), ["F2"], ["ych"])
    b.dma("sp", wkv_p.rearrange("h i j -> i h j"), ych, reads=["ych"])

    if stop_after == "B":
        b.barrier(); b.emit(); ncd.__exit__(None, None, None); es.close()
        return nc
    b.barrier()
    off = 0
    Wq, off = carve(off, [128, 8, 1544], BF16)
    stg, off = carve(off, [128, 8, 512])
    score, off = carve(off, [128, T])
    selm, off = carve(off, [128, T], BF16)
    selT, off = carve(off, [128, NT, 128], BF16)
    rl, off = carve(off, [128, 512], BF16)
    rl2, off = carve(off, [128, 512], BF16)
    rlf, off = carve(off, [128, 256])
    diagw, off = carve(off, [128, 8, 128], BF16)
    qfin, off = carve(off, [128, 512])
    qifin, off = carve(off, [128, 512])
    qT, off = carve(off, [64, 8, 128], BF16)
    qiT, off = carve(off, [64, 8, 128], BF16)
    ga, off = carve(off, [128, 512])
    eT, off = carve(off, [128, 4, 128], BF16)
    pTt, off = carve(off, [128, 4, 128], BF16)
    eT2, off = carve(off, [128, 4, 128], BF16)
    pTt2, off = carve(off, [128, 4, 128], BF16)
    cat, off = carve(off, [128, D], BF16)
    catT, off = carve(off, [128, 8, 128], BF16)
    rwo, off = carve(off, [128, 512])
    rwo2, off = carve(off, [128, 512])
    att, off = carve(off, [128, 8, 64])
    ybuf, off = carve(off, [128, D])
    bs, off = carve(off, [128, 16])
    wis, off = carve(off, [128, 8])
    oacc, off = carve(off, [128, 2, 4, 65])
    Wout, off = carve(off, [128, 8, D], BF16)
    gate_bc, off = carve(off, [128, D])
    iota256, off = carve(off, [128, 256])
    assert off <= AW, off
    b.dma("sp", gate_bc, gscr[16:17, :].partition_broadcast(128) if False else gscr[16, :].partition_broadcast(128), reads=["gscr"], writes=["gate_bc"])
    b.dma("sp", iota256, iota_d[:, :], writes=["iota256"])
    load_cols(Wq, 0, C_Q, 512, tag="Wq")
    load_cols(Wq, 512, C_QI, 520, tag="Wq")
    load_cols(Wq, 1032, C_GA, 512, tag="Wq")
    w_out_v = w_out.rearrange("(k p) c -> p k c", p=128)
    for hh in range(2):
        b.dma("sp", stg[:, :, :], w_out_v[:, :, hh * 512:(hh + 1) * 512], writes=["stg"])
        CP("pool", Wout[:, :, hh * 512:(hh + 1) * 512], stg[:, :, :], ["stg"], ["Wout"])


    for j in range(NO if no_lim is None else no_lim):
        ti = NT + j
        x_, h_, xr, hr = front(ti, 0)
        NKT = 2 * (j + 1)
        NK = NKT * 128
        for (c0, n, dst, nm) in ((0, 512, R2[:, 0:512], "R2a"), (512, 512, R2[:, 512:1024], "R2b"),
                                 (1024, 8, F2[:, 0:8], "F2"), (1032, 512, K2[:, 0:512], "K2a")):
            for k in range(8):
                MM(dst, h_[:, k, :], Wq[:, k, c0:c0 + n], k == 0, k == 7, [hr, "Wq"], [nm])
        q3 = qfin.rearrange("p (h d) -> p h d", h=8)
        qknorm(R2[:, 0:512].rearrange("p (h d) -> p h d", h=8), q3, 8, qnw_bc, 0.125, ["R2a"], ["qfin"])
        rope("dve", q3, 8, None, ["qfin"])
        qi3 = qifin.rearrange("p (h d) -> p h d", h=8)
        CP("act", qifin, R2[:, 512:1024], ["R2b"], ["qifin"])
        rope("pool", qi3, 8, None, ["qifin"])
        TS("dve", wis, F2[:, 0:8], 0.044194173824159216, None, ALU.mult, None, ["F2"], ["wis"])
        ACT(ga, K2[:, 0:512], AF.Silu, ["K2a"], ["ga"])
        for (src, srcn, dstT, dn) in ((qfin, "qfin", qT, "qT"), (qifin, "qifin", qiT, "qiT")):
            pv = K2[0:64, :].rearrange("p (h t) -> p h t", h=8)
            for h in range(8):
                TR(pv[:, h, :], src[:, h * 64:(h + 1) * 64], identf[:], [srcn, "identf"], ["K2a" if h < 4 else "K2b"])
            CP("act", dstT, pv, ["K2a", "K2b"], [dn])
        TT("dve", diagw, bc(identb[:].unsqueeze(1), [128, 8, 128]), bc(wis.unsqueeze(2), [128, 8, 128]), ALU.mult, ["identb", "wis"], ["diagw"])
        nchk = (NK + 511) // 512
        ib = 0
        for kc in range(nchk):
            w = min(512, NK - kc * 512)
            pend = None
            for h in range(8):
                pb = (R2[:, 0:512], R2[:, 512:1024])[ib % 2]
                pbn = ("R2a", "R2b")[ib % 2]
                rlb = (rl, rl2)[ib % 2]
                rln = ("rl", "rl2")[ib % 2]
                ib += 1
                MM(pb[:, 0:w], qiT[:, h, :], kiT_all[:, kc * 512:kc * 512 + w], True, True, ["qiT", "kiT_all"], [pbn])
                ACT(rlb[:, 0:w], pb[:, 0:w], AF.Relu, [pbn], [rln])
                if pend is not None:
                    ph, prl, prn = pend
                    MM(F2[:, 0:w], diagw[:, ph, :], prl[:, 0:w], ph == 0, False, ["diagw", prn], ["F2"])
                pend = (h, rlb, rln)
            ph, prl, prn = pend
            MM(F2[:, 0:w], diagw[:, ph, :], prl[:, 0:w], False, True, ["diagw", prn], ["F2"])
            CP("dve", score[:, kc * 512:kc * 512 + w], F2[:, 0:w], ["F2"], ["score"])
        RED(bs[:, 0:1], score[:, 0:NK], ALU.max, ["score"], ["bs"])
        RED(bs[:, 1:2], score[:, 0:NK], ALU.min, ["score"], ["bs"])
        TS("dve", rlf, iota256, qrel[:, 0:1], -1e30, ALU.is_gt, ALU.mult, ["iota256", "qrel"], ["rlf"])
        TT("dve", score[:, NK - 256:NK], score[:, NK - 256:NK], rlf, ALU.add, ["score", "rlf"], ["score"])
        TS("dve", bs[:, 2:3], bs[:, 1:2], -1.0, None, ALU.add, None, ["bs"], ["bs"])
        STT(bs[:, 3:4], bs[:, 0:1], 2.0, bs[:, 1:2], ALU.add, ALU.subtract, ["bs"], ["bs"])
        for it in range(1, n_bis + 1):
            sc_ = float(2.0 ** (-it))
            STT(bs[:, 4:5], bs[:, 3:4], sc_, bs[:, 2:3], ALU.mult, ALU.add, ["bs"], ["bs"])
            TS("dve", selm[:, 0:NK], score[:, 0:NK], bs[:, 4:5], 0.0, ALU.is_gt, ALU.add, ["score", "bs"], ["selm", "bs"], accum=bs[:, 5:6])
            TS("dve", bs[:, 6:7], bs[:, 5:6], float(topk_p) - 0.5, bs[:, 3:4], ALU.is_gt, ALU.mult, ["bs"], ["bs"])
            STT(bs[:, 2:3], bs[:, 6:7], sc_, bs[:, 2:3], ALU.mult, ALU.add, ["bs"], ["bs"])
        TS("dve", selm[:, 0:NK], score[:, 0:NK], bs[:, 2:3], None, ALU.is_gt, None, ["score", "bs"], ["selm"])
        for kt in range(NKT):
            TR(PTb[:, (kt % 8) * 128:(kt % 8 + 1) * 128], selm[:, kt * 128:(kt + 1) * 128], identb[:], ["selm", "identb"], ["PTb"])
            if kt % 8 == 7 or kt == NKT - 1:
                k0 = (kt // 8) * 8
                n_ = kt - k0 + 1
                ACT(selT[:, k0:k0 + n_, :], PTb[:, 0:n_ * 128].rearrange("p (a t) -> p a t", a=n_), AF.Identity, ["PTb", "cst"], ["selT"],
                    scale=30000.0, bias=cst[:, 3:4])
        po = [V2[:, 0:260].rearrange("p (h e) -> p h e", h=4), V2[:, 512:772].rearrange("p (h e) -> p h e", h=4)]
        def att_front(kt, g_):
            lp = K2[:, g_ * 512:(g_ + 1) * 512]
            kn_ = ("K2a", "K2b")[g_]
            pTb = (pTt, pTt2)[g_]
            pn_ = ("pTt", "pTt2")[g_]
            MM(lp, kT_all[:, g_, kt * 128:(kt + 1) * 128], qT[:, g_ * 4:(g_ + 1) * 4, :].rearrange("p h t -> p (h t)"),
               True, False, ["kT_all", "qT"], [kn_])
            for hh in range(4):
                MM(lp[:, hh * 128:(hh + 1) * 128], identb[:], selT[:, kt, :], False, hh == 3, ["identb", "selT"], [kn_])
            ACT(pTb, lp.rearrange("p (h t) -> p h t", h=4), AF.Exp, [kn_], [pn_])

        def att_back(kt, g_):
            vn_ = ("V2a", "V2b")[g_]
            pTb = (pTt, pTt2)[g_]
            pn_ = ("pTt", "pTt2")[g_]
            on_ = ("oacc0", "oacc1")[g_]
            for hh in range(4):
                MM(po[g_][:, hh, :], pTb[:, hh, :], Vaug[:, kt, g_, :], True, True, [pn_, "Vaug"], [vn_])
            if kt == 0:
                CP("act", oacc[:, g_], po[g_], [vn_], [on_])
            else:
                TT("dve", oacc[:, g_], oacc[:, g_], po[g_], ALU.add, [vn_, on_], [on_])
        att_front(0, 0)
        for kt in range(NKT):
            att_front(kt, 1)
            att_back(kt, 0)
            if kt + 1 < NKT:
                att_front(kt + 1, 0)
            att_back(kt, 1)
        for g_ in range(2):
            b.op("dve", lambda g, g_=g_: g.reciprocal(out=bs[:, 8 + g_ * 4:12 + g_ * 4], in_=oacc[:, g_, :, 64]), ["oacc0", "oacc1"], ["bs"])
            TT("dve", att[:, g_ * 4:(g_ + 1) * 4, :], oacc[:, g_, :, 0:64], bc(bs[:, 8 + g_ * 4:12 + g_ * 4].unsqueeze(2), [128, 4, 64]),
               ALU.mult, ["oacc0", "oacc1", "bs"], ["att"])
        TT("dve", cat[:, 0:512], att.rearrange("p h d -> p (h d)"), ga, ALU.mult, ["att", "ga"], ["cat"])
        b.dma("sp", rwo, rwscr[(2 * j) * 128:(2 * j + 1) * 128, :], reads=["rwscr"], writes=["rwo"])
        b.dma("sp", rwo2, rwscr[(2 * j + 1) * 128:(2 * j + 2) * 128, :], reads=["rwscr"], writes=["rwo2"])
        TS("dve", rwo, rwo, parsel[:, 1:2], None, ALU.mult, None, ["rwo", "parsel"], ["rwo"])
        STT(rwo, rwo2, parsel[:, 0:1], rwo, ALU.mult, ALU.add, ["rwo2", "parsel", "rwo"], ["rwo"])
        CP("act", cat[:, 512:1024], rwo, ["rwo"], ["cat"])
        for k in range(8):
            TR(PTb[:, k * 128:(k + 1) * 128], cat[:, k * 128:(k + 1) * 128], identb[:], ["cat", "identb"], ["PTb"])
        CP("act", catT, PTb[:, :].rearrange("p (k t) -> p k t", k=8), ["PTb"], ["catT"])
        for hh in range(2):
            for k in range(8):
                MM(R2[:, hh * 512:(hh + 1) * 512], catT[:, k, :], Wout[:, k, hh * 512:(hh + 1) * 512], k == 0, k == 7, ["catT", "Wout"], [("R2a", "R2b")[hh]])
        TT("dve", ybuf, R2[:, :], gate_bc, ALU.mult, ["R2a", "R2b", "gate_bc"], ["ybuf"])
        TT("pool", ybuf, ybuf, x_[:], ALU.add, ["ybuf", xr], ["ybuf"])
        b.dma("sp", y_own[j * 128:(j + 1) * 128, :], ybuf, reads=["ybuf"])


    if do_sample:
        b.barrier()
        PW = 10500
        off = 0
        proj, off = carve(off, [16, DIN])
        tk = {}
        for nm in ("qs", "ga", "grs", "ta", "tb"):
            tk[nm], off = carve(off, [16, 512])
        ks_, off = carve(off, [16, 128])
        s16, off = carve(off, [16, 64])
        tokd, off = carve(off, [16, 1040])
        ysb, off = carve(off, [16, D])
        cats, off = carve(off, [16, D], BF16)
        catTs, off = carve(off, [128, 8, 16], BF16)
        assert off <= PW, off
        off = PW
        stg, off = carve(off, [128, 8, 512])
        wbf, off = carve(off, [128, 8, 512], BF16)
        sshift_t, off = carve(off, [16, SHW])
        mu16, off = carve(off, [16, SHW])
        X1 = off
        xm, off = carve(off, [16, SHW])
        prm, off = carve(off, [16, 5, 512])
        vecs, off = carve(off, [16, 8, 6, 64])
        for nm in ("dec", "asg", "kkv", "kkn", "kmod"):
            tk[nm], off = carve(off, [16, 512])
        wdt, off = carve(off, [16, 128])
        wdT, off = carve(off, [64, 32])
        assert off <= AW, off
        NPAIR = NS // 2

        x_, h_, xr, hr = front(NT + NO, 0, m_prompt=False, ntok=16)
        for ch in range(8):
            c0 = ch * 505
            b.dma("sp", stg[:, :, 0:505], w_in_v[:, :, c0:c0 + 505], writes=["stg"])
            CP("dve", wbf[:, :, 0:505], stg[:, :, 0:505], ["stg"], ["wbf"])
            for k in range(8):
                MM(R2[0:16, 0:505], h_[:, k, 0:16], wbf[:, k, 0:505], k == 0, k == 7, [hr, "wbf"], ["R2"])
            CP("act", proj[:, c0:c0 + 505], R2[0:16, 0:505], ["R2"], ["proj"])
        b.dma("sp", sshift_t, sshift_d[:, :], writes=["sshift"])
        b.dma("sp", mu16, mu.partition_broadcast(16), writes=["mu16"])
        for i_, src in enumerate((pw0, pa0, pkk, pka, prk)):
            b.dma("sp", prm[:, i_, :], src.partition_broadcast(16), writes=["prm"])
        qs3 = tk["qs"].rearrange("p (h d) -> p h d", h=8)
        qknorm(proj[:, 0:512].rearrange("p (h d) -> p h d", h=8), qs3, 8, qnw_bc, 0.125, ["proj"], ["qs"], nrows=16)
        rope("dve", qs3, 8, None, ["qs"], nrows=16)
        ks3 = ks_.rearrange("p (g d) -> p g d", g=2)
        qknorm(proj[:, 512:640].rearrange("p (g d) -> p g d", g=2), ks3, 2, knw_bc, 1.0, ["proj"], ["ks"], nrows=16)
        rope("dve", ks3, 2, None, ["ks"], nrows=16)
        b.dma("sp", k_s[:, :], ks_, reads=["ks"])
        b.dma("sp", v_s[:, :], proj[:, 640:768], reads=["proj"])
        b.dma("sp", shift_s[:, :], proj[:, C_R:C_R + SHW], reads=["proj"])
        rope("dve", proj[:, 768:1280].rearrange("p (h d) -> p h d", h=8), 8, None, ["proj"], nrows=16)
        rope("dve", proj[:, 1288:1352].unsqueeze(1), 1, None, ["proj"], nrows=16)
        b.dma("sp", ki_s[:, :], proj[:, 1288:1352], reads=["proj"])
        ACT(tk["ga"], proj[:, C_GA:C_GA + 512], AF.Silu, ["proj"], ["ga"])
        ACT(tk["grs"], proj[:, C_GR:C_GR + 512], AF.Silu, ["proj"], ["grs"])
        xs_ = proj[:, C_R:C_R + SHW]
        TT("dve", xm, sshift_t, xs_, ALU.subtract, ["sshift", "proj"], ["xm"])
        TT("dve", xm, xm, mu16, ALU.mult, ["xm", "mu16"], ["xm"])
        TT("dve", xm, xm, xs_, ALU.add, ["xm", "proj"], ["xm"])
        ACT(wdt[:, 0:64], xm[:, 1536:1600], AF.Tanh, ["xm"], ["wdt"])
        CP("dve", wdt[:, 64:128], xm[:, 1600:1664], ["xm"], ["wdt"])
        TR(F2[0:64, 0:16], wdt[:, 0:64], identf[0:16, 0:16], ["wdt", "identf"], ["F2"])
        TR(F2[0:64, 16:32], wdt[:, 64:128], identf[0:16, 0:16], ["wdt", "identf"], ["F2"])
        CP("dve", wdT, F2[0:64, 0:32], ["F2"], ["wdT"])
        MM(R2[0:16, 0:512], wdT[:, 0:16], wupS[:, :], True, True, ["wdT", "wupS"], ["R2"])
        MM(R2[0:16, 512:1024], wdT[:, 16:32], aupS[:, :], True, True, ["wdT", "aupS"], ["R2"])
        TT("dve", tk["dec"], R2[0:16, 0:512], prm[:, 0, :], ALU.add, ["R2", "prm"], ["dec"])
        ACT(tk["dec"], tk["dec"], AF.Sigmoid, ["dec"], ["dec"])
        ACT(tk["dec"], tk["dec"], AF.Exp, ["dec"], ["dec"], scale=-0.6065306597126334)
        TT("dve", tk["asg"], R2[0:16, 512:1024], prm[:, 1, :], ALU.add, ["R2", "prm"], ["asg"])
        ACT(tk["asg"], tk["asg"], AF.Sigmoid, ["asg"], ["asg"])
        xr_, xk_, xv_ = xm[:, 0:512], xm[:, 512:1024], xm[:, 1024:1536]
        TT("dve", tk["kkv"], xk_, prm[:, 2, :], ALU.mult, ["xm", "prm"], ["kkv"])
        ACT(tk["ta"], tk["kkv"], AF.Square, ["kkv"], ["ta"])
        RED(s16[:, 0:8], tk["ta"].rearrange("p (h d) -> p h d", h=8), ALU.add, ["ta"], ["s16"])
        ACT(s16[:, 8:16], s16[:, 0:8], AF.Sqrt, ["s16", "cst"], ["s16"], bias=cst[0:16, 2:3])
        b.op("dve", lambda g: g.reciprocal(out=s16[:, 16:24], in_=s16[:, 8:16]), ["s16"], ["s16"])
        TT("dve", tk["kkn"].rearrange("p (h d) -> p h d", h=8), tk["kkv"].rearrange("p (h d) -> p h d", h=8),
           bc(s16[:, 16:24].unsqueeze(2), [16, 8, 64]), ALU.mult, ["kkv", "s16"], ["kkn"])
        STT(tk["ta"], tk["asg"], -1.0, prm[:, 3, :], ALU.add, ALU.mult, ["asg", "prm"], ["ta"])
        STT(tk["kmod"], tk["ta"], 1.0, xk_, ALU.add, ALU.mult, ["ta", "xm"], ["kmod"])

        def v8(ap):
            return ap.rearrange("p (h d) -> p h d", h=8)
        CP("dve", vecs[:, :, 0, :], v8(tk["dec"]), ["dec"], ["vecs"])
        TS("dve", vecs[:, :, 1, :], v8(tk["kkn"]), -1.0, None, ALU.mult, None, ["kkn"], ["vecs"])
        TT("dve", vecs[:, :, 2, :], v8(tk["kkn"]), v8(tk["asg"]), ALU.mult, ["kkn", "asg"], ["vecs"])
        CP("dve", vecs[:, :, 3, :], v8(tk["kmod"]), ["kmod"], ["vecs"])
        CP("dve", vecs[:, :, 4, :], v8(xr_), ["xm"], ["vecs"])
        CP("dve", vecs[:, :, 5, :], v8(xv_), ["xm"], ["vecs"])
        TT("dve", tk["ta"], xr_, prm[:, 4, :], ALU.mult, ["xm", "prm"], ["ta"])
        TT("dve", tk["ta"], tk["ta"], tk["kmod"], ALU.mult, ["ta", "kmod"], ["ta"])
        RED(s16[:, 24:32], v8(tk["ta"]), ALU.add, ["ta"], ["s16"])
        b.dma("sp", scr1[:, :], vecs.rearrange("p h v j -> p (h v j)"), reads=["vecs"], writes=["scr1"])
        b.barrier()
        off = PW
        S_, off = carve(off, [128, 4096])
        tmpS, off = carve(off, [128, 4096])
        vsh, off = carve(off, [128, 384])
        ysh, off = carve(off, [128, 128])
        assert off <= X1
        b.dma("sp", S_, swkv_d[:, :], writes=["S"])
        b.dma("sp", vsh, scr1.rearrange("s (h x) -> (s h) x", h=8), reads=["scr1"], writes=["vsh"])
        S3 = S_.rearrange("p (i j) -> p i j", i=64)
        T3 = tmpS.rearrange("p (i j) -> p i j", i=64)

        def jb(vi):
            return bc(vsh[:, vi * 64:(vi + 1) * 64].unsqueeze(1), [128, 64, 64])

        def ib(ap):
            return bc(ap.unsqueeze(2), [128, 64, 64])
        TT("dve", T3, S3, jb(1), ALU.mult, ["S", "vsh"], ["tmpS"])
        RED(ysh[:, 0:64], T3, ALU.add, ["tmpS"], ["ysh"])
        TT("dve", S3, S3, jb(0), ALU.mult, ["S", "vsh"], ["S"])
        TT("dve", T3, jb(2), ib(ysh[:, 0:64]), ALU.mult, ["vsh", "ysh"], ["tmpS"])
        TT("dve", S3, S3, T3, ALU.add, ["S", "tmpS"], ["S"])
        TT("dve", T3, jb(3), ib(vsh[:, 320:384]), ALU.mult, ["vsh"], ["tmpS"])
        TT("dve", S3, S3, T3, ALU.add, ["S", "tmpS"], ["S"])
        b.dma("sp", wkv_s[:, :], S_, reads=["S"])
        TT("dve", T3, S3, jb(4), ALU.mult, ["S", "vsh"], ["tmpS"])
        RED(ysh[:, 64:128], T3, ALU.add, ["tmpS"], ["ysh"])
        b.dma("sp", scr2[:, :], ysh[:, 64:128], reads=["ysh"], writes=["scr2"])
        yS = tk["tb"]
        b.dma("sp", yS, scr2.rearrange("(s h) i -> s (h i)", h=8), reads=["scr2"], writes=["tb"])
        y3 = v8(yS)
        RED(s16[:, 32:40], y3, ALU.add, ["tb"], ["s16"])
        ACT(tk["ta"], yS, AF.Square, ["tb"], ["ta"])
        RED(s16[:, 40:48], v8(tk["ta"]), ALU.add, ["ta"], ["s16"])
        TS("dve", s16[:, 32:48], s16[:, 32:48], 1.0 / 64, None, ALU.mult, None, ["s16"], ["s16"])
        TT("dve", s16[:, 48:56], s16[:, 32:40], s16[:, 32:40], ALU.mult, ["s16"], ["s16"])
        TT("dve", s16[:, 48:56], s16[:, 40:48], s16[:, 48:56], ALU.subtract, ["s16"], ["s16"])
        ACT(s16[:, 56:64], s16[:, 48:56], AF.Sqrt, ["s16", "cst"], ["s16"], bias=cst[0:16, 1:2])
        b.op("dve", lambda g: g.reciprocal(out=s16[:, 56:64], in_=s16[:, 56:64]), ["s16"], ["s16"])
        TT("dve", y3, y3, bc(s16[:, 32:40].unsqueeze(2), [16, 8, 64]), ALU.subtract, ["tb", "s16"], ["tb"])
        TT("dve", y3, y3, bc(s16[:, 56:64].unsqueeze(2), [16, 8, 64]), ALU.mult, ["tb", "s16"], ["tb"])
        TT("dve", yS, yS, lnw_bc[0:16, :], ALU.mult, ["tb", "lnw_bc"], ["tb"])
        TT("dve", yS, yS, lnb_bc[0:16, :], ALU.add, ["tb", "lnb_bc"], ["tb"])
        TT("dve", v8(tk["ta"]), v8(xv_), bc(s16[:, 24:32].unsqueeze(2), [16, 8, 64]), ALU.mult, ["xm", "s16"], ["ta"])
        TT("dve", yS, yS, tk["ta"], ALU.add, ["tb", "ta"], ["tb"])
        TT("dve", cats[:, 512:1024], yS, tk["grs"], ALU.mult, ["tb", "grs"], ["cats"])
        b.barrier()
        NCAND = 16
        off = PW
        Gi, off = carve(off, [128, 8192])
        tmpG, off = carve(off, [128, 64, 64])
        Kc, off = carve(off, [128, NCAND, 128])
        Vcd, off = carve(off, [128, NCAND, 128])
        tmpc, off = carve(off, [128, NCAND, 64])
        repd, off = carve(off, [128, 1040])
        opd, off = carve(off, [128, 520])
        repS, off = carve(off, [16, 1024])
        repTS, off = carve(off, [128, 128])
        blkS, off = carve(off, [128, 128])
        sc, off = carve(off, [128, 132])
        msc, off = carve(off, [128, 132])
        sh_, off = carve(off, [128, 128])
        cv, off = carve(off, [128, NCAND])
        ci, off = carve(off, [128, NCAND], I32)
        cif, off = carve(off, [128, NCAND])
        rowi, off = carve(off, [128, NCAND], I32)
        lg, off = carve(off, [128, 8, NCAND])
        b2, off = carve(off, [128, 16])
        ptab, off = carve(off, [128, 8], I32)
        ptf, off = carve(off, [128, 8])
        oh0, off = carve(off, [128, 1])
        assert off <= AW, off
        Gi3 = Gi.rearrange("p (t d) -> p t d", t=128)
        for (t_, d_, nm) in ((ptab, ptab_d, "ptab"), (repS, rep_d, "repS"), (repTS, repT_d, "repTS"), (blkS, blk_d, "blkS"), (oh0, oh0_d, "oh0")):
            b.dma("sp", t_, d_[:, :], writes=[nm])
        CP("dve", tokd[:, 0:512], proj[:, 768:1280], ["proj"], ["tokd"])
        TS("dve", tokd[:, 512:520], proj[:, 1280:1288], 0.044194173824159216, None, ALU.mult, None, ["proj"], ["tokd"])
        CP("dve", tokd[:, 520:1032], tk["qs"], ["qs"], ["tokd"])
        TT("dve", v8(tk["ta"]), v8(tokd[:, 0:512]), bc(proj[:, 1288:1352].unsqueeze(1), [16, 8, 64]), ALU.mult, ["tokd", "proj"], ["ta"])
        RED(s16[:, 0:8], v8(tk["ta"]), ALU.add, ["ta"], ["s16"])
        TS("dve", s16[:, 0:8], s16[:, 0:8], 0.0, None, ALU.max, None, ["s16"], ["s16"])
        TT("dve", s16[:, 0:8], s16[:, 0:8], tokd[:, 512:520], ALU.mult, ["s16", "tokd"], ["s16"])
        RED(tokd[:, 1032:1033], s16[:, 0:8], ALU.add, ["s16"], ["tokd"])
        CP("dve", ptf, ptab, ["ptab"], ["ptf"])
        TS("dve", ptf, ptf, 128.0, None, ALU.mult, None, ["ptf"], ["ptf"])
        ck_rows = cache_k
        cv_rows = cache_v
        cvA, off = carve(off, [128, 8, NCAND])
        ciA, off = carve(off, [128, 8, NCAND], I32)
        thrA, off = carve(off, [128, 8])
        tmpGf = tmpG.rearrange("p a b -> p (a b)")
        cand16 = tmpGf[0:16, 0:1540]
        candj = tmpGf[0:16, 1540:3080]
        assert off <= AW, off
        for sp in range(NPAIR):
            b.op("pool", lambda g, sp=sp: g.indirect_dma_start(out=Gi, out_offset=None, in_=cache_ki[:, :],
                                                                in_offset=bass.IndirectOffsetOnAxis(ap=ptab[:, sp:sp + 1], axis=0)),
                 ["ptab"], ["Gi"], dma=True)
            for (c0, n) in ((0, 512), (512, 8)):
                MM(K2[:, 0:n], repS[:, sp * 128:(sp + 1) * 128], tokd[:, c0:c0 + n], True, True, ["repS", "tokd"], ["K2"])
                CP("act", repd[:, c0:c0 + n], K2[:, 0:n], ["K2"], ["repd"])
            for h in range(8):
                for hf in range(2):
                    TT("dve", tmpG, Gi3[:, hf * 64:(hf + 1) * 64, :], bc(repd[:, h * 64:(h + 1) * 64].unsqueeze(1), [128, 64, 64]), ALU.mult, ["Gi", "repd"], ["tmpG"])
                    RED(sh_[:, hf * 64:(hf + 1) * 64], tmpG, ALU.add, ["tmpG"], ["sh"])
                if h == 0:
                    TS("dve", sc[:, 0:128], sh_, 0.0, repd[:, 512:513], ALU.max, ALU.mult, ["sh", "repd"], ["sc"])
                else:
                    TS("dve", sh_, sh_, 0.0, repd[:, 512 + h:513 + h], ALU.max, ALU.mult, ["sh", "repd"], ["sh"])
                    TT("dve", sc[:, 0:128], sc[:, 0:128], sh_, ALU.add, ["sc", "sh"], ["sc"])
            for r_ in range(NCAND // 8):
                b.op("dve", lambda g, r_=r_, sp=sp: g.max(out=cvA[:, sp, r_ * 8:(r_ + 1) * 8], in_=sc[:, 0:128]), ["sc"], ["cvA"])
                b.op("dve", lambda g, r_=r_, sp=sp: g.max_index(out=ciA[:, sp, r_ * 8:(r_ + 1) * 8].bitcast(mybir.dt.uint32),
                                                                in_max=cvA[:, sp, r_ * 8:(r_ + 1) * 8], in_values=sc[:, 0:128]), ["sc", "cvA"], ["ciA"])
                if r_ < NCAND // 8 - 1:
                    b.op("dve", lambda g, r_=r_, sp=sp: g.match_replace(out=sc[:, 0:128], in_to_replace=cvA[:, sp, r_ * 8:(r_ + 1) * 8],
                                                                        in_values=sc[:, 0:128], imm_value=-3e30), ["sc", "cvA"], ["sc"])
        b.dma("sp", scr4.rearrange("(sp s2) g c -> (s2 g) sp c", s2=2), cvA, reads=["cvA"], writes=["scr4"])
        b.dma("sp", cand16[:, 0:64 * NCAND], scr4.rearrange("s g c -> s (g c)"), reads=["scr4"], writes=["tmpG"])
        CP("dve", cand16[:, 64 * NCAND:64 * NCAND + 1], tokd[:, 1032:1033], ["tokd"], ["tmpG"])
        cnd = cand16[:, 0:64 * NCAND + 1]
        RED(s16[:, 40:41], cnd, ALU.max, ["tmpG"], ["s16"])
        RED(s16[:, 41:42], cnd, ALU.min, ["tmpG"], ["s16"])
        TS("dve", s16[:, 42:43], s16[:, 41:42], -1.0, None, ALU.add, None, ["s16"], ["s16"])
        STT(s16[:, 43:44], s16[:, 40:41], 2.0, s16[:, 41:42], ALU.add, ALU.subtract, ["s16"], ["s16"])
        for it in range(1, n_bis + 3):
            sc_ = float(2.0 ** (-it))
            STT(s16[:, 44:45], s16[:, 43:44], sc_, s16[:, 42:43], ALU.mult, ALU.add, ["s16"], ["s16"])
            TS("dve", candj[:, 0:64 * NCAND + 1], cnd, s16[:, 44:45], 0.0, ALU.is_gt, ALU.add, ["tmpG", "s16"], ["tmpG", "s16"], accum=s16[:, 45:46])
            TS("dve", s16[:, 46:47], s16[:, 45:46], float(topk_s) - 0.5, s16[:, 43:44], ALU.is_gt, ALU.mult, ["s16"], ["s16"])
            STT(s16[:, 42:43], s16[:, 46:47], sc_, s16[:, 42:43], ALU.mult, ALU.add, ["s16"], ["s16"])
        TT("dve", s16[:, 32:33], tokd[:, 1032:1033], s16[:, 42:43], ALU.is_gt, ["tokd", "s16"], ["s16"])
        for sp in range(NPAIR):
            MM(F2[:, sp:sp + 1], repS[:, sp * 128:(sp + 1) * 128], s16[:, 42:43], True, True, ["repS", "s16"], ["F2"])
        CP("dve", thrA, F2[:, 0:8], ["F2"], ["thrA"])
        for sp in range(NPAIR):
            MM(K2[:, 0:512], repS[:, sp * 128:(sp + 1) * 128], tokd[:, 520:1032], True, True, ["repS", "tokd"], ["K2"])
            CP("act", repd[:, 520:1032], K2[:, 0:512], ["K2"], ["repd"])
            CP("dve", cif, ciA[:, sp, :], ["ciA"], ["cif"])
            TS("dve", cif, cif, ptf[:, sp:sp + 1], None, ALU.add, None, ["cif", "ptf"], ["cif"])
            CP("dve", rowi, cif, ["cif"], ["rowi"])
            TS("dve", cv, cvA[:, sp, :], thrA[:, sp:sp + 1], None, ALU.is_gt, None, ["cvA", "thrA"], ["cv"])
            for c_ in range(NCAND):
                b.op("pool", lambda g, c_=c_: g.indirect_dma_start(out=Kc[:, c_, :], out_offset=None, in_=ck_rows[:, :],
                                                                  in_offset=bass.IndirectOffsetOnAxis(ap=rowi[:, c_:c_ + 1], axis=0)),
                     ["rowi"], ["Kc"], dma=True)
                b.op("pool", lambda g, c_=c_: g.indirect_dma_start(out=Vcd[:, c_, :], out_offset=None, in_=cv_rows[:, :],
                                                                  in_offset=bass.IndirectOffsetOnAxis(ap=rowi[:, c_:c_ + 1], axis=0)),
                     ["rowi"], ["Vcd"], dma=True)
            Kc4 = Kc.rearrange("p c (g d) -> p c g d", g=2)
            Vc4 = Vcd.rearrange("p c (g d) -> p c g d", g=2)
            for h in range(8):
                TT("dve", tmpc, Kc4[:, :, h // 4, :], bc(repd[:, 520 + h * 64:520 + (h + 1) * 64].unsqueeze(1), [128, NCAND, 64]), ALU.mult, ["Kc", "repd"], ["tmpc"])
                RED(lg[:, h, :], tmpc, ALU.add, ["tmpc"], ["lg"])
            ACT(lg, lg, AF.Exp, ["lg"], ["lg"])
            TT("dve", lg, lg, bc(cv.unsqueeze(1), [128, 8, NCAND]), ALU.mult, ["lg", "cv"], ["lg"])
            RED(opd[:, 512:520], lg, ALU.add, ["lg"], ["opd"])
            for h in range(8):
                TT("dve", tmpc, Vc4[:, :, h // 4, :], bc(lg[:, h, :].unsqueeze(2), [128, NCAND, 64]), ALU.mult, ["Vcd", "lg"], ["tmpc"])
                RED(opd[:, h * 64:(h + 1) * 64], tmpc.rearrange("p c d -> p d c"), ALU.add, ["tmpc"], ["opd"])
            MM(V2[0:16, 0:512], repTS[:, sp * 16:(sp + 1) * 16], opd[:, 0:512], sp == 0, sp == NPAIR - 1, ["repTS", "opd"], ["V2"])
            MM(V2[0:16, 512:520], repTS[:, sp * 16:(sp + 1) * 16], opd[:, 512:520], sp == 0, sp == NPAIR - 1, ["repTS", "opd"], ["V2"])
        qv = v8(tk["qs"])
        for g_ in range(2):
            TT("dve", v8(tk["ta"])[:, g_ * 4:(g_ + 1) * 4, :], qv[:, g_ * 4:(g_ + 1) * 4, :],
               bc(ks_[:, g_ * 64:(g_ + 1) * 64].unsqueeze(1), [16, 4, 64]), ALU.mult, ["qs", "ks"], ["ta"])
        RED(s16[:, 0:8], v8(tk["ta"]), ALU.add, ["ta"], ["s16"])
        ACT(s16[:, 0:8], s16[:, 0:8], AF.Exp, ["s16"], ["s16"])
        TS("dve", s16[:, 0:8], s16[:, 0:8], s16[:, 32:33], None, ALU.mult, None, ["s16"], ["s16"])
        TT("dve", s16[:, 8:16], V2[0:16, 512:520], s16[:, 0:8], ALU.add, ["V2", "s16"], ["s16"])
        b.op("dve", lambda g: g.reciprocal(out=s16[:, 8:16], in_=s16[:, 8:16]), ["s16"], ["s16"])
        for g_ in range(2):
            TT("dve", v8(tk["ta"])[:, g_ * 4:(g_ + 1) * 4, :], bc(proj[:, 640 + g_ * 64:640 + (g_ + 1) * 64].unsqueeze(1), [16, 4, 64]),
               bc(s16[:, g_ * 4:(g_ + 1) * 4].unsqueeze(2), [16, 4, 64]), ALU.mult, ["proj", "s16"], ["ta"])
        TT("dve", tk["ta"], tk["ta"], V2[0:16, 0:512], ALU.add, ["ta", "V2"], ["ta"])
        TT("dve", v8(tk["ta"]), v8(tk["ta"]), bc(s16[:, 8:16].unsqueeze(2), [16, 8, 64]), ALU.mult, ["ta", "s16"], ["ta"])
        TT("dve", cats[:, 0:512], tk["ta"], tk["ga"], ALU.mult, ["ta", "ga"], ["cats"])
        for k in range(8):
            TR(PTb[:, k * 16:(k + 1) * 16], cats[:, k * 128:(k + 1) * 128], identb[0:16, 0:16], ["cats", "identb"], ["PTb"])
        CP("act", catTs, PTb[:, 0:128].rearrange("p (k t) -> p k t", k=8), ["PTb"], ["catTs"])
        b.barrier()
        off = PW
        stg2, off = carve(off, [128, 8, 512])
        wbf2, off = carve(off, [128, 8, 512], BF16)
        b.dma("sp", ysb, gscr[0:16, :], reads=["gscr"], writes=["ysb"])
        w_out_v2 = w_out.rearrange("(k p) c -> p k c", p=128)
        for hh in range(2):
            b.dma("sp", stg2, w_out_v2[:, :, hh * 512:(hh + 1) * 512], writes=["stg2"])
            CP("dve", wbf2, stg2, ["stg2"], ["wbf2"])
            for k in range(8):
                MM(R2[0:16, hh * 512:(hh + 1) * 512], catTs[:, k, :], wbf2[:, k, :], k == 0, k == 7, ["catTs", "wbf2"], ["R2"])
        TT("dve", ysb, ysb, R2[0:16, :], ALU.mult, ["ysb", "R2"], ["ysb"])
        TT("dve", ysb, ysb, x_[0:16, :], ALU.add, ["ysb", xr], ["ysb"])
        b.dma("sp", y_s[:, :], ysb, reads=["ysb"])

    b.barrier()
    b.emit()
    ncd.__exit__(None, None, None)
    es.close()
    return nc


def _consts(T):
    NT = T // 128
    NO = NT // 2
    cst = {}
    cst["identf"] = np.eye(128, dtype=np.float32)
    cst["iota256"] = np.tile(np.arange(256, dtype=np.float32)[None, :], (128, 1))
    s = np.arange(64)[:, None]
    t = np.arange(64)[None, :]
    lt = (s < t).astype(np.float32)
    le = (s <= t).astype(np.float32)
    cst["maskT"] = np.concatenate([lt, le, lt, le], axis=1)
    cst["maskL"] = (np.arange(64)[None, :] < np.arange(64)[:, None]).astype(np.float32)
    r = np.ones((64, 1024), np.float32)
    r[:, ::64] = 0.0
    cst["resetm"] = r
    sel = np.zeros((17, 128), np.float32)
    sel[16, :] = 1.0
    cst["sel16"] = sel
    cst["ones64"] = np.ones((64, 64), np.float32)
    return cst


def _rope_table(pos):
    half = 8
    inv = np.power(np.float32(ROPE_THETA), -np.arange(half, dtype=np.float32) / np.float32(half)).astype(np.float32)
    ang = pos.astype(np.float32)[:, None] * inv[None, :]
    return np.concatenate([np.cos(ang), np.sin(ang)], axis=1).astype(np.float32)


def _core_inputs(inp, c, T, NS, past_len):
    NT = T // 128
    NO = NT // 2
    bi, par = c // 2, c % 2
    xp = np.asarray(inp["x_prompt"][bi], np.float32)
    own_tiles = [2 * j + par for j in range(NO)]
    own_rows = np.concatenate([np.arange(t * 128, (t + 1) * 128) for t in own_tiles])
    xs = np.zeros((128, D), np.float32)
    xs[:NS] = np.asarray(inp["x_sample"][c * NS:(c + 1) * NS, 0], np.float32)
    m = {}
    m["xall"] = np.ascontiguousarray(np.concatenate([xp, xp[own_rows], xs], axis=0))
    m["call"] = np.ascontiguousarray(np.concatenate([inp["c_sample"][c * NS:(c + 1) * NS], inp["c_prompt"][bi:bi + 1]], axis=0).astype(np.float32))
    pos = np.concatenate([np.arange(T), own_rows, np.full(128, past_len)])
    m["cs_all"] = _rope_table(pos)
    m["parsel"] = np.tile(np.array([[par, 1 - par]], np.float32), (128, 1))
    m["qrel"] = (par * 128 + np.arange(128, dtype=np.float32)).reshape(128, 1)
    m["ownidx"] = np.ascontiguousarray(own_rows.reshape(NO, 128).T.astype(np.int32))
    for k_, v_ in (("w_in", "w_in"), ("w_ada", "w_ada"), ("b_ada", "b_ada"), ("norm_w", "norm_w"), ("w_out", "w_out"),
                   ("qnw", "q_norm_w"), ("knw", "k_norm_w"), ("mu", "mu_shift"), ("w0", "w0"), ("a0", "a0"),
                   ("k_k", "k_k"), ("k_a", "k_a"), ("ln_x_w", "ln_x_w"), ("ln_x_b", "ln_x_b"), ("w_up", "w_up"), ("a_up", "a_up")):
        m[k_] = np.ascontiguousarray(np.asarray(inp[v_], np.float32))
    m["r_k"] = np.ascontiguousarray(np.asarray(inp["r_k"], np.float32).reshape(512))
    m["swkv"] = np.ascontiguousarray(np.asarray(inp["state_wkv"][c * NS:(c + 1) * NS], np.float32).reshape(NS * 8, 4096))
    m["sshift"] = np.ascontiguousarray(np.asarray(inp["state_shift"][c * NS:(c + 1) * NS, 0], np.float32))
    pt = np.asarray(inp["page_table"][c * NS:(c + 1) * NS], np.int32)
    m["ptab"] = np.ascontiguousarray(pt.reshape(NS // 2, 128).T)
    nphys = inp["cache_k"].shape[0]
    m["cache_k"] = np.asarray(inp["cache_k"], np.float32).reshape(nphys * 128, 128)
    m["cache_v"] = np.asarray(inp["cache_v"], np.float32).reshape(nphys * 128, 128)
    m["cache_kidx"] = np.asarray(inp["cache_kidx"], np.float32).reshape(nphys, 8192)
    rep = np.zeros((16, 8, 128), np.float32)
    for sp in range(8):
        for p in range(128):
            rep[2 * sp + p // 64, sp, p] = 1.0
    m["rep"] = rep.reshape(16, 1024)
    m["repT"] = np.ascontiguousarray(rep.transpose(2, 1, 0).reshape(128, 128))
    blk = np.zeros((128, 128), np.float32)
    blk[:64, :64] = 1.0
    blk[64:, 64:] = 1.0
    m["blk"] = blk
    oh = np.zeros((128, 1), np.float32)
    oh[0, 0] = 1.0
    oh[64, 0] = 1.0
    m["oh0"] = oh
    m.update(_consts(T))
    return m


_NC_CACHE = {}


def kernel(**inp):
    T = 4096
    NS = 16
    past_len = 8192
    inp = {k: np.asarray(v) for k, v in inp.items()}
    if "nc" not in _NC_CACHE:
        _NC_CACHE["nc"] = build(T=T, NPHYS=int(inp["cache_k"].shape[0]))
    nc = _NC_CACHE["nc"]
    in_maps = [_core_inputs(inp, c, T, NS, past_len) for c in range(8)]
    res = run_bass_kernel_spmd(nc, in_maps, core_ids=list(range(8)))
    outs = res.results
    B = 4
    NO = T // 256
    y_p = np.zeros((B, T, D), np.float32)
    for c in range(8):
        bi, par = c // 2, c % 2
        yo = np.asarray(outs[c]["y_own"]).reshape(NO, 128, D)
        y_p[bi].reshape(T // 256, 2, 128, D)[:, par] = yo
    k_p = np.stack([np.asarray(outs[2 * bi]["k_nat"]).reshape(T, 2, 64) for bi in range(B)])
    v_p = np.stack([np.asarray(outs[2 * bi]["v_nat"]).reshape(T, 2, 64) for bi in range(B)])
    ki_p = np.stack([np.asarray(outs[2 * bi]["ki_nat"]).reshape(T, 64) for bi in range(B)])
    wkv_pp = np.stack([np.asarray(outs[2 * bi]["wkv_p"]).reshape(8, 64, 64) for bi in range(B)])
    sh_p = np.stack([np.asarray(outs[2 * bi]["shift_p"]).reshape(1, SHW) for bi in range(B)])
    y_s = np.concatenate([np.asarray(outs[c]["y_s"]) for c in range(8)]).reshape(128, 1, D)
    k_s = np.concatenate([np.asarray(outs[c]["k_s"]) for c in range(8)]).reshape(128, 1, 2, 64)
    v_s = np.concatenate([np.asarray(outs[c]["v_s"]) for c in range(8)]).reshape(128, 1, 2, 64)
    ki_s = np.concatenate([np.asarray(outs[c]["ki_s"]) for c in range(8)]).reshape(128, 1, 64)
    wkv_s = np.concatenate([np.asarray(outs[c]["wkv_s"]) for c in range(8)]).reshape(128, 8, 64, 64)
    sh_s = np.concatenate([np.asarray(outs[c]["shift_s"]) for c in range(8)]).reshape(128, 1, SHW)
    f = lambda a: np.ascontiguousarray(a, dtype=np.float32)
    return (f(y_p), f(y_s), f(k_p), f(v_p), f(ki_p), f(wkv_pp), f(sh_p), f(k_s), f(v_s), f(ki_s), f(wkv_s), f(sh_s))
```

```python
import os
import numpy as np
from contextlib import ExitStack
import concourse.bass as bass
import concourse.mybir as mybir
from concourse.bass_utils import run_bass_kernel_spmd

F32 = mybir.dt.float32
BF16 = mybir.dt.bfloat16
I32 = mybir.dt.int32
AF = mybir.ActivationFunctionType
ALU = mybir.AluOpType
AX = mybir.AxisListType

ENGS = ("pe", "act", "dve", "pool", "sp")
NDMA = 32
NSW = 8

D = 1024
HD = 64
DIN = 4040
C_Q, C_K, C_V, C_QI, C_WI, C_KI, C_GA = 0, 512, 640, 768, 1280, 1288, 1352
C_R, C_RK, C_RV, C_WD, C_AD, C_GR = 1864, 2376, 2888, 3400, 3464, 3528
SHW = 1664
NORM_EPS = 1e-6
GN_EPS = 64e-5
ROPE_THETA = 500000.0


USE_POOL = bool(int(os.environ.get('USE_POOL', '0')))
PSUM_RES = {"PTb", "F2", "R2", "K2", "V2", "R2a", "R2b", "K2a", "K2b", "V2a", "V2b"}


class Res:
    __slots__ = ("w", "r")

    def __init__(self):
        self.w = None
        self.r = []


class Bld:
    def __init__(self, nc, es):
        self.nc = nc
        self.es = es
        self.sem = {e: es.enter_context(nc.semaphore("s_" + e)) for e in ENGS}
        self.dsem = [es.enter_context(nc.semaphore("d%d" % i)) for i in range(NDMA)]
        self.dval = [0] * NDMA
        self.dnext = 0
        self.dnext_sw = 0
        self.cnt = {e: 0 for e in ENGS}
        self.waited = {e: {} for e in ENGS}
        self.ops = {e: [] for e in ENGS}
        self.res = {}

    def sb(self, name, shape, dt=F32):
        return self.es.enter_context(self.nc.sbuf_tensor("sb_" + name, list(shape), dt))

    def ps(self, name, shape, dt=F32):
        return self.es.enter_context(self.nc.psum_tensor("ps_" + name, list(shape), dt))

    def _r(self, key):
        r = self.res.get(key)
        if r is None:
            r = self.res[key] = Res()
        return r

    def _need(self, e, tok, waits):
        if tok is None:
            return
        key, val = tok
        if key == "pe" and e == "pe":
            return
        if self.waited[e].get(key, 0) >= val:
            return
        self.waited[e][key] = val
        waits.append((key, val))

    def op(self, e, fn, reads=(), writes=(), dma=False):
        if e == "pool" and not dma and not USE_POOL:
            e = "dve"
        pr = [k for k in reads if k in PSUM_RES]
        if pr:
            reads = [k for k in reads if k not in PSUM_RES]
            writes = list(writes) + pr
        waits = []
        for k in reads:
            self._need(e, self._r(k).w, waits)
        for k in writes:
            r = self._r(k)
            self._need(e, r.w, waits)
            for t in r.r:
                self._need(e, t, waits)
        if dma:
            if e == "pool":
                i = NDMA - NSW + self.dnext_sw
                self.dnext_sw = (self.dnext_sw + 1) % NSW
            else:
                i = self.dnext
                self.dnext = (self.dnext + 1) % (NDMA - NSW)
            if self.dval[i] > 0:
                self._need(e, (("d", i), self.dval[i]), waits)
            self.dval[i] += 16
            tok = (("d", i), self.dval[i])
            inc = (self.dsem[i], 16)
        else:
            self.cnt[e] += 1
            tok = (e, self.cnt[e])
            inc = (self.sem[e], 1)
        self.ops[e].append((waits, fn, inc))
        for k in reads:
            self._r(k).r.append(tok)
        for k in writes:
            r = self._r(k)
            r.w = tok
            r.r = []
        return tok

    def dma(self, e, out, in_, reads=(), writes=()):
        return self.op(e, lambda g: g.dma_start(out=out, in_=in_), reads, writes, dma=True)

    def barrier(self, dmas=True):
        for e in ENGS:
            waits = []
            for e2 in ENGS:
                if e2 != e and self.cnt[e2] > 0:
                    self._need(e, (e2, self.cnt[e2]), waits)
            if dmas:
                for i in range(NDMA):
                    if self.dval[i] > 0:
                        self._need(e, (("d", i), self.dval[i]), waits)
            self.ops[e].append((waits, None, None))

    def emit(self):
        nc = self.nc
        with nc.Block() as block:
            def mk(e):
                def body(g):
                    for waits, fn, inc in self.ops[e]:
                        for key, val in waits:
                            s = self.dsem[key[1]] if isinstance(key, tuple) else self.sem[key]
                            g.wait_ge(s, val)
                        if fn is not None:
                            fn(g).then_inc(inc[0], inc[1])
                return body
            block.tensor(mk("pe"))
            block.scalar(mk("act"))
            block.vector(mk("dve"))
            block.gpsimd(mk("pool"))
            block.sync(mk("sp"))


def build(T=4096, NS=16, NPG=64, NPHYS=10240, topk_p=256, topk_s=256, n_bis=15, do_sample=True, AW=36000, stop_after=None, nt_lim=None, no_lim=None):
    NT = T // 128
    NO = NT // 2
    NTILES = NT + NO + 1
    nc = bass.Bass("TRN2", target_bir_lowering=False)
    es = ExitStack()
    b = Bld(nc, es)

    def din(name, shape, dt=F32):
        return nc.dram_tensor(name, list(shape), dt, kind="ExternalInput").ap()

    def dout(name, shape, dt=F32):
        return nc.dram_tensor(name, list(shape), dt, kind="ExternalOutput").ap()

    xall = din("xall", [NTILES * 128, D])
    call = din("call", [17, D])
    w_in = din("w_in", [D, DIN])
    w_ada = din("w_ada", [D, 3 * D])
    b_ada = din("b_ada", [3 * D])
    norm_w = din("norm_w", [D])
    w_out = din("w_out", [D, D])
    qnw = din("qnw", [HD])
    knw = din("knw", [HD])
    mu = din("mu", [SHW])
    pw0 = din("w0", [512]); pa0 = din("a0", [512]); pkk = din("k_k", [512]); pka = din("k_a", [512])
    prk = din("r_k", [512]); plnw = din("ln_x_w", [512]); plnb = din("ln_x_b", [512])
    w_up = din("w_up", [64, 512]); a_up = din("a_up", [64, 512])
    identf_d = din("identf", [128, 128])
    cs_all = din("cs_all", [NTILES * 128, 16])
    parsel_d = din("parsel", [128, 2])
    qrel_d = din("qrel", [128, 1])
    ownidx_d = din("ownidx", [128, NO], I32)
    iota_d = din("iota256", [128, 256])
    maskT_d = din("maskT", [64, 256])
    maskL_d = din("maskL", [64, 64])
    reset_d = din("resetm", [64, 1024])
    sel16_d = din("sel16", [17, 128])
    ones64_d = din("ones64", [64, 64])

    swkv_d = din("swkv", [128, 4096]); sshift_d = din("sshift", [16, SHW]); ptab_d = din("ptab", [128, 8], I32)
    if do_sample:
        cache_k = din("cache_k", [NPHYS * 128, 128]); cache_v = din("cache_v", [NPHYS * 128, 128])
        cache_ki = din("cache_kidx", [NPHYS, 8192])
    rep_d = din("rep", [16, 8 * 128]); repT_d = din("repT", [128, 8 * 16]); blk_d = din("blk", [128, 128]); oh0_d = din("oh0", [128, 1])
    y_s = dout("y_s", [16, D]); k_s = dout("k_s", [16, 128]); v_s = dout("v_s", [16, 128]); ki_s = dout("ki_s", [16, 64])
    wkv_s = dout("wkv_s", [128, 4096]); shift_s = dout("shift_s", [16, SHW])
    gscr = nc.dram_tensor("gscr", [17, D], F32, kind="Internal").ap()
    scr1 = nc.dram_tensor("scr1", [16, 3072], F32, kind="Internal").ap()
    scr2 = nc.dram_tensor("scr2", [128, 64], F32, kind="Internal").ap()
    scr3 = nc.dram_tensor("scr3", [16, 512], F32, kind="Internal").ap()
    scr4 = nc.dram_tensor("scr4", [16, 64, 16], F32, kind="Internal").ap()
    y_own = dout("y_own", [NO * 128, D])
    k_nat = dout("k_nat", [T, 128]); v_nat = dout("v_nat", [T, 128]); ki_nat = dout("ki_nat", [T, 64])
    wkv_p = dout("wkv_p", [8, 64, 64]); shift_p = dout("shift_p", [SHW])
    rwscr = nc.dram_tensor("rwscr", [T, 512], F32, kind="Internal").ap()

    PTb = b.ps("PTb", [128, 1024], BF16)
    F2 = b.ps("F2", [128, 512])
    R2 = b.ps("R2", [128, 1024])
    K2 = b.ps("K2", [128, 1024])
    V2 = b.ps("V2", [128, 1024])

    identf = b.sb("identf", [128, 128]); identb = b.sb("identb", [128, 128], BF16)
    cst = b.sb("cst", [128, 4])
    kT_all = b.sb("kT_all", [64, 2, T], BF16)
    kiT_all = b.sb("kiT_all", [64, T], BF16)
    Vaug = b.sb("Vaug", [128, NT, 2, 65], BF16)
    modT = b.sb("modT", [128, 24, 17])
    g1 = b.sb("g1", [128, 8, 17])
    nwT = b.sb("nwT", [128, 8]); badaT = b.sb("badaT", [128, 24])
    lnw_bc = b.sb("lnw_bc", [64, 512]); lnb_bc = b.sb("lnb_bc", [64, 512])
    qnw_bc = b.sb("qnw_bc", [128, 64]); knw_bc = b.sb("knw_bc", [128, 64])
    sel16 = b.sb("sel16", [17, 128]); ones64 = b.sb("ones64", [64, 64])
    maskT = b.sb("maskT", [64, 256]); maskL = b.sb("maskL", [64, 64]); resetm = b.sb("resetm", [64, 1024])
    qrel = b.sb("qrel", [128, 1]); parsel = b.sb("parsel", [128, 2]); ownidx = b.sb("ownidx", [128, NO], I32)
    fp = {}
    for nm in ("w0", "a0", "kk", "ka", "rk"):
        fp[nm] = b.sb("fp_" + nm, [64, 8])
    muT = b.sb("muT", [64, 26]); wupS = b.sb("wupS", [64, 512]); aupS = b.sb("aupS", [64, 512])
    xt0 = b.sb("xt0", [128, D]); xt = [xt0, xt0]
    xn = b.sb("xn", [128, D], BF16)
    hT0 = b.sb("hT0", [128, 8, 128], BF16); hT = [hT0, hT0]
    hTs = b.sb("hTs", [128, 8, 128], BF16)
    hlast = b.sb("hlast", [128, 8, 1], BF16)
    cs_t = b.sb("cs_t", [128, 16])
    sm = b.sb("sm", [128, 64])
    ARENA = b.sb("ARENA", [128, AW])
    csT = b.sb("csT", [128, 8, 17])

    def TT(e, out, in0, in1, op, R, W):
        b.op(e, lambda g: g.tensor_tensor(out=out, in0=in0, in1=in1, op=op), R, W)

    def TS(e, out, in0, s1, s2, op0, op1, R, W, accum=None):
        if op1 is None:
            b.op(e, lambda g: g.tensor_scalar(out=out, in0=in0, scalar1=s1, scalar2=None, op0=op0), R, W)
        elif accum is None:
            b.op(e, lambda g: g.tensor_scalar(out=out, in0=in0, scalar1=s1, scalar2=s2, op0=op0, op1=op1), R, W)
        else:
            b.op(e, lambda g: g.tensor_scalar(out=out, in0=in0, scalar1=s1, scalar2=s2, op0=op0, op1=op1,
                                              accum_out=accum), R, W)

    def STT(out, in0, scalar, in1, op0, op1, R, W):
        b.op("dve", lambda g: g.scalar_tensor_tensor(out=out, in0=in0, scalar=scalar, in1=in1, op0=op0, op1=op1), R, W)

    def ACT(out, in_, func, R, W, scale=1.0, bias=None, accum=None):
        kw = {}
        if bias is not None:
            kw["bias"] = bias
        if accum is not None:
            kw["accum_out"] = accum
        b.op("act", lambda g: g.activation(out=out, in_=in_, func=func, scale=scale, **kw), R, W)

    def MM(out, lhsT, rhs, start, stop, R, W):
        b.op("pe", lambda g: g.matmul(out=out, lhsT=lhsT, rhs=rhs, start=start, stop=stop), R, W)

    def TR(out, in_, ident, R, W):
        b.op("pe", lambda g: g.transpose(out=out, in_=in_, identity=ident), R, W)

    def CP(e, out, in_, R, W):
        if e == "act":
            b.op(e, lambda g: g.copy(out=out, in_=in_), R, W)
        else:
            b.op(e, lambda g: g.tensor_copy(out=out, in_=in_), R, W)

    def RED(out, in_, op, R, W, axis=AX.X):
        b.op("dve", lambda g: g.tensor_reduce(out=out, in_=in_, axis=axis, op=op), R, W)

    def MS(e, ap, val, W):
        b.op(e, lambda g: g.memset(ap, val), (), W)

    def bc(ap, shape):
        return ap.to_broadcast(list(shape))

    ncd = nc.allow_non_contiguous_dma(reason="small parameter layouts")
    ncd.__enter__()

    b.dma("sp", identf[:], identf_d[:, :], writes=["identf"])
    CP("dve", identb[:], identf[:], ["identf"], ["identb"])
    MS("dve", cst[:, 0:1], NORM_EPS, ["cst"]); MS("dve", cst[:, 1:2], GN_EPS, ["cst"]); MS("dve", cst[:, 2:3], 1e-24, ["cst"]); MS("dve", cst[:, 3:4], -30000.0, ["cst"])
    for (t_, d_, nm) in ((sel16, sel16_d, "sel16"), (ones64, ones64_d, "ones64"), (maskT, maskT_d, "maskT"),
                         (maskL, maskL_d, "maskL"), (resetm, reset_d, "resetm"),
                         (qrel, qrel_d, "qrel"), (parsel, parsel_d, "parsel"), (ownidx, ownidx_d, "ownidx"), (wupS, w_up, "wupS"), (aupS, a_up, "aupS")):
        b.dma("sp", t_[:], d_[:, :], writes=[nm])
    for nm, src in (("w0", pw0), ("a0", pa0), ("kk", pkk), ("ka", pka), ("rk", prk)):
        b.dma("sp", fp[nm][:], src.rearrange("(h j) -> j h", j=64), writes=["fp_" + nm])
    b.dma("sp", muT[:], mu.rearrange("(c j) -> j c", j=64), writes=["muT"])
    b.dma("sp", nwT[:], norm_w.rearrange("(k p) -> p k", p=128), writes=["nwT"])
    b.dma("sp", badaT[:], b_ada.rearrange("(t p) -> p t", p=128), writes=["badaT"])
    b.dma("sp", lnw_bc[:], plnw.partition_broadcast(64), writes=["lnw_bc"])
    b.dma("sp", lnb_bc[:], plnb.partition_broadcast(64), writes=["lnb_bc"])
    b.dma("sp", qnw_bc[:], qnw.partition_broadcast(128), writes=["qnw_bc"])
    b.dma("sp", knw_bc[:], knw.partition_broadcast(128), writes=["knw_bc"])
    MS("pool", Vaug[:, :, :, 64:65], 1.0, ["Vaug"])

    def carve(off, shape, dt=F32):
        n = int(np.prod(shape[1:]))
        words = n if dt in (F32, I32) else (n + 1) // 2
        v = ARENA[0:shape[0], off:off + words]
        if dt != F32:
            v = v.bitcast(dt)
        if len(shape) == 3:
            v = v.rearrange("p (a b) -> p a b", a=shape[1])
        elif len(shape) == 4:
            v = v.rearrange("p (a b c) -> p a b c", a=shape[1], b=shape[2])
        return v, off + words

    off = 0
    Wn, off = carve(off, [128, 8, 832], BF16)
    Wm, off = carve(off, [128, 8, SHW], BF16)
    Wom, off = carve(off, [128, 8, SHW], BF16)
    W_end = off
    stg, off = carve(off, [128, 8, 512])
    mu_bc, off = carve(off, [128, SHW])
    omu_bc, off = carve(off, [128, SHW])
    gtok, off = carve(off, [17, D])
    bgate, off = carve(off, [17, D])
    csall_sil, off = carve(off, [17, D])

    b.dma("sp", mu_bc, mu.partition_broadcast(128), writes=["mu_bc"])
    b.dma("sp", bgate, b_ada[2 * D:3 * D].partition_broadcast(17), writes=["bgate"])
    TS("dve", omu_bc, mu_bc, -1.0, 1.0, ALU.mult, ALU.add, ["mu_bc"], ["omu_bc"])
    w_in_v = w_in.rearrange("(k p) c -> p k c", p=128)

    def load_cols(dst, dcol, c0, n, scale_bc=None, scale_off=0, tag=""):
        done = 0
        while done < n:
            w = min(512, n - done)
            b.dma("sp", stg[:, :, 0:w], w_in_v[:, :, c0 + done:c0 + done + w], writes=["stg"])
            if scale_bc is None:
                CP("pool", dst[:, :, dcol + done:dcol + done + w], stg[:, :, 0:w], ["stg"], [tag])
            else:
                for sname, sbcv, d2 in scale_bc:
                    TT("dve", d2[:, :, dcol + done:dcol + done + w], stg[:, :, 0:w],
                       bc(sbcv[:, scale_off + done:scale_off + done + w].unsqueeze(1), [128, 8, w]),
                       ALU.mult, ["stg", sname], [tag])
            done += w

    load_cols(Wn, 0, C_K, 256, tag="Wn")
    load_cols(Wn, 256, C_KI, 64, tag="Wn")
    load_cols(Wn, 320, C_GR, 512, tag="Wn")
    load_cols(None, 0, C_R, SHW, scale_bc=[("mu_bc", mu_bc, Wm), ("omu_bc", omu_bc, Wom)], tag="Wm")
    calt = sm
    b.dma("sp", csall_sil, call[:, :], writes=["csil"])
    ACT(csall_sil, csall_sil, AF.Silu, ["csil"], ["csil"])
    for k in range(8):
        TR(F2[:, k * 17:(k + 1) * 17], csall_sil[:, k * 128:(k + 1) * 128], identf[0:17, 0:17], ["csil", "identf"], ["F2"])
    CP("dve", csT[:], F2[:, 0:136].rearrange("p (k m) -> p k m", k=8), ["F2"], ["csT"])
    w_ada_v = w_ada.rearrange("(k p) c -> p k c", p=128)
    for ch in range(6):
        b.dma("sp", stg[:, :, :], w_ada_v[:, :, ch * 512:(ch + 1) * 512], writes=["stg"])
        for ct in range(4):
            for k in range(8):
                MM(R2[:, ct * 17:(ct + 1) * 17], stg[:, k, ct * 128:(ct + 1) * 128], csT[:, k, :], k == 0, k == 7,
                   ["stg", "csT"], ["R2"])
        TT("dve", modT[:, ch * 4:(ch + 1) * 4, :], R2[:, 0:68].rearrange("p (c m) -> p c m", c=4),
           bc(badaT[:, ch * 4:(ch + 1) * 4].unsqueeze(2), [128, 4, 17]), ALU.add, ["R2", "badaT"], ["modT"])
        if ch >= 4:
            for k in range(8):
                MM(K2[0:17, 0:512], csT[:, k, :], stg[:, k, :], k == 0, k == 7, ["stg", "csT"], ["K2"])
            TT("dve", gtok[:, (ch - 4) * 512:(ch - 3) * 512], K2[0:17, 0:512], bgate[:, (ch - 4) * 512:(ch - 3) * 512],
               ALU.add, ["K2", "bgate"], ["gtok"])
    b.dma("sp", gscr[:, :], gtok[0:17, :], reads=["gtok"], writes=["gscr"])
    STT(g1[:], modT[:, 8:16, :], 1.0, bc(nwT[:].unsqueeze(2), [128, 8, 17]), ALU.add, ALU.mult, ["modT", "nwT"], ["g1"])
    def front(ti, par, m_prompt=True, ntok=128):
        x_ = xt[par]
        h_ = hT[par]
        xr, hr = "xt0", "hT0"
        b.dma("sp", x_[:], xall[ti * 128:(ti + 1) * 128, :], writes=[xr])
        b.dma("sp", cs_t[:], cs_all[ti * 128:(ti + 1) * 128, :], writes=["cs_t"])
        ACT(xn[:], x_[:], AF.Square, [xr], ["xn", "sm"], accum=sm[:, 0:1])
        ACT(sm[:, 1:2], sm[:, 0:1], AF.Sqrt, ["sm", "cst"], ["sm"], scale=1.0 / D, bias=cst[:, 0:1])
        b.op("dve", lambda g: g.reciprocal(out=sm[:, 2:3], in_=sm[:, 1:2]), ["sm"], ["sm"])
        TS("dve", xn[:], x_[:], sm[:, 2:3], None, ALU.mult, None, [xr, "sm"], ["xn"])
        for k in range(8):
            TR(PTb[:, k * 128:(k + 1) * 128], xn[:, k * 128:(k + 1) * 128], identb[:], ["xn", "identb"], ["PTb"])
        pv = PTb[:, :].rearrange("p (k t) -> p k t", k=8)
        if m_prompt:
            TT("dve", h_[:], pv, bc(g1[:, :, 16:17], [128, 8, 128]), ALU.mult, ["PTb", "g1"], [hr])
            TT("pool", h_[:], h_[:], bc(modT[:, 0:8, 16:17], [128, 8, 128]), ALU.add, [hr, "modT"], [hr])
        else:
            TT("dve", h_[:, :, 0:ntok], pv[:, :, 0:ntok], g1[:, :, 0:ntok], ALU.mult, ["PTb", "g1"], [hr])
            TT("pool", h_[:, :, 0:ntok], h_[:, :, 0:ntok], modT[:, 0:8, 0:ntok], ALU.add, [hr, "modT"], [hr])
        return x_, h_, xr, hr

    def rope(e, buf, nh, hd_stride_view, R, nrows=128):
        x1 = buf[:, :, 0:8]
        x2 = buf[:, :, 8:16]
        cosb = bc(cs_t[0:nrows, 0:8].unsqueeze(1), [nrows, nh, 8])
        sinb = bc(cs_t[0:nrows, 8:16].unsqueeze(1), [nrows, nh, 8])
        t = ropet[0:nrows, 0:4 * nh * 8].rearrange("p (a h d) -> p a h d", a=4, h=nh)
        TT(e, t[:, 0], x1, cosb, ALU.mult, R + ["cs_t"], ["ropet"])
        TT(e, t[:, 1], x2, sinb, ALU.mult, R + ["cs_t"], ["ropet"])
        TT(e, t[:, 2], x2, cosb, ALU.mult, R + ["cs_t"], ["ropet"])
        TT(e, t[:, 3], x1, sinb, ALU.mult, R + ["cs_t"], ["ropet"])
        TT(e, x1, t[:, 0], t[:, 1], ALU.subtract, ["ropet"], R)
        TT(e, x2, t[:, 2], t[:, 3], ALU.add, ["ropet"], R)

    ropet = b.sb("ropet", [128, 256])

    def qknorm(src_ps, dst, nh, wbc, extra_scale, Rsrc, Wdst, nrows=128):
        sq = nrm_t[0:nrows, 0:nh * 64].rearrange("p (h d) -> p h d", h=nh)
        ACT(sq, src_ps, AF.Square, Rsrc, ["nrm_t"])
        RED(sm[0:nrows, 8:8 + nh], sq, ALU.add, ["nrm_t"], ["sm"])
        ACT(sm[0:nrows, 16:16 + nh], sm[0:nrows, 8:8 + nh], AF.Sqrt, ["sm", "cst"], ["sm"], scale=1.0 / 64, bias=cst[0:nrows, 0:1])
        b.op("dve", lambda g: g.reciprocal(out=sm[0:nrows, 24:24 + nh], in_=sm[0:nrows, 16:16 + nh]), ["sm"], ["sm"])
        TT("dve", dst, src_ps, bc(sm[0:nrows, 24:24 + nh].unsqueeze(2), [nrows, nh, 64]), ALU.mult, Rsrc + ["sm"], Wdst)
        STT(dst, dst, float(extra_scale), bc(wbc[0:nrows, :].unsqueeze(1), [nrows, nh, 64]), ALU.mult, ALU.mult, Wdst + ["qnw_bc", "knw_bc"], Wdst)

    nrm_t = b.sb("nrm_t", [128, 512])
    kfin = b.sb("kfin", [128, 128]); vfin = b.sb("vfin", [128, 128]); kifin = b.sb("kifin", [128, 64])
    gr_s = b.sb("gr_s", [128, 512])

    off = W_end
    rw = {}
    for nm in ("tw", "adc"):
        rw[nm], off = carve(off, [64, 128])
    for nm in ("sg", "L", "g", "t1"):
        rw[nm], off = carve(off, [64, 8, 128])
    blkA, off = carve(off, [64, 2048])
    blkB, off = carve(off, [64, 3072])
    rw["gprev"] = blkA[:, 0:1024].rearrange("p (h t) -> p h t", h=8)
    rw["ginv"] = blkA[:, 1024:2048].rearrange("p (h t) -> p h t", h=8)
    rw["asig"] = blkB[:, 0:1024].rearrange("p (h t) -> p h t", h=8)
    rw["kkn"] = blkB[:, 1024:2048].rearrange("p (h t) -> p h t", h=8)
    rw["kmod"] = blkB[:, 2048:3072].rearrange("p (h t) -> p h t", h=8)
    AMx = blkA.bitcast(BF16).rearrange("p (h x) -> p h x", h=16)
    LNPx = blkB.bitcast(BF16).rearrange("p (a h x) -> p a h x", a=6, h=16)
    rw["sg"] = rw["sg"]
    QTt, off = carve(off, [64, 8, 2, 128], BF16)
    KTt, off = carve(off, [64, 8, 2, 128], BF16)
    def alias(view64, shape):
        return view64.rearrange("p h t -> p (h t)").bitcast(BF16)
    AM = AMx
    Lm = [LNPx[:, 0], LNPx[:, 1]]
    Nm = [LNPx[:, 2], LNPx[:, 3]]
    Pm = [LNPx[:, 4], LNPx[:, 5]]
    BKtok, off = carve(off, [64, 8, 2, 64], BF16)
    Vc, off = carve(off, [64, 2, 8, 64], BF16)
    P0s, off = carve(off, [64, 8, 64], BF16)
    Us, off = carve(off, [64, 8, 64], BF16)
    H32, off = carve(off, [64, 8, 64])
    Hb, off = carve(off, [64, 8, 64], BF16)
    ych, off = carve(off, [64, 8, 64])
    yt1, off = carve(off, [64, 8, 64])
    bon, off = carve(off, [128, 8])
    rawl, off = carve(off, [64, 26])
    xmt, off = carve(off, [128, SHW])
    st8, off = carve(off, [64, 64])
    assert off <= AW, off
    identb64 = identb[0:64, 0:64]

    KR = int(os.environ.get('KR', '9'))
    KQ = int(os.environ.get('KQ', '9'))

    def rwkv_tile(ti, h_, hr):
        CP("pool", hTs[:, :, 1:128], h_[:, :, 0:127], [hr], ["hTs"])
        CP("pool", hTs[:, :, 0:1], hlast[:], ["hlast"], ["hTs"])
        CP("pool", hlast[:], h_[:, :, 127:128], [hr], ["hlast"])
        if KQ < 1:
            return
        for gi, (c0, n) in enumerate(((0, 512), (512, 512), (1024, 512), (1536, 128))):
            dst = (R2[:, 0:512], R2[:, 512:1024], K2[:, 0:512], K2[:, 512:640])[gi]
            nm = ("R2", "R2", "K2", "K2")[gi]
            for k in range(8):
                MM(dst, h_[:, k, :], Wom[:, k, c0:c0 + n], k == 0, False, ["Wm", hr], [nm])
            for k in range(8):
                MM(dst, hTs[:, k, :], Wm[:, k, c0:c0 + n], False, k == 7, ["Wm", "hTs"], [nm])
        CP("act", xmt[:, 0:1024], R2[:, :], ["R2"], ["xmt"])
        CP("dve", xmt[:, 1024:1664], K2[:, 0:640], ["K2"], ["xmt"])
        TR(F2[0:64, 0:128], xmt[:, 1536:1600], identf[:], ["xmt", "identf"], ["F2"])
        TR(F2[0:64, 128:256], xmt[:, 1600:1664], identf[:], ["xmt", "identf"], ["F2"])
        KW = int(os.environ.get('KW', '3'))
        if KW & 1:
            ACT(rw["tw"], F2[0:64, 0:128], AF.Tanh, ["F2"], ["tw"])
        if KW & 2:
            CP("dve", rw["adc"], F2[0:64, 128:256], ["F2"], ["adc"])
        if KR < 1:
            return
        R2v = R2[0:64, :].rearrange("p (h t) -> p h t", h=8)
        K2v = K2[0:64, :].rearrange("p (h t) -> p h t", h=8)
        V2v = V2[0:64, :].rearrange("p (h t) -> p h t", h=8)
        for h in range(8):
            MM(R2v[:, h, :], wupS[:, h * 64:(h + 1) * 64], rw["tw"], True, True, ["wupS", "tw"], ["R2"])
            MM(K2v[:, h, :], aupS[:, h * 64:(h + 1) * 64], rw["adc"], True, True, ["aupS", "adc"], ["K2"])
        TT("dve", rw["sg"], R2v, bc(fp["w0"][:].unsqueeze(2), [64, 8, 128]), ALU.add, ["R2", "fp_w0"], ["sg"])
        ACT(rw["sg"], rw["sg"], AF.Sigmoid, ["sg"], ["sg"])
        TT("dve", rw["asig"], K2v, bc(fp["a0"][:].unsqueeze(2), [64, 8, 128]), ALU.add, ["K2", "fp_a0"], ["asig"])
        ACT(rw["asig"], rw["asig"], AF.Sigmoid, ["asig"], ["asig"])
        TS("dve", rw["sg"], rw["sg"], -0.6065306597126334, None, ALU.mult, None, ["sg"], ["sg"])
        b.op("dve", lambda g: g.tensor_tensor_scan(out=rw["L"].rearrange("p h t -> p (h t)"), data0=resetm[:, :],
                                                   data1=rw["sg"].rearrange("p h t -> p (h t)"), initial=0.0,
                                                   op0=ALU.mult, op1=ALU.add), ["sg", "resetm"], ["L"])
        ACT(rw["g"], rw["L"], AF.Exp, ["L"], ["g"])
        ACT(rw["ginv"], rw["L"], AF.Exp, ["L"], ["ginv"], scale=-1.0)
        TT("pool", rw["gprev"], rw["L"], rw["sg"], ALU.subtract, ["L", "sg"], ["gprev"])
        ACT(rw["gprev"], rw["gprev"], AF.Exp, ["gprev"], ["gprev"])
        if KR < 2:
            return
        for h in range(8):
            TR(R2v[:, h, :], xmt[:, h * 64:(h + 1) * 64], identf[:], ["xmt", "identf"], ["R2"])
            TR(K2v[:, h, :], xmt[:, 512 + h * 64:512 + (h + 1) * 64], identf[:], ["xmt", "identf"], ["K2"])
        TT("dve", rw["L"], K2v, bc(fp["kk"][:].unsqueeze(2), [64, 8, 128]), ALU.mult, ["K2", "fp_kk"], ["L"])
        ACT(rw["t1"], rw["L"], AF.Square, ["L"], ["t1"])
        t1f = rw["t1"].rearrange("p h t -> p (h t)")
        for hh in range(2):
            MM(V2[0:64, hh * 512:(hh + 1) * 512], ones64[:, :], t1f[:, hh * 512:(hh + 1) * 512], True, True, ["ones64", "t1"], ["V2"])
        ACT(rw["t1"], V2v, AF.Sqrt, ["V2", "cst"], ["t1"], bias=cst[0:64, 2:3])
        b.op("dve", lambda g: g.reciprocal(out=rw["t1"], in_=rw["t1"]), ["t1"], ["t1"])
        TT("dve", rw["kkn"], rw["L"], rw["t1"], ALU.mult, ["L", "t1"], ["kkn"])
        STT(rw["t1"], rw["asig"], -1.0, bc(fp["ka"][:].unsqueeze(2), [64, 8, 128]), ALU.add, ALU.mult, ["asig", "fp_ka"], ["t1"])
        STT(rw["kmod"], rw["t1"], 1.0, K2v, ALU.add, ALU.mult, ["t1", "K2"], ["kmod"])
        if KR < 3:
            return
        QTv = QTt.rearrange("p h c (q t) -> p h c q t", q=2)
        KTv = KTt.rearrange("p h c (q t) -> p h c q t", q=2)

        def ch(v):
            return v.rearrange("p h (c t) -> p h c t", c=2)
        STT(QTv[:, :, :, 0, :], ch(rw["kkn"]), -1.0, ch(rw["gprev"]), ALU.mult, ALU.mult, ["kkn", "gprev"], ["QTt"])
        TT("dve", QTv[:, :, :, 1, :], ch(R2v), ch(rw["g"]), ALU.mult, ["R2", "g"], ["QTt"])
        TT("pool", rw["t1"], rw["kkn"], rw["asig"], ALU.mult, ["kkn", "asig"], ["t1"])
        TT("pool", KTv[:, :, :, 0, :], ch(rw["t1"]), ch(rw["ginv"]), ALU.mult, ["t1", "ginv"], ["KTt"])
        TT("pool", KTv[:, :, :, 1, :], ch(rw["kmod"]), ch(rw["ginv"]), ALU.mult, ["kmod", "ginv"], ["KTt"])
        TT("dve", rw["L"], R2v, bc(fp["rk"][:].unsqueeze(2), [64, 8, 128]), ALU.mult, ["R2", "fp_rk"], ["L"])
        TT("dve", rw["L"], rw["L"], rw["kmod"], ALU.mult, ["L", "kmod"], ["L"])
        for h in range(8):
            MM(F2[:, 256 + h:257 + h], rw["L"][:, h, :], ones64[:, 0:1], True, True, ["L", "ones64"], ["F2"])
        CP("dve", bon, F2[:, 256:264], ["F2"], ["bon"])
        if KR < 4:
            return
        CP("act", Vc[:, 0], xmt[0:64, 1024:1536].rearrange("p (h i) -> p h i", h=8), ["xmt"], ["Vc"])
        MM(V2[0:64, 0:512], identf[:, 64:128], xmt[:, 1024:1536], True, True, ["identf", "xmt"], ["V2"])
        CP("act", Vc[:, 1], V2[0:64, 0:512].rearrange("p (h i) -> p h i", h=8), ["V2"], ["Vc"])
        if int(os.environ.get("KLVL", "9")) < 3:
            return
        for q in range(4):
            c, hg = q // 2, q % 2
            bk, bkn = (K2, "K2") if q % 2 == 0 else (V2, "V2")
            AMp = bk[0:64, :].rearrange("p (h x) -> p h x", h=4)
            for hd in range(4):
                h = hg * 4 + hd
                MM(AMp[:, hd, 0:128], KTt[:, h, c, 0:64], QTt[:, h, c, :], True, True, ["KTt", "QTt"], [bkn])
                MM(AMp[:, hd, 128:256], KTt[:, h, c, 64:128], QTt[:, h, c, :], True, True, ["KTt", "QTt"], [bkn])
            TT("dve", AM[:, q * 4:(q + 1) * 4, :], AMp, bc(maskT[:].unsqueeze(1), [64, 4, 256]), ALU.mult, [bkn, "maskT"], ["AM"])
        Lp = R2[0:64, :].rearrange("p (h x) -> p h x", h=16)
        for q in range(4):
            c, hg = q // 2, q % 2
            for hd in range(4):
                h = hg * 4 + hd
                MM(Lp[:, q * 4 + hd, :], QTt[:, h, c, 0:64], KTt[:, h, c, 0:64], True, True, ["KTt", "QTt"], ["R2"])
        TT("dve", Lm[0], Lp, bc(maskL[:].unsqueeze(1), [64, 16, 64]), ALU.mult, ["R2", "maskL"], ["Lm0"])
        CP("act", Nm[0], AM[:, :, 0:64], ["AM"], ["Nm0"])
        TT("dve", Pm[0], AM[:, :, 0:64], bc(identb64.unsqueeze(1), [64, 16, 64]), ALU.add, ["AM", "identb"], ["Pm0"])
        cur = 0
        Np = K2[0:64, :].rearrange("p (h x) -> p h x", h=16)
        Lpp = V2[0:64, :].rearrange("p (h x) -> p h x", h=16)
        PPp = R2[0:64, :].rearrange("p (h x) -> p h x", h=16)
        for lvl in range(1, 6):
            nx = 1 - cur
            for i in range(16):
                if lvl < 5:
                    MM(Np[:, i, :], Lm[cur][:, i, :], Nm[cur][:, i, :], True, True, ["Lm%d" % cur, "Nm%d" % cur], ["K2"])
                MM(Lpp[:, i, :], Nm[cur][:, i, :], Lm[cur][:, i, :], True, True, ["Lm%d" % cur, "Nm%d" % cur], ["V2"])
            if lvl < 5:
                CP("act", Nm[nx], Np, ["K2"], ["Nm%d" % nx])
            CP("dve", Lm[nx], Lpp, ["V2"], ["Lm%d" % nx])
            for i in range(16):
                MM(PPp[:, i, :], Lm[nx][:, i, :], Pm[cur][:, i, :], True, True, ["Lm%d" % nx, "Pm%d" % cur], ["R2"])
            TT("dve", Pm[nx], PPp, Pm[cur], ALU.add, ["R2", "Pm%d" % cur], ["Pm%d" % nx])
            cur = nx
        P6 = Pm[cur]
        P6n = "Pm%d" % cur
        for c in range(2):
            BKp = PTb[0:64, :].rearrange("p (h q j) -> p h q j", h=8, q=2)
            for h in range(8):
                TR(BKp[:, h, 0, :], KTt[:, h, c, 0:64], identb64, ["KTt", "identb"], ["PTb"])
                TR(BKp[:, h, 1, :], KTt[:, h, c, 64:128], identb64, ["KTt", "identb"], ["PTb"])
            CP("act", BKtok, BKp, ["PTb"], ["BKtok"])
            P0p = F2[0:64, :].rearrange("p (h i) -> p h i", h=8)
            Up = R2[0:64, 0:512].rearrange("p (h i) -> p h i", h=8)
            Yp = K2[0:64, 0:512].rearrange("p (h i) -> p h i", h=8)
            Hp = V2[0:64, 0:512].rearrange("p (h i) -> p h i", h=8)

            def ai(h):
                return (c * 2 + h // 4) * 4 + h % 4
            for h in range(8):
                MM(P0p[:, h, :], QTt[:, h, c, 0:64], Hb[:, h, :], True, False, ["QTt", "Hb"], ["F2"])
                MM(P0p[:, h, :], AM[:, ai(h), 128:192], Vc[:, c, h, :], False, True, ["AM", "Vc"], ["F2"])
            CP("act", P0s, P0p, ["F2"], ["P0s"])
            for h in range(8):
                MM(Up[:, h, :], P6[:, ai(h), :], P0s[:, h, :], True, True, [P6n, "P0s"], ["R2"])
            CP("act", Us, Up, ["R2"], ["Us"])
            for h in range(8):
                MM(Yp[:, h, :], QTt[:, h, c, 64:128], Hb[:, h, :], True, False, ["QTt", "Hb"], ["K2"])
                MM(Yp[:, h, :], AM[:, ai(h), 64:128], Us[:, h, :], False, False, ["AM", "Us"], ["K2"])
                MM(Yp[:, h, :], AM[:, ai(h), 192:256], Vc[:, c, h, :], False, True, ["AM", "Vc"], ["K2"])
            for h in range(8):
                MM(Hp[:, h, :], BKtok[:, h, 0, :], Us[:, h, :], True, False, ["BKtok", "Us"], ["V2"])
                MM(Hp[:, h, :], BKtok[:, h, 1, :], Vc[:, c, h, :], False, True, ["BKtok", "Vc"], ["V2"])
            CP("act", ych, Yp, ["K2"], ["ych"])
            TT("dve", H32, H32, Hp, ALU.add, ["H32", "V2"], ["H32"])
            TT("dve", H32, H32, bc(rw["g"][:, :, c * 64 + 63:c * 64 + 64], [64, 8, 64]), ALU.mult, ["H32", "g"], ["H32"])
            CP("act", Hb, H32, ["H32"], ["Hb"])
            RED(st8[:, 0:8], ych, ALU.add, ["ych"], ["st8"])
            TT("dve", yt1, ych, ych, ALU.mult, ["ych"], ["yt1"])
            RED(st8[:, 8:16], yt1, ALU.add, ["yt1"], ["st8"])
            TS("dve", st8[:, 0:16], st8[:, 0:16], 1.0 / 64, None, ALU.mult, None, ["st8"], ["st8"])
            TT("dve", st8[:, 16:24], st8[:, 0:8], st8[:, 0:8], ALU.mult, ["st8"], ["st8"])
            TT("dve", st8[:, 24:32], st8[:, 8:16], st8[:, 16:24], ALU.subtract, ["st8"], ["st8"])
            ACT(st8[:, 32:40], st8[:, 24:32], AF.Sqrt, ["st8", "cst"], ["st8"], bias=cst[0:64, 1:2])
            b.op("dve", lambda g: g.reciprocal(out=st8[:, 40:48], in_=st8[:, 32:40]), ["st8"], ["st8"])
            TT("dve", yt1, ych, bc(st8[:, 0:8].unsqueeze(2), [64, 8, 64]), ALU.subtract, ["ych", "st8"], ["yt1"])
            TT("dve", yt1, yt1, bc(st8[:, 40:48].unsqueeze(2), [64, 8, 64]), ALU.mult, ["yt1", "st8"], ["yt1"])
            lnwv = lnw_bc[:].rearrange("p (h i) -> p h i", h=8)
            lnbv = lnb_bc[:].rearrange("p (h i) -> p h i", h=8)
            TT("dve", yt1, yt1, lnwv, ALU.mult, ["yt1", "lnw_bc"], ["yt1"])
            TT("pool", yt1, yt1, lnbv, ALU.add, ["yt1", "lnb_bc"], ["yt1"])
            MM(F2[0:64, 264:272], identf[:, c * 64:(c + 1) * 64], bon, True, True, ["identf", "bon"], ["F2"])
            CP("act", st8[:, 48:56], F2[0:64, 264:272], ["F2"], ["st8"])
            TT("dve", ych, Vc[:, c], bc(st8[:, 48:56].unsqueeze(2), [64, 8, 64]), ALU.mult, ["Vc", "st8"], ["ych"])
            TT("pool", yt1, yt1, ych, ALU.add, ["yt1", "ych"], ["yt1"])
            MM(R2[0:64, 0:512], identf[:, c * 64:(c + 1) * 64], gr_s[:, :], True, True, ["identf", "gr_s"], ["R2"])
            TT("dve", yt1.rearrange("p h i -> p (h i)"), yt1.rearrange("p h i -> p (h i)"), R2[0:64, 0:512], ALU.mult, ["yt1", "R2"], ["yt1"])
            b.dma("sp", rwscr[ti * 128 + c * 64: ti * 128 + (c + 1) * 64, :], yt1.rearrange("p h i -> p (h i)"), reads=["yt1"], writes=["rwscr"])
        b.barrier(dmas=False)

    if stop_after == "A":
        b.barrier(); b.emit(); ncd.__exit__(None, None, None); es.close()
        return nc
    b.barrier()
    MS("dve", H32, 0.0, ["H32"]); MS("dve", Hb, 0.0, ["Hb"]); MS("pool", hlast[:], 0.0, ["hlast"])
    KSUB = int(os.environ.get('KSUB', '9'))
    V2a = V2[:, 0:512]
    V2b = V2[:, 512:1024]
    for ti in range(NT if nt_lim is None else nt_lim):
        par = ti % 2
        x_, h_, xr, hr = front(ti, par)
        if KSUB >= 1:
            for (c0, n, dst) in ((0, 256, V2a[:, 0:256]), (256, 64, V2a[:, 256:320]), (320, 512, V2b)):
                for k in range(8):
                    MM(dst, h_[:, k, :], Wn[:, k, c0:c0 + n], k == 0, k == 7, [hr, "Wn"], ["V2"])
        if KSUB >= 2:
            kv3 = kfin[:].rearrange("p (g d) -> p g d", g=2)
            qknorm(V2a[:, 0:128].rearrange("p (g d) -> p g d", g=2), kv3, 2, knw_bc, 1.0, ["V2"], ["kfin"])
            rope("dve", kv3, 2, None, ["kfin"])
            CP("act", vfin[:], V2a[:, 128:256], ["V2"], ["vfin"])
            CP("act", Vaug[:, ti, :, 0:64], V2a[:, 128:256].rearrange("p (g d) -> p g d", g=2), ["V2"], ["Vaug"])
            CP("act", kifin[:], V2a[:, 256:320], ["V2"], ["kifin"])
            rope("pool", kifin[:].unsqueeze(1), 1, None, ["kifin"])
            ACT(gr_s[:], V2b, AF.Silu, ["V2"], ["gr_s"])
        if KSUB >= 3:
            b.dma("sp", k_nat[ti * 128:(ti + 1) * 128, :], kfin[:], reads=["kfin"])
            b.dma("sp", v_nat[ti * 128:(ti + 1) * 128, :], vfin[:], reads=["vfin"])
            b.dma("sp", ki_nat[ti * 128:(ti + 1) * 128, :], kifin[:], reads=["kifin"])
        if KSUB >= 4:
            for g_ in range(2):
                TR(F2[0:64, g_ * 128:(g_ + 1) * 128], kfin[:, g_ * 64:(g_ + 1) * 64], identf[:], ["kfin", "identf"], ["F2"])
            TR(F2[0:64, 256:384], kifin[:, :], identf[:], ["kifin", "identf"], ["F2"])
            if KSUB >= 5:
                CP("act", kT_all[:, :, ti * 128:(ti + 1) * 128], F2[0:64, 0:256].rearrange("p (g t) -> p g t", g=2), ["F2"], ["kT_all"])
            if KSUB >= 6:
                if os.environ.get("KV") == "1":
                    CP("act", nrm_t[0:64, 0:128], F2[0:64, 256:384], ["F2"], ["nrm_t"])
                elif os.environ.get("KV") == "2":
                    CP("act", kiT_all[:, ti * 128:(ti + 1) * 128], F2[0:64, 0:128], ["F2"], ["kiT_all"])
                else:
                    CP("act", kiT_all[:, ti * 128:(ti + 1) * 128], F2[0:64, 256:384], ["F2"], ["kiT_all"])

        if int(os.environ.get("KLVL", "9")) >= 2:
            rwkv_tile(ti, h_, hr)
        if ti == NT - 1:
            for gi, (c0, n) in enumerate(((0, 512), (512, 512), (1024, 512), (1536, 128))):
                dst = (R2[0:1, 0:512], R2[0:1, 512:1024], K2[0:1, 0:512], K2[0:1, 512:640])[gi]
                nm = ("R2", "R2", "K2", "K2")[gi]
                for k in range(8):
                    MM(dst, h_[:, k, 127:128], Wom[:, k, c0:c0 + n], k == 0, False, ["Wm", hr], [nm])
                for k in range(8):
                    MM(dst, h_[:, k, 127:128], Wm[:, k, c0:c0 + n], False, k == 7, ["Wm", hr], [nm])
            CP("act", xmt[0:1, 0:1024], R2[0:1, :], ["R2"], ["xmt"])
            CP("dve", xmt[0:1, 1024:1664], K2[0:1, 0:640], ["K2"], ["xmt"])
            b.dma("sp", shift_p.rearrange("(a n) -> a n", a=1), xmt[0:1, :], reads=["xmt"])
    for h in range(8):
        TR(F2[0:64, h * 64:(h + 1) * 64], H32[:, h, :], identf[0:64, 0:64], ["H32", "identf"], ["F2"])
    CP("dve", ych, F2[0:64, 0:512].rearrange("p (h j) -> p h j", h=8), ["F2"], ["ych"])
    b.dma("sp", wkv_p.rearrange("h i j -> i h j"), ych, reads=["ych"])

    if stop_after == "B":
        b.barrier(); b.emit(); ncd.__exit__(None, None, None); es.close()
        return nc
    b.barrier()
    off = 0
    Wq, off = carve(off, [128, 8, 1544], BF16)
    stg, off = carve(off, [128, 8, 512])
    score, off = carve(off, [128, T])
    selm, off = carve(off, [128, T], BF16)
    selT, off = carve(off, [128, NT, 128], BF16)
    rl, off = carve(off, [128, 512], BF16)
    rl2, off = carve(off, [128, 512], BF16)
    rlf, off = carve(off, [128, 256])
    diagw, off = carve(off, [128, 8, 128], BF16)
    qfin, off = carve(off, [128, 512])
    qifin, off = carve(off, [128, 512])
    qT, off = carve(off, [64, 8, 128], BF16)
    qiT, off = carve(off, [64, 8, 128], BF16)
    ga, off = carve(off, [128, 512])
    eT, off = carve(off, [128, 4, 128], BF16)
    pTt, off = carve(off, [128, 4, 128], BF16)
    eT2, off = carve(off, [128, 4, 128], BF16)
    pTt2, off = carve(off, [128, 4, 128], BF16)
    cat, off = carve(off, [128, D], BF16)
    catT, off = carve(off, [128, 8, 128], BF16)
    rwo, off = carve(off, [128, 512])
    rwo2, off = carve(off, [128, 512])
    att, off = carve(off, [128, 8, 64])
    ybuf, off = carve(off, [128, D])
    bs, off = carve(off, [128, 16])
    wis, off = carve(off, [128, 8])
    oacc, off = carve(off, [128, 2, 4, 65])
    Wout, off = carve(off, [128, 8, D], BF16)
    gate_bc, off = carve(off, [128, D])
    iota256, off = carve(off, [128, 256])
    assert off <= AW, off
    b.dma("sp", gate_bc, gscr[16:17, :].partition_broadcast(128) if False else gscr[16, :].partition_broadcast(128), reads=["gscr"], writes=["gate_bc"])
    b.dma("sp", iota256, iota_d[:, :], writes=["iota256"])
    load_cols(Wq, 0, C_Q, 512, tag="Wq")
    load_cols(Wq, 512, C_QI, 520, tag="Wq")
    load_cols(Wq, 1032, C_GA, 512, tag="Wq")
    w_out_v = w_out.rearrange("(k p) c -> p k c", p=128)
    for hh in range(2):
        b.dma("sp", stg[:, :, :], w_out_v[:, :, hh * 512:(hh + 1) * 512], writes=["stg"])
        CP("pool", Wout[:, :, hh * 512:(hh + 1) * 512], stg[:, :, :], ["stg"], ["Wout"])


    for j in range(NO if no_lim is None else no_lim):
        ti = NT + j
        x_, h_, xr, hr = front(ti, 0)
        NKT = 2 * (j + 1)
        NK = NKT * 128
        for (c0, n, dst, nm) in ((0, 512, R2[:, 0:512], "R2a"), (512, 512, R2[:, 512:1024], "R2b"),
                                 (1024, 8, F2[:, 0:8], "F2"), (1032, 512, K2[:, 0:512], "K2a")):
            for k in range(8):
                MM(dst, h_[:, k, :], Wq[:, k, c0:c0 + n], k == 0, k == 7, [hr, "Wq"], [nm])
        q3 = qfin.rearrange("p (h d) -> p h d", h=8)
        qknorm(R2[:, 0:512].rearrange("p (h d) -> p h d", h=8), q3, 8, qnw_bc, 0.125, ["R2a"], ["qfin"])
        rope("dve", q3, 8, None, ["qfin"])
        qi3 = qifin.rearrange("p (h d) -> p h d", h=8)
        CP("act", qifin, R2[:, 512:1024], ["R2b"], ["qifin"])
        rope("pool", qi3, 8, None, ["qifin"])
        TS("dve", wis, F2[:, 0:8], 0.044194173824159216, None, ALU.mult, None, ["F2"], ["wis"])
        ACT(ga, K2[:, 0:512], AF.Silu, ["K2a"], ["ga"])
        for (src, srcn, dstT, dn) in ((qfin, "qfin", qT, "qT"), (qifin, "qifin", qiT, "qiT")):
            pv = K2[0:64, :].rearrange("p (h t) -> p h t", h=8)
            for h in range(8):
                TR(pv[:, h, :], src[:, h * 64:(h + 1) * 64], identf[:], [srcn, "identf"], ["K2a" if h < 4 else "K2b"])
            CP("act", dstT, pv, ["K2a", "K2b"], [dn])
        TT("dve", diagw, bc(identb[:].unsqueeze(1), [128, 8, 128]), bc(wis.unsqueeze(2), [128, 8, 128]), ALU.mult, ["identb", "wis"], ["diagw"])
        nchk = (NK + 511) // 512
        ib = 0
        for kc in range(nchk):
            w = min(512, NK - kc * 512)
            pend = None
            for h in range(8):
                pb = (R2[:, 0:512], R2[:, 512:1024])[ib % 2]
                pbn = ("R2a", "R2b")[ib % 2]
                rlb = (rl, rl2)[ib % 2]
                rln = ("rl", "rl2")[ib % 2]
                ib += 1
                MM(pb[:, 0:w], qiT[:, h, :], kiT_all[:, kc * 512:kc * 512 + w], True, True, ["qiT", "kiT_all"], [pbn])
                ACT(rlb[:, 0:w], pb[:, 0:w], AF.Relu, [pbn], [rln])
                if pend is not None:
                    ph, prl, prn = pend
                    MM(F2[:, 0:w], diagw[:, ph, :], prl[:, 0:w], ph == 0, False, ["diagw", prn], ["F2"])
                pend = (h, rlb, rln)
            ph, prl, prn = pend
            MM(F2[:, 0:w], diagw[:, ph, :], prl[:, 0:w], False, True, ["diagw", prn], ["F2"])
            CP("dve", score[:, kc * 512:kc * 512 + w], F2[:, 0:w], ["F2"], ["score"])
        RED(bs[:, 0:1], score[:, 0:NK], ALU.max, ["score"], ["bs"])
        RED(bs[:, 1:2], score[:, 0:NK], ALU.min, ["score"], ["bs"])
        TS("dve", rlf, iota256, qrel[:, 0:1], -1e30, ALU.is_gt, ALU.mult, ["iota256", "qrel"], ["rlf"])
        TT("dve", score[:, NK - 256:NK], score[:, NK - 256:NK], rlf, ALU.add, ["score", "rlf"], ["score"])
        TS("dve", bs[:, 2:3], bs[:, 1:2], -1.0, None, ALU.add, None, ["bs"], ["bs"])
        STT(bs[:, 3:4], bs[:, 0:1], 2.0, bs[:, 1:2], ALU.add, ALU.subtract, ["bs"], ["bs"])
        for it in range(1, n_bis + 1):
            sc_ = float(2.0 ** (-it))
            STT(bs[:, 4:5], bs[:, 3:4], sc_, bs[:, 2:3], ALU.mult, ALU.add, ["bs"], ["bs"])
            TS("dve", selm[:, 0:NK], score[:, 0:NK], bs[:, 4:5], 0.0, ALU.is_gt, ALU.add, ["score", "bs"], ["selm", "bs"], accum=bs[:, 5:6])
            TS("dve", bs[:, 6:7], bs[:, 5:6], float(topk_p) - 0.5, bs[:, 3:4], ALU.is_gt, ALU.mult, ["bs"], ["bs"])
            STT(bs[:, 2:3], bs[:, 6:7], sc_, bs[:, 2:3], ALU.mult, ALU.add, ["bs"], ["bs"])
        TS("dve", selm[:, 0:NK], score[:, 0:NK], bs[:, 2:3], None, ALU.is_gt, None, ["score", "bs"], ["selm"])
        for kt in range(NKT):
            TR(PTb[:, (kt % 8) * 128:(kt % 8 + 1) * 128], selm[:, kt * 128:(kt + 1) * 128], identb[:], ["selm", "identb"], ["PTb"])
            if kt % 8 == 7 or kt == NKT - 1:
                k0 = (kt // 8) * 8
                n_ = kt - k0 + 1
                ACT(selT[:, k0:k0 + n_, :], PTb[:, 0:n_ * 128].rearrange("p (a t) -> p a t", a=n_), AF.Identity, ["PTb", "cst"], ["selT"],
                    scale=30000.0, bias=cst[:, 3:4])
        po = [V2[:, 0:260].rearrange("p (h e) -> p h e", h=4), V2[:, 512:772].rearrange("p (h e) -> p h e", h=4)]
        def att_front(kt, g_):
            lp = K2[:, g_ * 512:(g_ + 1) * 512]
            kn_ = ("K2a", "K2b")[g_]
            pTb = (pTt, pTt2)[g_]
            pn_ = ("pTt", "pTt2")[g_]
            MM(lp, kT_all[:, g_, kt * 128:(kt + 1) * 128], qT[:, g_ * 4:(g_ + 1) * 4, :].rearrange("p h t -> p (h t)"),
               True, False, ["kT_all", "qT"], [kn_])
            for hh in range(4):
                MM(lp[:, hh * 128:(hh + 1) * 128], identb[:], selT[:, kt, :], False, hh == 3, ["identb", "selT"], [kn_])
            ACT(pTb, lp.rearrange("p (h t) -> p h t", h=4), AF.Exp, [kn_], [pn_])

        def att_back(kt, g_):
            vn_ = ("V2a", "V2b")[g_]
            pTb = (pTt, pTt2)[g_]
            pn_ = ("pTt", "pTt2")[g_]
            on_ = ("oacc0", "oacc1")[g_]
            for hh in range(4):
                MM(po[g_][:, hh, :], pTb[:, hh, :], Vaug[:, kt, g_, :], True, True, [pn_, "Vaug"], [vn_])
            if kt == 0:
                CP("act", oacc[:, g_], po[g_], [vn_], [on_])
            else:
                TT("dve", oacc[:, g_], oacc[:, g_], po[g_], ALU.add, [vn_, on_], [on_])
        att_front(0, 0)
        for kt in range(NKT):
            att_front(kt, 1)
            att_back(kt, 0)
            if kt + 1 < NKT:
                att_front(kt + 1, 0)
            att_back(kt, 1)
        for g_ in range(2):
            b.op("dve", lambda g, g_=g_: g.reciprocal(out=bs[:, 8 + g_ * 4:12 + g_ * 4], in_=oacc[:, g_, :, 64]), ["oacc0", "oacc1"], ["bs"])
            TT("dve", att[:, g_ * 4:(g_ + 1) * 4, :], oacc[:, g_, :, 0:64], bc(bs[:, 8 + g_ * 4:12 + g_ * 4].unsqueeze(2), [128, 4, 64]),
               ALU.mult, ["oacc0", "oacc1", "bs"], ["att"])
        TT("dve", cat[:, 0:512], att.rearrange("p h d -> p (h d)"), ga, ALU.mult, ["att", "ga"], ["cat"])
        b.dma("sp", rwo, rwscr[(2 * j) * 128:(2 * j + 1) * 128, :], reads=["rwscr"], writes=["rwo"])
        b.dma("sp", rwo2, rwscr[(2 * j + 1) * 128:(2 * j + 2) * 128, :], reads=["rwscr"], writes=["rwo2"])
        TS("dve", rwo, rwo, parsel[:, 1:2], None, ALU.mult, None, ["rwo", "parsel"], ["rwo"])
        STT(rwo, rwo2, parsel[:, 0:1], rwo, ALU.mult, ALU.add, ["rwo2", "parsel", "rwo"], ["rwo"])
        CP("act", cat[:, 512:1024], rwo, ["rwo"], ["cat"])
        for k in range(8):
            TR(PTb[:, k * 128:(k + 1) * 128], cat[:, k * 128:(k + 1) * 128], identb[:], ["cat", "identb"], ["PTb"])
        CP("act", catT, PTb[:, :].rearrange("p (k t) -> p k t", k=8), ["PTb"], ["catT"])
        for hh in range(2):
            for k in range(8):
                MM(R2[:, hh * 512:(hh + 1) * 512], catT[:, k, :], Wout[:, k, hh * 512:(hh + 1) * 512], k == 0, k == 7, ["catT", "Wout"], [("R2a", "R2b")[hh]])
        TT("dve", ybuf, R2[:, :], gate_bc, ALU.mult, ["R2a", "R2b", "gate_bc"], ["ybuf"])
        TT("pool", ybuf, ybuf, x_[:], ALU.add, ["ybuf", xr], ["ybuf"])
        b.dma("sp", y_own[j * 128:(j + 1) * 128, :], ybuf, reads=["ybuf"])


    if do_sample:
        b.barrier()
        PW = 10500
        off = 0
        proj, off = carve(off, [16, DIN])
        tk = {}
        for nm in ("qs", "ga", "grs", "ta", "tb"):
            tk[nm], off = carve(off, [16, 512])
        ks_, off = carve(off, [16, 128])
        s16, off = carve(off, [16, 64])
        tokd, off = carve(off, [16, 1040])
        ysb, off = carve(off, [16, D])
        cats, off = carve(off, [16, D], BF16)
        catTs, off = carve(off, [128, 8, 16], BF16)
        assert off <= PW, off
        off = PW
        stg, off = carve(off, [128, 8, 512])
        wbf, off = carve(off, [128, 8, 512], BF16)
        sshift_t, off = carve(off, [16, SHW])
        mu16, off = carve(off, [16, SHW])
        X1 = off
        xm, off = carve(off, [16, SHW])
        prm, off = carve(off, [16, 5, 512])
        vecs, off = carve(off, [16, 8, 6, 64])
        for nm in ("dec", "asg", "kkv", "kkn", "kmod"):
            tk[nm], off = carve(off, [16, 512])
        wdt, off = carve(off, [16, 128])
        wdT, off = carve(off, [64, 32])
        assert off <= AW, off
        NPAIR = NS // 2

        x_, h_, xr, hr = front(NT + NO, 0, m_prompt=False, ntok=16)
        for ch in range(8):
            c0 = ch * 505
            b.dma("sp", stg[:, :, 0:505], w_in_v[:, :, c0:c0 + 505], writes=["stg"])
            CP("dve", wbf[:, :, 0:505], stg[:, :, 0:505], ["stg"], ["wbf"])
            for k in range(8):
                MM(R2[0:16, 0:505], h_[:, k, 0:16], wbf[:, k, 0:505], k == 0, k == 7, [hr, "wbf"], ["R2"])
            CP("act", proj[:, c0:c0 + 505], R2[0:16, 0:505], ["R2"], ["proj"])
        b.dma("sp", sshift_t, sshift_d[:, :], writes=["sshift"])
        b.dma("sp", mu16, mu.partition_broadcast(16), writes=["mu16"])
        for i_, src in enumerate((pw0, pa0, pkk, pka, prk)):
            b.dma("sp", prm[:, i_, :], src.partition_broadcast(16), writes=["prm"])
        qs3 = tk["qs"].rearrange("p (h d) -> p h d", h=8)
        qknorm(proj[:, 0:512].rearrange("p (h d) -> p h d", h=8), qs3, 8, qnw_bc, 0.125, ["proj"], ["qs"], nrows=16)
        rope("dve", qs3, 8, None, ["qs"], nrows=16)
        ks3 = ks_.rearrange("p (g d) -> p g d", g=2)
        qknorm(proj[:, 512:640].rearrange("p (g d) -> p g d", g=2), ks3, 2, knw_bc, 1.0, ["proj"], ["ks"], nrows=16)
        rope("dve", ks3, 2, None, ["ks"], nrows=16)
        b.dma("sp", k_s[:, :], ks_, reads=["ks"])
        b.dma("sp", v_s[:, :], proj[:, 640:768], reads=["proj"])
        b.dma("sp", shift_s[:, :], proj[:, C_R:C_R + SHW], reads=["proj"])
        rope("dve", proj[:, 768:1280].rearrange("p (h d) -> p h d", h=8), 8, None, ["proj"], nrows=16)
        rope("dve", proj[:, 1288:1352].unsqueeze(1), 1, None, ["proj"], nrows=16)
        b.dma("sp", ki_s[:, :], proj[:, 1288:1352], reads=["proj"])
        ACT(tk["ga"], proj[:, C_GA:C_GA + 512], AF.Silu, ["proj"], ["ga"])
        ACT(tk["grs"], proj[:, C_GR:C_GR + 512], AF.Silu, ["proj"], ["grs"])
        xs_ = proj[:, C_R:C_R + SHW]
        TT("dve", xm, sshift_t, xs_, ALU.subtract, ["sshift", "proj"], ["xm"])
        TT("dve", xm, xm, mu16, ALU.mult, ["xm", "mu16"], ["xm"])
        TT("dve", xm, xm, xs_, ALU.add, ["xm", "proj"], ["xm"])
        ACT(wdt[:, 0:64], xm[:, 1536:1600], AF.Tanh, ["xm"], ["wdt"])
        CP("dve", wdt[:, 64:128], xm[:, 1600:1664], ["xm"], ["wdt"])
        TR(F2[0:64, 0:16], wdt[:, 0:64], identf[0:16, 0:16], ["wdt", "identf"], ["F2"])
        TR(F2[0:64, 16:32], wdt[:, 64:128], identf[0:16, 0:16], ["wdt", "identf"], ["F2"])
        CP("dve", wdT, F2[0:64, 0:32], ["F2"], ["wdT"])
        MM(R2[0:16, 0:512], wdT[:, 0:16], wupS[:, :], True, True, ["wdT", "wupS"], ["R2"])
        MM(R2[0:16, 512:1024], wdT[:, 16:32], aupS[:, :], True, True, ["wdT", "aupS"], ["R2"])
        TT("dve", tk["dec"], R2[0:16, 0:512], prm[:, 0, :], ALU.add, ["R2", "prm"], ["dec"])
        ACT(tk["dec"], tk["dec"], AF.Sigmoid, ["dec"], ["dec"])
        ACT(tk["dec"], tk["dec"], AF.Exp, ["dec"], ["dec"], scale=-0.6065306597126334)
        TT("dve", tk["asg"], R2[0:16, 512:1024], prm[:, 1, :], ALU.add, ["R2", "prm"], ["asg"])
        ACT(tk["asg"], tk["asg"], AF.Sigmoid, ["asg"], ["asg"])
        xr_, xk_, xv_ = xm[:, 0:512], xm[:, 512:1024], xm[:, 1024:1536]
        TT("dve", tk["kkv"], xk_, prm[:, 2, :], ALU.mult, ["xm", "prm"], ["kkv"])
        ACT(tk["ta"], tk["kkv"], AF.Square, ["kkv"], ["ta"])
        RED(s16[:, 0:8], tk["ta"].rearrange("p (h d) -> p h d", h=8), ALU.add, ["ta"], ["s16"])
        ACT(s16[:, 8:16], s16[:, 0:8], AF.Sqrt, ["s16", "cst"], ["s16"], bias=cst[0:16, 2:3])
        b.op("dve", lambda g: g.reciprocal(out=s16[:, 16:24], in_=s16[:, 8:16]), ["s16"], ["s16"])
        TT("dve", tk["kkn"].rearrange("p (h d) -> p h d", h=8), tk["kkv"].rearrange("p (h d) -> p h d", h=8),
           bc(s16[:, 16:24].unsqueeze(2), [16, 8, 64]), ALU.mult, ["kkv", "s16"], ["kkn"])
        STT(tk["ta"], tk["asg"], -1.0, prm[:, 3, :], ALU.add, ALU.mult, ["asg", "prm"], ["ta"])
        STT(tk["kmod"], tk["ta"], 1.0, xk_, ALU.add, ALU.mult, ["ta", "xm"], ["kmod"])

        def v8(ap):
            return ap.rearrange("p (h d) -> p h d", h=8)
        CP("dve", vecs[:, :, 0, :], v8(tk["dec"]), ["dec"], ["vecs"])
        TS("dve", vecs[:, :, 1, :], v8(tk["kkn"]), -1.0, None, ALU.mult, None, ["kkn"], ["vecs"])
        TT("dve", vecs[:, :, 2, :], v8(tk["kkn"]), v8(tk["asg"]), ALU.mult, ["kkn", "asg"], ["vecs"])
        CP("dve", vecs[:, :, 3, :], v8(tk["kmod"]), ["kmod"], ["vecs"])
        CP("dve", vecs[:, :, 4, :], v8(xr_), ["xm"], ["vecs"])
        CP("dve", vecs[:, :, 5, :], v8(xv_), ["xm"], ["vecs"])
        TT("dve", tk["ta"], xr_, prm[:, 4, :], ALU.mult, ["xm", "prm"], ["ta"])
        TT("dve", tk["ta"], tk["ta"], tk["kmod"], ALU.mult, ["ta", "kmod"], ["ta"])
        RED(s16[:, 24:32], v8(tk["ta"]), ALU.add, ["ta"], ["s16"])
        b.dma("sp", scr1[:, :], vecs.rearrange("p h v j -> p (h v j)"), reads=["vecs"], writes=["scr1"])
        b.barrier()
        off = PW
        S_, off = carve(off, [128, 4096])
        tmpS, off = carve(off, [128, 4096])
        vsh, off = carve(off, [128, 384])
        ysh, off = carve(off, [128, 128])
        assert off <= X1
        b.dma("sp", S_, swkv_d[:, :], writes=["S"])
        b.dma("sp", vsh, scr1.rearrange("s (h x) -> (s h) x", h=8), reads=["scr1"], writes=["vsh"])
        S3 = S_.rearrange("p (i j) -> p i j", i=64)
        T3 = tmpS.rearrange("p (i j) -> p i j", i=64)

        def jb(vi):
            return bc(vsh[:, vi * 64:(vi + 1) * 64].unsqueeze(1), [128, 64, 64])

        def ib(ap):
            return bc(ap.unsqueeze(2), [128, 64, 64])
        TT("dve", T3, S3, jb(1), ALU.mult, ["S", "vsh"], ["tmpS"])
        RED(ysh[:, 0:64], T3, ALU.add, ["tmpS"], ["ysh"])
        TT("dve", S3, S3, jb(0), ALU.mult, ["S", "vsh"], ["S"])
        TT("dve", T3, jb(2), ib(ysh[:, 0:64]), ALU.mult, ["vsh", "ysh"], ["tmpS"])
        TT("dve", S3, S3, T3, ALU.add, ["S", "tmpS"], ["S"])
        TT("dve", T3, jb(3), ib(vsh[:, 320:384]), ALU.mult, ["vsh"], ["tmpS"])
        TT("dve", S3, S3, T3, ALU.add, ["S", "tmpS"], ["S"])
        b.dma("sp", wkv_s[:, :], S_, reads=["S"])
        TT("dve", T3, S3, jb(4), ALU.mult, ["S", "vsh"], ["tmpS"])
        RED(ysh[:, 64:128], T3, ALU.add, ["tmpS"], ["ysh"])
        b.dma("sp", scr2[:, :], ysh[:, 64:128], reads=["ysh"], writes=["scr2"])
        yS = tk["tb"]
        b.dma("sp", yS, scr2.rearrange("(s h) i -> s (h i)", h=8), reads=["scr2"], writes=["tb"])
        y3 = v8(yS)
        RED(s16[:, 32:40], y3, ALU.add, ["tb"], ["s16"])
        ACT(tk["ta"], yS, AF.Square, ["tb"], ["ta"])
        RED(s16[:, 40:48], v8(tk["ta"]), ALU.add, ["ta"], ["s16"])
        TS("dve", s16[:, 32:48], s16[:, 32:48], 1.0 / 64, None, ALU.mult, None, ["s16"], ["s16"])
        TT("dve", s16[:, 48:56], s16[:, 32:40], s16[:, 32:40], ALU.mult, ["s16"], ["s16"])
        TT("dve", s16[:, 48:56], s16[:, 40:48], s16[:, 48:56], ALU.subtract, ["s16"], ["s16"])
        ACT(s16[:, 56:64], s16[:, 48:56], AF.Sqrt, ["s16", "cst"], ["s16"], bias=cst[0:16, 1:2])
        b.op("dve", lambda g: g.reciprocal(out=s16[:, 56:64], in_=s16[:, 56:64]), ["s16"], ["s16"])
        TT("dve", y3, y3, bc(s16[:, 32:40].unsqueeze(2), [16, 8, 64]), ALU.subtract, ["tb", "s16"], ["tb"])
        TT("dve", y3, y3, bc(s16[:, 56:64].unsqueeze(2), [16, 8, 64]), ALU.mult, ["tb", "s16"], ["tb"])
        TT("dve", yS, yS, lnw_bc[0:16, :], ALU.mult, ["tb", "lnw_bc"], ["tb"])
        TT("dve", yS, yS, lnb_bc[0:16, :], ALU.add, ["tb", "lnb_bc"], ["tb"])
        TT("dve", v8(tk["ta"]), v8(xv_), bc(s16[:, 24:32].unsqueeze(2), [16, 8, 64]), ALU.mult, ["xm", "s16"], ["ta"])
        TT("dve", yS, yS, tk["ta"], ALU.add, ["tb", "ta"], ["tb"])
        TT("dve", cats[:, 512:1024], yS, tk["grs"], ALU.mult, ["tb", "grs"], ["cats"])
        b.barrier()
        NCAND = 16
        off = PW
        Gi, off = carve(off, [128, 8192])
        tmpG, off = carve(off, [128, 64, 64])
        Kc, off = carve(off, [128, NCAND, 128])
        Vcd, off = carve(off, [128, NCAND, 128])
        tmpc, off = carve(off, [128, NCAND, 64])
        repd, off = carve(off, [128, 1040])
        opd, off = carve(off, [128, 520])
        repS, off = carve(off, [16, 1024])
        repTS, off = carve(off, [128, 128])
        blkS, off = carve(off, [128, 128])
        sc, off = carve(off, [128, 132])
        msc, off = carve(off, [128, 132])
        sh_, off = carve(off, [128, 128])
        cv, off = carve(off, [128, NCAND])
        ci, off = carve(off, [128, NCAND], I32)
        cif, off = carve(off, [128, NCAND])
        rowi, off = carve(off, [128, NCAND], I32)
        lg, off = carve(off, [128, 8, NCAND])
        b2, off = carve(off, [128, 16])
        ptab, off = carve(off, [128, 8], I32)
        ptf, off = carve(off, [128, 8])
        oh0, off = carve(off, [128, 1])
        assert off <= AW, off
        Gi3 = Gi.rearrange("p (t d) -> p t d", t=128)
        for (t_, d_, nm) in ((ptab, ptab_d, "ptab"), (repS, rep_d, "repS"), (repTS, repT_d, "repTS"), (blkS, blk_d, "blkS"), (oh0, oh0_d, "oh0")):
            b.dma("sp", t_, d_[:, :], writes=[nm])
        CP("dve", tokd[:, 0:512], proj[:, 768:1280], ["proj"], ["tokd"])
        TS("dve", tokd[:, 512:520], proj[:, 1280:1288], 0.044194173824159216, None, ALU.mult, None, ["proj"], ["tokd"])
        CP("dve", tokd[:, 520:1032], tk["qs"], ["qs"], ["tokd"])
        TT("dve", v8(tk["ta"]), v8(tokd[:, 0:512]), bc(proj[:, 1288:1352].unsqueeze(1), [16, 8, 64]), ALU.mult, ["tokd", "proj"], ["ta"])
        RED(s16[:, 0:8], v8(tk["ta"]), ALU.add, ["ta"], ["s16"])
        TS("dve", s16[:, 0:8], s16[:, 0:8], 0.0, None, ALU.max, None, ["s16"], ["s16"])
        TT("dve", s16[:, 0:8], s16[:, 0:8], tokd[:, 512:520], ALU.mult, ["s16", "tokd"], ["s16"])
        RED(tokd[:, 1032:1033], s16[:, 0:8], ALU.add, ["s16"], ["tokd"])
        CP("dve", ptf, ptab, ["ptab"], ["ptf"])
        TS("dve", ptf, ptf, 128.0, None, ALU.mult, None, ["ptf"], ["ptf"])
        ck_rows = cache_k
        cv_rows = cache_v
        cvA, off = carve(off, [128, 8, NCAND])
        ciA, off = carve(off, [128, 8, NCAND], I32)
        thrA, off = carve(off, [128, 8])
        tmpGf = tmpG.rearrange("p a b -> p (a b)")
        cand16 = tmpGf[0:16, 0:1540]
        candj = tmpGf[0:16, 1540:3080]
        assert off <= AW, off
        for sp in range(NPAIR):
            b.op("pool", lambda g, sp=sp: g.indirect_dma_start(out=Gi, out_offset=None, in_=cache_ki[:, :],
                                                                in_offset=bass.IndirectOffsetOnAxis(ap=ptab[:, sp:sp + 1], axis=0)),
                 ["ptab"], ["Gi"], dma=True)
            for (c0, n) in ((0, 512), (512, 8)):
                MM(K2[:, 0:n], repS[:, sp * 128:(sp + 1) * 128], tokd[:, c0:c0 + n], True, True, ["repS", "tokd"], ["K2"])
                CP("act", repd[:, c0:c0 + n], K2[:, 0:n], ["K2"], ["repd"])
            for h in range(8):
                for hf in range(2):
                    TT("dve", tmpG, Gi3[:, hf * 64:(hf + 1) * 64, :], bc(repd[:, h * 64:(h + 1) * 64].unsqueeze(1), [128, 64, 64]), ALU.mult, ["Gi", "repd"], ["tmpG"])
                    RED(sh_[:, hf * 64:(hf + 1) * 64], tmpG, ALU.add, ["tmpG"], ["sh"])
                if h == 0:
                    TS("dve", sc[:, 0:128], sh_, 0.0, repd[:, 512:513], ALU.max, ALU.mult, ["sh", "repd"], ["sc"])
                else:
                    TS("dve", sh_, sh_, 0.0, repd[:, 512 + h:513 + h], ALU.max, ALU.mult, ["sh", "repd"], ["sh"])
                    TT("dve", sc[:, 0:128], sc[:, 0:128], sh_, ALU.add, ["sc", "sh"], ["sc"])
            for r_ in range(NCAND // 8):
                b.op("dve", lambda g, r_=r_, sp=sp: g.max(out=cvA[:, sp, r_ * 8:(r_ + 1) * 8], in_=sc[:, 0:128]), ["sc"], ["cvA"])
                b.op("dve", lambda g, r_=r_, sp=sp: g.max_index(out=ciA[:, sp, r_ * 8:(r_ + 1) * 8].bitcast(mybir.dt.uint32),
                                                                in_max=cvA[:, sp, r_ * 8:(r_ + 1) * 8], in_values=sc[:, 0:128]), ["sc", "cvA"], ["ciA"])
                if r_ < NCAND // 8 - 1:
                    b.op("dve", lambda g, r_=r_, sp=sp: g.match_replace(out=sc[:, 0:128], in_to_replace=cvA[:, sp, r_ * 8:(r_ + 1) * 8],
                                                                        in_values=sc[:, 0:128], imm_value=-3e30), ["sc", "cvA"], ["sc"])
        b.dma("sp", scr4.rearrange("(sp s2) g c -> (s2 g) sp c", s2=2), cvA, reads=["cvA"], writes=["scr4"])
        b.dma("sp", cand16[:, 0:64 * NCAND], scr4.rearrange("s g c -> s (g c)"), reads=["scr4"], writes=["tmpG"])
        CP("dve", cand16[:, 64 * NCAND:64 * NCAND + 1], tokd[:, 1032:1033], ["tokd"], ["tmpG"])
        cnd = cand16[:, 0:64 * NCAND + 1]
        RED(s16[:, 40:41], cnd, ALU.max, ["tmpG"], ["s16"])
        RED(s16[:, 41:42], cnd, ALU.min, ["tmpG"], ["s16"])
        TS("dve", s16[:, 42:43], s16[:, 41:42], -1.0, None, ALU.add, None, ["s16"], ["s16"])
        STT(s16[:, 43:44], s16[:, 40:41], 2.0, s16[:, 41:42], ALU.add, ALU.subtract, ["s16"], ["s16"])
        for it in range(1, n_bis + 2):
            sc_ = float(2.0 ** (-it))
            STT(s16[:, 44:45], s16[:, 43:44], sc_, s16[:, 42:43], ALU.mult, ALU.add, ["s16"], ["s16"])
            TS("dve", candj[:, 0:64 * NCAND + 1], cnd, s16[:, 44:45], 0.0, ALU.is_gt, ALU.add, ["tmpG", "s16"], ["tmpG", "s16"], accum=s16[:, 45:46])
            TS("dve", s16[:, 46:47], s16[:, 45:46], float(topk_s) - 0.5, s16[:, 43:44], ALU.is_gt, ALU.mult, ["s16"], ["s16"])
            STT(s16[:, 42:43], s16[:, 46:47], sc_, s16[:, 42:43], ALU.mult, ALU.add, ["s16"], ["s16"])
        TT("dve", s16[:, 32:33], tokd[:, 1032:1033], s16[:, 42:43], ALU.is_gt, ["tokd", "s16"], ["s16"])
        for sp in range(NPAIR):
            MM(F2[:, sp:sp + 1], repS[:, sp * 128:(sp + 1) * 128], s16[:, 42:43], True, True, ["repS", "s16"], ["F2"])
        CP("dve", thrA, F2[:, 0:8], ["F2"], ["thrA"])
        for sp in range(NPAIR):
            MM(K2[:, 0:512], repS[:, sp * 128:(sp + 1) * 128], tokd[:, 520:1032], True, True, ["repS", "tokd"], ["K2"])
            CP("act", repd[:, 520:1032], K2[:, 0:512], ["K2"], ["repd"])
            CP("dve", cif, ciA[:, sp, :], ["ciA"], ["cif"])
            TS("dve", cif, cif, ptf[:, sp:sp + 1], None, ALU.add, None, ["cif", "ptf"], ["cif"])
            CP("dve", rowi, cif, ["cif"], ["rowi"])
            TS("dve", cv, cvA[:, sp, :], thrA[:, sp:sp + 1], None, ALU.is_gt, None, ["cvA", "thrA"], ["cv"])
            for c_ in range(NCAND):
                b.op("pool", lambda g, c_=c_: g.indirect_dma_start(out=Kc[:, c_, :], out_offset=None, in_=ck_rows[:, :],
                                                                  in_offset=bass.IndirectOffsetOnAxis(ap=rowi[:, c_:c_ + 1], axis=0)),
                     ["rowi"], ["Kc"], dma=True)
                b.op("pool", lambda g, c_=c_: g.indirect_dma_start(out=Vcd[:, c_, :], out_offset=None, in_=cv_rows[:, :],
                                                                  in_offset=bass.IndirectOffsetOnAxis(ap=rowi[:, c_:c_ + 1], axis=0)),
                     ["rowi"], ["Vcd"], dma=True)
            Kc4 = Kc.rearrange("p c (g d) -> p c g d", g=2)
            Vc4 = Vcd.rearrange("p c (g d) -> p c g d", g=2)
            for h in range(8):
                TT("dve", tmpc, Kc4[:, :, h // 4, :], bc(repd[:, 520 + h * 64:520 + (h + 1) * 64].unsqueeze(1), [128, NCAND, 64]), ALU.mult, ["Kc", "repd"], ["tmpc"])
                RED(lg[:, h, :], tmpc, ALU.add, ["tmpc"], ["lg"])
            ACT(lg, lg, AF.Exp, ["lg"], ["lg"])
            TT("dve", lg, lg, bc(cv.unsqueeze(1), [128, 8, NCAND]), ALU.mult, ["lg", "cv"], ["lg"])
            RED(opd[:, 512:520], lg, ALU.add, ["lg"], ["opd"])
            for h in range(8):
                TT("dve", tmpc, Vc4[:, :, h // 4, :], bc(lg[:, h, :].unsqueeze(2), [128, NCAND, 64]), ALU.mult, ["Vcd", "lg"], ["tmpc"])
                RED(opd[:, h * 64:(h + 1) * 64], tmpc.rearrange("p c d -> p d c"), ALU.add, ["tmpc"], ["opd"])
            MM(V2[0:16, 0:512], repTS[:, sp * 16:(sp + 1) * 16], opd[:, 0:512], sp == 0, sp == NPAIR - 1, ["repTS", "opd"], ["V2"])
            MM(V2[0:16, 512:520], repTS[:, sp * 16:(sp + 1) * 16], opd[:, 512:520], sp == 0, sp == NPAIR - 1, ["repTS", "opd"], ["V2"])
        qv = v8(tk["qs"])
        for g_ in range(2):
            TT("dve", v8(tk["ta"])[:, g_ * 4:(g_ + 1) * 4, :], qv[:, g_ * 4:(g_ + 1) * 4, :],
               bc(ks_[:, g_ * 64:(g_ + 1) * 64].unsqueeze(1), [16, 4, 64]), ALU.mult, ["qs", "ks"], ["ta"])
        RED(s16[:, 0:8], v8(tk["ta"]), ALU.add, ["ta"], ["s16"])
        ACT(s16[:, 0:8], s16[:, 0:8], AF.Exp, ["s16"], ["s16"])
        TS("dve", s16[:, 0:8], s16[:, 0:8], s16[:, 32:33], None, ALU.mult, None, ["s16"], ["s16"])
        TT("dve", s16[:, 8:16], V2[0:16, 512:520], s16[:, 0:8], ALU.add, ["V2", "s16"], ["s16"])
        b.op("dve", lambda g: g.reciprocal(out=s16[:, 8:16], in_=s16[:, 8:16]), ["s16"], ["s16"])
        for g_ in range(2):
            TT("dve", v8(tk["ta"])[:, g_ * 4:(g_ + 1) * 4, :], bc(proj[:, 640 + g_ * 64:640 + (g_ + 1) * 64].unsqueeze(1), [16, 4, 64]),
               bc(s16[:, g_ * 4:(g_ + 1) * 4].unsqueeze(2), [16, 4, 64]), ALU.mult, ["proj", "s16"], ["ta"])
        TT("dve", tk["ta"], tk["ta"], V2[0:16, 0:512], ALU.add, ["ta", "V2"], ["ta"])
        TT("dve", v8(tk["ta"]), v8(tk["ta"]), bc(s16[:, 8:16].unsqueeze(2), [16, 8, 64]), ALU.mult, ["ta", "s16"], ["ta"])
        TT("dve", cats[:, 0:512], tk["ta"], tk["ga"], ALU.mult, ["ta", "ga"], ["cats"])
        for k in range(8):
            TR(PTb[:, k * 16:(k + 1) * 16], cats[:, k * 128:(k + 1) * 128], identb[0:16, 0:16], ["cats", "identb"], ["PTb"])
        CP("act", catTs, PTb[:, 0:128].rearrange("p (k t) -> p k t", k=8), ["PTb"], ["catTs"])
        b.barrier()
        off = PW
        stg2, off = carve(off, [128, 8, 512])
        wbf2, off = carve(off, [128, 8, 512], BF16)
        b.dma("sp", ysb, gscr[0:16, :], reads=["gscr"], writes=["ysb"])
        w_out_v2 = w_out.rearrange("(k p) c -> p k c", p=128)
        for hh in range(2):
            b.dma("sp", stg2, w_out_v2[:, :, hh * 512:(hh + 1) * 512], writes=["stg2"])
            CP("dve", wbf2, stg2, ["stg2"], ["wbf2"])
            for k in range(8):
                MM(R2[0:16, hh * 512:(hh + 1) * 512], catTs[:, k, :], wbf2[:, k, :], k == 0, k == 7, ["catTs", "wbf2"], ["R2"])
        TT("dve", ysb, ysb, R2[0:16, :], ALU.mult, ["ysb", "R2"], ["ysb"])
        TT("dve", ysb, ysb, x_[0:16, :], ALU.add, ["ysb", xr], ["ysb"])
        b.dma("sp", y_s[:, :], ysb, reads=["ysb"])

    b.barrier()
    b.emit()
    ncd.__exit__(None, None, None)
    es.close()
    return nc


def _consts(T):
    NT = T // 128
    NO = NT // 2
    cst = {}
    cst["identf"] = np.eye(128, dtype=np.float32)
    cst["iota256"] = np.tile(np.arange(256, dtype=np.float32)[None, :], (128, 1))
    s = np.arange(64)[:, None]
    t = np.arange(64)[None, :]
    lt = (s < t).astype(np.float32)
    le = (s <= t).astype(np.float32)
    cst["maskT"] = np.concatenate([lt, le, lt, le], axis=1)
    cst["maskL"] = (np.arange(64)[None, :] < np.arange(64)[:, None]).astype(np.float32)
    r = np.ones((64, 1024), np.float32)
    r[:, ::64] = 0.0
    cst["resetm"] = r
    sel = np.zeros((17, 128), np.float32)
    sel[16, :] = 1.0
    cst["sel16"] = sel
    cst["ones64"] = np.ones((64, 64), np.float32)
    return cst


def _rope_table(pos):
    half = 8
    inv = np.power(np.float32(ROPE_THETA), -np.arange(half, dtype=np.float32) / np.float32(half)).astype(np.float32)
    ang = pos.astype(np.float32)[:, None] * inv[None, :]
    return np.concatenate([np.cos(ang), np.sin(ang)], axis=1).astype(np.float32)


def _core_inputs(inp, c, T, NS, past_len):
    NT = T // 128
    NO = NT // 2
    bi, par = c // 2, c % 2
    xp = np.asarray(inp["x_prompt"][bi], np.float32)
    own_tiles = [2 * j + par for j in range(NO)]
    own_rows = np.concatenate([np.arange(t * 128, (t + 1) * 128) for t in own_tiles])
    xs = np.zeros((128, D), np.float32)
    xs[:NS] = np.asarray(inp["x_sample"][c * NS:(c + 1) * NS, 0], np.float32)
    m = {}
    m["xall"] = np.ascontiguousarray(np.concatenate([xp, xp[own_rows], xs], axis=0))
    m["call"] = np.ascontiguousarray(np.concatenate([inp["c_sample"][c * NS:(c + 1) * NS], inp["c_prompt"][bi:bi + 1]], axis=0).astype(np.float32))
    pos = np.concatenate([np.arange(T), own_rows, np.full(128, past_len)])
    m["cs_all"] = _rope_table(pos)
    m["parsel"] = np.tile(np.array([[par, 1 - par]], np.float32), (128, 1))
    m["qrel"] = (par * 128 + np.arange(128, dtype=np.float32)).reshape(128, 1)
    m["ownidx"] = np.ascontiguousarray(own_rows.reshape(NO, 128).T.astype(np.int32))
    for k_, v_ in (("w_in", "w_in"), ("w_ada", "w_ada"), ("b_ada", "b_ada"), ("norm_w", "norm_w"), ("w_out", "w_out"),
                   ("qnw", "q_norm_w"), ("knw", "k_norm_w"), ("mu", "mu_shift"), ("w0", "w0"), ("a0", "a0"),
                   ("k_k", "k_k"), ("k_a", "k_a"), ("ln_x_w", "ln_x_w"), ("ln_x_b", "ln_x_b"), ("w_up", "w_up"), ("a_up", "a_up")):
        m[k_] = np.ascontiguousarray(np.asarray(inp[v_], np.float32))
    m["r_k"] = np.ascontiguousarray(np.asarray(inp["r_k"], np.float32).reshape(512))
    m["swkv"] = np.ascontiguousarray(np.asarray(inp["state_wkv"][c * NS:(c + 1) * NS], np.float32).reshape(NS * 8, 4096))
    m["sshift"] = np.ascontiguousarray(np.asarray(inp["state_shift"][c * NS:(c + 1) * NS, 0], np.float32))
    pt = np.asarray(inp["page_table"][c * NS:(c + 1) * NS], np.int32)
    m["ptab"] = np.ascontiguousarray(pt.reshape(NS // 2, 128).T)
    nphys = inp["cache_k"].shape[0]
    m["cache_k"] = np.asarray(inp["cache_k"], np.float32).reshape(nphys * 128, 128)
    m["cache_v"] = np.asarray(inp["cache_v"], np.float32).reshape(nphys * 128, 128)
    m["cache_kidx"] = np.asarray(inp["cache_kidx"], np.float32).reshape(nphys, 8192)
    rep = np.zeros((16, 8, 128), np.float32)
    for sp in range(8):
        for p in range(128):
            rep[2 * sp + p // 64, sp, p] = 1.0
    m["rep"] = rep.reshape(16, 1024)
    m["repT"] = np.ascontiguousarray(rep.transpose(2, 1, 0).reshape(128, 128))
    blk = np.zeros((128, 128), np.float32)
    blk[:64, :64] = 1.0
    blk[64:, 64:] = 1.0
    m["blk"] = blk
    oh = np.zeros((128, 1), np.float32)
    oh[0, 0] = 1.0
    oh[64, 0] = 1.0
    m["oh0"] = oh
    m.update(_consts(T))
    return m


_NC_CACHE = {}


def kernel(**inp):
    T = 4096
    NS = 16
    past_len = 8192
    inp = {k: np.asarray(v) for k, v in inp.items()}
    if "nc" not in _NC_CACHE:
        _NC_CACHE["nc"] = build(T=T, NPHYS=int(inp["cache_k"].shape[0]))
    nc = _NC_CACHE["nc"]
    in_maps = [_core_inputs(inp, c, T, NS, past_len) for c in range(8)]
    res = run_bass_kernel_spmd(nc, in_maps, core_ids=list(range(8)))
    outs = res.results
    B = 4
    NO = T // 256
    y_p = np.zeros((B, T, D), np.float32)
    for c in range(8):
        bi, par = c // 2, c % 2
        yo = np.asarray(outs[c]["y_own"]).reshape(NO, 128, D)
        y_p[bi].reshape(T // 256, 2, 128, D)[:, par] = yo
    k_p = np.stack([np.asarray(outs[2 * bi]["k_nat"]).reshape(T, 2, 64) for bi in range(B)])
    v_p = np.stack([np.asarray(outs[2 * bi]["v_nat"]).reshape(T, 2, 64) for bi in range(B)])
    ki_p = np.stack([np.asarray(outs[2 * bi]["ki_nat"]).reshape(T, 64) for bi in range(B)])
    wkv_pp = np.stack([np.asarray(outs[2 * bi]["wkv_p"]).reshape(8, 64, 64) for bi in range(B)])
    sh_p = np.stack([np.asarray(outs[2 * bi]["shift_p"]).reshape(1, SHW) for bi in range(B)])
    y_s = np.concatenate([np.asarray(outs[c]["y_s"]) for c in range(8)]).reshape(128, 1, D)
    k_s = np.concatenate([np.asarray(outs[c]["k_s"]) for c in range(8)]).reshape(128, 1, 2, 64)
    v_s = np.concatenate([np.asarray(outs[c]["v_s"]) for c in range(8)]).reshape(128, 1, 2, 64)
    ki_s = np.concatenate([np.asarray(outs[c]["ki_s"]) for c in range(8)]).reshape(128, 1, 64)
    wkv_s = np.concatenate([np.asarray(outs[c]["wkv_s"]) for c in range(8)]).reshape(128, 8, 64, 64)
    sh_s = np.concatenate([np.asarray(outs[c]["shift_s"]) for c in range(8)]).reshape(128, 1, SHW)
    f = lambda a: np.ascontiguousarray(a, dtype=np.float32)
    return (f(y_p), f(y_s), f(k_p), f(v_p), f(ki_p), f(wkv_pp), f(sh_p), f(k_s), f(v_s), f(ki_s), f(wkv_s), f(sh_s))
```

```python
import os
import numpy as np
from contextlib import ExitStack
import concourse.bass as bass
import concourse.mybir as mybir
from concourse.bass_utils import run_bass_kernel_spmd

F32 = mybir.dt.float32
BF16 = mybir.dt.bfloat16
I32 = mybir.dt.int32
AF = mybir.ActivationFunctionType
ALU = mybir.AluOpType
AX = mybir.AxisListType

ENGS = ("pe", "act", "dve", "pool", "sp")
NDMA = 32
NSW = 8

D = 1024
HD = 64
DIN = 4040
C_Q, C_K, C_V, C_QI, C_WI, C_KI, C_GA = 0, 512, 640, 768, 1280, 1288, 1352
C_R, C_RK, C_RV, C_WD, C_AD, C_GR = 1864, 2376, 2888, 3400, 3464, 3528
SHW = 1664
NORM_EPS = 1e-6
GN_EPS = 64e-5
ROPE_THETA = 500000.0


USE_POOL = bool(int(os.environ.get('USE_POOL', '0')))
PSUM_RES = {"PTb", "F2", "R2", "K2", "V2", "R2a", "R2b", "K2a", "K2b", "V2a", "V2b"}


class Res:
    __slots__ = ("w", "r")

    def __init__(self):
        self.w = None
        self.r = []


class Bld:
    def __init__(self, nc, es):
        self.nc = nc
        self.es = es
        self.sem = {e: es.enter_context(nc.semaphore("s_" + e)) for e in ENGS}
        self.dsem = [es.enter_context(nc.semaphore("d%d" % i)) for i in range(NDMA)]
        self.dval = [0] * NDMA
        self.dnext = 0
        self.dnext_sw = 0
        self.cnt = {e: 0 for e in ENGS}
        self.waited = {e: {} for e in ENGS}
        self.ops = {e: [] for e in ENGS}
        self.res = {}

    def sb(self, name, shape, dt=F32):
        return self.es.enter_context(self.nc.sbuf_tensor("sb_" + name, list(shape), dt))

    def ps(self, name, shape, dt=F32):
        return self.es.enter_context(self.nc.psum_tensor("ps_" + name, list(shape), dt))

    def _r(self, key):
        r = self.res.get(key)
        if r is None:
            r = self.res[key] = Res()
        return r

    def _need(self, e, tok, waits):
        if tok is None:
            return
        key, val = tok
        if key == "pe" and e == "pe":
            return
        if self.waited[e].get(key, 0) >= val:
            return
        self.waited[e][key] = val
        waits.append((key, val))

    def op(self, e, fn, reads=(), writes=(), dma=False):
        if e == "pool" and not dma and not USE_POOL:
            e = "dve"
        pr = [k for k in reads if k in PSUM_RES]
        if pr:
            reads = [k for k in reads if k not in PSUM_RES]
            writes = list(writes) + pr
        waits = []
        for k in reads:
            self._need(e, self._r(k).w, waits)
        for k in writes:
            r = self._r(k)
            self._need(e, r.w, waits)
            for t in r.r:
                self._need(e, t, waits)
        if dma:
            if e == "pool":
                i = NDMA - NSW + self.dnext_sw
                self.dnext_sw = (self.dnext_sw + 1) % NSW
            else:
                i = self.dnext
                self.dnext = (self.dnext + 1) % (NDMA - NSW)
            if self.dval[i] > 0:
                self._need(e, (("d", i), self.dval[i]), waits)
            self.dval[i] += 16
            tok = (("d", i), self.dval[i])
            inc = (self.dsem[i], 16)
        else:
            self.cnt[e] += 1
            tok = (e, self.cnt[e])
            inc = (self.sem[e], 1)
        self.ops[e].append((waits, fn, inc))
        for k in reads:
            self._r(k).r.append(tok)
        for k in writes:
            r = self._r(k)
            r.w = tok
            r.r = []
        return tok

    def dma(self, e, out, in_, reads=(), writes=()):
        return self.op(e, lambda g: g.dma_start(out=out, in_=in_), reads, writes, dma=True)

    def barrier(self, dmas=True):
        for e in ENGS:
            waits = []
            for e2 in ENGS:
                if e2 != e and self.cnt[e2] > 0:
                    self._need(e, (e2, self.cnt[e2]), waits)
            if dmas:
                for i in range(NDMA):
                    if self.dval[i] > 0:
                        self._need(e, (("d", i), self.dval[i]), waits)
            self.ops[e].append((waits, None, None))

    def emit(self):
        nc = self.nc
        with nc.Block() as block:
            def mk(e):
                def body(g):
                    for waits, fn, inc in self.ops[e]:
                        for key, val in waits:
                            s = self.dsem[key[1]] if isinstance(key, tuple) else self.sem[key]
                            g.wait_ge(s, val)
                        if fn is not None:
                            fn(g).then_inc(inc[0], inc[1])
                return body
            block.tensor(mk("pe"))
            block.scalar(mk("act"))
            block.vector(mk("dve"))
            block.gpsimd(mk("pool"))
            block.sync(mk("sp"))


def build(T=4096, NS=16, NPG=64, NPHYS=10240, topk_p=256, topk_s=256, n_bis=15, do_sample=True, AW=36000, stop_after=None, nt_lim=None, no_lim=None):
    NT = T // 128
    NO = NT // 2
    NTILES = NT + NO + 1
    nc = bass.Bass("TRN2", target_bir_lowering=False)
    es = ExitStack()
    b = Bld(nc, es)

    def din(name, shape, dt=F32):
        return nc.dram_tensor(name, list(shape), dt, kind="ExternalInput").ap()

    def dout(name, shape, dt=F32):
        return nc.dram_tensor(name, list(shape), dt, kind="ExternalOutput").ap()

    xall = din("xall", [NTILES * 128, D])
    call = din("call", [17, D])
    w_in = din("w_in", [D, DIN])
    w_ada = din("w_ada", [D, 3 * D])
    b_ada = din("b_ada", [3 * D])
    norm_w = din("norm_w", [D])
    w_out = din("w_out", [D, D])
    qnw = din("qnw", [HD])
    knw = din("knw", [HD])
    mu = din("mu", [SHW])
    pw0 = din("w0", [512]); pa0 = din("a0", [512]); pkk = din("k_k", [512]); pka = din("k_a", [512])
    prk = din("r_k", [512]); plnw = din("ln_x_w", [512]); plnb = din("ln_x_b", [512])
    w_up = din("w_up", [64, 512]); a_up = din("a_up", [64, 512])
    identf_d = din("identf", [128, 128])
    cs_all = din("cs_all", [NTILES * 128, 16])
    parsel_d = din("parsel", [128, 2])
    qrel_d = din("qrel", [128, 1])
    ownidx_d = din("ownidx", [128, NO], I32)
    iota_d = din("iota256", [128, 256])
    maskT_d = din("maskT", [64, 256])
    maskL_d = din("maskL", [64, 64])
    reset_d = din("resetm", [64, 1024])
    sel16_d = din("sel16", [17, 128])
    ones64_d = din("ones64", [64, 64])

    swkv_d = din("swkv", [128, 4096]); sshift_d = din("sshift", [16, SHW]); ptab_d = din("ptab", [128, 8], I32)
    if do_sample:
        cache_k = din("cache_k", [NPHYS * 128, 128]); cache_v = din("cache_v", [NPHYS * 128, 128])
        cache_ki = din("cache_kidx", [NPHYS, 8192])
    rep_d = din("rep", [16, 8 * 128]); repT_d = din("repT", [128, 8 * 16]); blk_d = din("blk", [128, 128]); oh0_d = din("oh0", [128, 1])
    y_s = dout("y_s", [16, D]); k_s = dout("k_s", [16, 128]); v_s = dout("v_s", [16, 128]); ki_s = dout("ki_s", [16, 64])
    wkv_s = dout("wkv_s", [128, 4096]); shift_s = dout("shift_s", [16, SHW])
    gscr = nc.dram_tensor("gscr", [17, D], F32, kind="Internal").ap()
    scr1 = nc.dram_tensor("scr1", [16, 3072], F32, kind="Internal").ap()
    scr2 = nc.dram_tensor("scr2", [128, 64], F32, kind="Internal").ap()
    scr3 = nc.dram_tensor("scr3", [16, 512], F32, kind="Internal").ap()
    scr4 = nc.dram_tensor("scr4", [16, 64, 16], F32, kind="Internal").ap()
    y_own = dout("y_own", [NO * 128, D])
    k_nat = dout("k_nat", [T, 128]); v_nat = dout("v_nat", [T, 128]); ki_nat = dout("ki_nat", [T, 64])
    wkv_p = dout("wkv_p", [8, 64, 64]); shift_p = dout("shift_p", [SHW])
    rwscr = nc.dram_tensor("rwscr", [T, 512], F32, kind="Internal").ap()

    PTb = b.ps("PTb", [128, 1024], BF16)
    F2 = b.ps("F2", [128, 512])
    R2 = b.ps("R2", [128, 1024])
    K2 = b.ps("K2", [128, 1024])
    V2 = b.ps("V2", [128, 1024])

    identf = b.sb("identf", [128, 128]); identb = b.sb("identb", [128, 128], BF16)
    cst = b.sb("cst", [128, 4])
    kT_all = b.sb("kT_all", [64, 2, T], BF16)
    kiT_all = b.sb("kiT_all", [64, T], BF16)
    Vaug = b.sb("Vaug", [128, NT, 2, 65], BF16)
    modT = b.sb("modT", [128, 24, 17])
    g1 = b.sb("g1", [128, 8, 17])
    nwT = b.sb("nwT", [128, 8]); badaT = b.sb("badaT", [128, 24])
    lnw_bc = b.sb("lnw_bc", [64, 512]); lnb_bc = b.sb("lnb_bc", [64, 512])
    qnw_bc = b.sb("qnw_bc", [128, 64]); knw_bc = b.sb("knw_bc", [128, 64])
    sel16 = b.sb("sel16", [17, 128]); ones64 = b.sb("ones64", [64, 64])
    maskT = b.sb("maskT", [64, 256]); maskL = b.sb("maskL", [64, 64]); resetm = b.sb("resetm", [64, 1024])
    qrel = b.sb("qrel", [128, 1]); parsel = b.sb("parsel", [128, 2]); ownidx = b.sb("ownidx", [128, NO], I32)
    fp = {}
    for nm in ("w0", "a0", "kk", "ka", "rk"):
        fp[nm] = b.sb("fp_" + nm, [64, 8])
    muT = b.sb("muT", [64, 26]); wupS = b.sb("wupS", [64, 512]); aupS = b.sb("aupS", [64, 512])
    xt0 = b.sb("xt0", [128, D]); xt = [xt0, xt0]
    xn = b.sb("xn", [128, D], BF16)
    hT0 = b.sb("hT0", [128, 8, 128], BF16); hT = [hT0, hT0]
    hTs = b.sb("hTs", [128, 8, 128], BF16)
    hlast = b.sb("hlast", [128, 8, 1], BF16)
    cs_t = b.sb("cs_t", [128, 16])
    sm = b.sb("sm", [128, 64])
    ARENA = b.sb("ARENA", [128, AW])
    csT = b.sb("csT", [128, 8, 17])

    def TT(e, out, in0, in1, op, R, W):
        b.op(e, lambda g: g.tensor_tensor(out=out, in0=in0, in1=in1, op=op), R, W)

    def TS(e, out, in0, s1, s2, op0, op1, R, W, accum=None):
        if op1 is None:
            b.op(e, lambda g: g.tensor_scalar(out=out, in0=in0, scalar1=s1, scalar2=None, op0=op0), R, W)
        elif accum is None:
            b.op(e, lambda g: g.tensor_scalar(out=out, in0=in0, scalar1=s1, scalar2=s2, op0=op0, op1=op1), R, W)
        else:
            b.op(e, lambda g: g.tensor_scalar(out=out, in0=in0, scalar1=s1, scalar2=s2, op0=op0, op1=op1,
                                              accum_out=accum), R, W)

    def STT(out, in0, scalar, in1, op0, op1, R, W):
        b.op("dve", lambda g: g.scalar_tensor_tensor(out=out, in0=in0, scalar=scalar, in1=in1, op0=op0, op1=op1), R, W)

    def ACT(out, in_, func, R, W, scale=1.0, bias=None, accum=None):
        kw = {}
        if bias is not None:
            kw["bias"] = bias
        if accum is not None:
            kw["accum_out"] = accum
        b.op("act", lambda g: g.activation(out=out, in_=in_, func=func, scale=scale, **kw), R, W)

    def MM(out, lhsT, rhs, start, stop, R, W):
        b.op("pe", lambda g: g.matmul(out=out, lhsT=lhsT, rhs=rhs, start=start, stop=stop), R, W)

    def TR(out, in_, ident, R, W):
        b.op("pe", lambda g: g.transpose(out=out, in_=in_, identity=ident), R, W)

    def CP(e, out, in_, R, W):
        if e == "act":
            b.op(e, lambda g: g.copy(out=out, in_=in_), R, W)
        else:
            b.op(e, lambda g: g.tensor_copy(out=out, in_=in_), R, W)

    def RED(out, in_, op, R, W, axis=AX.X):
        b.op("dve", lambda g: g.tensor_reduce(out=out, in_=in_, axis=axis, op=op), R, W)

    def MS(e, ap, val, W):
        b.op(e, lambda g: g.memset(ap, val), (), W)

    def bc(ap, shape):
        return ap.to_broadcast(list(shape))

    ncd = nc.allow_non_contiguous_dma(reason="small parameter layouts")
    ncd.__enter__()

    b.dma("sp", identf[:], identf_d[:, :], writes=["identf"])
    CP("dve", identb[:], identf[:], ["identf"], ["identb"])
    MS("dve", cst[:, 0:1], NORM_EPS, ["cst"]); MS("dve", cst[:, 1:2], GN_EPS, ["cst"]); MS("dve", cst[:, 2:3], 1e-24, ["cst"]); MS("dve", cst[:, 3:4], -30000.0, ["cst"])
    for (t_, d_, nm) in ((sel16, sel16_d, "sel16"), (ones64, ones64_d, "ones64"), (maskT, maskT_d, "maskT"),
                         (maskL, maskL_d, "maskL"), (resetm, reset_d, "resetm"),
                         (qrel, qrel_d, "qrel"), (parsel, parsel_d, "parsel"), (ownidx, ownidx_d, "ownidx"), (wupS, w_up, "wupS"), (aupS, a_up, "aupS")):
        b.dma("sp", t_[:], d_[:, :], writes=[nm])
    for nm, src in (("w0", pw0), ("a0", pa0), ("kk", pkk), ("ka", pka), ("rk", prk)):
        b.dma("sp", fp[nm][:], src.rearrange("(h j) -> j h", j=64), writes=["fp_" + nm])
    b.dma("sp", muT[:], mu.rearrange("(c j) -> j c", j=64), writes=["muT"])
    b.dma("sp", nwT[:], norm_w.rearrange("(k p) -> p k", p=128), writes=["nwT"])
    b.dma("sp", badaT[:], b_ada.rearrange("(t p) -> p t", p=128), writes=["badaT"])
    b.dma("sp", lnw_bc[:], plnw.partition_broadcast(64), writes=["lnw_bc"])
    b.dma("sp", lnb_bc[:], plnb.partition_broadcast(64), writes=["lnb_bc"])
    b.dma("sp", qnw_bc[:], qnw.partition_broadcast(128), writes=["qnw_bc"])
    b.dma("sp", knw_bc[:], knw.partition_broadcast(128), writes=["knw_bc"])
    MS("pool", Vaug[:, :, :, 64:65], 1.0, ["Vaug"])

    def carve(off, shape, dt=F32):
        n = int(np.prod(shape[1:]))
        words = n if dt in (F32, I32) else (n + 1) // 2
        v = ARENA[0:shape[0], off:off + words]
        if dt != F32:
            v = v.bitcast(dt)
        if len(shape) == 3:
            v = v.rearrange("p (a b) -> p a b", a=shape[1])
        elif len(shape) == 4:
            v = v.rearrange("p (a b c) -> p a b c", a=shape[1], b=shape[2])
        return v, off + words

    off = 0
    Wn, off = carve(off, [128, 8, 832], BF16)
    Wm, off = carve(off, [128, 8, SHW], BF16)
    Wom, off = carve(off, [128, 8, SHW], BF16)
    W_end = off
    stg, off = carve(off, [128, 8, 512])
    mu_bc, off = carve(off, [128, SHW])
    omu_bc, off = carve(off, [128, SHW])
    gtok, off = carve(off, [17, D])
    bgate, off = carve(off, [17, D])
    csall_sil, off = carve(off, [17, D])

    b.dma("sp", mu_bc, mu.partition_broadcast(128), writes=["mu_bc"])
    b.dma("sp", bgate, b_ada[2 * D:3 * D].partition_broadcast(17), writes=["bgate"])
    TS("dve", omu_bc, mu_bc, -1.0, 1.0, ALU.mult, ALU.add, ["mu_bc"], ["omu_bc"])
    w_in_v = w_in.rearrange("(k p) c -> p k c", p=128)

    def load_cols(dst, dcol, c0, n, scale_bc=None, scale_off=0, tag=""):
        done = 0
        while done < n:
            w = min(512, n - done)
            b.dma("sp", stg[:, :, 0:w], w_in_v[:, :, c0 + done:c0 + done + w], writes=["stg"])
            if scale_bc is None:
                CP("pool", dst[:, :, dcol + done:dcol + done + w], stg[:, :, 0:w], ["stg"], [tag])
            else:
                for sname, sbcv, d2 in scale_bc:
                    TT("dve", d2[:, :, dcol + done:dcol + done + w], stg[:, :, 0:w],
                       bc(sbcv[:, scale_off + done:scale_off + done + w].unsqueeze(1), [128, 8, w]),
                       ALU.mult, ["stg", sname], [tag])
            done += w

    load_cols(Wn, 0, C_K, 256, tag="Wn")
    load_cols(Wn, 256, C_KI, 64, tag="Wn")
    load_cols(Wn, 320, C_GR, 512, tag="Wn")
    load_cols(None, 0, C_R, SHW, scale_bc=[("mu_bc", mu_bc, Wm), ("omu_bc", omu_bc, Wom)], tag="Wm")
    calt = sm
    b.dma("sp", csall_sil, call[:, :], writes=["csil"])
    ACT(csall_sil, csall_sil, AF.Silu, ["csil"], ["csil"])
    for k in range(8):
        TR(F2[:, k * 17:(k + 1) * 17], csall_sil[:, k * 128:(k + 1) * 128], identf[0:17, 0:17], ["csil", "identf"], ["F2"])
    CP("dve", csT[:], F2[:, 0:136].rearrange("p (k m) -> p k m", k=8), ["F2"], ["csT"])
    w_ada_v = w_ada.rearrange("(k p) c -> p k c", p=128)
    for ch in range(6):
        b.dma("sp", stg[:, :, :], w_ada_v[:, :, ch * 512:(ch + 1) * 512], writes=["stg"])
        for ct in range(4):
            for k in range(8):
                MM(R2[:, ct * 17:(ct + 1) * 17], stg[:, k, ct * 128:(ct + 1) * 128], csT[:, k, :], k == 0, k == 7,
                   ["stg", "csT"], ["R2"])
        TT("dve", modT[:, ch * 4:(ch + 1) * 4, :], R2[:, 0:68].rearrange("p (c m) -> p c m", c=4),
           bc(badaT[:, ch * 4:(ch + 1) * 4].unsqueeze(2), [128, 4, 17]), ALU.add, ["R2", "badaT"], ["modT"])
        if ch >= 4:
            for k in range(8):
                MM(K2[0:17, 0:512], csT[:, k, :], stg[:, k, :], k == 0, k == 7, ["stg", "csT"], ["K2"])
            TT("dve", gtok[:, (ch - 4) * 512:(ch - 3) * 512], K2[0:17, 0:512], bgate[:, (ch - 4) * 512:(ch - 3) * 512],
               ALU.add, ["K2", "bgate"], ["gtok"])
    b.dma("sp", gscr[:, :], gtok[0:17, :], reads=["gtok"], writes=["gscr"])
    STT(g1[:], modT[:, 8:16, :], 1.0, bc(nwT[:].unsqueeze(2), [128, 8, 17]), ALU.add, ALU.mult, ["modT", "nwT"], ["g1"])
    def front(ti, par, m_prompt=True, ntok=128):
        x_ = xt[par]
        h_ = hT[par]
        xr, hr = "xt0", "hT0"
        b.dma("sp", x_[:], xall[ti * 128:(ti + 1) * 128, :], writes=[xr])
        b.dma("sp", cs_t[:], cs_all[ti * 128:(ti + 1) * 128, :], writes=["cs_t"])
        ACT(xn[:], x_[:], AF.Square, [xr], ["xn", "sm"], accum=sm[:, 0:1])
        ACT(sm[:, 1:2], sm[:, 0:1], AF.Sqrt, ["sm", "cst"], ["sm"], scale=1.0 / D, bias=cst[:, 0:1])
        b.op("dve", lambda g: g.reciprocal(out=sm[:, 2:3], in_=sm[:, 1:2]), ["sm"], ["sm"])
        TS("dve", xn[:], x_[:], sm[:, 2:3], None, ALU.mult, None, [xr, "sm"], ["xn"])
        for k in range(8):
            TR(PTb[:, k * 128:(k + 1) * 128], xn[:, k * 128:(k + 1) * 128], identb[:], ["xn", "identb"], ["PTb"])
        pv = PTb[:, :].rearrange("p (k t) -> p k t", k=8)
        if m_prompt:
            TT("dve", h_[:], pv, bc(g1[:, :, 16:17], [128, 8, 128]), ALU.mult, ["PTb", "g1"], [hr])
            TT("pool", h_[:], h_[:], bc(modT[:, 0:8, 16:17], [128, 8, 128]), ALU.add, [hr, "modT"], [hr])
        else:
            TT("dve", h_[:, :, 0:ntok], pv[:, :, 0:ntok], g1[:, :, 0:ntok], ALU.mult, ["PTb", "g1"], [hr])
            TT("pool", h_[:, :, 0:ntok], h_[:, :, 0:ntok], modT[:, 0:8, 0:ntok], ALU.add, [hr, "modT"], [hr])
        return x_, h_, xr, hr

    def rope(e, buf, nh, hd_stride_view, R, nrows=128):
        x1 = buf[:, :, 0:8]
        x2 = buf[:, :, 8:16]
        cosb = bc(cs_t[0:nrows, 0:8].unsqueeze(1), [nrows, nh, 8])
        sinb = bc(cs_t[0:nrows, 8:16].unsqueeze(1), [nrows, nh, 8])
        t = ropet[0:nrows, 0:4 * nh * 8].rearrange("p (a h d) -> p a h d", a=4, h=nh)
        TT(e, t[:, 0], x1, cosb, ALU.mult, R + ["cs_t"], ["ropet"])
        TT(e, t[:, 1], x2, sinb, ALU.mult, R + ["cs_t"], ["ropet"])
        TT(e, t[:, 2], x2, cosb, ALU.mult, R + ["cs_t"], ["ropet"])
        TT(e, t[:, 3], x1, sinb, ALU.mult, R + ["cs_t"], ["ropet"])
        TT(e, x1, t[:, 0], t[:, 1], ALU.subtract, ["ropet"], R)
        TT(e, x2, t[:, 2], t[:, 3], ALU.add, ["ropet"], R)

    ropet = b.sb("ropet", [128, 256])

    def qknorm(src_ps, dst, nh, wbc, extra_scale, Rsrc, Wdst, nrows=128):
        sq = nrm_t[0:nrows, 0:nh * 64].rearrange("p (h d) -> p h d", h=nh)
        ACT(sq, src_ps, AF.Square, Rsrc, ["nrm_t"])
        RED(sm[0:nrows, 8:8 + nh], sq, ALU.add, ["nrm_t"], ["sm"])
        ACT(sm[0:nrows, 16:16 + nh], sm[0:nrows, 8:8 + nh], AF.Sqrt, ["sm", "cst"], ["sm"], scale=1.0 / 64, bias=cst[0:nrows, 0:1])
        b.op("dve", lambda g: g.reciprocal(out=sm[0:nrows, 24:24 + nh], in_=sm[0:nrows, 16:16 + nh]), ["sm"], ["sm"])
        TT("dve", dst, src_ps, bc(sm[0:nrows, 24:24 + nh].unsqueeze(2), [nrows, nh, 64]), ALU.mult, Rsrc + ["sm"], Wdst)
        STT(dst, dst, float(extra_scale), bc(wbc[0:nrows, :].unsqueeze(1), [nrows, nh, 64]), ALU.mult, ALU.mult, Wdst + ["qnw_bc", "knw_bc"], Wdst)

    nrm_t = b.sb("nrm_t", [128, 512])
    kfin = b.sb("kfin", [128, 128]); vfin = b.sb("vfin", [128, 128]); kifin = b.sb("kifin", [128, 64])
    gr_s = b.sb("gr_s", [128, 512])

    off = W_end
    rw = {}
    for nm in ("tw", "adc"):
        rw[nm], off = carve(off, [64, 128])
    for nm in ("sg", "L", "g", "t1"):
        rw[nm], off = carve(off, [64, 8, 128])
    blkA, off = carve(off, [64, 2048])
    blkB, off = carve(off, [64, 3072])
    rw["gprev"] = blkA[:, 0:1024].rearrange("p (h t) -> p h t", h=8)
    rw["ginv"] = blkA[:, 1024:2048].rearrange("p (h t) -> p h t", h=8)
    rw["asig"] = blkB[:, 0:1024].rearrange("p (h t) -> p h t", h=8)
    rw["kkn"] = blkB[:, 1024:2048].rearrange("p (h t) -> p h t", h=8)
    rw["kmod"] = blkB[:, 2048:3072].rearrange("p (h t) -> p h t", h=8)
    AMx = blkA.bitcast(BF16).rearrange("p (h x) -> p h x", h=16)
    LNPx = blkB.bitcast(BF16).rearrange("p (a h x) -> p a h x", a=6, h=16)
    rw["sg"] = rw["sg"]
    QTt, off = carve(off, [64, 8, 2, 128], BF16)
    KTt, off = carve(off, [64, 8, 2, 128], BF16)
    def alias(view64, shape):
        return view64.rearrange("p h t -> p (h t)").bitcast(BF16)
    AM = AMx
    Lm = [LNPx[:, 0], LNPx[:, 1]]
    Nm = [LNPx[:, 2], LNPx[:, 3]]
    Pm = [LNPx[:, 4], LNPx[:, 5]]
    BKtok, off = carve(off, [64, 8, 2, 64], BF16)
    Vc, off = carve(off, [64, 2, 8, 64], BF16)
    P0s, off = carve(off, [64, 8, 64], BF16)
    Us, off = carve(off, [64, 8, 64], BF16)
    H32, off = carve(off, [64, 8, 64])
    Hb, off = carve(off, [64, 8, 64], BF16)
    ych, off = carve(off, [64, 8, 64])
    yt1, off = carve(off, [64, 8, 64])
    bon, off = carve(off, [128, 8])
    rawl, off = carve(off, [64, 26])
    xmt, off = carve(off, [128, SHW])
    st8, off = carve(off, [64, 64])
    assert off <= AW, off
    identb64 = identb[0:64, 0:64]

    KR = int(os.environ.get('KR', '9'))
    KQ = int(os.environ.get('KQ', '9'))

    def rwkv_tile(ti, h_, hr):
        CP("pool", hTs[:, :, 1:128], h_[:, :, 0:127], [hr], ["hTs"])
        CP("pool", hTs[:, :, 0:1], hlast[:], ["hlast"], ["hTs"])
        CP("pool", hlast[:], h_[:, :, 127:128], [hr], ["hlast"])
        if KQ < 1:
            return
        for gi, (c0, n) in enumerate(((0, 512), (512, 512), (1024, 512), (1536, 128))):
            dst = (R2[:, 0:512], R2[:, 512:1024], K2[:, 0:512], K2[:, 512:640])[gi]
            nm = ("R2", "R2", "K2", "K2")[gi]
            for k in range(8):
                MM(dst, h_[:, k, :], Wom[:, k, c0:c0 + n], k == 0, False, ["Wm", hr], [nm])
            for k in range(8):
                MM(dst, hTs[:, k, :], Wm[:, k, c0:c0 + n], False, k == 7, ["Wm", "hTs"], [nm])
        CP("act", xmt[:, 0:1024], R2[:, :], ["R2"], ["xmt"])
        CP("dve", xmt[:, 1024:1664], K2[:, 0:640], ["K2"], ["xmt"])
        TR(F2[0:64, 0:128], xmt[:, 1536:1600], identf[:], ["xmt", "identf"], ["F2"])
        TR(F2[0:64, 128:256], xmt[:, 1600:1664], identf[:], ["xmt", "identf"], ["F2"])
        KW = int(os.environ.get('KW', '3'))
        if KW & 1:
            ACT(rw["tw"], F2[0:64, 0:128], AF.Tanh, ["F2"], ["tw"])
        if KW & 2:
            CP("dve", rw["adc"], F2[0:64, 128:256], ["F2"], ["adc"])
        if KR < 1:
            return
        R2v = R2[0:64, :].rearrange("p (h t) -> p h t", h=8)
        K2v = K2[0:64, :].rearrange("p (h t) -> p h t", h=8)
        V2v = V2[0:64, :].rearrange("p (h t) -> p h t", h=8)
        for h in range(8):
            MM(R2v[:, h, :], wupS[:, h * 64:(h + 1) * 64], rw["tw"], True, True, ["wupS", "tw"], ["R2"])
            MM(K2v[:, h, :], aupS[:, h * 64:(h + 1) * 64], rw["adc"], True, True, ["aupS", "adc"], ["K2"])
        TT("dve", rw["sg"], R2v, bc(fp["w0"][:].unsqueeze(2), [64, 8, 128]), ALU.add, ["R2", "fp_w0"], ["sg"])
        ACT(rw["sg"], rw["sg"], AF.Sigmoid, ["sg"], ["sg"])
        TT("dve", rw["asig"], K2v, bc(fp["a0"][:].unsqueeze(2), [64, 8, 128]), ALU.add, ["K2", "fp_a0"], ["asig"])
        ACT(rw["asig"], rw["asig"], AF.Sigmoid, ["asig"], ["asig"])
        TS("dve", rw["sg"], rw["sg"], -0.6065306597126334, None, ALU.mult, None, ["sg"], ["sg"])
        b.op("dve", lambda g: g.tensor_tensor_scan(out=rw["L"].rearrange("p h t -> p (h t)"), data0=resetm[:, :],
                                                   data1=rw["sg"].rearrange("p h t -> p (h t)"), initial=0.0,
                                                   op0=ALU.mult, op1=ALU.add), ["sg", "resetm"], ["L"])
        ACT(rw["g"], rw["L"], AF.Exp, ["L"], ["g"])
        ACT(rw["ginv"], rw["L"], AF.Exp, ["L"], ["ginv"], scale=-1.0)
        TT("pool", rw["gprev"], rw["L"], rw["sg"], ALU.subtract, ["L", "sg"], ["gprev"])
        ACT(rw["gprev"], rw["gprev"], AF.Exp, ["gprev"], ["gprev"])
        if KR < 2:
            return
        for h in range(8):
            TR(R2v[:, h, :], xmt[:, h * 64:(h + 1) * 64], identf[:], ["xmt", "identf"], ["R2"])
            TR(K2v[:, h, :], xmt[:, 512 + h * 64:512 + (h + 1) * 64], identf[:], ["xmt", "identf"], ["K2"])
        TT("dve", rw["L"], K2v, bc(fp["kk"][:].unsqueeze(2), [64, 8, 128]), ALU.mult, ["K2", "fp_kk"], ["L"])
        ACT(rw["t1"], rw["L"], AF.Square, ["L"], ["t1"])
        t1f = rw["t1"].rearrange("p h t -> p (h t)")
        for hh in range(2):
            MM(V2[0:64, hh * 512:(hh + 1) * 512], ones64[:, :], t1f[:, hh * 512:(hh + 1) * 512], True, True, ["ones64", "t1"], ["V2"])
        ACT(rw["t1"], V2v, AF.Sqrt, ["V2", "cst"], ["t1"], bias=cst[0:64, 2:3])
        b.op("dve", lambda g: g.reciprocal(out=rw["t1"], in_=rw["t1"]), ["t1"], ["t1"])
        TT("dve", rw["kkn"], rw["L"], rw["t1"], ALU.mult, ["L", "t1"], ["kkn"])
        STT(rw["t1"], rw["asig"], -1.0, bc(fp["ka"][:].unsqueeze(2), [64, 8, 128]), ALU.add, ALU.mult, ["asig", "fp_ka"], ["t1"])
        STT(rw["kmod"], rw["t1"], 1.0, K2v, ALU.add, ALU.mult, ["t1", "K2"], ["kmod"])
        if KR < 3:
            return
        QTv = QTt.rearrange("p h c (q t) -> p h c q t", q=2)
        KTv = KTt.rearrange("p h c (q t) -> p h c q t", q=2)

        def ch(v):
            return v.rearrange("p h (c t) -> p h c t", c=2)
        STT(QTv[:, :, :, 0, :], ch(rw["kkn"]), -1.0, ch(rw["gprev"]), ALU.mult, ALU.mult, ["kkn", "gprev"], ["QTt"])
        TT("dve", QTv[:, :, :, 1, :], ch(R2v), ch(rw["g"]), ALU.mult, ["R2", "g"], ["QTt"])
        TT("pool", rw["t1"], rw["kkn"], rw["asig"], ALU.mult, ["kkn", "asig"], ["t1"])
        TT("pool", KTv[:, :, :, 0, :], ch(rw["t1"]), ch(rw["ginv"]), ALU.mult, ["t1", "ginv"], ["KTt"])
        TT("pool", KTv[:, :, :, 1, :], ch(rw["kmod"]), ch(rw["ginv"]), ALU.mult, ["kmod", "ginv"], ["KTt"])
        TT("dve", rw["L"], R2v, bc(fp["rk"][:].unsqueeze(2), [64, 8, 128]), ALU.mult, ["R2", "fp_rk"], ["L"])
        TT("dve", rw["L"], rw["L"], rw["kmod"], ALU.mult, ["L", "kmod"], ["L"])
        for h in range(8):
            MM(F2[:, 256 + h:257 + h], rw["L"][:, h, :], ones64[:, 0:1], True, True, ["L", "ones64"], ["F2"])
        CP("dve", bon, F2[:, 256:264], ["F2"], ["bon"])
        if KR < 4:
            return
        CP("act", Vc[:, 0], xmt[0:64, 1024:1536].rearrange("p (h i) -> p h i", h=8), ["xmt"], ["Vc"])
        MM(V2[0:64, 0:512], identf[:, 64:128], xmt[:, 1024:1536], True, True, ["identf", "xmt"], ["V2"])
        CP("act", Vc[:, 1], V2[0:64, 0:512].rearrange("p (h i) -> p h i", h=8), ["V2"], ["Vc"])
        if int(os.environ.get("KLVL", "9")) < 3:
            return
        for q in range(4):
            c, hg = q // 2, q % 2
            bk, bkn = (K2, "K2") if q % 2 == 0 else (V2, "V2")
            AMp = bk[0:64, :].rearrange("p (h x) -> p h x", h=4)
            for hd in range(4):
                h = hg * 4 + hd
                MM(AMp[:, hd, 0:128], KTt[:, h, c, 0:64], QTt[:, h, c, :], True, True, ["KTt", "QTt"], [bkn])
                MM(AMp[:, hd, 128:256], KTt[:, h, c, 64:128], QTt[:, h, c, :], True, True, ["KTt", "QTt"], [bkn])
            TT("dve", AM[:, q * 4:(q + 1) * 4, :], AMp, bc(maskT[:].unsqueeze(1), [64, 4, 256]), ALU.mult, [bkn, "maskT"], ["AM"])
        Lp = R2[0:64, :].rearrange("p (h x) -> p h x", h=16)
        for q in range(4):
            c, hg = q // 2, q % 2
            for hd in range(4):
                h = hg * 4 + hd
                MM(Lp[:, q * 4 + hd, :], QTt[:, h, c, 0:64], KTt[:, h, c, 0:64], True, True, ["KTt", "QTt"], ["R2"])
        TT("dve", Lm[0], Lp, bc(maskL[:].unsqueeze(1), [64, 16, 64]), ALU.mult, ["R2", "maskL"], ["Lm0"])
        CP("act", Nm[0], AM[:, :, 0:64], ["AM"], ["Nm0"])
        TT("dve", Pm[0], AM[:, :, 0:64], bc(identb64.unsqueeze(1), [64, 16, 64]), ALU.add, ["AM", "identb"], ["Pm0"])
        cur = 0
        Np = K2[0:64, :].rearrange("p (h x) -> p h x", h=16)
        Lpp = V2[0:64, :].rearrange("p (h x) -> p h x", h=16)
        PPp = R2[0:64, :].rearrange("p (h x) -> p h x", h=16)
        for lvl in range(1, 6):
            nx = 1 - cur
            for i in range(16):
                if lvl < 5:
                    MM(Np[:, i, :], Lm[cur][:, i, :], Nm[cur][:, i, :], True, True, ["Lm%d" % cur, "Nm%d" % cur], ["K2"])
                MM(Lpp[:, i, :], Nm[cur][:, i, :], Lm[cur][:, i, :], True, True, ["Lm%d" % cur, "Nm%d" % cur], ["V2"])
            if lvl < 5:
                CP("act", Nm[nx], Np, ["K2"], ["Nm%d" % nx])
            CP("dve", Lm[nx], Lpp, ["V2"], ["Lm%d" % nx])
            for i in range(16):
                MM(PPp[:, i, :], Lm[nx][:, i, :], Pm[cur][:, i, :], True, True, ["Lm%d" % nx, "Pm%d" % cur], ["R2"])
            TT("dve", Pm[nx], PPp, Pm[cur], ALU.add, ["R2", "Pm%d" % cur], ["Pm%d" % nx])
            cur = nx
        P6 = Pm[cur]
        P6n = "Pm%d" % cur
        for c in range(2):
            BKp = PTb[0:64, :].rearrange("p (h q j) -> p h q j", h=8, q=2)
            for h in range(8):
                TR(BKp[:, h, 0, :], KTt[:, h, c, 0:64], identb64, ["KTt", "identb"], ["PTb"])
                TR(BKp[:, h, 1, :], KTt[:, h, c, 64:128], identb64, ["KTt", "identb"], ["PTb"])
            CP("act", BKtok, BKp, ["PTb"], ["BKtok"])
            P0p = F2[0:64, :].rearrange("p (h i) -> p h i", h=8)
            Up = R2[0:64, 0:512].rearrange("p (h i) -> p h i", h=8)
            Yp = K2[0:64, 0:512].rearrange("p (h i) -> p h i", h=8)
            Hp = V2[0:64, 0:512].rearrange("p (h i) -> p h i", h=8)

            def ai(h):
                return (c * 2 + h // 4) * 4 + h % 4
            for h in range(8):
                MM(P0p[:, h, :], QTt[:, h, c, 0:64], Hb[:, h, :], True, False, ["QTt", "Hb"], ["F2"])
                MM(P0p[:, h, :], AM[:, ai(h), 128:192], Vc[:, c, h, :], False, True, ["AM", "Vc"], ["F2"])
            CP("act", P0s, P0p, ["F2"], ["P0s"])
            for h in range(8):
                MM(Up[:, h, :], P6[:, ai(h), :], P0s[:, h, :], True, True, [P6n, "P0s"], ["R2"])
            CP("act", Us, Up, ["R2"], ["Us"])
            for h in range(8):
                MM(Yp[:, h, :], QTt[:, h, c, 64:128], Hb[:, h, :], True, False, ["QTt", "Hb"], ["K2"])
                MM(Yp[:, h, :], AM[:, ai(h), 64:128], Us[:, h, :], False, False, ["AM", "Us"], ["K2"])
                MM(Yp[:, h, :], AM[:, ai(h), 192:256], Vc[:, c, h, :], False, True, ["AM", "Vc"], ["K2"])
            for h in range(8):
                MM(Hp[:, h, :], BKtok[:, h, 0, :], Us[:, h, :], True, False, ["BKtok", "Us"], ["V2"])
                MM(Hp[:, h, :], BKtok[:, h, 1, :], Vc[:, c, h, :], False, True, ["BKtok", "Vc"], ["V2"])
            CP("act", ych, Yp, ["K2"], ["ych"])
            TT("dve", H32, H32, Hp, ALU.add, ["H32", "V2"], ["H32"])
            TT("dve", H32, H32, bc(rw["g"][:, :, c * 64 + 63:c * 64 + 64], [64, 8, 64]), ALU.mult, ["H32", "g"], ["H32"])
            CP("act", Hb, H32, ["H32"], ["Hb"])
            RED(st8[:, 0:8], ych, ALU.add, ["ych"], ["st8"])
            TT("dve", yt1, ych, ych, ALU.mult, ["ych"], ["yt1"])
            RED(st8[:, 8:16], yt1, ALU.add, ["yt1"], ["st8"])
            TS("dve", st8[:, 0:16], st8[:, 0:16], 1.0 / 64, None, ALU.mult, None, ["st8"], ["st8"])
            TT("dve", st8[:, 16:24], st8[:, 0:8], st8[:, 0:8], ALU.mult, ["st8"], ["st8"])
            TT("dve", st8[:, 24:32], st8[:, 8:16], st8[:, 16:24], ALU.subtract, ["st8"], ["st8"])
            ACT(st8[:, 32:40], st8[:, 24:32], AF.Sqrt, ["st8", "cst"], ["st8"], bias=cst[0:64, 1:2])
            b.op("dve", lambda g: g.reciprocal(out=st8[:, 40:48], in_=st8[:, 32:40]), ["st8"], ["st8"])
            TT("dve", yt1, ych, bc(st8[:, 0:8].unsqueeze(2), [64, 8, 64]), ALU.subtract, ["ych", "st8"], ["yt1"])
            TT("dve", yt1, yt1, bc(st8[:, 40:48].unsqueeze(2), [64, 8, 64]), ALU.mult, ["yt1", "st8"], ["yt1"])
            lnwv = lnw_bc[:].rearrange("p (h i) -> p h i", h=8)
            lnbv = lnb_bc[:].rearrange("p (h i) -> p h i", h=8)
            TT("dve", yt1, yt1, lnwv, ALU.mult, ["yt1", "lnw_bc"], ["yt1"])
            TT("pool", yt1, yt1, lnbv, ALU.add, ["yt1", "lnb_bc"], ["yt1"])
            MM(F2[0:64, 264:272], identf[:, c * 64:(c + 1) * 64], bon, True, True, ["identf", "bon"], ["F2"])
            CP("act", st8[:, 48:56], F2[0:64, 264:272], ["F2"], ["st8"])
            TT("dve", ych, Vc[:, c], bc(st8[:, 48:56].unsqueeze(2), [64, 8, 64]), ALU.mult, ["Vc", "st8"], ["ych"])
            TT("pool", yt1, yt1, ych, ALU.add, ["yt1", "ych"], ["yt1"])
            MM(R2[0:64, 0:512], identf[:, c * 64:(c + 1) * 64], gr_s[:, :], True, True, ["identf", "gr_s"], ["R2"])
            TT("dve", yt1.rearrange("p h i -> p (h i)"), yt1.rearrange("p h i -> p (h i)"), R2[0:64, 0:512], ALU.mult, ["yt1", "R2"], ["yt1"])
            b.dma("sp", rwscr[ti * 128 + c * 64: ti * 128 + (c + 1) * 64, :], yt1.rearrange("p h i -> p (h i)"), reads=["yt1"], writes=["rwscr"])
        b.barrier(dmas=False)

    if stop_after == "A":
        b.barrier(); b.emit(); ncd.__exit__(None, None, None); es.close()
        return nc
    b.barrier()
    MS("dve", H32, 0.0, ["H32"]); MS("dve", Hb, 0.0, ["Hb"]); MS("pool", hlast[:], 0.0, ["hlast"])
    KSUB = int(os.environ.get('KSUB', '9'))
    V2a = V2[:, 0:512]
    V2b = V2[:, 512:1024]
    for ti in range(NT if nt_lim is None else nt_lim):
        par = ti % 2
        x_, h_, xr, hr = front(ti, par)
        if KSUB >= 1:
            for (c0, n, dst) in ((0, 256, V2a[:, 0:256]), (256, 64, V2a[:, 256:320]), (320, 512, V2b)):
                for k in range(8):
                    MM(dst, h_[:, k, :], Wn[:, k, c0:c0 + n], k == 0, k == 7, [hr, "Wn"], ["V2"])
        if KSUB >= 2:
            kv3 = kfin[:].rearrange("p (g d) -> p g d", g=2)
            qknorm(V2a[:, 0:128].rearrange("p (g d) -> p g d", g=2), kv3, 2, knw_bc, 1.0, ["V2"], ["kfin"])
            rope("dve", kv3, 2, None, ["kfin"])
            CP("act", vfin[:], V2a[:, 128:256], ["V2"], ["vfin"])
            CP("act", Vaug[:, ti, :, 0:64], V2a[:, 128:256].rearrange("p (g d) -> p g d", g=2), ["V2"], ["Vaug"])
            CP("act", kifin[:], V2a[:, 256:320], ["V2"], ["kifin"])
            rope("pool", kifin[:].unsqueeze(1), 1, None, ["kifin"])
            ACT(gr_s[:], V2b, AF.Silu, ["V2"], ["gr_s"])
        if KSUB >= 3:
            b.dma("sp", k_nat[ti * 128:(ti + 1) * 128, :], kfin[:], reads=["kfin"])
            b.dma("sp", v_nat[ti * 128:(ti + 1) * 128, :], vfin[:], reads=["vfin"])
            b.dma("sp", ki_nat[ti * 128:(ti + 1) * 128, :], kifin[:], reads=["kifin"])
        if KSUB >= 4:
            for g_ in range(2):
                TR(F2[0:64, g_ * 128:(g_ + 1) * 128], kfin[:, g_ * 64:(g_ + 1) * 64], identf[:], ["kfin", "identf"], ["F2"])
            TR(F2[0:64, 256:384], kifin[:, :], identf[:], ["kifin", "identf"], ["F2"])
            if KSUB >= 5:
                CP("act", kT_all[:, :, ti * 128:(ti + 1) * 128], F2[0:64, 0:256].rearrange("p (g t) -> p g t", g=2), ["F2"], ["kT_all"])
            if KSUB >= 6:
                if os.environ.get("KV") == "1":
                    CP("act", nrm_t[0:64, 0:128], F2[0:64, 256:384], ["F2"], ["nrm_t"])
                elif os.environ.get("KV") == "2":
                    CP("act", kiT_all[:, ti * 128:(ti + 1) * 128], F2[0:64, 0:128], ["F2"], ["kiT_all"])
                else:
                    CP("act", kiT_all[:, ti * 128:(ti + 1) * 128], F2[0:64, 256:384], ["F2"], ["kiT_all"])

        if int(os.environ.get("KLVL", "9")) >= 2:
            rwkv_tile(ti, h_, hr)
        if ti == NT - 1:
            for gi, (c0, n) in enumerate(((0, 512), (512, 512), (1024, 512), (1536, 128))):
                dst = (R2[0:1, 0:512], R2[0:1, 512:1024], K2[0:1, 0:512], K2[0:1, 512:640])[gi]
                nm = ("R2", "R2", "K2", "K2")[gi]
                for k in range(8):
                    MM(dst, h_[:, k, 127:128], Wom[:, k, c0:c0 + n], k == 0, False, ["Wm", hr], [nm])
                for k in range(8):
                    MM(dst, h_[:, k, 127:128], Wm[:, k, c0:c0 + n], False, k == 7, ["Wm", hr], [nm])
            CP("act", xmt[0:1, 0:1024], R2[0:1, :], ["R2"], ["xmt"])
            CP("dve", xmt[0:1, 1024:1664], K2[0:1, 0:640], ["K2"], ["xmt"])
            b.dma("sp", shift_p.rearrange("(a n) -> a n", a=1), xmt[0:1, :], reads=["xmt"])
    for h in range(8):
        TR(F2[0:64, h * 64:(h + 1) * 64], H32[:, h, :], identf[0:64, 0:64], ["H32", "identf"], ["F2"])
    CP("dve", ych, F2[0:64, 0:512].rearrange("p (h j) -> p h j", h=8), ["F2"], ["ych"])
    b.dma("sp", wkv_p.rearrange("h i j -> i h j"), ych, reads=["ych"])

    if stop_after == "B":
        b.barrier(); b.emit(); ncd.__exit__(None, None, None); es.close()
        return nc
    b.barrier()
    off = 0
    Wq, off = carve(off, [128, 8, 1544], BF16)
    stg, off = carve(off, [128, 8, 512])
    score, off = carve(off, [128, T])
    selm, off = carve(off, [128, T], BF16)
    selT, off = carve(off, [128, NT, 128], BF16)
    rl, off = carve(off, [128, 512], BF16)
    rl2, off = carve(off, [128, 512], BF16)
    rlf, off = carve(off, [128, 256])
    diagw, off = carve(off, [128, 8, 128], BF16)
    qfin, off = carve(off, [128, 512])
    qifin, off = carve(off, [128, 512])
    qT, off = carve(off, [64, 8, 128], BF16)
    qiT, off = carve(off, [64, 8, 128], BF16)
    ga, off = carve(off, [128, 512])
    eT, off = carve(off, [128, 4, 128], BF16)
    pTt, off = carve(off, [128, 4, 128], BF16)
    eT2, off = carve(off, [128, 4, 128], BF16)
    pTt2, off = carve(off, [128, 4, 128], BF16)
    cat, off = carve(off, [128, D], BF16)
    catT, off = carve(off, [128, 8, 128], BF16)
    rwo, off = carve(off, [128, 512])
    rwo2, off = carve(off, [128, 512])
    att, off = carve(off, [128, 8, 64])
    ybuf, off = carve(off, [128, D])
    bs, off = carve(off, [128, 16])
    wis, off = carve(off, [128, 8])
    oacc, off = carve(off, [128, 2, 4, 65])
    Wout, off = carve(off, [128, 8, D], BF16)
    gate_bc, off = carve(off, [128, D])
    iota256, off = carve(off, [128, 256])
    assert off <= AW, off
    b.dma("sp", gate_bc, gscr[16:17, :].partition_broadcast(128) if False else gscr[16, :].partition_broadcast(128), reads=["gscr"], writes=["gate_bc"])
    b.dma("sp", iota256, iota_d[:, :], writes=["iota256"])
    load_cols(Wq, 0, C_Q, 512, tag="Wq")
    load_cols(Wq, 512, C_QI, 520, tag="Wq")
    load_cols(Wq, 1032, C_GA, 512, tag="Wq")
    w_out_v = w_out.rearrange("(k p) c -> p k c", p=128)
    for hh in range(2):
        b.dma("sp", stg[:, :, :], w_out_v[:, :, hh * 512:(hh + 1) * 512], writes=["stg"])
        CP("pool", Wout[:, :, hh * 512:(hh + 1) * 512], stg[:, :, :], ["stg"], ["Wout"])


    for j in range(NO if no_lim is None else no_lim):
        ti = NT + j
        x_, h_, xr, hr = front(ti, 0)
        NKT = 2 * (j + 1)
        NK = NKT * 128
        for (c0, n, dst, nm) in ((0, 512, R2[:, 0:512], "R2a"), (512, 512, R2[:, 512:1024], "R2b"),
                                 (1024, 8, F2[:, 0:8], "F2"), (1032, 512, K2[:, 0:512], "K2a")):
            for k in range(8):
                MM(dst, h_[:, k, :], Wq[:, k, c0:c0 + n], k == 0, k == 7, [hr, "Wq"], [nm])
        q3 = qfin.rearrange("p (h d) -> p h d", h=8)
        qknorm(R2[:, 0:512].rearrange("p (h d) -> p h d", h=8), q3, 8, qnw_bc, 0.125, ["R2a"], ["qfin"])
        rope("dve", q3, 8, None, ["qfin"])
        qi3 = qifin.rearrange("p (h d) -> p h d", h=8)
        CP("act", qifin, R2[:, 512:1024], ["R2b"], ["qifin"])
        rope("pool", qi3, 8, None, ["qifin"])
        TS("dve", wis, F2[:, 0:8], 0.044194173824159216, None, ALU.mult, None, ["F2"], ["wis"])
        ACT(ga, K2[:, 0:512], AF.Silu, ["K2a"], ["ga"])
        for (src, srcn, dstT, dn) in ((qfin, "qfin", qT, "qT"), (qifin, "qifin", qiT, "qiT")):
            pv = K2[0:64, :].rearrange("p (h t) -> p h t", h=8)
            for h in range(8):
                TR(pv[:, h, :], src[:, h * 64:(h + 1) * 64], identf[:], [srcn, "identf"], ["K2a" if h < 4 else "K2b"])
            CP("act", dstT, pv, ["K2a", "K2b"], [dn])
        TT("dve", diagw, bc(identb[:].unsqueeze(1), [128, 8, 128]), bc(wis.unsqueeze(2), [128, 8, 128]), ALU.mult, ["identb", "wis"], ["diagw"])
        nchk = (NK + 511) // 512
        ib = 0
        for kc in range(nchk):
            w = min(512, NK - kc * 512)
            pend = None
            for h in range(8):
                pb = (R2[:, 0:512], R2[:, 512:1024])[ib % 2]
                pbn = ("R2a", "R2b")[ib % 2]
                rlb = (rl, rl2)[ib % 2]
                rln = ("rl", "rl2")[ib % 2]
                ib += 1
                MM(pb[:, 0:w], qiT[:, h, :], kiT_all[:, kc * 512:kc * 512 + w], True, True, ["qiT", "kiT_all"], [pbn])
                ACT(rlb[:, 0:w], pb[:, 0:w], AF.Relu, [pbn], [rln])
                if pend is not None:
                    ph, prl, prn = pend
                    MM(F2[:, 0:w], diagw[:, ph, :], prl[:, 0:w], ph == 0, False, ["diagw", prn], ["F2"])
                pend = (h, rlb, rln)
            ph, prl, prn = pend
            MM(F2[:, 0:w], diagw[:, ph, :], prl[:, 0:w], False, True, ["diagw", prn], ["F2"])
            CP("dve", score[:, kc * 512:kc * 512 + w], F2[:, 0:w], ["F2"], ["score"])
        RED(bs[:, 0:1], score[:, 0:NK], ALU.max, ["score"], ["bs"])
        RED(bs[:, 1:2], score[:, 0:NK], ALU.min, ["score"], ["bs"])
        TS("dve", rlf, iota256, qrel[:, 0:1], -1e30, ALU.is_gt, ALU.mult, ["iota256", "qrel"], ["rlf"])
        TT("dve", score[:, NK - 256:NK], score[:, NK - 256:NK], rlf, ALU.add, ["score", "rlf"], ["score"])
        TS("dve", bs[:, 2:3], bs[:, 1:2], -1.0, None, ALU.add, None, ["bs"], ["bs"])
        STT(bs[:, 3:4], bs[:, 0:1], 2.0, bs[:, 1:2], ALU.add, ALU.subtract, ["bs"], ["bs"])
        for it in range(1, n_bis + 1):
            sc_ = float(2.0 ** (-it))
            STT(bs[:, 4:5], bs[:, 3:4], sc_, bs[:, 2:3], ALU.mult, ALU.add, ["bs"], ["bs"])
            TS("dve", selm[:, 0:NK], score[:, 0:NK], bs[:, 4:5], 0.0, ALU.is_gt, ALU.add, ["score", "bs"], ["selm", "bs"], accum=bs[:, 5:6])
            TS("dve", bs[:, 6:7], bs[:, 5:6], float(topk_p) - 0.5, bs[:, 3:4], ALU.is_gt, ALU.mult, ["bs"], ["bs"])
            STT(bs[:, 2:3], bs[:, 6:7], sc_, bs[:, 2:3], ALU.mult, ALU.add, ["bs"], ["bs"])
        TS("dve", selm[:, 0:NK], score[:, 0:NK], bs[:, 2:3], None, ALU.is_gt, None, ["score", "bs"], ["selm"])
        for kt in range(NKT):
            TR(PTb[:, (kt % 8) * 128:(kt % 8 + 1) * 128], selm[:, kt * 128:(kt + 1) * 128], identb[:], ["selm", "identb"], ["PTb"])
            if kt % 8 == 7 or kt == NKT - 1:
                k0 = (kt // 8) * 8
                n_ = kt - k0 + 1
                ACT(selT[:, k0:k0 + n_, :], PTb[:, 0:n_ * 128].rearrange("p (a t) -> p a t", a=n_), AF.Identity, ["PTb", "cst"], ["selT"],
                    scale=30000.0, bias=cst[:, 3:4])
        po = [V2[:, 0:260].rearrange("p (h e) -> p h e", h=4), V2[:, 512:772].rearrange("p (h e) -> p h e", h=4)]
        def att_front(kt, g_):
            lp = K2[:, g_ * 512:(g_ + 1) * 512]
            kn_ = ("K2a", "K2b")[g_]
            pTb = (pTt, pTt2)[g_]
            pn_ = ("pTt", "pTt2")[g_]
            MM(lp, kT_all[:, g_, kt * 128:(kt + 1) * 128], qT[:, g_ * 4:(g_ + 1) * 4, :].rearrange("p h t -> p (h t)"),
               True, False, ["kT_all", "qT"], [kn_])
            for hh in range(4):
                MM(lp[:, hh * 128:(hh + 1) * 128], identb[:], selT[:, kt, :], False, hh == 3, ["identb", "selT"], [kn_])
            ACT(pTb, lp.rearrange("p (h t) -> p h t", h=4), AF.Exp, [kn_], [pn_])

        def att_back(kt, g_):
            vn_ = ("V2a", "V2b")[g_]
            pTb = (pTt, pTt2)[g_]
            pn_ = ("pTt", "pTt2")[g_]
            on_ = ("oacc0", "oacc1")[g_]
            for hh in range(4):
                MM(po[g_][:, hh, :], pTb[:, hh, :], Vaug[:, kt, g_, :], True, True, [pn_, "Vaug"], [vn_])
            if kt == 0:
                CP("act", oacc[:, g_], po[g_], [vn_], [on_])
            else:
                TT("dve", oacc[:, g_], oacc[:, g_], po[g_], ALU.add, [vn_, on_], [on_])
        att_front(0, 0)
        for kt in range(NKT):
            att_front(kt, 1)
            att_back(kt, 0)
            if kt + 1 < NKT:
                att_front(kt + 1, 0)
            att_back(kt, 1)
        for g_ in range(2):
            b.op("dve", lambda g, g_=g_: g.reciprocal(out=bs[:, 8 + g_ * 4:12 + g_ * 4], in_=oacc[:, g_, :, 64]), ["oacc0", "oacc1"], ["bs"])
            TT("dve", att[:, g_ * 4:(g_ + 1) * 4, :], oacc[:, g_, :, 0:64], bc(bs[:, 8 + g_ * 4:12 + g_ * 4].unsqueeze(2), [128, 4, 64]),
               ALU.mult, ["oacc0", "oacc1", "bs"], ["att"])
        TT("dve", cat[:, 0:512], att.rearrange("p h d -> p (h d)"), ga, ALU.mult, ["att", "ga"], ["cat"])
        b.dma("sp", rwo, rwscr[(2 * j) * 128:(2 * j + 1) * 128, :], reads=["rwscr"], writes=["rwo"])
        b.dma("sp", rwo2, rwscr[(2 * j + 1) * 128:(2 * j + 2) * 128, :], reads=["rwscr"], writes=["rwo2"])
        TS("dve", rwo, rwo, parsel[:, 1:2], None, ALU.mult, None, ["rwo", "parsel"], ["rwo"])
        STT(rwo, rwo2, parsel[:, 0:1], rwo, ALU.mult, ALU.add, ["rwo2", "parsel", "rwo"], ["rwo"])
        CP("act", cat[:, 512:1024], rwo, ["rwo"], ["cat"])
        for k in range(8):
            TR(PTb[:, k * 128:(k + 1) * 128], cat[:, k * 128:(k + 1) * 128], identb[:], ["cat", "identb"], ["PTb"])
        CP("act", catT, PTb[:, :].rearrange("p (k t) -> p k t", k=8), ["PTb"], ["catT"])
        for hh in range(2):
            for k in range(8):
                MM(R2[:, hh * 512:(hh + 1) * 512], catT[:, k, :], Wout[:, k, hh * 512:(hh + 1) * 512], k == 0, k == 7, ["catT", "Wout"], [("R2a", "R2b")[hh]])
        TT("dve", ybuf, R2[:, :], gate_bc, ALU.mult, ["R2a", "R2b", "gate_bc"], ["ybuf"])
        TT("pool", ybuf, ybuf, x_[:], ALU.add, ["ybuf", xr], ["ybuf"])
        b.dma("sp", y_own[j * 128:(j + 1) * 128, :], ybuf, reads=["ybuf"])


    if do_sample:
        b.barrier()
        PW = 10500
        off = 0
        proj, off = carve(off, [16, DIN])
        tk = {}
        for nm in ("qs", "ga", "grs", "ta", "tb"):
            tk[nm], off = carve(off, [16, 512])
        ks_, off = carve(off, [16, 128])
        s16, off = carve(off, [16, 64])
        tokd, off = carve(off, [16, 1040])
        ysb, off = carve(off, [16, D])
        cats, off = carve(off, [16, D], BF16)
        catTs, off = carve(off, [128, 8, 16], BF16)
        assert off <= PW, off
        off = PW
        stg, off = carve(off, [128, 8, 512])
        wbf, off = carve(off, [128, 8, 512], BF16)
        sshift_t, off = carve(off, [16, SHW])
        mu16, off = carve(off, [16, SHW])
        X1 = off
        xm, off = carve(off, [16, SHW])
        prm, off = carve(off, [16, 5, 512])
        vecs, off = carve(off, [16, 8, 6, 64])
        for nm in ("dec", "asg", "kkv", "kkn", "kmod"):
            tk[nm], off = carve(off, [16, 512])
        wdt, off = carve(off, [16, 128])
        wdT, off = carve(off, [64, 32])
        assert off <= AW, off
        NPAIR = NS // 2

        x_, h_, xr, hr = front(NT + NO, 0, m_prompt=False, ntok=16)
        for ch in range(8):
            c0 = ch * 505
            b.dma("sp", stg[:, :, 0:505], w_in_v[:, :, c0:c0 + 505], writes=["stg"])
            CP("dve", wbf[:, :, 0:505], stg[:, :, 0:505], ["stg"], ["wbf"])
            for k in range(8):
                MM(R2[0:16, 0:505], h_[:, k, 0:16], wbf[:, k, 0:505], k == 0, k == 7, [hr, "wbf"], ["R2"])
            CP("act", proj[:, c0:c0 + 505], R2[0:16, 0:505], ["R2"], ["proj"])
        b.dma("sp", sshift_t, sshift_d[:, :], writes=["sshift"])
        b.dma("sp", mu16, mu.partition_broadcast(16), writes=["mu16"])
        for i_, src in enumerate((pw0, pa0, pkk, pka, prk)):
            b.dma("sp", prm[:, i_, :], src.partition_broadcast(16), writes=["prm"])
        qs3 = tk["qs"].rearrange("p (h d) -> p h d", h=8)
        qknorm(proj[:, 0:512].rearrange("p (h d) -> p h d", h=8), qs3, 8, qnw_bc, 0.125, ["proj"], ["qs"], nrows=16)
        rope("dve", qs3, 8, None, ["qs"], nrows=16)
        ks3 = ks_.rearrange("p (g d) -> p g d", g=2)
        qknorm(proj[:, 512:640].rearrange("p (g d) -> p g d", g=2), ks3, 2, knw_bc, 1.0, ["proj"], ["ks"], nrows=16)
        rope("dve", ks3, 2, None, ["ks"], nrows=16)
        b.dma("sp", k_s[:, :], ks_, reads=["ks"])
        b.dma("sp", v_s[:, :], proj[:, 640:768], reads=["proj"])
        b.dma("sp", shift_s[:, :], proj[:, C_R:C_R + SHW], reads=["proj"])
        rope("dve", proj[:, 768:1280].rearrange("p (h d) -> p h d", h=8), 8, None, ["proj"], nrows=16)
        rope("dve", proj[:, 1288:1352].unsqueeze(1), 1, None, ["proj"], nrows=16)
        b.dma("sp", ki_s[:, :], proj[:, 1288:1352], reads=["proj"])
        ACT(tk["ga"], proj[:, C_GA:C_GA + 512], AF.Silu, ["proj"], ["ga"])
        ACT(tk["grs"], proj[:, C_GR:C_GR + 512], AF.Silu, ["proj"], ["grs"])
        xs_ = proj[:, C_R:C_R + SHW]
        TT("dve", xm, sshift_t, xs_, ALU.subtract, ["sshift", "proj"], ["xm"])
        TT("dve", xm, xm, mu16, ALU.mult, ["xm", "mu16"], ["xm"])
        TT("dve", xm, xm, xs_, ALU.add, ["xm", "proj"], ["xm"])
        ACT(wdt[:, 0:64], xm[:, 1536:1600], AF.Tanh, ["xm"], ["wdt"])
        CP("dve", wdt[:, 64:128], xm[:, 1600:1664], ["xm"], ["wdt"])
        TR(F2[0:64, 0:16], wdt[:, 0:64], identf[0:16, 0:16], ["wdt", "identf"], ["F2"])
        TR(F2[0:64, 16:32], wdt[:, 64:128], identf[0:16, 0:16], ["wdt", "identf"], ["F2"])
        CP("dve", wdT, F2[0:64, 0:32], ["F2"], ["wdT"])
        MM(R2[0:16, 0:512], wdT[:, 0:16], wupS[:, :], True, True, ["wdT", "wupS"], ["R2"])
        MM(R2[0:16, 512:1024], wdT[:, 16:32], aupS[:, :], True, True, ["wdT", "aupS"], ["R2"])
        TT("dve", tk["dec"], R2[0:16, 0:512], prm[:, 0, :], ALU.add, ["R2", "prm"], ["dec"])
        ACT(tk["dec"], tk["dec"], AF.Sigmoid, ["dec"], ["dec"])
        ACT(tk["dec"], tk["dec"], AF.Exp, ["dec"], ["dec"], scale=-0.6065306597126334)
        TT("dve", tk["asg"], R2[0:16, 512:1024], prm[:, 1, :], ALU.add, ["R2", "prm"], ["asg"])
        ACT(tk["asg"], tk["asg"], AF.Sigmoid, ["asg"], ["asg"])
        xr_, xk_, xv_ = xm[:, 0:512], xm[:, 512:1024], xm[:, 1024:1536]
        TT("dve", tk["kkv"], xk_, prm[:, 2, :], ALU.mult, ["xm", "prm"], ["kkv"])
        ACT(tk["ta"], tk["kkv"], AF.Square, ["kkv"], ["ta"])
        RED(s16[:, 0:8], tk["ta"].rearrange("p (h d) -> p h d", h=8), ALU.add, ["ta"], ["s16"])
        ACT(s16[:, 8:16], s16[:, 0:8], AF.Sqrt, ["s16", "cst"], ["s16"], bias=cst[0:16, 2:3])
        b.op("dve", lambda g: g.reciprocal(out=s16[:, 16:24], in_=s16[:, 8:16]), ["s16"], ["s16"])
        TT("dve", tk["kkn"].rearrange("p (h d) -> p h d", h=8), tk["kkv"].rearrange("p (h d) -> p h d", h=8),
           bc(s16[:, 16:24].unsqueeze(2), [16, 8, 64]), ALU.mult, ["kkv", "s16"], ["kkn"])
        STT(tk["ta"], tk["asg"], -1.0, prm[:, 3, :], ALU.add, ALU.mult, ["asg", "prm"], ["ta"])
        STT(tk["kmod"], tk["ta"], 1.0, xk_, ALU.add, ALU.mult, ["ta", "xm"], ["kmod"])

        def v8(ap):
            return ap.rearrange("p (h d) -> p h d", h=8)
        CP("dve", vecs[:, :, 0, :], v8(tk["dec"]), ["dec"], ["vecs"])
        TS("dve", vecs[:, :, 1, :], v8(tk["kkn"]), -1.0, None, ALU.mult, None, ["kkn"], ["vecs"])
        TT("dve", vecs[:, :, 2, :], v8(tk["kkn"]), v8(tk["asg"]), ALU.mult, ["kkn", "asg"], ["vecs"])
        CP("dve", vecs[:, :, 3, :], v8(tk["kmod"]), ["kmod"], ["vecs"])
        CP("dve", vecs[:, :, 4, :], v8(xr_), ["xm"], ["vecs"])
        CP("dve", vecs[:, :, 5, :], v8(xv_), ["xm"], ["vecs"])
        TT("dve", tk["ta"], xr_, prm[:, 4, :], ALU.mult, ["xm", "prm"], ["ta"])
        TT("dve", tk["ta"], tk["ta"], tk["kmod"], ALU.mult, ["ta", "kmod"], ["ta"])
        RED(s16[:, 24:32], v8(tk["ta"]), ALU.add, ["ta"], ["s16"])
        b.dma("sp", scr1[:, :], vecs.rearrange("p h v j -> p (h v j)"), reads=["vecs"], writes=["scr1"])
        b.barrier()
        off = PW
        S_, off = carve(off, [128, 4096])
        tmpS, off = carve(off, [128, 4096])
        vsh, off = carve(off, [128, 384])
        ysh, off = carve(off, [128, 128])
        assert off <= X1
        b.dma("sp", S_, swkv_d[:, :], writes=["S"])
        b.dma("sp", vsh, scr1.rearrange("s (h x) -> (s h) x", h=8), reads=["scr1"], writes=["vsh"])
        S3 = S_.rearrange("p (i j) -> p i j", i=64)
        T3 = tmpS.rearrange("p (i j) -> p i j", i=64)

        def jb(vi):
            return bc(vsh[:, vi * 64:(vi + 1) * 64].unsqueeze(1), [128, 64, 64])

        def ib(ap):
            return bc(ap.unsqueeze(2), [128, 64, 64])
        TT("dve", T3, S3, jb(1), ALU.mult, ["S", "vsh"], ["tmpS"])
        RED(ysh[:, 0:64], T3, ALU.add, ["tmpS"], ["ysh"])
        TT("dve", S3, S3, jb(0), ALU.mult, ["S", "vsh"], ["S"])
        TT("dve", T3, jb(2), ib(ysh[:, 0:64]), ALU.mult, ["vsh", "ysh"], ["tmpS"])
        TT("dve", S3, S3, T3, ALU.add, ["S", "tmpS"], ["S"])
        TT("dve", T3, jb(3), ib(vsh[:, 320:384]), ALU.mult, ["vsh"], ["tmpS"])
        TT("dve", S3, S3, T3, ALU.add, ["S", "tmpS"], ["S"])
        b.dma("sp", wkv_s[:, :], S_, reads=["S"])
        TT("dve", T3, S3, jb(4), ALU.mult, ["S", "vsh"], ["tmpS"])
        RED(ysh[:, 64:128], T3, ALU.add, ["tmpS"], ["ysh"])
        b.dma("sp", scr2[:, :], ysh[:, 64:128], reads=["ysh"], writes=["scr2"])
        yS = tk["tb"]
        b.dma("sp", yS, scr2.rearrange("(s h) i -> s (h i)", h=8), reads=["scr2"], writes=["tb"])
        y3 = v8(yS)
        RED(s16[:, 32:40], y3, ALU.add, ["tb"], ["s16"])
        ACT(tk["ta"], yS, AF.Square, ["tb"], ["ta"])
        RED(s16[:, 40:48], v8(tk["ta"]), ALU.add, ["ta"], ["s16"])
        TS("dve", s16[:, 32:48], s16[:, 32:48], 1.0 / 64, None, ALU.mult, None, ["s16"], ["s16"])
        TT("dve", s16[:, 48:56], s16[:, 32:40], s16[:, 32:40], ALU.mult, ["s16"], ["s16"])
        TT("dve", s16[:, 48:56], s16[:, 40:48], s16[:, 48:56], ALU.subtract, ["s16"], ["s16"])
        ACT(s16[:, 56:64], s16[:, 48:56], AF.Sqrt, ["s16", "cst"], ["s16"], bias=cst[0:16, 1:2])
        b.op("dve", lambda g: g.reciprocal(out=s16[:, 56:64], in_=s16[:, 56:64]), ["s16"], ["s16"])
        TT("dve", y3, y3, bc(s16[:, 32:40].unsqueeze(2), [16, 8, 64]), ALU.subtract, ["tb", "s16"], ["tb"])
        TT("dve", y3, y3, bc(s16[:, 56:64].unsqueeze(2), [16, 8, 64]), ALU.mult, ["tb", "s16"], ["tb"])
        TT("dve", yS, yS, lnw_bc[0:16, :], ALU.mult, ["tb", "lnw_bc"], ["tb"])
        TT("dve", yS, yS, lnb_bc[0:16, :], ALU.add, ["tb", "lnb_bc"], ["tb"])
        TT("dve", v8(tk["ta"]), v8(xv_), bc(s16[:, 24:32].unsqueeze(2), [16, 8, 64]), ALU.mult, ["xm", "s16"], ["ta"])
        TT("dve", yS, yS, tk["ta"], ALU.add, ["tb", "ta"], ["tb"])
        TT("dve", cats[:, 512:1024], yS, tk["grs"], ALU.mult, ["tb", "grs"], ["cats"])
        b.barrier()
        NCAND = 16
        off = PW
        Gi, off = carve(off, [128, 8192])
        tmpG, off = carve(off, [128, 64, 64])
        Kc, off = carve(off, [128, NCAND, 128])
        Vcd, off = carve(off, [128, NCAND, 128])
        tmpc, off = carve(off, [128, NCAND, 64])
        repd, off = carve(off, [128, 1040])
        opd, off = carve(off, [128, 520])
        repS, off = carve(off, [16, 1024])
        repTS, off = carve(off, [128, 128])
        blkS, off = carve(off, [128, 128])
        sc, off = carve(off, [128, 132])
        msc, off = carve(off, [128, 132])
        sh_, off = carve(off, [128, 128])
        cv, off = carve(off, [128, NCAND])
        ci, off = carve(off, [128, NCAND], I32)
        cif, off = carve(off, [128, NCAND])
        rowi, off = carve(off, [128, NCAND], I32)
        lg, off = carve(off, [128, 8, NCAND])
        b2, off = carve(off, [128, 16])
        ptab, off = carve(off, [128, 8], I32)
        ptf, off = carve(off, [128, 8])
        oh0, off = carve(off, [128, 1])
        assert off <= AW, off
        Gi3 = Gi.rearrange("p (t d) -> p t d", t=128)
        for (t_, d_, nm) in ((ptab, ptab_d, "ptab"), (repS, rep_d, "repS"), (repTS, repT_d, "repTS"), (blkS, blk_d, "blkS"), (oh0, oh0_d, "oh0")):
            b.dma("sp", t_, d_[:, :], writes=[nm])
        CP("dve", tokd[:, 0:512], proj[:, 768:1280], ["proj"], ["tokd"])
        TS("dve", tokd[:, 512:520], proj[:, 1280:1288], 0.044194173824159216, None, ALU.mult, None, ["proj"], ["tokd"])
        CP("dve", tokd[:, 520:1032], tk["qs"], ["qs"], ["tokd"])
        TT("dve", v8(tk["ta"]), v8(tokd[:, 0:512]), bc(proj[:, 1288:1352].unsqueeze(1), [16, 8, 64]), ALU.mult, ["tokd", "proj"], ["ta"])
        RED(s16[:, 0:8], v8(tk["ta"]), ALU.add, ["ta"], ["s16"])
        TS("dve", s16[:, 0:8], s16[:, 0:8], 0.0, None, ALU.max, None, ["s16"], ["s16"])
        TT("dve", s16[:, 0:8], s16[:, 0:8], tokd[:, 512:520], ALU.mult, ["s16", "tokd"], ["s16"])
        RED(tokd[:, 1032:1033], s16[:, 0:8], ALU.add, ["s16"], ["tokd"])
        CP("dve", ptf, ptab, ["ptab"], ["ptf"])
        TS("dve", ptf, ptf, 128.0, None, ALU.mult, None, ["ptf"], ["ptf"])
        ck_rows = cache_k
        cv_rows = cache_v
        cvA, off = carve(off, [128, 8, NCAND])
        Kc2, off = carve(off, [128, NCAND, 128])
        Vcd2, off = carve(off, [128, NCAND, 128])
        rowiA, off = carve(off, [128, 8, NCAND], I32)
        cvalA, off = carve(off, [128, 8, NCAND])
        ciA, off = carve(off, [128, 8, NCAND], I32)
        thrA, off = carve(off, [128, 8])
        tmpGf = tmpG.rearrange("p a b -> p (a b)")
        cand16 = tmpGf[0:16, 0:1540]
        candj = tmpGf[0:16, 1540:3080]
        assert off <= AW, off
        for sp in range(NPAIR):
            b.op("pool", lambda g, sp=sp: g.indirect_dma_start(out=Gi, out_offset=None, in_=cache_ki[:, :],
                                                                in_offset=bass.IndirectOffsetOnAxis(ap=ptab[:, sp:sp + 1], axis=0)),
                 ["ptab"], ["Gi"], dma=True)
            for (c0, n) in ((0, 512), (512, 8)):
                MM(K2[:, 0:n], repS[:, sp * 128:(sp + 1) * 128], tokd[:, c0:c0 + n], True, True, ["repS", "tokd"], ["K2"])
                CP("act", repd[:, c0:c0 + n], K2[:, 0:n], ["K2"], ["repd"])
            for h in range(8):
                for hf in range(2):
                    TT("dve", tmpG, Gi3[:, hf * 64:(hf + 1) * 64, :], bc(repd[:, h * 64:(h + 1) * 64].unsqueeze(1), [128, 64, 64]), ALU.mult, ["Gi", "repd"], ["tmpG"])
                    RED(sh_[:, hf * 64:(hf + 1) * 64], tmpG, ALU.add, ["tmpG"], ["sh"])
                if h == 0:
                    TS("dve", sc[:, 0:128], sh_, 0.0, repd[:, 512:513], ALU.max, ALU.mult, ["sh", "repd"], ["sc"])
                else:
                    TS("dve", sh_, sh_, 0.0, repd[:, 512 + h:513 + h], ALU.max, ALU.mult, ["sh", "repd"], ["sh"])
                    TT("dve", sc[:, 0:128], sc[:, 0:128], sh_, ALU.add, ["sc", "sh"], ["sc"])
            for r_ in range(NCAND // 8):
                b.op("dve", lambda g, r_=r_, sp=sp: g.max(out=cvA[:, sp, r_ * 8:(r_ + 1) * 8], in_=sc[:, 0:128]), ["sc"], ["cvA"])
                b.op("dve", lambda g, r_=r_, sp=sp: g.max_index(out=ciA[:, sp, r_ * 8:(r_ + 1) * 8].bitcast(mybir.dt.uint32),
                                                                in_max=cvA[:, sp, r_ * 8:(r_ + 1) * 8], in_values=sc[:, 0:128]), ["sc", "cvA"], ["ciA"])
                if r_ < NCAND // 8 - 1:
                    b.op("dve", lambda g, r_=r_, sp=sp: g.match_replace(out=sc[:, 0:128], in_to_replace=cvA[:, sp, r_ * 8:(r_ + 1) * 8],
                                                                        in_values=sc[:, 0:128], imm_value=-3e30), ["sc", "cvA"], ["sc"])
        b.dma("sp", scr4.rearrange("(sp s2) g c -> (s2 g) sp c", s2=2), cvA, reads=["cvA"], writes=["scr4"])
        b.dma("sp", cand16[:, 0:64 * NCAND], scr4.rearrange("s g c -> s (g c)"), reads=["scr4"], writes=["tmpG"])
        CP("dve", cand16[:, 64 * NCAND:64 * NCAND + 1], tokd[:, 1032:1033], ["tokd"], ["tmpG"])
        cnd = cand16[:, 0:64 * NCAND + 1]
        RED(s16[:, 40:41], cnd, ALU.max, ["tmpG"], ["s16"])
        RED(s16[:, 41:42], cnd, ALU.min, ["tmpG"], ["s16"])
        TS("dve", s16[:, 42:43], s16[:, 41:42], -1.0, None, ALU.add, None, ["s16"], ["s16"])
        STT(s16[:, 43:44], s16[:, 40:41], 2.0, s16[:, 41:42], ALU.add, ALU.subtract, ["s16"], ["s16"])
        for it in range(1, n_bis + 2):
            sc_ = float(2.0 ** (-it))
            STT(s16[:, 44:45], s16[:, 43:44], sc_, s16[:, 42:43], ALU.mult, ALU.add, ["s16"], ["s16"])
            TS("dve", candj[:, 0:64 * NCAND + 1], cnd, s16[:, 44:45], 0.0, ALU.is_gt, ALU.add, ["tmpG", "s16"], ["tmpG", "s16"], accum=s16[:, 45:46])
            TS("dve", s16[:, 46:47], s16[:, 45:46], float(topk_s) - 0.5, s16[:, 43:44], ALU.is_gt, ALU.mult, ["s16"], ["s16"])
            STT(s16[:, 42:43], s16[:, 46:47], sc_, s16[:, 42:43], ALU.mult, ALU.add, ["s16"], ["s16"])
        TT("dve", s16[:, 32:33], tokd[:, 1032:1033], s16[:, 42:43], ALU.is_gt, ["tokd", "s16"], ["s16"])
        for sp in range(NPAIR):
            MM(F2[:, sp:sp + 1], repS[:, sp * 128:(sp + 1) * 128], s16[:, 42:43], True, True, ["repS", "s16"], ["F2"])
        CP("dve", thrA, F2[:, 0:8], ["F2"], ["thrA"])
        for sp in range(NPAIR):
            CP("dve", cif, ciA[:, sp, :], ["ciA"], ["cif"])
            TS("dve", cif, cif, ptf[:, sp:sp + 1], None, ALU.add, None, ["cif", "ptf"], ["cif"])
            CP("dve", rowiA[:, sp, :], cif, ["cif"], ["rowiA"])
            TS("dve", cvalA[:, sp, :], cvA[:, sp, :], thrA[:, sp:sp + 1], None, ALU.is_gt, None, ["cvA", "thrA"], ["cvalA"])
        KcB = (Kc, Kc2)
        VcB = (Vcd, Vcd2)

        def gathers(sp):
            kb, vb = KcB[sp % 2], VcB[sp % 2]
            kn, vn = "Kc%d" % (sp % 2), "Vcd%d" % (sp % 2)
            for c_ in range(NCAND):
                b.op("pool", lambda g, c_=c_: g.indirect_dma_start(out=kb[:, c_, :], out_offset=None, in_=ck_rows[:, :],
                                                                  in_offset=bass.IndirectOffsetOnAxis(ap=rowiA[:, sp, c_:c_ + 1], axis=0)),
                     ["rowiA"], [kn], dma=True)
                b.op("pool", lambda g, c_=c_: g.indirect_dma_start(out=vb[:, c_, :], out_offset=None, in_=cv_rows[:, :],
                                                                  in_offset=bass.IndirectOffsetOnAxis(ap=rowiA[:, sp, c_:c_ + 1], axis=0)),
                     ["rowiA"], [vn], dma=True)
        gathers(0)
        for sp in range(NPAIR):
            if sp + 1 < NPAIR:
                gathers(sp + 1)
            kn, vn = "Kc%d" % (sp % 2), "Vcd%d" % (sp % 2)
            MM(K2[:, 0:512], repS[:, sp * 128:(sp + 1) * 128], tokd[:, 520:1032], True, True, ["repS", "tokd"], ["K2"])
            CP("act", repd[:, 520:1032], K2[:, 0:512], ["K2"], ["repd"])
            Kc4 = KcB[sp % 2].rearrange("p c (g d) -> p c g d", g=2)
            Vc4 = VcB[sp % 2].rearrange("p c (g d) -> p c g d", g=2)
            cvv = cvalA[:, sp, :]
            for h in range(8):
                TT("dve", tmpc, Kc4[:, :, h // 4, :], bc(repd[:, 520 + h * 64:520 + (h + 1) * 64].unsqueeze(1), [128, NCAND, 64]), ALU.mult, [kn, "repd"], ["tmpc"])
                RED(lg[:, h, :], tmpc, ALU.add, ["tmpc"], ["lg"])
            ACT(lg, lg, AF.Exp, ["lg"], ["lg"])
            TT("dve", lg, lg, bc(cvv.unsqueeze(1), [128, 8, NCAND]), ALU.mult, ["lg", "cvalA"], ["lg"])
            RED(opd[:, 512:520], lg, ALU.add, ["lg"], ["opd"])
            for h in range(8):
                TT("dve", tmpc, Vc4[:, :, h // 4, :], bc(lg[:, h, :].unsqueeze(2), [128, NCAND, 64]), ALU.mult, [vn, "lg"], ["tmpc"])
                RED(opd[:, h * 64:(h + 1) * 64], tmpc.rearrange("p c d -> p d c"), ALU.add, ["tmpc"], ["opd"])
            MM(V2[0:16, 0:512], repTS[:, sp * 16:(sp + 1) * 16], opd[:, 0:512], sp == 0, sp == NPAIR - 1, ["repTS", "opd"], ["V2"])
            MM(V2[0:16, 512:520], repTS[:, sp * 16:(sp + 1) * 16], opd[:, 512:520], sp == 0, sp == NPAIR - 1, ["repTS", "opd"], ["V2"])
        qv = v8(tk["qs"])
        for g_ in range(2):
            TT("dve", v8(tk["ta"])[:, g_ * 4:(g_ + 1) * 4, :], qv[:, g_ * 4:(g_ + 1) * 4, :],
               bc(ks_[:, g_ * 64:(g_ + 1) * 64].unsqueeze(1), [16, 4, 64]), ALU.mult, ["qs", "ks"], ["ta"])
        RED(s16[:, 0:8], v8(tk["ta"]), ALU.add, ["ta"], ["s16"])
        ACT(s16[:, 0:8], s16[:, 0:8], AF.Exp, ["s16"], ["s16"])
        TS("dve", s16[:, 0:8], s16[:, 0:8], s16[:, 32:33], None, ALU.mult, None, ["s16"], ["s16"])
        TT("dve", s16[:, 8:16], V2[0:16, 512:520], s16[:, 0:8], ALU.add, ["V2", "s16"], ["s16"])
        b.op("dve", lambda g: g.reciprocal(out=s16[:, 8:16], in_=s16[:, 8:16]), ["s16"], ["s16"])
        for g_ in range(2):
            TT("dve", v8(tk["ta"])[:, g_ * 4:(g_ + 1) * 4, :], bc(proj[:, 640 + g_ * 64:640 + (g_ + 1) * 64].unsqueeze(1), [16, 4, 64]),
               bc(s16[:, g_ * 4:(g_ + 1) * 4].unsqueeze(2), [16, 4, 64]), ALU.mult, ["proj", "s16"], ["ta"])
        TT("dve", tk["ta"], tk["ta"], V2[0:16, 0:512], ALU.add, ["ta", "V2"], ["ta"])
        TT("dve", v8(tk["ta"]), v8(tk["ta"]), bc(s16[:, 8:16].unsqueeze(2), [16, 8, 64]), ALU.mult, ["ta", "s16"], ["ta"])
        TT("dve", cats[:, 0:512], tk["ta"], tk["ga"], ALU.mult, ["ta", "ga"], ["cats"])
        for k in range(8):
            TR(PTb[:, k * 16:(k + 1) * 16], cats[:, k * 128:(k + 1) * 128], identb[0:16, 0:16], ["cats", "identb"], ["PTb"])
        CP("act", catTs, PTb[:, 0:128].rearrange("p (k t) -> p k t", k=8), ["PTb"], ["catTs"])
        b.barrier()
        off = PW
        stg2, off = carve(off, [128, 8, 512])
        wbf2, off = carve(off, [128, 8, 512], BF16)
        b.dma("sp", ysb, gscr[0:16, :], reads=["gscr"], writes=["ysb"])
        w_out_v2 = w_out.rearrange("(k p) c -> p k c", p=128)
        for hh in range(2):
            b.dma("sp", stg2, w_out_v2[:, :, hh * 512:(hh + 1) * 512], writes=["stg2"])
            CP("dve", wbf2, stg2, ["stg2"], ["wbf2"])
            for k in range(8):
                MM(R2[0:16, hh * 512:(hh + 1) * 512], catTs[:, k, :], wbf2[:, k, :], k == 0, k == 7, ["catTs", "wbf2"], ["R2"])
        TT("dve", ysb, ysb, R2[0:16, :], ALU.mult, ["ysb", "R2"], ["ysb"])
        TT("dve", ysb, ysb, x_[0:16, :], ALU.add, ["ysb", xr], ["ysb"])
        b.dma("sp", y_s[:, :], ysb, reads=["ysb"])

    b.barrier()
    b.emit()
    ncd.__exit__(None, None, None)
    es.close()
    return nc


def _consts(T):
    NT = T // 128
    NO = NT // 2
    cst = {}
    cst["identf"] = np.eye(128, dtype=np.float32)
    cst["iota256"] = np.tile(np.arange(256, dtype=np.float32)[None, :], (128, 1))
    s = np.arange(64)[:, None]
    t = np.arange(64)[None, :]
    lt = (s < t).astype(np.float32)
    le = (s <= t).astype(np.float32)
    cst["maskT"] = np.concatenate([lt, le, lt, le], axis=1)
    cst["maskL"] = (np.arange(64)[None, :] < np.arange(64)[:, None]).astype(np.float32)
    r = np.ones((64, 1024), np.float32)
    r[:, ::64] = 0.0
    cst["resetm"] = r
    sel = np.zeros((17, 128), np.float32)
    sel[16, :] = 1.0
    cst["sel16"] = sel
    cst["ones64"] = np.ones((64, 64), np.float32)
    return cst


def _rope_table(pos):
    half = 8
    inv = np.power(np.float32(ROPE_THETA), -np.arange(half, dtype=np.float32) / np.float32(half)).astype(np.float32)
    ang = pos.astype(np.float32)[:, None] * inv[None, :]
    return np.concatenate([np.cos(ang), np.sin(ang)], axis=1).astype(np.float32)


def _core_inputs(inp, c, T, NS, past_len):
    NT = T // 128
    NO = NT // 2
    bi, par = c // 2, c % 2
    xp = np.asarray(inp["x_prompt"][bi], np.float32)
    own_tiles = [2 * j + par for j in range(NO)]
    own_rows = np.concatenate([np.arange(t * 128, (t + 1) * 128) for t in own_tiles])
    xs = np.zeros((128, D), np.float32)
    xs[:NS] = np.asarray(inp["x_sample"][c * NS:(c + 1) * NS, 0], np.float32)
    m = {}
    m["xall"] = np.ascontiguousarray(np.concatenate([xp, xp[own_rows], xs], axis=0))
    m["call"] = np.ascontiguousarray(np.concatenate([inp["c_sample"][c * NS:(c + 1) * NS], inp["c_prompt"][bi:bi + 1]], axis=0).astype(np.float32))
    pos = np.concatenate([np.arange(T), own_rows, np.full(128, past_len)])
    m["cs_all"] = _rope_table(pos)
    m["parsel"] = np.tile(np.array([[par, 1 - par]], np.float32), (128, 1))
    m["qrel"] = (par * 128 + np.arange(128, dtype=np.float32)).reshape(128, 1)
    m["ownidx"] = np.ascontiguousarray(own_rows.reshape(NO, 128).T.astype(np.int32))
    for k_, v_ in (("w_in", "w_in"), ("w_ada", "w_ada"), ("b_ada", "b_ada"), ("norm_w", "norm_w"), ("w_out", "w_out"),
                   ("qnw", "q_norm_w"), ("knw", "k_norm_w"), ("mu", "mu_shift"), ("w0", "w0"), ("a0", "a0"),
                   ("k_k", "k_k"), ("k_a", "k_a"), ("ln_x_w", "ln_x_w"), ("ln_x_b", "ln_x_b"), ("w_up", "w_up"), ("a_up", "a_up")):
        m[k_] = np.ascontiguousarray(np.asarray(inp[v_], np.float32))
    m["r_k"] = np.ascontiguousarray(np.asarray(inp["r_k"], np.float32).reshape(512))
    m["swkv"] = np.ascontiguousarray(np.asarray(inp["state_wkv"][c * NS:(c + 1) * NS], np.float32).reshape(NS * 8, 4096))
    m["sshift"] = np.ascontiguousarray(np.asarray(inp["state_shift"][c * NS:(c + 1) * NS, 0], np.float32))
    pt = np.asarray(inp["page_table"][c * NS:(c + 1) * NS], np.int32)
    m["ptab"] = np.ascontiguousarray(pt.reshape(NS // 2, 128).T)
    nphys = inp["cache_k"].shape[0]
    m["cache_k"] = np.asarray(inp["cache_k"], np.float32).reshape(nphys * 128, 128)
    m["cache_v"] = np.asarray(inp["cache_v"], np.float32).reshape(nphys * 128, 128)
    m["cache_kidx"] = np.asarray(inp["cache_kidx"], np.float32).reshape(nphys, 8192)
    rep = np.zeros((16, 8, 128), np.float32)
    for sp in range(8):
        for p in range(128):
            rep[2 * sp + p // 64, sp, p] = 1.0
    m["rep"] = rep.reshape(16, 1024)
    m["repT"] = np.ascontiguousarray(rep.transpose(2, 1, 0).reshape(128, 128))
    blk = np.zeros((128, 128), np.float32)
    blk[:64, :64] = 1.0
    blk[64:, 64:] = 1.0
    m["blk"] = blk
    oh = np.zeros((128, 1), np.float32)
    oh[0, 0] = 1.0
    oh[64, 0] = 1.0
    m["oh0"] = oh
    m.update(_consts(T))
    return m


_NC_CACHE = {}


def kernel(**inp):
    T = 4096
    NS = 16
    past_len = 8192
    inp = {k: np.asarray(v) for k, v in inp.items()}
    if "nc" not in _NC_CACHE:
        _NC_CACHE["nc"] = build(T=T, NPHYS=int(inp["cache_k"].shape[0]))
    nc = _NC_CACHE["nc"]
    in_maps = [_core_inputs(inp, c, T, NS, past_len) for c in range(8)]
    res = run_bass_kernel_spmd(nc, in_maps, core_ids=list(range(8)))
    outs = res.results
    B = 4
    NO = T // 256
    y_p = np.zeros((B, T, D), np.float32)
    for c in range(8):
        bi, par = c // 2, c % 2
        yo = np.asarray(outs[c]["y_own"]).reshape(NO, 128, D)
        y_p[bi].reshape(T // 256, 2, 128, D)[:, par] = yo
    k_p = np.stack([np.asarray(outs[2 * bi]["k_nat"]).reshape(T, 2, 64) for bi in range(B)])
    v_p = np.stack([np.asarray(outs[2 * bi]["v_nat"]).reshape(T, 2, 64) for bi in range(B)])
    ki_p = np.stack([np.asarray(outs[2 * bi]["ki_nat"]).reshape(T, 64) for bi in range(B)])
    wkv_pp = np.stack([np.asarray(outs[2 * bi]["wkv_p"]).reshape(8, 64, 64) for bi in range(B)])
    sh_p = np.stack([np.asarray(outs[2 * bi]["shift_p"]).reshape(1, SHW) for bi in range(B)])
    y_s = np.concatenate([np.asarray(outs[c]["y_s"]) for c in range(8)]).reshape(128, 1, D)
    k_s = np.concatenate([np.asarray(outs[c]["k_s"]) for c in range(8)]).reshape(128, 1, 2, 64)
    v_s = np.concatenate([np.asarray(outs[c]["v_s"]) for c in range(8)]).reshape(128, 1, 2, 64)
    ki_s = np.concatenate([np.asarray(outs[c]["ki_s"]) for c in range(8)]).reshape(128, 1, 64)
    wkv_s = np.concatenate([np.asarray(outs[c]["wkv_s"]) for c in range(8)]).reshape(128, 8, 64, 64)
    sh_s = np.concatenate([np.asarray(outs[c]["shift_s"]) for c in range(8)]).reshape(128, 1, SHW)
    f = lambda a: np.ascontiguousarray(a, dtype=np.float32)
    return (f(y_p), f(y_s), f(k_p), f(v_p), f(ki_p), f(wkv_pp), f(sh_p), f(k_s), f(v_s), f(ki_s), f(wkv_s), f(sh_s))
```

```python
import os
import numpy as np
from contextlib import ExitStack
import concourse.bass as bass
import concourse.mybir as mybir
from concourse.bass_utils import run_bass_kernel_spmd

F32 = mybir.dt.float32
BF16 = mybir.dt.bfloat16
I32 = mybir.dt.int32
AF = mybir.ActivationFunctionType
ALU = mybir.AluOpType
AX = mybir.AxisListType

ENGS = ("pe", "act", "dve", "pool", "sp")
NDMA = 32
NSW = 8

D = 1024
HD = 64
DIN = 4040
C_Q, C_K, C_V, C_QI, C_WI, C_KI, C_GA = 0, 512, 640, 768, 1280, 1288, 1352
C_R, C_RK, C_RV, C_WD, C_AD, C_GR = 1864, 2376, 2888, 3400, 3464, 3528
SHW = 1664
NORM_EPS = 1e-6
GN_EPS = 64e-5
ROPE_THETA = 500000.0


USE_POOL = bool(int(os.environ.get('USE_POOL', '0')))
PSUM_RES = {"PTb", "F2", "R2", "K2", "V2", "R2a", "R2b", "K2a", "K2b", "V2a", "V2b"}


class Res:
    __slots__ = ("w", "r")

    def __init__(self):
        self.w = None
        self.r = []


class Bld:
    def __init__(self, nc, es):
        self.nc = nc
        self.es = es
        self.sem = {e: es.enter_context(nc.semaphore("s_" + e)) for e in ENGS}
        self.dsem = [es.enter_context(nc.semaphore("d%d" % i)) for i in range(NDMA)]
        self.dval = [0] * NDMA
        self.dnext = 0
        self.dnext_sw = 0
        self.cnt = {e: 0 for e in ENGS}
        self.waited = {e: {} for e in ENGS}
        self.ops = {e: [] for e in ENGS}
        self.res = {}

    def sb(self, name, shape, dt=F32):
        return self.es.enter_context(self.nc.sbuf_tensor("sb_" + name, list(shape), dt))

    def ps(self, name, shape, dt=F32):
        return self.es.enter_context(self.nc.psum_tensor("ps_" + name, list(shape), dt))

    def _r(self, key):
        r = self.res.get(key)
        if r is None:
            r = self.res[key] = Res()
        return r

    def _need(self, e, tok, waits):
        if tok is None:
            return
        key, val = tok
        if key == "pe" and e == "pe":
            return
        if self.waited[e].get(key, 0) >= val:
            return
        self.waited[e][key] = val
        waits.append((key, val))

    def op(self, e, fn, reads=(), writes=(), dma=False):
        if e == "pool" and not dma and not USE_POOL:
            e = "dve"
        pr = [k for k in reads if k in PSUM_RES]
        if pr:
            reads = [k for k in reads if k not in PSUM_RES]
            writes = list(writes) + pr
        waits = []
        for k in reads:
            self._need(e, self._r(k).w, waits)
        for k in writes:
            r = self._r(k)
            self._need(e, r.w, waits)
            for t in r.r:
                self._need(e, t, waits)
        if dma:
            if e == "pool":
                i = NDMA - NSW + self.dnext_sw
                self.dnext_sw = (self.dnext_sw + 1) % NSW
            else:
                i = self.dnext
                self.dnext = (self.dnext + 1) % (NDMA - NSW)
            if self.dval[i] > 0:
                self._need(e, (("d", i), self.dval[i]), waits)
            self.dval[i] += 16
            tok = (("d", i), self.dval[i])
            inc = (self.dsem[i], 16)
        else:
            self.cnt[e] += 1
            tok = (e, self.cnt[e])
            inc = (self.sem[e], 1)
        self.ops[e].append((waits, fn, inc))
        for k in reads:
            self._r(k).r.append(tok)
        for k in writes:
            r = self._r(k)
            r.w = tok
            r.r = []
        return tok

    def dma(self, e, out, in_, reads=(), writes=()):
        return self.op(e, lambda g: g.dma_start(out=out, in_=in_), reads, writes, dma=True)

    def barrier(self, dmas=True):
        for e in ENGS:
            waits = []
            for e2 in ENGS:
                if e2 != e and self.cnt[e2] > 0:
                    self._need(e, (e2, self.cnt[e2]), waits)
            if dmas:
                for i in range(NDMA):
                    if self.dval[i] > 0:
                        self._need(e, (("d", i), self.dval[i]), waits)
            self.ops[e].append((waits, None, None))

    def emit(self):
        nc = self.nc
        with nc.Block() as block:
            def mk(e):
                def body(g):
                    for waits, fn, inc in self.ops[e]:
                        for key, val in waits:
                            s = self.dsem[key[1]] if isinstance(key, tuple) else self.sem[key]
                            g.wait_ge(s, val)
                        if fn is not None:
                            fn(g).then_inc(inc[0], inc[1])
                return body
            block.tensor(mk("pe"))
            block.scalar(mk("act"))
            block.vector(mk("dve"))
            block.gpsimd(mk("pool"))
            block.sync(mk("sp"))


def build(T=4096, NS=16, NPG=64, NPHYS=10240, topk_p=256, topk_s=256, n_bis=15, do_sample=True, AW=36000, stop_after=None, nt_lim=None, no_lim=None):
    NT = T // 128
    NO = NT // 2
    NTILES = NT + NO + 1
    nc = bass.Bass("TRN2", target_bir_lowering=False)
    es = ExitStack()
    b = Bld(nc, es)

    def din(name, shape, dt=F32):
        return nc.dram_tensor(name, list(shape), dt, kind="ExternalInput").ap()

    def dout(name, shape, dt=F32):
        return nc.dram_tensor(name, list(shape), dt, kind="ExternalOutput").ap()

    xall = din("xall", [NTILES * 128, D])
    call = din("call", [17, D])
    w_in = din("w_in", [D, DIN])
    w_ada = din("w_ada", [D, 3 * D])
    b_ada = din("b_ada", [3 * D])
    norm_w = din("norm_w", [D])
    w_out = din("w_out", [D, D])
    qnw = din("qnw", [HD])
    knw = din("knw", [HD])
    mu = din("mu", [SHW])
    pw0 = din("w0", [512]); pa0 = din("a0", [512]); pkk = din("k_k", [512]); pka = din("k_a", [512])
    prk = din("r_k", [512]); plnw = din("ln_x_w", [512]); plnb = din("ln_x_b", [512])
    w_up = din("w_up", [64, 512]); a_up = din("a_up", [64, 512])
    identf_d = din("identf", [128, 128])
    cs_all = din("cs_all", [NTILES * 128, 16])
    parsel_d = din("parsel", [128, 2])
    qrel_d = din("qrel", [128, 1])
    ownidx_d = din("ownidx", [128, NO], I32)
    iota_d = din("iota256", [128, 256])
    maskT_d = din("maskT", [64, 256])
    maskL_d = din("maskL", [64, 64])
    reset_d = din("resetm", [64, 1024])
    sel16_d = din("sel16", [17, 128])
    ones64_d = din("ones64", [64, 64])

    swkv_d = din("swkv", [128, 4096]); sshift_d = din("sshift", [16, SHW]); ptab_d = din("ptab", [128, 8], I32)
    if do_sample:
        cache_k = din("cache_k", [NPHYS * 128, 128]); cache_v = din("cache_v", [NPHYS * 128, 128])
        cache_ki = din("cache_kidx", [NPHYS, 8192])
    rep_d = din("rep", [16, 8 * 128]); repT_d = din("repT", [128, 8 * 16]); blk_d = din("blk", [128, 128]); oh0_d = din("oh0", [128, 1])
    y_s = dout("y_s", [16, D]); k_s = dout("k_s", [16, 128]); v_s = dout("v_s", [16, 128]); ki_s = dout("ki_s", [16, 64])
    wkv_s = dout("wkv_s", [128, 4096]); shift_s = dout("shift_s", [16, SHW])
    gscr = nc.dram_tensor("gscr", [17, D], F32, kind="Internal").ap()
    scr1 = nc.dram_tensor("scr1", [16, 3072], F32, kind="Internal").ap()
    scr2 = nc.dram_tensor("scr2", [128, 64], F32, kind="Internal").ap()
    scr3 = nc.dram_tensor("scr3", [16, 512], F32, kind="Internal").ap()
    scr4 = nc.dram_tensor("scr4", [16, 64, 16], F32, kind="Internal").ap()
    y_own = dout("y_own", [NO * 128, D])
    k_nat = dout("k_nat", [T, 128]); v_nat = dout("v_nat", [T, 128]); ki_nat = dout("ki_nat", [T, 64])
    wkv_p = dout("wkv_p", [8, 64, 64]); shift_p = dout("shift_p", [SHW])
    rwscr = nc.dram_tensor("rwscr", [T, 512], F32, kind="Internal").ap()

    PTb = b.ps("PTb", [128, 1024], BF16)
    F2 = b.ps("F2", [128, 512])
    R2 = b.ps("R2", [128, 1024])
    K2 = b.ps("K2", [128, 1024])
    V2 = b.ps("V2", [128, 1024])

    identf = b.sb("identf", [128, 128]); identb = b.sb("identb", [128, 128], BF16)
    cst = b.sb("cst", [128, 4])
    kT_all = b.sb("kT_all", [64, 2, T], BF16)
    kiT_all = b.sb("kiT_all", [64, T], BF16)
    Vaug = b.sb("Vaug", [128, NT, 2, 65], BF16)
    modT = b.sb("modT", [128, 24, 17])
    g1 = b.sb("g1", [128, 8, 17])
    nwT = b.sb("nwT", [128, 8]); badaT = b.sb("badaT", [128, 24])
    lnw_bc = b.sb("lnw_bc", [64, 512]); lnb_bc = b.sb("lnb_bc", [64, 512])
    qnw_bc = b.sb("qnw_bc", [128, 64]); knw_bc = b.sb("knw_bc", [128, 64])
    sel16 = b.sb("sel16", [17, 128]); ones64 = b.sb("ones64", [64, 64])
    maskT = b.sb("maskT", [64, 256]); maskL = b.sb("maskL", [64, 64]); resetm = b.sb("resetm", [64, 1024])
    qrel = b.sb("qrel", [128, 1]); parsel = b.sb("parsel", [128, 2]); ownidx = b.sb("ownidx", [128, NO], I32)
    fp = {}
    for nm in ("w0", "a0", "kk", "ka", "rk"):
        fp[nm] = b.sb("fp_" + nm, [64, 8])
    muT = b.sb("muT", [64, 26]); wupS = b.sb("wupS", [64, 512]); aupS = b.sb("aupS", [64, 512])
    xt0 = b.sb("xt0", [128, D]); xt = [xt0, xt0]
    xn = b.sb("xn", [128, D], BF16)
    hT0 = b.sb("hT0", [128, 8, 128], BF16); hT = [hT0, hT0]
    hTs = b.sb("hTs", [128, 8, 128], BF16)
    hlast = b.sb("hlast", [128, 8, 1], BF16)
    cs_t = b.sb("cs_t", [128, 16])
    sm = b.sb("sm", [128, 64])
    ARENA = b.sb("ARENA", [128, AW])
    csT = b.sb("csT", [128, 8, 17])

    def TT(e, out, in0, in1, op, R, W):
        b.op(e, lambda g: g.tensor_tensor(out=out, in0=in0, in1=in1, op=op), R, W)

    def TS(e, out, in0, s1, s2, op0, op1, R, W, accum=None):
        if op1 is None:
            b.op(e, lambda g: g.tensor_scalar(out=out, in0=in0, scalar1=s1, scalar2=None, op0=op0), R, W)
        elif accum is None:
            b.op(e, lambda g: g.tensor_scalar(out=out, in0=in0, scalar1=s1, scalar2=s2, op0=op0, op1=op1), R, W)
        else:
            b.op(e, lambda g: g.tensor_scalar(out=out, in0=in0, scalar1=s1, scalar2=s2, op0=op0, op1=op1,
                                              accum_out=accum), R, W)

    def STT(out, in0, scalar, in1, op0, op1, R, W):
        b.op("dve", lambda g: g.scalar_tensor_tensor(out=out, in0=in0, scalar=scalar, in1=in1, op0=op0, op1=op1), R, W)

    def ACT(out, in_, func, R, W, scale=1.0, bias=None, accum=None):
        kw = {}
        if bias is not None:
            kw["bias"] = bias
        if accum is not None:
            kw["accum_out"] = accum
        b.op("act", lambda g: g.activation(out=out, in_=in_, func=func, scale=scale, **kw), R, W)

    def MM(out, lhsT, rhs, start, stop, R, W):
        b.op("pe", lambda g: g.matmul(out=out, lhsT=lhsT, rhs=rhs, start=start, stop=stop), R, W)

    def TR(out, in_, ident, R, W):
        b.op("pe", lambda g: g.transpose(out=out, in_=in_, identity=ident), R, W)

    def CP(e, out, in_, R, W):
        if e == "act":
            b.op(e, lambda g: g.copy(out=out, in_=in_), R, W)
        else:
            b.op(e, lambda g: g.tensor_copy(out=out, in_=in_), R, W)

    def RED(out, in_, op, R, W, axis=AX.X):
        b.op("dve", lambda g: g.tensor_reduce(out=out, in_=in_, axis=axis, op=op), R, W)

    def MS(e, ap, val, W):
        b.op(e, lambda g: g.memset(ap, val), (), W)

    def bc(ap, shape):
        return ap.to_broadcast(list(shape))

    ncd = nc.allow_non_contiguous_dma(reason="small parameter layouts")
    ncd.__enter__()

    b.dma("sp", identf[:], identf_d[:, :], writes=["identf"])
    CP("dve", identb[:], identf[:], ["identf"], ["identb"])
    MS("dve", cst[:, 0:1], NORM_EPS, ["cst"]); MS("dve", cst[:, 1:2], GN_EPS, ["cst"]); MS("dve", cst[:, 2:3], 1e-24, ["cst"]); MS("dve", cst[:, 3:4], -30000.0, ["cst"])
    for (t_, d_, nm) in ((sel16, sel16_d, "sel16"), (ones64, ones64_d, "ones64"), (maskT, maskT_d, "maskT"),
                         (maskL, maskL_d, "maskL"), (resetm, reset_d, "resetm"),
                         (qrel, qrel_d, "qrel"), (parsel, parsel_d, "parsel"), (ownidx, ownidx_d, "ownidx"), (wupS, w_up, "wupS"), (aupS, a_up, "aupS")):
        b.dma("sp", t_[:], d_[:, :], writes=[nm])
    for nm, src in (("w0", pw0), ("a0", pa0), ("kk", pkk), ("ka", pka), ("rk", prk)):
        b.dma("sp", fp[nm][:], src.rearrange("(h j) -> j h", j=64), writes=["fp_" + nm])
    b.dma("sp", muT[:], mu.rearrange("(c j) -> j c", j=64), writes=["muT"])
    b.dma("sp", nwT[:], norm_w.rearrange("(k p) -> p k", p=128), writes=["nwT"])
    b.dma("sp", badaT[:], b_ada.rearrange("(t p) -> p t", p=128), writes=["badaT"])
    b.dma("sp", lnw_bc[:], plnw.partition_broadcast(64), writes=["lnw_bc"])
    b.dma("sp", lnb_bc[:], plnb.partition_broadcast(64), writes=["lnb_bc"])
    b.dma("sp", qnw_bc[:], qnw.partition_broadcast(128), writes=["qnw_bc"])
    b.dma("sp", knw_bc[:], knw.partition_broadcast(128), writes=["knw_bc"])
    MS("pool", Vaug[:, :, :, 64:65], 1.0, ["Vaug"])

    def carve(off, shape, dt=F32):
        n = int(np.prod(shape[1:]))
        words = n if dt in (F32, I32) else (n + 1) // 2
        v = ARENA[0:shape[0], off:off + words]
        if dt != F32:
            v = v.bitcast(dt)
        if len(shape) == 3:
            v = v.rearrange("p (a b) -> p a b", a=shape[1])
        elif len(shape) == 4:
            v = v.rearrange("p (a b c) -> p a b c", a=shape[1], b=shape[2])
        return v, off + words

    off = 0
    Wn, off = carve(off, [128, 8, 832], BF16)
    Wm, off = carve(off, [128, 8, SHW], BF16)
    Wom, off = carve(off, [128, 8, SHW], BF16)
    W_end = off
    stg, off = carve(off, [128, 8, 512])
    mu_bc, off = carve(off, [128, SHW])
    omu_bc, off = carve(off, [128, SHW])
    gtok, off = carve(off, [17, D])
    bgate, off = carve(off, [17, D])
    csall_sil, off = carve(off, [17, D])

    b.dma("sp", mu_bc, mu.partition_broadcast(128), writes=["mu_bc"])
    b.dma("sp", bgate, b_ada[2 * D:3 * D].partition_broadcast(17), writes=["bgate"])
    TS("dve", omu_bc, mu_bc, -1.0, 1.0, ALU.mult, ALU.add, ["mu_bc"], ["omu_bc"])
    w_in_v = w_in.rearrange("(k p) c -> p k c", p=128)

    def load_cols(dst, dcol, c0, n, scale_bc=None, scale_off=0, tag=""):
        done = 0
        while done < n:
            w = min(512, n - done)
            b.dma("sp", stg[:, :, 0:w], w_in_v[:, :, c0 + done:c0 + done + w], writes=["stg"])
            if scale_bc is None:
                CP("pool", dst[:, :, dcol + done:dcol + done + w], stg[:, :, 0:w], ["stg"], [tag])
            else:
                for sname, sbcv, d2 in scale_bc:
                    TT("dve", d2[:, :, dcol + done:dcol + done + w], stg[:, :, 0:w],
                       bc(sbcv[:, scale_off + done:scale_off + done + w].unsqueeze(1), [128, 8, w]),
                       ALU.mult, ["stg", sname], [tag])
            done += w

    load_cols(Wn, 0, C_K, 256, tag="Wn")
    load_cols(Wn, 256, C_KI, 64, tag="Wn")
    load_cols(Wn, 320, C_GR, 512, tag="Wn")
    load_cols(None, 0, C_R, SHW, scale_bc=[("mu_bc", mu_bc, Wm), ("omu_bc", omu_bc, Wom)], tag="Wm")
    calt = sm
    b.dma("sp", csall_sil, call[:, :], writes=["csil"])
    ACT(csall_sil, csall_sil, AF.Silu, ["csil"], ["csil"])
    for k in range(8):
        TR(F2[:, k * 17:(k + 1) * 17], csall_sil[:, k * 128:(k + 1) * 128], identf[0:17, 0:17], ["csil", "identf"], ["F2"])
    CP("dve", csT[:], F2[:, 0:136].rearrange("p (k m) -> p k m", k=8), ["F2"], ["csT"])
    w_ada_v = w_ada.rearrange("(k p) c -> p k c", p=128)
    for ch in range(6):
        b.dma("sp", stg[:, :, :], w_ada_v[:, :, ch * 512:(ch + 1) * 512], writes=["stg"])
        for ct in range(4):
            for k in range(8):
                MM(R2[:, ct * 17:(ct + 1) * 17], stg[:, k, ct * 128:(ct + 1) * 128], csT[:, k, :], k == 0, k == 7,
                   ["stg", "csT"], ["R2"])
        TT("dve", modT[:, ch * 4:(ch + 1) * 4, :], R2[:, 0:68].rearrange("p (c m) -> p c m", c=4),
           bc(badaT[:, ch * 4:(ch + 1) * 4].unsqueeze(2), [128, 4, 17]), ALU.add, ["R2", "badaT"], ["modT"])
        if ch >= 4:
            for k in range(8):
                MM(K2[0:17, 0:512], csT[:, k, :], stg[:, k, :], k == 0, k == 7, ["stg", "csT"], ["K2"])
            TT("dve", gtok[:, (ch - 4) * 512:(ch - 3) * 512], K2[0:17, 0:512], bgate[:, (ch - 4) * 512:(ch - 3) * 512],
               ALU.add, ["K2", "bgate"], ["gtok"])
    b.dma("sp", gscr[:, :], gtok[0:17, :], reads=["gtok"], writes=["gscr"])
    STT(g1[:], modT[:, 8:16, :], 1.0, bc(nwT[:].unsqueeze(2), [128, 8, 17]), ALU.add, ALU.mult, ["modT", "nwT"], ["g1"])
    def front(ti, par, m_prompt=True, ntok=128):
        x_ = xt[par]
        h_ = hT[par]
        xr, hr = "xt0", "hT0"
        b.dma("sp", x_[:], xall[ti * 128:(ti + 1) * 128, :], writes=[xr])
        b.dma("sp", cs_t[:], cs_all[ti * 128:(ti + 1) * 128, :], writes=["cs_t"])
        ACT(xn[:], x_[:], AF.Square, [xr], ["xn", "sm"], accum=sm[:, 0:1])
        ACT(sm[:, 1:2], sm[:, 0:1], AF.Sqrt, ["sm", "cst"], ["sm"], scale=1.0 / D, bias=cst[:, 0:1])
        b.op("dve", lambda g: g.reciprocal(out=sm[:, 2:3], in_=sm[:, 1:2]), ["sm"], ["sm"])
        TS("dve", xn[:], x_[:], sm[:, 2:3], None, ALU.mult, None, [xr, "sm"], ["xn"])
        for k in range(8):
            TR(PTb[:, k * 128:(k + 1) * 128], xn[:, k * 128:(k + 1) * 128], identb[:], ["xn", "identb"], ["PTb"])
        pv = PTb[:, :].rearrange("p (k t) -> p k t", k=8)
        if m_prompt:
            TT("dve", h_[:], pv, bc(g1[:, :, 16:17], [128, 8, 128]), ALU.mult, ["PTb", "g1"], [hr])
            TT("pool", h_[:], h_[:], bc(modT[:, 0:8, 16:17], [128, 8, 128]), ALU.add, [hr, "modT"], [hr])
        else:
            TT("dve", h_[:, :, 0:ntok], pv[:, :, 0:ntok], g1[:, :, 0:ntok], ALU.mult, ["PTb", "g1"], [hr])
            TT("pool", h_[:, :, 0:ntok], h_[:, :, 0:ntok], modT[:, 0:8, 0:ntok], ALU.add, [hr, "modT"], [hr])
        return x_, h_, xr, hr

    def rope(e, buf, nh, hd_stride_view, R, nrows=128):
        x1 = buf[:, :, 0:8]
        x2 = buf[:, :, 8:16]
        cosb = bc(cs_t[0:nrows, 0:8].unsqueeze(1), [nrows, nh, 8])
        sinb = bc(cs_t[0:nrows, 8:16].unsqueeze(1), [nrows, nh, 8])
        t = ropet[0:nrows, 0:4 * nh * 8].rearrange("p (a h d) -> p a h d", a=4, h=nh)
        TT(e, t[:, 0], x1, cosb, ALU.mult, R + ["cs_t"], ["ropet"])
        TT(e, t[:, 1], x2, sinb, ALU.mult, R + ["cs_t"], ["ropet"])
        TT(e, t[:, 2], x2, cosb, ALU.mult, R + ["cs_t"], ["ropet"])
        TT(e, t[:, 3], x1, sinb, ALU.mult, R + ["cs_t"], ["ropet"])
        TT(e, x1, t[:, 0], t[:, 1], ALU.subtract, ["ropet"], R)
        TT(e, x2, t[:, 2], t[:, 3], ALU.add, ["ropet"], R)

    ropet = b.sb("ropet", [128, 256])

    def qknorm(src_ps, dst, nh, wbc, extra_scale, Rsrc, Wdst, nrows=128):
        sq = nrm_t[0:nrows, 0:nh * 64].rearrange("p (h d) -> p h d", h=nh)
        ACT(sq, src_ps, AF.Square, Rsrc, ["nrm_t"])
        RED(sm[0:nrows, 8:8 + nh], sq, ALU.add, ["nrm_t"], ["sm"])
        ACT(sm[0:nrows, 16:16 + nh], sm[0:nrows, 8:8 + nh], AF.Sqrt, ["sm", "cst"], ["sm"], scale=1.0 / 64, bias=cst[0:nrows, 0:1])
        b.op("dve", lambda g: g.reciprocal(out=sm[0:nrows, 24:24 + nh], in_=sm[0:nrows, 16:16 + nh]), ["sm"], ["sm"])
        TT("dve", dst, src_ps, bc(sm[0:nrows, 24:24 + nh].unsqueeze(2), [nrows, nh, 64]), ALU.mult, Rsrc + ["sm"], Wdst)
        STT(dst, dst, float(extra_scale), bc(wbc[0:nrows, :].unsqueeze(1), [nrows, nh, 64]), ALU.mult, ALU.mult, Wdst + ["qnw_bc", "knw_bc"], Wdst)

    nrm_t = b.sb("nrm_t", [128, 512])
    kfin = b.sb("kfin", [128, 128]); vfin = b.sb("vfin", [128, 128]); kifin = b.sb("kifin", [128, 64])
    gr_s = b.sb("gr_s", [128, 512])

    off = W_end
    rw = {}
    for nm in ("tw", "adc"):
        rw[nm], off = carve(off, [64, 128])
    for nm in ("sg", "L", "g", "t1"):
        rw[nm], off = carve(off, [64, 8, 128])
    blkA, off = carve(off, [64, 2048])
    blkB, off = carve(off, [64, 3072])
    rw["gprev"] = blkA[:, 0:1024].rearrange("p (h t) -> p h t", h=8)
    rw["ginv"] = blkA[:, 1024:2048].rearrange("p (h t) -> p h t", h=8)
    rw["asig"] = blkB[:, 0:1024].rearrange("p (h t) -> p h t", h=8)
    rw["kkn"] = blkB[:, 1024:2048].rearrange("p (h t) -> p h t", h=8)
    rw["kmod"] = blkB[:, 2048:3072].rearrange("p (h t) -> p h t", h=8)
    AMx = blkA.bitcast(BF16).rearrange("p (h x) -> p h x", h=16)
    LNPx = blkB.bitcast(BF16).rearrange("p (a h x) -> p a h x", a=6, h=16)
    rw["sg"] = rw["sg"]
    QTt, off = carve(off, [64, 8, 2, 128], BF16)
    KTt, off = carve(off, [64, 8, 2, 128], BF16)
    def alias(view64, shape):
        return view64.rearrange("p h t -> p (h t)").bitcast(BF16)
    AM = AMx
    Lm = [LNPx[:, 0], LNPx[:, 1]]
    Nm = [LNPx[:, 2], LNPx[:, 3]]
    Pm = [LNPx[:, 4], LNPx[:, 5]]
    BKtok, off = carve(off, [64, 8, 2, 64], BF16)
    Vc, off = carve(off, [64, 2, 8, 64], BF16)
    P0s, off = carve(off, [64, 8, 64], BF16)
    Us, off = carve(off, [64, 8, 64], BF16)
    H32, off = carve(off, [64, 8, 64])
    Hb, off = carve(off, [64, 8, 64], BF16)
    ych, off = carve(off, [64, 8, 64])
    yt1, off = carve(off, [64, 8, 64])
    bon, off = carve(off, [128, 8])
    rawl, off = carve(off, [64, 26])
    xmt, off = carve(off, [128, SHW])
    st8, off = carve(off, [64, 64])
    assert off <= AW, off
    identb64 = identb[0:64, 0:64]

    KR = int(os.environ.get('KR', '9'))
    KQ = int(os.environ.get('KQ', '9'))

    def rwkv_tile(ti, h_, hr):
        CP("pool", hTs[:, :, 1:128], h_[:, :, 0:127], [hr], ["hTs"])
        CP("pool", hTs[:, :, 0:1], hlast[:], ["hlast"], ["hTs"])
        CP("pool", hlast[:], h_[:, :, 127:128], [hr], ["hlast"])
        if KQ < 1:
            return
        for gi, (c0, n) in enumerate(((0, 512), (512, 512), (1024, 512), (1536, 128))):
            dst = (R2[:, 0:512], R2[:, 512:1024], K2[:, 0:512], K2[:, 512:640])[gi]
            nm = ("R2", "R2", "K2", "K2")[gi]
            for k in range(8):
                MM(dst, h_[:, k, :], Wom[:, k, c0:c0 + n], k == 0, False, ["Wm", hr], [nm])
            for k in range(8):
                MM(dst, hTs[:, k, :], Wm[:, k, c0:c0 + n], False, k == 7, ["Wm", "hTs"], [nm])
        CP("act", xmt[:, 0:1024], R2[:, :], ["R2"], ["xmt"])
        CP("dve", xmt[:, 1024:1664], K2[:, 0:640], ["K2"], ["xmt"])
        TR(F2[0:64, 0:128], xmt[:, 1536:1600], identf[:], ["xmt", "identf"], ["F2"])
        TR(F2[0:64, 128:256], xmt[:, 1600:1664], identf[:], ["xmt", "identf"], ["F2"])
        KW = int(os.environ.get('KW', '3'))
        if KW & 1:
            ACT(rw["tw"], F2[0:64, 0:128], AF.Tanh, ["F2"], ["tw"])
        if KW & 2:
            CP("dve", rw["adc"], F2[0:64, 128:256], ["F2"], ["adc"])
        if KR < 1:
            return
        R2v = R2[0:64, :].rearrange("p (h t) -> p h t", h=8)
        K2v = K2[0:64, :].rearrange("p (h t) -> p h t", h=8)
        V2v = V2[0:64, :].rearrange("p (h t) -> p h t", h=8)
        for h in range(8):
            MM(R2v[:, h, :], wupS[:, h * 64:(h + 1) * 64], rw["tw"], True, True, ["wupS", "tw"], ["R2"])
            MM(K2v[:, h, :], aupS[:, h * 64:(h + 1) * 64], rw["adc"], True, True, ["aupS", "adc"], ["K2"])
        TT("dve", rw["sg"], R2v, bc(fp["w0"][:].unsqueeze(2), [64, 8, 128]), ALU.add, ["R2", "fp_w0"], ["sg"])
        ACT(rw["sg"], rw["sg"], AF.Sigmoid, ["sg"], ["sg"])
        TT("dve", rw["asig"], K2v, bc(fp["a0"][:].unsqueeze(2), [64, 8, 128]), ALU.add, ["K2", "fp_a0"], ["asig"])
        ACT(rw["asig"], rw["asig"], AF.Sigmoid, ["asig"], ["asig"])
        TS("dve", rw["sg"], rw["sg"], -0.6065306597126334, None, ALU.mult, None, ["sg"], ["sg"])
        b.op("dve", lambda g: g.tensor_tensor_scan(out=rw["L"].rearrange("p h t -> p (h t)"), data0=resetm[:, :],
                                                   data1=rw["sg"].rearrange("p h t -> p (h t)"), initial=0.0,
                                                   op0=ALU.mult, op1=ALU.add), ["sg", "resetm"], ["L"])
        ACT(rw["g"], rw["L"], AF.Exp, ["L"], ["g"])
        ACT(rw["ginv"], rw["L"], AF.Exp, ["L"], ["ginv"], scale=-1.0)
        TT("pool", rw["gprev"], rw["L"], rw["sg"], ALU.subtract, ["L", "sg"], ["gprev"])
        ACT(rw["gprev"], rw["gprev"], AF.Exp, ["gprev"], ["gprev"])
        if KR < 2:
            return
        for h in range(8):
            TR(R2v[:, h, :], xmt[:, h * 64:(h + 1) * 64], identf[:], ["xmt", "identf"], ["R2"])
            TR(K2v[:, h, :], xmt[:, 512 + h * 64:512 + (h + 1) * 64], identf[:], ["xmt", "identf"], ["K2"])
        TT("dve", rw["L"], K2v, bc(fp["kk"][:].unsqueeze(2), [64, 8, 128]), ALU.mult, ["K2", "fp_kk"], ["L"])
        ACT(rw["t1"], rw["L"], AF.Square, ["L"], ["t1"])
        t1f = rw["t1"].rearrange("p h t -> p (h t)")
        for hh in range(2):
            MM(V2[0:64, hh * 512:(hh + 1) * 512], ones64[:, :], t1f[:, hh * 512:(hh + 1) * 512], True, True, ["ones64", "t1"], ["V2"])
        ACT(rw["t1"], V2v, AF.Sqrt, ["V2", "cst"], ["t1"], bias=cst[0:64, 2:3])
        b.op("dve", lambda g: g.reciprocal(out=rw["t1"], in_=rw["t1"]), ["t1"], ["t1"])
        TT("dve", rw["kkn"], rw["L"], rw["t1"], ALU.mult, ["L", "t1"], ["kkn"])
        STT(rw["t1"], rw["asig"], -1.0, bc(fp["ka"][:].unsqueeze(2), [64, 8, 128]), ALU.add, ALU.mult, ["asig", "fp_ka"], ["t1"])
        STT(rw["kmod"], rw["t1"], 1.0, K2v, ALU.add, ALU.mult, ["t1", "K2"], ["kmod"])
        if KR < 3:
            return
        QTv = QTt.rearrange("p h c (q t) -> p h c q t", q=2)
        KTv = KTt.rearrange("p h c (q t) -> p h c q t", q=2)

        def ch(v):
            return v.rearrange("p h (c t) -> p h c t", c=2)
        STT(QTv[:, :, :, 0, :], ch(rw["kkn"]), -1.0, ch(rw["gprev"]), ALU.mult, ALU.mult, ["kkn", "gprev"], ["QTt"])
        TT("dve", QTv[:, :, :, 1, :], ch(R2v), ch(rw["g"]), ALU.mult, ["R2", "g"], ["QTt"])
        TT("pool", rw["t1"], rw["kkn"], rw["asig"], ALU.mult, ["kkn", "asig"], ["t1"])
        TT("pool", KTv[:, :, :, 0, :], ch(rw["t1"]), ch(rw["ginv"]), ALU.mult, ["t1", "ginv"], ["KTt"])
        TT("pool", KTv[:, :, :, 1, :], ch(rw["kmod"]), ch(rw["ginv"]), ALU.mult, ["kmod", "ginv"], ["KTt"])
        TT("dve", rw["L"], R2v, bc(fp["rk"][:].unsqueeze(2), [64, 8, 128]), ALU.mult, ["R2", "fp_rk"], ["L"])
        TT("dve", rw["L"], rw["L"], rw["kmod"], ALU.mult, ["L", "kmod"], ["L"])
        for h in range(8):
            MM(F2[:, 256 + h:257 + h], rw["L"][:, h, :], ones64[:, 0:1], True, True, ["L", "ones64"], ["F2"])
        CP("dve", bon, F2[:, 256:264], ["F2"], ["bon"])
        if KR < 4:
            return
        CP("act", Vc[:, 0], xmt[0:64, 1024:1536].rearrange("p (h i) -> p h i", h=8), ["xmt"], ["Vc"])
        MM(V2[0:64, 0:512], identf[:, 64:128], xmt[:, 1024:1536], True, True, ["identf", "xmt"], ["V2"])
        CP("act", Vc[:, 1], V2[0:64, 0:512].rearrange("p (h i) -> p h i", h=8), ["V2"], ["Vc"])
        if int(os.environ.get("KLVL", "9")) < 3:
            return
        for q in range(4):
            c, hg = q // 2, q % 2
            bk, bkn = (K2, "K2") if q % 2 == 0 else (V2, "V2")
            AMp = bk[0:64, :].rearrange("p (h x) -> p h x", h=4)
            for hd in range(4):
                h = hg * 4 + hd
                MM(AMp[:, hd, 0:128], KTt[:, h, c, 0:64], QTt[:, h, c, :], True, True, ["KTt", "QTt"], [bkn])
                MM(AMp[:, hd, 128:256], KTt[:, h, c, 64:128], QTt[:, h, c, :], True, True, ["KTt", "QTt"], [bkn])
            TT("dve", AM[:, q * 4:(q + 1) * 4, :], AMp, bc(maskT[:].unsqueeze(1), [64, 4, 256]), ALU.mult, [bkn, "maskT"], ["AM"])
        Lp = R2[0:64, :].rearrange("p (h x) -> p h x", h=16)
        for q in range(4):
            c, hg = q // 2, q % 2
            for hd in range(4):
                h = hg * 4 + hd
                MM(Lp[:, q * 4 + hd, :], QTt[:, h, c, 0:64], KTt[:, h, c, 0:64], True, True, ["KTt", "QTt"], ["R2"])
        TT("dve", Lm[0], Lp, bc(maskL[:].unsqueeze(1), [64, 16, 64]), ALU.mult, ["R2", "maskL"], ["Lm0"])
        CP("act", Nm[0], AM[:, :, 0:64], ["AM"], ["Nm0"])
        TT("dve", Pm[0], AM[:, :, 0:64], bc(identb64.unsqueeze(1), [64, 16, 64]), ALU.add, ["AM", "identb"], ["Pm0"])
        cur = 0
        Np = K2[0:64, :].rearrange("p (h x) -> p h x", h=16)
        Lpp = V2[0:64, :].rearrange("p (h x) -> p h x", h=16)
        PPp = R2[0:64, :].rearrange("p (h x) -> p h x", h=16)
        for lvl in range(1, 6):
            nx = 1 - cur
            for i in range(16):
                if lvl < 5:
                    MM(Np[:, i, :], Lm[cur][:, i, :], Nm[cur][:, i, :], True, True, ["Lm%d" % cur, "Nm%d" % cur], ["K2"])
                MM(Lpp[:, i, :], Nm[cur][:, i, :], Lm[cur][:, i, :], True, True, ["Lm%d" % cur, "Nm%d" % cur], ["V2"])
            if lvl < 5:
                CP("act", Nm[nx], Np, ["K2"], ["Nm%d" % nx])
            CP("dve", Lm[nx], Lpp, ["V2"], ["Lm%d" % nx])
            for i in range(16):
                MM(PPp[:, i, :], Lm[nx][:, i, :], Pm[cur][:, i, :], True, True, ["Lm%d" % nx, "Pm%d" % cur], ["R2"])
            TT("dve", Pm[nx], PPp, Pm[cur], ALU.add, ["R2", "Pm%d" % cur], ["Pm%d" % nx])
            cur = nx
        P6 = Pm[cur]
        P6n = "Pm%d" % cur
        for c in range(2):
            BKp = PTb[0:64, :].rearrange("p (h q j) -> p h q j", h=8, q=2)
            for h in range(8):
                TR(BKp[:, h, 0, :], KTt[:, h, c, 0:64], identb64, ["KTt", "identb"], ["PTb"])
                TR(BKp[:, h, 1, :], KTt[:, h, c, 64:128], identb64, ["KTt", "identb"], ["PTb"])
            CP("act", BKtok, BKp, ["PTb"], ["BKtok"])
            P0p = F2[0:64, :].rearrange("p (h i) -> p h i", h=8)
            Up = R2[0:64, 0:512].rearrange("p (h i) -> p h i", h=8)
            Yp = K2[0:64, 0:512].rearrange("p (h i) -> p h i", h=8)
            Hp = V2[0:64, 0:512].rearrange("p (h i) -> p h i", h=8)

            def ai(h):
                return (c * 2 + h // 4) * 4 + h % 4
            for h in range(8):
                MM(P0p[:, h, :], QTt[:, h, c, 0:64], Hb[:, h, :], True, False, ["QTt", "Hb"], ["F2"])
                MM(P0p[:, h, :], AM[:, ai(h), 128:192], Vc[:, c, h, :], False, True, ["AM", "Vc"], ["F2"])
            CP("act", P0s, P0p, ["F2"], ["P0s"])
            for h in range(8):
                MM(Up[:, h, :], P6[:, ai(h), :], P0s[:, h, :], True, True, [P6n, "P0s"], ["R2"])
            CP("act", Us, Up, ["R2"], ["Us"])
            for h in range(8):
                MM(Yp[:, h, :], QTt[:, h, c, 64:128], Hb[:, h, :], True, False, ["QTt", "Hb"], ["K2"])
                MM(Yp[:, h, :], AM[:, ai(h), 64:128], Us[:, h, :], False, False, ["AM", "Us"], ["K2"])
                MM(Yp[:, h, :], AM[:, ai(h), 192:256], Vc[:, c, h, :], False, True, ["AM", "Vc"], ["K2"])
            for h in range(8):
                MM(Hp[:, h, :], BKtok[:, h, 0, :], Us[:, h, :], True, False, ["BKtok", "Us"], ["V2"])
                MM(Hp[:, h, :], BKtok[:, h, 1, :], Vc[:, c, h, :], False, True, ["BKtok", "Vc"], ["V2"])
            CP("act", ych, Yp, ["K2"], ["ych"])
            TT("dve", H32, H32, Hp, ALU.add, ["H32", "V2"], ["H32"])
            TT("dve", H32, H32, bc(rw["g"][:, :, c * 64 + 63:c * 64 + 64], [64, 8, 64]), ALU.mult, ["H32", "g"], ["H32"])
            CP("act", Hb, H32, ["H32"], ["Hb"])
            RED(st8[:, 0:8], ych, ALU.add, ["ych"], ["st8"])
            TT("dve", yt1, ych, ych, ALU.mult, ["ych"], ["yt1"])
            RED(st8[:, 8:16], yt1, ALU.add, ["yt1"], ["st8"])
            TS("dve", st8[:, 0:16], st8[:, 0:16], 1.0 / 64, None, ALU.mult, None, ["st8"], ["st8"])
            TT("dve", st8[:, 16:24], st8[:, 0:8], st8[:, 0:8], ALU.mult, ["st8"], ["st8"])
            TT("dve", st8[:, 24:32], st8[:, 8:16], st8[:, 16:24], ALU.subtract, ["st8"], ["st8"])
            ACT(st8[:, 32:40], st8[:, 24:32], AF.Sqrt, ["st8", "cst"], ["st8"], bias=cst[0:64, 1:2])
            b.op("dve", lambda g: g.reciprocal(out=st8[:, 40:48], in_=st8[:, 32:40]), ["st8"], ["st8"])
            TT("dve", yt1, ych, bc(st8[:, 0:8].unsqueeze(2), [64, 8, 64]), ALU.subtract, ["ych", "st8"], ["yt1"])
            TT("dve", yt1, yt1, bc(st8[:, 40:48].unsqueeze(2), [64, 8, 64]), ALU.mult, ["yt1", "st8"], ["yt1"])
            lnwv = lnw_bc[:].rearrange("p (h i) -> p h i", h=8)
            lnbv = lnb_bc[:].rearrange("p (h i) -> p h i", h=8)
            TT("dve", yt1, yt1, lnwv, ALU.mult, ["yt1", "lnw_bc"], ["yt1"])
            TT("pool", yt1, yt1, lnbv, ALU.add, ["yt1", "lnb_bc"], ["yt1"])
            MM(F2[0:64, 264:272], identf[:, c * 64:(c + 1) * 64], bon, True, True, ["identf", "bon"], ["F2"])
            CP("act", st8[:, 48:56], F2[0:64, 264:272], ["F2"], ["st8"])
            TT("dve", ych, Vc[:, c], bc(st8[:, 48:56].unsqueeze(2), [64, 8, 64]), ALU.mult, ["Vc", "st8"], ["ych"])
            TT("pool", yt1, yt1, ych, ALU.add, ["yt1", "ych"], ["yt1"])
            MM(R2[0:64, 0:512], identf[:, c * 64:(c + 1) * 64], gr_s[:, :], True, True, ["identf", "gr_s"], ["R2"])
            TT("dve", yt1.rearrange("p h i -> p (h i)"), yt1.rearrange("p h i -> p (h i)"), R2[0:64, 0:512], ALU.mult, ["yt1", "R2"], ["yt1"])
            b.dma("sp", rwscr[ti * 128 + c * 64: ti * 128 + (c + 1) * 64, :], yt1.rearrange("p h i -> p (h i)"), reads=["yt1"], writes=["rwscr"])
        b.barrier(dmas=False)

    if stop_after == "A":
        b.barrier(); b.emit(); ncd.__exit__(None, None, None); es.close()
        return nc
    b.barrier()
    MS("dve", H32, 0.0, ["H32"]); MS("dve", Hb, 0.0, ["Hb"]); MS("pool", hlast[:], 0.0, ["hlast"])
    KSUB = int(os.environ.get('KSUB', '9'))
    V2a = V2[:, 0:512]
    V2b = V2[:, 512:1024]
    for ti in range(NT if nt_lim is None else nt_lim):
        par = ti % 2
        x_, h_, xr, hr = front(ti, par)
        if KSUB >= 1:
            for (c0, n, dst) in ((0, 256, V2a[:, 0:256]), (256, 64, V2a[:, 256:320]), (320, 512, V2b)):
                for k in range(8):
                    MM(dst, h_[:, k, :], Wn[:, k, c0:c0 + n], k == 0, k == 7, [hr, "Wn"], ["V2"])
        if KSUB >= 2:
            kv3 = kfin[:].rearrange("p (g d) -> p g d", g=2)
            qknorm(V2a[:, 0:128].rearrange("p (g d) -> p g d", g=2), kv3, 2, knw_bc, 1.0, ["V2"], ["kfin"])
            rope("dve", kv3, 2, None, ["kfin"])
            CP("act", vfin[:], V2a[:, 128:256], ["V2"], ["vfin"])
            CP("act", Vaug[:, ti, :, 0:64], V2a[:, 128:256].rearrange("p (g d) -> p g d", g=2), ["V2"], ["Vaug"])
            CP("act", kifin[:], V2a[:, 256:320], ["V2"], ["kifin"])
            rope("pool", kifin[:].unsqueeze(1), 1, None, ["kifin"])
            ACT(gr_s[:], V2b, AF.Silu, ["V2"], ["gr_s"])
        if KSUB >= 3:
            b.dma("sp", k_nat[ti * 128:(ti + 1) * 128, :], kfin[:], reads=["kfin"])
            b.dma("sp", v_nat[ti * 128:(ti + 1) * 128, :], vfin[:], reads=["vfin"])
            b.dma("sp", ki_nat[ti * 128:(ti + 1) * 128, :], kifin[:], reads=["kifin"])
        if KSUB >= 4:
            for g_ in range(2):
                TR(F2[0:64, g_ * 128:(g_ + 1) * 128], kfin[:, g_ * 64:(g_ + 1) * 64], identf[:], ["kfin", "identf"], ["F2"])
            TR(F2[0:64, 256:384], kifin[:, :], identf[:], ["kifin", "identf"], ["F2"])
            if KSUB >= 5:
                CP("act", kT_all[:, :, ti * 128:(ti + 1) * 128], F2[0:64, 0:256].rearrange("p (g t) -> p g t", g=2), ["F2"], ["kT_all"])
            if KSUB >= 6:
                if os.environ.get("KV") == "1":
                    CP("act", nrm_t[0:64, 0:128], F2[0:64, 256:384], ["F2"], ["nrm_t"])
                elif os.environ.get("KV") == "2":
                    CP("act", kiT_all[:, ti * 128:(ti + 1) * 128], F2[0:64, 0:128], ["F2"], ["kiT_all"])
                else:
                    CP("act", kiT_all[:, ti * 128:(ti + 1) * 128], F2[0:64, 256:384], ["F2"], ["kiT_all"])

        if int(os.environ.get("KLVL", "9")) >= 2:
            rwkv_tile(ti, h_, hr)
        if ti == NT - 1:
            for gi, (c0, n) in enumerate(((0, 512), (512, 512), (1024, 512), (1536, 128))):
                dst = (R2[0:1, 0:512], R2[0:1, 512:1024], K2[0:1, 0:512], K2[0:1, 512:640])[gi]
                nm = ("R2", "R2", "K2", "K2")[gi]
                for k in range(8):
                    MM(dst, h_[:, k, 127:128], Wom[:, k, c0:c0 + n], k == 0, False, ["Wm", hr], [nm])
                for k in range(8):
                    MM(dst, h_[:, k, 127:128], Wm[:, k, c0:c0 + n], False, k == 7, ["Wm", hr], [nm])
            CP("act", xmt[0:1, 0:1024], R2[0:1, :], ["R2"], ["xmt"])
            CP("dve", xmt[0:1, 1024:1664], K2[0:1, 0:640], ["K2"], ["xmt"])
            b.dma("sp", shift_p.rearrange("(a n) -> a n", a=1), xmt[0:1, :], reads=["xmt"])
    for h in range(8):
        TR(F2[0:64, h * 64:(h + 1) * 64], H32[:, h, :], identf[0:64, 0:64], ["H32", "identf"], ["F2"])
    CP("dve", ych, F2[0:64, 0:512].rearrange("p (h j) -> p h j", h=8), ["F2"], ["ych"])
    b.dma("sp", wkv_p.rearrange("h i j -> i h j"), ych, reads=["ych"])

    if stop_after == "B":
        b.barrier(); b.emit(); ncd.__exit__(None, None, None); es.close()
        return nc
    b.barrier()
    off = 0
    Wq, off = carve(off, [128, 8, 1544], BF16)
    stg, off = carve(off, [128, 8, 512])
    score, off = carve(off, [128, T])
    selm, off = carve(off, [128, T], BF16)
    selT, off = carve(off, [128, NT, 128], BF16)
    rl, off = carve(off, [128, 512], BF16)
    rl2, off = carve(off, [128, 512], BF16)
    rlf, off = carve(off, [128, 256])
    diagw, off = carve(off, [128, 8, 128], BF16)
    qfin, off = carve(off, [128, 512])
    qifin, off = carve(off, [128, 512])
    qT, off = carve(off, [64, 8, 128], BF16)
    qiT, off = carve(off, [64, 8, 128], BF16)
    ga, off = carve(off, [128, 512])
    eT, off = carve(off, [128, 4, 128], BF16)
    pTt, off = carve(off, [128, 4, 128], BF16)
    eT2, off = carve(off, [128, 4, 128], BF16)
    pTt2, off = carve(off, [128, 4, 128], BF16)
    cat, off = carve(off, [128, D], BF16)
    catT, off = carve(off, [128, 8, 128], BF16)
    rwo, off = carve(off, [128, 512])
    rwo2, off = carve(off, [128, 512])
    att, off = carve(off, [128, 8, 64])
    ybuf, off = carve(off, [128, D])
    bs, off = carve(off, [128, 16])
    wis, off = carve(off, [128, 8])
    oacc, off = carve(off, [128, 2, 4, 65])
    Wout, off = carve(off, [128, 8, D], BF16)
    gate_bc, off = carve(off, [128, D])
    iota256, off = carve(off, [128, 256])
    assert off <= AW, off
    b.dma("sp", gate_bc, gscr[16:17, :].partition_broadcast(128) if False else gscr[16, :].partition_broadcast(128), reads=["gscr"], writes=["gate_bc"])
    b.dma("sp", iota256, iota_d[:, :], writes=["iota256"])
    load_cols(Wq, 0, C_Q, 512, tag="Wq")
    load_cols(Wq, 512, C_QI, 520, tag="Wq")
    load_cols(Wq, 1032, C_GA, 512, tag="Wq")
    w_out_v = w_out.rearrange("(k p) c -> p k c", p=128)
    for hh in range(2):
        b.dma("sp", stg[:, :, :], w_out_v[:, :, hh * 512:(hh + 1) * 512], writes=["stg"])
        CP("pool", Wout[:, :, hh * 512:(hh + 1) * 512], stg[:, :, :], ["stg"], ["Wout"])


    for j in range(NO if no_lim is None else no_lim):
        ti = NT + j
        x_, h_, xr, hr = front(ti, 0)
        NKT = 2 * (j + 1)
        NK = NKT * 128
        for (c0, n, dst, nm) in ((0, 512, R2[:, 0:512], "R2a"), (512, 512, R2[:, 512:1024], "R2b"),
                                 (1024, 8, F2[:, 0:8], "F2"), (1032, 512, K2[:, 0:512], "K2a")):
            for k in range(8):
                MM(dst, h_[:, k, :], Wq[:, k, c0:c0 + n], k == 0, k == 7, [hr, "Wq"], [nm])
        q3 = qfin.rearrange("p (h d) -> p h d", h=8)
        qknorm(R2[:, 0:512].rearrange("p (h d) -> p h d", h=8), q3, 8, qnw_bc, 0.125, ["R2a"], ["qfin"])
        rope("dve", q3, 8, None, ["qfin"])
        qi3 = qifin.rearrange("p (h d) -> p h d", h=8)
        CP("act", qifin, R2[:, 512:1024], ["R2b"], ["qifin"])
        rope("pool", qi3, 8, None, ["qifin"])
        TS("dve", wis, F2[:, 0:8], 0.044194173824159216, None, ALU.mult, None, ["F2"], ["wis"])
        ACT(ga, K2[:, 0:512], AF.Silu, ["K2a"], ["ga"])
        for (src, srcn, dstT, dn) in ((qfin, "qfin", qT, "qT"), (qifin, "qifin", qiT, "qiT")):
            pv = K2[0:64, :].rearrange("p (h t) -> p h t", h=8)
            for h in range(8):
                TR(pv[:, h, :], src[:, h * 64:(h + 1) * 64], identf[:], [srcn, "identf"], ["K2a" if h < 4 else "K2b"])
            CP("act", dstT, pv, ["K2a", "K2b"], [dn])
        TT("dve", diagw, bc(identb[:].unsqueeze(1), [128, 8, 128]), bc(wis.unsqueeze(2), [128, 8, 128]), ALU.mult, ["identb", "wis"], ["diagw"])
        nchk = (NK + 511) // 512
        ib = 0
        for kc in range(nchk):
            w = min(512, NK - kc * 512)
            pend = None
            for h in range(8):
                pb = (R2[:, 0:512], R2[:, 512:1024])[ib % 2]
                pbn = ("R2a", "R2b")[ib % 2]
                rlb = (rl, rl2)[ib % 2]
                rln = ("rl", "rl2")[ib % 2]
                ib += 1
                MM(pb[:, 0:w], qiT[:, h, :], kiT_all[:, kc * 512:kc * 512 + w], True, True, ["qiT", "kiT_all"], [pbn])
                ACT(rlb[:, 0:w], pb[:, 0:w], AF.Relu, [pbn], [rln])
                if pend is not None:
                    ph, prl, prn = pend
                    MM(F2[:, 0:w], diagw[:, ph, :], prl[:, 0:w], ph == 0, False, ["diagw", prn], ["F2"])
                pend = (h, rlb, rln)
            ph, prl, prn = pend
            MM(F2[:, 0:w], diagw[:, ph, :], prl[:, 0:w], False, True, ["diagw", prn], ["F2"])
            CP("dve", score[:, kc * 512:kc * 512 + w], F2[:, 0:w], ["F2"], ["score"])
        RED(bs[:, 0:1], score[:, 0:NK], ALU.max, ["score"], ["bs"])
        RED(bs[:, 1:2], score[:, 0:NK], ALU.min, ["score"], ["bs"])
        TS("dve", rlf, iota256, qrel[:, 0:1], -1e30, ALU.is_gt, ALU.mult, ["iota256", "qrel"], ["rlf"])
        TT("dve", score[:, NK - 256:NK], score[:, NK - 256:NK], rlf, ALU.add, ["score", "rlf"], ["score"])
        TS("dve", bs[:, 2:3], bs[:, 1:2], -1.0, None, ALU.add, None, ["bs"], ["bs"])
        STT(bs[:, 3:4], bs[:, 0:1], 2.0, bs[:, 1:2], ALU.add, ALU.subtract, ["bs"], ["bs"])
        for it in range(1, n_bis + 1):
            sc_ = float(2.0 ** (-it))
            STT(bs[:, 4:5], bs[:, 3:4], sc_, bs[:, 2:3], ALU.mult, ALU.add, ["bs"], ["bs"])
            TS("dve", selm[:, 0:NK], score[:, 0:NK], bs[:, 4:5], 0.0, ALU.is_gt, ALU.add, ["score", "bs"], ["selm", "bs"], accum=bs[:, 5:6])
            TS("dve", bs[:, 6:7], bs[:, 5:6], float(topk_p) - 0.5, bs[:, 3:4], ALU.is_gt, ALU.mult, ["bs"], ["bs"])
            STT(bs[:, 2:3], bs[:, 6:7], sc_, bs[:, 2:3], ALU.mult, ALU.add, ["bs"], ["bs"])
        TS("dve", selm[:, 0:NK], score[:, 0:NK], bs[:, 2:3], None, ALU.is_gt, None, ["score", "bs"], ["selm"])
        for kt in range(NKT):
            TR(PTb[:, (kt % 8) * 128:(kt % 8 + 1) * 128], selm[:, kt * 128:(kt + 1) * 128], identb[:], ["selm", "identb"], ["PTb"])
            if kt % 8 == 7 or kt == NKT - 1:
                k0 = (kt // 8) * 8
                n_ = kt - k0 + 1
                ACT(selT[:, k0:k0 + n_, :], PTb[:, 0:n_ * 128].rearrange("p (a t) -> p a t", a=n_), AF.Identity, ["PTb", "cst"], ["selT"],
                    scale=30000.0, bias=cst[:, 3:4])
        po = [V2[:, 0:260].rearrange("p (h e) -> p h e", h=4), V2[:, 512:772].rearrange("p (h e) -> p h e", h=4)]
        def att_front(kt, g_):
            lp = K2[:, g_ * 512:(g_ + 1) * 512]
            kn_ = ("K2a", "K2b")[g_]
            pTb = (pTt, pTt2)[g_]
            pn_ = ("pTt", "pTt2")[g_]
            MM(lp, kT_all[:, g_, kt * 128:(kt + 1) * 128], qT[:, g_ * 4:(g_ + 1) * 4, :].rearrange("p h t -> p (h t)"),
               True, False, ["kT_all", "qT"], [kn_])
            for hh in range(4):
                MM(lp[:, hh * 128:(hh + 1) * 128], identb[:], selT[:, kt, :], False, hh == 3, ["identb", "selT"], [kn_])
            ACT(pTb, lp.rearrange("p (h t) -> p h t", h=4), AF.Exp, [kn_], [pn_])

        def att_back(kt, g_):
            vn_ = ("V2a", "V2b")[g_]
            pTb = (pTt, pTt2)[g_]
            pn_ = ("pTt", "pTt2")[g_]
            on_ = ("oacc0", "oacc1")[g_]
            for hh in range(4):
                MM(po[g_][:, hh, :], pTb[:, hh, :], Vaug[:, kt, g_, :], True, True, [pn_, "Vaug"], [vn_])
            if kt == 0:
                CP("act", oacc[:, g_], po[g_], [vn_], [on_])
            else:
                TT("dve", oacc[:, g_], oacc[:, g_], po[g_], ALU.add, [vn_, on_], [on_])
        att_front(0, 0)
        for kt in range(NKT):
            att_front(kt, 1)
            att_back(kt, 0)
            if kt + 1 < NKT:
                att_front(kt + 1, 0)
            att_back(kt, 1)
        for g_ in range(2):
            b.op("dve", lambda g, g_=g_: g.reciprocal(out=bs[:, 8 + g_ * 4:12 + g_ * 4], in_=oacc[:, g_, :, 64]), ["oacc0", "oacc1"], ["bs"])
            TT("dve", att[:, g_ * 4:(g_ + 1) * 4, :], oacc[:, g_, :, 0:64], bc(bs[:, 8 + g_ * 4:12 + g_ * 4].unsqueeze(2), [128, 4, 64]),
               ALU.mult, ["oacc0", "oacc1", "bs"], ["att"])
        TT("dve", cat[:, 0:512], att.rearrange("p h d -> p (h d)"), ga, ALU.mult, ["att", "ga"], ["cat"])
        b.dma("sp", rwo, rwscr[(2 * j) * 128:(2 * j + 1) * 128, :], reads=["rwscr"], writes=["rwo"])
        b.dma("sp", rwo2, rwscr[(2 * j + 1) * 128:(2 * j + 2) * 128, :], reads=["rwscr"], writes=["rwo2"])
        TS("dve", rwo, rwo, parsel[:, 1:2], None, ALU.mult, None, ["rwo", "parsel"], ["rwo"])
        STT(rwo, rwo2, parsel[:, 0:1], rwo, ALU.mult, ALU.add, ["rwo2", "parsel", "rwo"], ["rwo"])
        CP("act", cat[:, 512:1024], rwo, ["rwo"], ["cat"])
        for k in range(8):
            TR(PTb[:, k * 128:(k + 1) * 128], cat[:, k * 128:(k + 1) * 128], identb[:], ["cat", "identb"], ["PTb"])
        CP("act", catT, PTb[:, :].rearrange("p (k t) -> p k t", k=8), ["PTb"], ["catT"])
        for hh in range(2):
            for k in range(8):
                MM(R2[:, hh * 512:(hh + 1) * 512], catT[:, k, :], Wout[:, k, hh * 512:(hh + 1) * 512], k == 0, k == 7, ["catT", "Wout"], [("R2a", "R2b")[hh]])
        TT("dve", ybuf, R2[:, :], gate_bc, ALU.mult, ["R2a", "R2b", "gate_bc"], ["ybuf"])
        TT("pool", ybuf, ybuf, x_[:], ALU.add, ["ybuf", xr], ["ybuf"])
        b.dma("sp", y_own[j * 128:(j + 1) * 128, :], ybuf, reads=["ybuf"])


    if do_sample:
        b.barrier()
        PW = 10500
        off = 0
        proj, off = carve(off, [16, DIN])
        tk = {}
        for nm in ("qs", "ga", "grs", "ta", "tb"):
            tk[nm], off = carve(off, [16, 512])
        ks_, off = carve(off, [16, 128])
        s16, off = carve(off, [16, 64])
        tokd, off = carve(off, [16, 1040])
        ysb, off = carve(off, [16, D])
        cats, off = carve(off, [16, D], BF16)
        catTs, off = carve(off, [128, 8, 16], BF16)
        assert off <= PW, off
        off = PW
        stg, off = carve(off, [128, 8, 512])
        wbf, off = carve(off, [128, 8, 512], BF16)
        sshift_t, off = carve(off, [16, SHW])
        mu16, off = carve(off, [16, SHW])
        X1 = off
        xm, off = carve(off, [16, SHW])
        prm, off = carve(off, [16, 5, 512])
        vecs, off = carve(off, [16, 8, 6, 64])
        for nm in ("dec", "asg", "kkv", "kkn", "kmod"):
            tk[nm], off = carve(off, [16, 512])
        wdt, off = carve(off, [16, 128])
        wdT, off = carve(off, [64, 32])
        assert off <= AW, off
        NPAIR = NS // 2

        x_, h_, xr, hr = front(NT + NO, 0, m_prompt=False, ntok=16)
        for ch in range(8):
            c0 = ch * 505
            b.dma("sp", stg[:, :, 0:505], w_in_v[:, :, c0:c0 + 505], writes=["stg"])
            CP("dve", wbf[:, :, 0:505], stg[:, :, 0:505], ["stg"], ["wbf"])
            for k in range(8):
                MM(R2[0:16, 0:505], h_[:, k, 0:16], wbf[:, k, 0:505], k == 0, k == 7, [hr, "wbf"], ["R2"])
            CP("act", proj[:, c0:c0 + 505], R2[0:16, 0:505], ["R2"], ["proj"])
        b.dma("sp", sshift_t, sshift_d[:, :], writes=["sshift"])
        b.dma("sp", mu16, mu.partition_broadcast(16), writes=["mu16"])
        for i_, src in enumerate((pw0, pa0, pkk, pka, prk)):
            b.dma("sp", prm[:, i_, :], src.partition_broadcast(16), writes=["prm"])
        qs3 = tk["qs"].rearrange("p (h d) -> p h d", h=8)
        qknorm(proj[:, 0:512].rearrange("p (h d) -> p h d", h=8), qs3, 8, qnw_bc, 0.125, ["proj"], ["qs"], nrows=16)
        rope("dve", qs3, 8, None, ["qs"], nrows=16)
        ks3 = ks_.rearrange("p (g d) -> p g d", g=2)
        qknorm(proj[:, 512:640].rearrange("p (g d) -> p g d", g=2), ks3, 2, knw_bc, 1.0, ["proj"], ["ks"], nrows=16)
        rope("dve", ks3, 2, None, ["ks"], nrows=16)
        b.dma("sp", k_s[:, :], ks_, reads=["ks"])
        b.dma("sp", v_s[:, :], proj[:, 640:768], reads=["proj"])
        b.dma("sp", shift_s[:, :], proj[:, C_R:C_R + SHW], reads=["proj"])
        rope("dve", proj[:, 768:1280].rearrange("p (h d) -> p h d", h=8), 8, None, ["proj"], nrows=16)
        rope("dve", proj[:, 1288:1352].unsqueeze(1), 1, None, ["proj"], nrows=16)
        b.dma("sp", ki_s[:, :], proj[:, 1288:1352], reads=["proj"])
        ACT(tk["ga"], proj[:, C_GA:C_GA + 512], AF.Silu, ["proj"], ["ga"])
        ACT(tk["grs"], proj[:, C_GR:C_GR + 512], AF.Silu, ["proj"], ["grs"])
        xs_ = proj[:, C_R:C_R + SHW]
        TT("dve", xm, sshift_t, xs_, ALU.subtract, ["sshift", "proj"], ["xm"])
        TT("dve", xm, xm, mu16, ALU.mult, ["xm", "mu16"], ["xm"])
        TT("dve", xm, xm, xs_, ALU.add, ["xm", "proj"], ["xm"])
        ACT(wdt[:, 0:64], xm[:, 1536:1600], AF.Tanh, ["xm"], ["wdt"])
        CP("dve", wdt[:, 64:128], xm[:, 1600:1664], ["xm"], ["wdt"])
        TR(F2[0:64, 0:16], wdt[:, 0:64], identf[0:16, 0:16], ["wdt", "identf"], ["F2"])
        TR(F2[0:64, 16:32], wdt[:, 64:128], identf[0:16, 0:16], ["wdt", "identf"], ["F2"])
        CP("dve", wdT, F2[0:64, 0:32], ["F2"], ["wdT"])
        MM(R2[0:16, 0:512], wdT[:, 0:16], wupS[:, :], True, True, ["wdT", "wupS"], ["R2"])
        MM(R2[0:16, 512:1024], wdT[:, 16:32], aupS[:, :], True, True, ["wdT", "aupS"], ["R2"])
        TT("dve", tk["dec"], R2[0:16, 0:512], prm[:, 0, :], ALU.add, ["R2", "prm"], ["dec"])
        ACT(tk["dec"], tk["dec"], AF.Sigmoid, ["dec"], ["dec"])
        ACT(tk["dec"], tk["dec"], AF.Exp, ["dec"], ["dec"], scale=-0.6065306597126334)
        TT("dve", tk["asg"], R2[0:16, 512:1024], prm[:, 1, :], ALU.add, ["R2", "prm"], ["asg"])
        ACT(tk["asg"], tk["asg"], AF.Sigmoid, ["asg"], ["asg"])
        xr_, xk_, xv_ = xm[:, 0:512], xm[:, 512:1024], xm[:, 1024:1536]
        TT("dve", tk["kkv"], xk_, prm[:, 2, :], ALU.mult, ["xm", "prm"], ["kkv"])
        ACT(tk["ta"], tk["kkv"], AF.Square, ["kkv"], ["ta"])
        RED(s16[:, 0:8], tk["ta"].rearrange("p (h d) -> p h d", h=8), ALU.add, ["ta"], ["s16"])
        ACT(s16[:, 8:16], s16[:, 0:8], AF.Sqrt, ["s16", "cst"], ["s16"], bias=cst[0:16, 2:3])
        b.op("dve", lambda g: g.reciprocal(out=s16[:, 16:24], in_=s16[:, 8:16]), ["s16"], ["s16"])
        TT("dve", tk["kkn"].rearrange("p (h d) -> p h d", h=8), tk["kkv"].rearrange("p (h d) -> p h d", h=8),
           bc(s16[:, 16:24].unsqueeze(2), [16, 8, 64]), ALU.mult, ["kkv", "s16"], ["kkn"])
        STT(tk["ta"], tk["asg"], -1.0, prm[:, 3, :], ALU.add, ALU.mult, ["asg", "prm"], ["ta"])
        STT(tk["kmod"], tk["ta"], 1.0, xk_, ALU.add, ALU.mult, ["ta", "xm"], ["kmod"])

        def v8(ap):
            return ap.rearrange("p (h d) -> p h d", h=8)
        CP("dve", vecs[:, :, 0, :], v8(tk["dec"]), ["dec"], ["vecs"])
        TS("dve", vecs[:, :, 1, :], v8(tk["kkn"]), -1.0, None, ALU.mult, None, ["kkn"], ["vecs"])
        TT("dve", vecs[:, :, 2, :], v8(tk["kkn"]), v8(tk["asg"]), ALU.mult, ["kkn", "asg"], ["vecs"])
        CP("dve", vecs[:, :, 3, :], v8(tk["kmod"]), ["kmod"], ["vecs"])
        CP("dve", vecs[:, :, 4, :], v8(xr_), ["xm"], ["vecs"])
        CP("dve", vecs[:, :, 5, :], v8(xv_), ["xm"], ["vecs"])
        TT("dve", tk["ta"], xr_, prm[:, 4, :], ALU.mult, ["xm", "prm"], ["ta"])
        TT("dve", tk["ta"], tk["ta"], tk["kmod"], ALU.mult, ["ta", "kmod"], ["ta"])
        RED(s16[:, 24:32], v8(tk["ta"]), ALU.add, ["ta"], ["s16"])
        b.dma("sp", scr1[:, :], vecs.rearrange("p h v j -> p (h v j)"), reads=["vecs"], writes=["scr1"])
        b.barrier()
        off = PW
        S_, off = carve(off, [128, 4096])
        tmpS, off = carve(off, [128, 4096])
        vsh, off = carve(off, [128, 384])
        ysh, off = carve(off, [128, 128])
        assert off <= X1
        b.dma("sp", S_, swkv_d[:, :], writes=["S"])
        b.dma("sp", vsh, scr1.rearrange("s (h x) -> (s h) x", h=8), reads=["scr1"], writes=["vsh"])
        S3 = S_.rearrange("p (i j) -> p i j", i=64)
        T3 = tmpS.rearrange("p (i j) -> p i j", i=64)

        def jb(vi):
            return bc(vsh[:, vi * 64:(vi + 1) * 64].unsqueeze(1), [128, 64, 64])

        def ib(ap):
            return bc(ap.unsqueeze(2), [128, 64, 64])
        TT("dve", T3, S3, jb(1), ALU.mult, ["S", "vsh"], ["tmpS"])
        RED(ysh[:, 0:64], T3, ALU.add, ["tmpS"], ["ysh"])
        TT("dve", S3, S3, jb(0), ALU.mult, ["S", "vsh"], ["S"])
        TT("dve", T3, jb(2), ib(ysh[:, 0:64]), ALU.mult, ["vsh", "ysh"], ["tmpS"])
        TT("dve", S3, S3, T3, ALU.add, ["S", "tmpS"], ["S"])
        TT("dve", T3, jb(3), ib(vsh[:, 320:384]), ALU.mult, ["vsh"], ["tmpS"])
        TT("dve", S3, S3, T3, ALU.add, ["S", "tmpS"], ["S"])
        b.dma("sp", wkv_s[:, :], S_, reads=["S"])
        TT("dve", T3, S3, jb(4), ALU.mult, ["S", "vsh"], ["tmpS"])
        RED(ysh[:, 64:128], T3, ALU.add, ["tmpS"], ["ysh"])
        b.dma("sp", scr2[:, :], ysh[:, 64:128], reads=["ysh"], writes=["scr2"])
        yS = tk["tb"]
        b.dma("sp", yS, scr2.rearrange("(s h) i -> s (h i)", h=8), reads=["scr2"], writes=["tb"])
        y3 = v8(yS)
        RED(s16[:, 32:40], y3, ALU.add, ["tb"], ["s16"])
        ACT(tk["ta"], yS, AF.Square, ["tb"], ["ta"])
        RED(s16[:, 40:48], v8(tk["ta"]), ALU.add, ["ta"], ["s16"])
        TS("dve", s16[:, 32:48], s16[:, 32:48], 1.0 / 64, None, ALU.mult, None, ["s16"], ["s16"])
        TT("dve", s16[:, 48:56], s16[:, 32:40], s16[:, 32:40], ALU.mult, ["s16"], ["s16"])
        TT("dve", s16[:, 48:56], s16[:, 40:48], s16[:, 48:56], ALU.subtract, ["s16"], ["s16"])
        ACT(s16[:, 56:64], s16[:, 48:56], AF.Sqrt, ["s16", "cst"], ["s16"], bias=cst[0:16, 1:2])
        b.op("dve", lambda g: g.reciprocal(out=s16[:, 56:64], in_=s16[:, 56:64]), ["s16"], ["s16"])
        TT("dve", y3, y3, bc(s16[:, 32:40].unsqueeze(2), [16, 8, 64]), ALU.subtract, ["tb", "s16"], ["tb"])
        TT("dve", y3, y3, bc(s16[:, 56:64].unsqueeze(2), [16, 8, 64]), ALU.mult, ["tb", "s16"], ["tb"])
        TT("dve", yS, yS, lnw_bc[0:16, :], ALU.mult, ["tb", "lnw_bc"], ["tb"])
        TT("dve", yS, yS, lnb_bc[0:16, :], ALU.add, ["tb", "lnb_bc"], ["tb"])
        TT("dve", v8(tk["ta"]), v8(xv_), bc(s16[:, 24:32].unsqueeze(2), [16, 8, 64]), ALU.mult, ["xm", "s16"], ["ta"])
        TT("dve", yS, yS, tk["ta"], ALU.add, ["tb", "ta"], ["tb"])
        TT("dve", cats[:, 512:1024], yS, tk["grs"], ALU.mult, ["tb", "grs"], ["cats"])
        b.barrier()
        NCAND = 16
        off = PW
        Gi, off = carve(off, [128, 8192], BF16)
        Gi2, off = carve(off, [128, 8192], BF16)
        tmpG, off = carve(off, [128, 64, 64])
        Kc, off = carve(off, [128, NCAND, 128])
        Vcd, off = carve(off, [128, NCAND, 128])
        tmpc, off = carve(off, [128, NCAND, 64])
        repd, off = carve(off, [128, 1040])
        opd, off = carve(off, [128, 520])
        repS, off = carve(off, [16, 1024])
        repTS, off = carve(off, [128, 128])
        blkS, off = carve(off, [128, 128])
        sc, off = carve(off, [128, 132])
        msc, off = carve(off, [128, 132])
        sh_, off = carve(off, [128, 128])
        cv, off = carve(off, [128, NCAND])
        ci, off = carve(off, [128, NCAND], I32)
        cif, off = carve(off, [128, NCAND])
        rowi, off = carve(off, [128, NCAND], I32)
        lg, off = carve(off, [128, 8, NCAND])
        b2, off = carve(off, [128, 16])
        ptab, off = carve(off, [128, 8], I32)
        ptf, off = carve(off, [128, 8])
        oh0, off = carve(off, [128, 1])
        assert off <= AW, off
        GiB = (Gi, Gi2)
        for (t_, d_, nm) in ((ptab, ptab_d, "ptab"), (repS, rep_d, "repS"), (repTS, repT_d, "repTS"), (blkS, blk_d, "blkS"), (oh0, oh0_d, "oh0")):
            b.dma("sp", t_, d_[:, :], writes=[nm])
        CP("dve", tokd[:, 0:512], proj[:, 768:1280], ["proj"], ["tokd"])
        TS("dve", tokd[:, 512:520], proj[:, 1280:1288], 0.044194173824159216, None, ALU.mult, None, ["proj"], ["tokd"])
        CP("dve", tokd[:, 520:1032], tk["qs"], ["qs"], ["tokd"])
        TT("dve", v8(tk["ta"]), v8(tokd[:, 0:512]), bc(proj[:, 1288:1352].unsqueeze(1), [16, 8, 64]), ALU.mult, ["tokd", "proj"], ["ta"])
        RED(s16[:, 0:8], v8(tk["ta"]), ALU.add, ["ta"], ["s16"])
        TS("dve", s16[:, 0:8], s16[:, 0:8], 0.0, None, ALU.max, None, ["s16"], ["s16"])
        TT("dve", s16[:, 0:8], s16[:, 0:8], tokd[:, 512:520], ALU.mult, ["s16", "tokd"], ["s16"])
        RED(tokd[:, 1032:1033], s16[:, 0:8], ALU.add, ["s16"], ["tokd"])
        CP("dve", ptf, ptab, ["ptab"], ["ptf"])
        TS("dve", ptf, ptf, 128.0, None, ALU.mult, None, ["ptf"], ["ptf"])
        ck_rows = cache_k
        cv_rows = cache_v
        cvA, off = carve(off, [128, 8, NCAND])
        Kc2, off = carve(off, [128, NCAND, 128])
        Vcd2, off = carve(off, [128, NCAND, 128])
        rowiA, off = carve(off, [128, 8, NCAND], I32)
        cvalA, off = carve(off, [128, 8, NCAND])
        ciA, off = carve(off, [128, 8, NCAND], I32)
        thrA, off = carve(off, [128, 8])
        tmpGf = tmpG.rearrange("p a b -> p (a b)")
        cand16 = tmpGf[0:16, 0:1540]
        candj = tmpGf[0:16, 1540:3080]
        assert off <= AW, off
        def gi_gather(sp):
            gb = GiB[sp % 2]
            b.op("pool", lambda g: g.indirect_dma_start(out=gb, out_offset=None, in_=cache_ki[:, :],
                                                        in_offset=bass.IndirectOffsetOnAxis(ap=ptab[:, sp:sp + 1], axis=0)),
                 ["ptab"], ["Gi%d" % (sp % 2)], dma=True)
        gi_gather(0)
        for sp in range(NPAIR):
            if sp + 1 < NPAIR:
                gi_gather(sp + 1)
            Gi3 = GiB[sp % 2].rearrange("p (t d) -> p t d", t=128)
            gin = "Gi%d" % (sp % 2)
            for (c0, n) in ((0, 512), (512, 8)):
                MM(K2[:, 0:n], repS[:, sp * 128:(sp + 1) * 128], tokd[:, c0:c0 + n], True, True, ["repS", "tokd"], ["K2"])
                CP("act", repd[:, c0:c0 + n], K2[:, 0:n], ["K2"], ["repd"])
            for h in range(8):
                for hf in range(2):
                    TT("dve", tmpG, Gi3[:, hf * 64:(hf + 1) * 64, :], bc(repd[:, h * 64:(h + 1) * 64].unsqueeze(1), [128, 64, 64]), ALU.mult, [gin, "repd"], ["tmpG"])
                    RED(sh_[:, hf * 64:(hf + 1) * 64], tmpG, ALU.add, ["tmpG"], ["sh"])
                if h == 0:
                    TS("dve", sc[:, 0:128], sh_, 0.0, repd[:, 512:513], ALU.max, ALU.mult, ["sh", "repd"], ["sc"])
                else:
                    TS("dve", sh_, sh_, 0.0, repd[:, 512 + h:513 + h], ALU.max, ALU.mult, ["sh", "repd"], ["sh"])
                    TT("dve", sc[:, 0:128], sc[:, 0:128], sh_, ALU.add, ["sc", "sh"], ["sc"])
            for r_ in range(NCAND // 8):
                b.op("dve", lambda g, r_=r_, sp=sp: g.max(out=cvA[:, sp, r_ * 8:(r_ + 1) * 8], in_=sc[:, 0:128]), ["sc"], ["cvA"])
                b.op("dve", lambda g, r_=r_, sp=sp: g.max_index(out=ciA[:, sp, r_ * 8:(r_ + 1) * 8].bitcast(mybir.dt.uint32),
                                                                in_max=cvA[:, sp, r_ * 8:(r_ + 1) * 8], in_values=sc[:, 0:128]), ["sc", "cvA"], ["ciA"])
                if r_ < NCAND // 8 - 1:
                    b.op("dve", lambda g, r_=r_, sp=sp: g.match_replace(out=sc[:, 0:128], in_to_replace=cvA[:, sp, r_ * 8:(r_ + 1) * 8],
                                                                        in_values=sc[:, 0:128], imm_value=-3e30), ["sc", "cvA"], ["sc"])
        b.dma("sp", scr4.rearrange("(sp s2) g c -> (s2 g) sp c", s2=2), cvA, reads=["cvA"], writes=["scr4"])
        b.dma("sp", cand16[:, 0:64 * NCAND], scr4.rearrange("s g c -> s (g c)"), reads=["scr4"], writes=["tmpG"])
        CP("dve", cand16[:, 64 * NCAND:64 * NCAND + 1], tokd[:, 1032:1033], ["tokd"], ["tmpG"])
        cnd = cand16[:, 0:64 * NCAND + 1]
        RED(s16[:, 40:41], cnd, ALU.max, ["tmpG"], ["s16"])
        RED(s16[:, 41:42], cnd, ALU.min, ["tmpG"], ["s16"])
        TS("dve", s16[:, 42:43], s16[:, 41:42], -1.0, None, ALU.add, None, ["s16"], ["s16"])
        STT(s16[:, 43:44], s16[:, 40:41], 2.0, s16[:, 41:42], ALU.add, ALU.subtract, ["s16"], ["s16"])
        for it in range(1, n_bis + 2):
            sc_ = float(2.0 ** (-it))
            STT(s16[:, 44:45], s16[:, 43:44], sc_, s16[:, 42:43], ALU.mult, ALU.add, ["s16"], ["s16"])
            TS("dve", candj[:, 0:64 * NCAND + 1], cnd, s16[:, 44:45], 0.0, ALU.is_gt, ALU.add, ["tmpG", "s16"], ["tmpG", "s16"], accum=s16[:, 45:46])
            TS("dve", s16[:, 46:47], s16[:, 45:46], float(topk_s) - 0.5, s16[:, 43:44], ALU.is_gt, ALU.mult, ["s16"], ["s16"])
            STT(s16[:, 42:43], s16[:, 46:47], sc_, s16[:, 42:43], ALU.mult, ALU.add, ["s16"], ["s16"])
        TT("dve", s16[:, 32:33], tokd[:, 1032:1033], s16[:, 42:43], ALU.is_gt, ["tokd", "s16"], ["s16"])
        for sp in range(NPAIR):
            MM(F2[:, sp:sp + 1], repS[:, sp * 128:(sp + 1) * 128], s16[:, 42:43], True, True, ["repS", "s16"], ["F2"])
        CP("dve", thrA, F2[:, 0:8], ["F2"], ["thrA"])
        for sp in range(NPAIR):
            CP("dve", cif, ciA[:, sp, :], ["ciA"], ["cif"])
            TS("dve", cif, cif, ptf[:, sp:sp + 1], None, ALU.add, None, ["cif", "ptf"], ["cif"])
            CP("dve", rowiA[:, sp, :], cif, ["cif"], ["rowiA"])
            TS("dve", cvalA[:, sp, :], cvA[:, sp, :], thrA[:, sp:sp + 1], None, ALU.is_gt, None, ["cvA", "thrA"], ["cvalA"])
        KcB = (Kc, Kc2)
        VcB = (Vcd, Vcd2)

        def gathers(sp):
            kb, vb = KcB[sp % 2], VcB[sp % 2]
            kn, vn = "Kc%d" % (sp % 2), "Vcd%d" % (sp % 2)
            for c_ in range(NCAND):
                b.op("pool", lambda g, c_=c_: g.indirect_dma_start(out=kb[:, c_, :], out_offset=None, in_=ck_rows[:, :],
                                                                  in_offset=bass.IndirectOffsetOnAxis(ap=rowiA[:, sp, c_:c_ + 1], axis=0)),
                     ["rowiA"], [kn], dma=True)
                b.op("pool", lambda g, c_=c_: g.indirect_dma_start(out=vb[:, c_, :], out_offset=None, in_=cv_rows[:, :],
                                                                  in_offset=bass.IndirectOffsetOnAxis(ap=rowiA[:, sp, c_:c_ + 1], axis=0)),
                     ["rowiA"], [vn], dma=True)
        gathers(0)
        for sp in range(NPAIR):
            if sp + 1 < NPAIR:
                gathers(sp + 1)
            kn, vn = "Kc%d" % (sp % 2), "Vcd%d" % (sp % 2)
            MM(K2[:, 0:512], repS[:, sp * 128:(sp + 1) * 128], tokd[:, 520:1032], True, True, ["repS", "tokd"], ["K2"])
            CP("act", repd[:, 520:1032], K2[:, 0:512], ["K2"], ["repd"])
            Kc4 = KcB[sp % 2].rearrange("p c (g d) -> p c g d", g=2)
            Vc4 = VcB[sp % 2].rearrange("p c (g d) -> p c g d", g=2)
            cvv = cvalA[:, sp, :]
            for h in range(8):
                TT("dve", tmpc, Kc4[:, :, h // 4, :], bc(repd[:, 520 + h * 64:520 + (h + 1) * 64].unsqueeze(1), [128, NCAND, 64]), ALU.mult, [kn, "repd"], ["tmpc"])
                RED(lg[:, h, :], tmpc, ALU.add, ["tmpc"], ["lg"])
            ACT(lg, lg, AF.Exp, ["lg"], ["lg"])
            TT("dve", lg, lg, bc(cvv.unsqueeze(1), [128, 8, NCAND]), ALU.mult, ["lg", "cvalA"], ["lg"])
            RED(opd[:, 512:520], lg, ALU.add, ["lg"], ["opd"])
            for h in range(8):
                TT("dve", tmpc, Vc4[:, :, h // 4, :], bc(lg[:, h, :].unsqueeze(2), [128, NCAND, 64]), ALU.mult, [vn, "lg"], ["tmpc"])
                RED(opd[:, h * 64:(h + 1) * 64], tmpc.rearrange("p c d -> p d c"), ALU.add, ["tmpc"], ["opd"])
            MM(V2[0:16, 0:512], repTS[:, sp * 16:(sp + 1) * 16], opd[:, 0:512], sp == 0, sp == NPAIR - 1, ["repTS", "opd"], ["V2"])
            MM(V2[0:16, 512:520], repTS[:, sp * 16:(sp + 1) * 16], opd[:, 512:520], sp == 0, sp == NPAIR - 1, ["repTS", "opd"], ["V2"])
        qv = v8(tk["qs"])
        for g_ in range(2):
            TT("dve", v8(tk["ta"])[:, g_ * 4:(g_ + 1) * 4, :], qv[:, g_ * 4:(g_ + 1) * 4, :],
               bc(ks_[:, g_ * 64:(g_ + 1) * 64].unsqueeze(1), [16, 4, 64]), ALU.mult, ["qs", "ks"], ["ta"])
        RED(s16[:, 0:8], v8(tk["ta"]), ALU.add, ["ta"], ["s16"])
        ACT(s16[:, 0:8], s16[:, 0:8], AF.Exp, ["s16"], ["s16"])
        TS("dve", s16[:, 0:8], s16[:, 0:8], s16[:, 32:33], None, ALU.mult, None, ["s16"], ["s16"])
        TT("dve", s16[:, 8:16], V2[0:16, 512:520], s16[:, 0:8], ALU.add, ["V2", "s16"], ["s16"])
        b.op("dve", lambda g: g.reciprocal(out=s16[:, 8:16], in_=s16[:, 8:16]), ["s16"], ["s16"])
        for g_ in range(2):
            TT("dve", v8(tk["ta"])[:, g_ * 4:(g_ + 1) * 4, :], bc(proj[:, 640 + g_ * 64:640 + (g_ + 1) * 64].unsqueeze(1), [16, 4, 64]),
               bc(s16[:, g_ * 4:(g_ + 1) * 4].unsqueeze(2), [16, 4, 64]), ALU.mult, ["proj", "s16"], ["ta"])
        TT("dve", tk["ta"], tk["ta"], V2[0:16, 0:512], ALU.add, ["ta", "V2"], ["ta"])
        TT("dve", v8(tk["ta"]), v8(tk["ta"]), bc(s16[:, 8:16].unsqueeze(2), [16, 8, 64]), ALU.mult, ["ta", "s16"], ["ta"])
        TT("dve", cats[:, 0:512], tk["ta"], tk["ga"], ALU.mult, ["ta", "ga"], ["cats"])
        for k in range(8):
            TR(PTb[:, k * 16:(k + 1) * 16], cats[:, k * 128:(k + 1) * 128], identb[0:16, 0:16], ["cats", "identb"], ["PTb"])
        CP("act", catTs, PTb[:, 0:128].rearrange("p (k t) -> p k t", k=8), ["PTb"], ["catTs"])
        b.barrier()
        off = PW
        stg2, off = carve(off, [128, 8, 512])
        wbf2, off = carve(off, [128, 8, 512], BF16)
        b.dma("sp", ysb, gscr[0:16, :], reads=["gscr"], writes=["ysb"])
        w_out_v2 = w_out.rearrange("(k p) c -> p k c", p=128)
        for hh in range(2):
            b.dma("sp", stg2, w_out_v2[:, :, hh * 512:(hh + 1) * 512], writes=["stg2"])
            CP("dve", wbf2, stg2, ["stg2"], ["wbf2"])
            for k in range(8):
                MM(R2[0:16, hh * 512:(hh + 1) * 512], catTs[:, k, :], wbf2[:, k, :], k == 0, k == 7, ["catTs", "wbf2"], ["R2"])
        TT("dve", ysb, ysb, R2[0:16, :], ALU.mult, ["ysb", "R2"], ["ysb"])
        TT("dve", ysb, ysb, x_[0:16, :], ALU.add, ["ysb", xr], ["ysb"])
        b.dma("sp", y_s[:, :], ysb, reads=["ysb"])

    b.barrier()
    b.emit()
    ncd.__exit__(None, None, None)
    es.close()
    return nc


def _consts(T):
    NT = T // 128
    NO = NT // 2
    cst = {}
    cst["identf"] = np.eye(128, dtype=np.float32)
    cst["iota256"] = np.tile(np.arange(256, dtype=np.float32)[None, :], (128, 1))
    s = np.arange(64)[:, None]
    t = np.arange(64)[None, :]
    lt = (s < t).astype(np.float32)
    le = (s <= t).astype(np.float32)
    cst["maskT"] = np.concatenate([lt, le, lt, le], axis=1)
    cst["maskL"] = (np.arange(64)[None, :] < np.arange(64)[:, None]).astype(np.float32)
    r = np.ones((64, 1024), np.float32)
    r[:, ::64] = 0.0
    cst["resetm"] = r
    sel = np.zeros((17, 128), np.float32)
    sel[16, :] = 1.0
    cst["sel16"] = sel
    cst["ones64"] = np.ones((64, 64), np.float32)
    return cst


def _rope_table(pos):
    half = 8
    inv = np.power(np.float32(ROPE_THETA), -np.arange(half, dtype=np.float32) / np.float32(half)).astype(np.float32)
    ang = pos.astype(np.float32)[:, None] * inv[None, :]
    return np.concatenate([np.cos(ang), np.sin(ang)], axis=1).astype(np.float32)


def _core_inputs(inp, c, T, NS, past_len):
    NT = T // 128
    NO = NT // 2
    bi, par = c // 2, c % 2
    xp = np.asarray(inp["x_prompt"][bi], np.float32)
    own_tiles = [2 * j + par for j in range(NO)]
    own_rows = np.concatenate([np.arange(t * 128, (t + 1) * 128) for t in own_tiles])
    xs = np.zeros((128, D), np.float32)
    xs[:NS] = np.asarray(inp["x_sample"][c * NS:(c + 1) * NS, 0], np.float32)
    m = {}
    m["xall"] = np.ascontiguousarray(np.concatenate([xp, xp[own_rows], xs], axis=0))
    m["call"] = np.ascontiguousarray(np.concatenate([inp["c_sample"][c * NS:(c + 1) * NS], inp["c_prompt"][bi:bi + 1]], axis=0).astype(np.float32))
    pos = np.concatenate([np.arange(T), own_rows, np.full(128, past_len)])
    m["cs_all"] = _rope_table(pos)
    m["parsel"] = np.tile(np.array([[par, 1 - par]], np.float32), (128, 1))
    m["qrel"] = (par * 128 + np.arange(128, dtype=np.float32)).reshape(128, 1)
    m["ownidx"] = np.ascontiguousarray(own_rows.reshape(NO, 128).T.astype(np.int32))
    for k_, v_ in (("w_in", "w_in"), ("w_ada", "w_ada"), ("b_ada", "b_ada"), ("norm_w", "norm_w"), ("w_out", "w_out"),
                   ("qnw", "q_norm_w"), ("knw", "k_norm_w"), ("mu", "mu_shift"), ("w0", "w0"), ("a0", "a0"),
                   ("k_k", "k_k"), ("k_a", "k_a"), ("ln_x_w", "ln_x_w"), ("ln_x_b", "ln_x_b"), ("w_up", "w_up"), ("a_up", "a_up")):
        m[k_] = np.ascontiguousarray(np.asarray(inp[v_], np.float32))
    m["r_k"] = np.ascontiguousarray(np.asarray(inp["r_k"], np.float32).reshape(512))
    m["swkv"] = np.ascontiguousarray(np.asarray(inp["state_wkv"][c * NS:(c + 1) * NS], np.float32).reshape(NS * 8, 4096))
    m["sshift"] = np.ascontiguousarray(np.asarray(inp["state_shift"][c * NS:(c + 1) * NS, 0], np.float32))
    pt = np.asarray(inp["page_table"][c * NS:(c + 1) * NS], np.int32)
    m["ptab"] = np.ascontiguousarray(pt.reshape(NS // 2, 128).T)
    nphys = inp["cache_k"].shape[0]
    m["cache_k"] = np.asarray(inp["cache_k"], np.float32).reshape(nphys * 128, 128)
    m["cache_v"] = np.asarray(inp["cache_v"], np.float32).reshape(nphys * 128, 128)
    m["cache_kidx"] = np.asarray(inp["cache_kidx"], np.float32).reshape(nphys, 8192)
    rep = np.zeros((16, 8, 128), np.float32)
    for sp in range(8):
        for p in range(128):
            rep[2 * sp + p // 64, sp, p] = 1.0
    m["rep"] = rep.reshape(16, 1024)
    m["repT"] = np.ascontiguousarray(rep.transpose(2, 1, 0).reshape(128, 128))
    blk = np.zeros((128, 128), np.float32)
    blk[:64, :64] = 1.0
    blk[64:, 64:] = 1.0
    m["blk"] = blk
    oh = np.zeros((128, 1), np.float32)
    oh[0, 0] = 1.0
    oh[64, 0] = 1.0
    m["oh0"] = oh
    m.update(_consts(T))
    return m


_NC_CACHE = {}


def kernel(**inp):
    T = 4096
    NS = 16
    past_len = 8192
    inp = {k: np.asarray(v) for k, v in inp.items()}
    if "nc" not in _NC_CACHE:
        _NC_CACHE["nc"] = build(T=T, NPHYS=int(inp["cache_k"].shape[0]))
    nc = _NC_CACHE["nc"]
    in_maps = [_core_inputs(inp, c, T, NS, past_len) for c in range(8)]
    res = run_bass_kernel_spmd(nc, in_maps, core_ids=list(range(8)))
    outs = res.results
    B = 4
    NO = T // 256
    y_p = np.zeros((B, T, D), np.float32)
    for c in range(8):
        bi, par = c // 2, c % 2
        yo = np.asarray(outs[c]["y_own"]).reshape(NO, 128, D)
        y_p[bi].reshape(T // 256, 2, 128, D)[:, par] = yo
    k_p = np.stack([np.asarray(outs[2 * bi]["k_nat"]).reshape(T, 2, 64) for bi in range(B)])
    v_p = np.stack([np.asarray(outs[2 * bi]["v_nat"]).reshape(T, 2, 64) for bi in range(B)])
    ki_p = np.stack([np.asarray(outs[2 * bi]["ki_nat"]).reshape(T, 64) for bi in range(B)])
    wkv_pp = np.stack([np.asarray(outs[2 * bi]["wkv_p"]).reshape(8, 64, 64) for bi in range(B)])
    sh_p = np.stack([np.asarray(outs[2 * bi]["shift_p"]).reshape(1, SHW) for bi in range(B)])
    y_s = np.concatenate([np.asarray(outs[c]["y_s"]) for c in range(8)]).reshape(128, 1, D)
    k_s = np.concatenate([np.asarray(outs[c]["k_s"]) for c in range(8)]).reshape(128, 1, 2, 64)
    v_s = np.concatenate([np.asarray(outs[c]["v_s"]) for c in range(8)]).reshape(128, 1, 2, 64)
    ki_s = np.concatenate([np.asarray(outs[c]["ki_s"]) for c in range(8)]).reshape(128, 1, 64)
    wkv_s = np.concatenate([np.asarray(outs[c]["wkv_s"]) for c in range(8)]).reshape(128, 8, 64, 64)
    sh_s = np.concatenate([np.asarray(outs[c]["shift_s"]) for c in range(8)]).reshape(128, 1, SHW)
    f = lambda a: np.ascontiguousarray(a, dtype=np.float32)
    return (f(y_p), f(y_s), f(k_p), f(v_p), f(ki_p), f(wkv_pp), f(sh_p), f(k_s), f(v_s), f(ki_s), f(wkv_s), f(sh_s))
```

```python
import os
import numpy as np
from contextlib import ExitStack
import concourse.bass as bass
import concourse.mybir as mybir
from concourse.bass_utils import run_bass_kernel_spmd

F32 = mybir.dt.float32
BF16 = mybir.dt.bfloat16
I32 = mybir.dt.int32
AF = mybir.ActivationFunctionType
ALU = mybir.AluOpType
AX = mybir.AxisListType

ENGS = ("pe", "act", "dve", "pool", "sp")
NDMA = 32
NSW = 8

D = 1024
HD = 64
DIN = 4040
C_Q, C_K, C_V, C_QI, C_WI, C_KI, C_GA = 0, 512, 640, 768, 1280, 1288, 1352
C_R, C_RK, C_RV, C_WD, C_AD, C_GR = 1864, 2376, 2888, 3400, 3464, 3528
SHW = 1664
NORM_EPS = 1e-6
GN_EPS = 64e-5
ROPE_THETA = 500000.0


USE_POOL = bool(int(os.environ.get('USE_POOL', '0')))
PSUM_RES = {"PTb", "F2", "R2", "K2", "V2", "R2a", "R2b", "K2a", "K2b", "V2a", "V2b"}


class Res:
    __slots__ = ("w", "r")

    def __init__(self):
        self.w = None
        self.r = []


class Bld:
    def __init__(self, nc, es):
        self.nc = nc
        self.es = es
        self.sem = {e: es.enter_context(nc.semaphore("s_" + e)) for e in ENGS}
        self.dsem = [es.enter_context(nc.semaphore("d%d" % i)) for i in range(NDMA)]
        self.dval = [0] * NDMA
        self.dnext = 0
        self.dnext_sw = 0
        self.cnt = {e: 0 for e in ENGS}
        self.waited = {e: {} for e in ENGS}
        self.ops = {e: [] for e in ENGS}
        self.res = {}

    def sb(self, name, shape, dt=F32):
        return self.es.enter_context(self.nc.sbuf_tensor("sb_" + name, list(shape), dt))

    def ps(self, name, shape, dt=F32):
        return self.es.enter_context(self.nc.psum_tensor("ps_" + name, list(shape), dt))

    def _r(self, key):
        r = self.res.get(key)
        if r is None:
            r = self.res[key] = Res()
        return r

    def _need(self, e, tok, waits):
        if tok is None:
            return
        key, val = tok
        if key == "pe" and e == "pe":
            return
        if self.waited[e].get(key, 0) >= val:
            return
        self.waited[e][key] = val
        waits.append((key, val))

    def op(self, e, fn, reads=(), writes=(), dma=False):
        if e == "pool" and not dma and not USE_POOL:
            e = "dve"
        pr = [k for k in reads if k in PSUM_RES]
        if pr:
            reads = [k for k in reads if k not in PSUM_RES]
            writes = list(writes) + pr
        waits = []
        for k in reads:
            self._need(e, self._r(k).w, waits)
        for k in writes:
            r = self._r(k)
            self._need(e, r.w, waits)
            for t in r.r:
                self._need(e, t, waits)
        if dma:
            if e == "pool":
                i = NDMA - NSW + self.dnext_sw
                self.dnext_sw = (self.dnext_sw + 1) % NSW
            else:
                i = self.dnext
                self.dnext = (self.dnext + 1) % (NDMA - NSW)
            if self.dval[i] > 0:
                self._need(e, (("d", i), self.dval[i]), waits)
            self.dval[i] += 16
            tok = (("d", i), self.dval[i])
            inc = (self.dsem[i], 16)
        else:
            self.cnt[e] += 1
            tok = (e, self.cnt[e])
            inc = (self.sem[e], 1)
        self.ops[e].append((waits, fn, inc))
        for k in reads:
            self._r(k).r.append(tok)
        for k in writes:
            r = self._r(k)
            r.w = tok
            r.r = []
        return tok

    def dma(self, e, out, in_, reads=(), writes=()):
        return self.op(e, lambda g: g.dma_start(out=out, in_=in_), reads, writes, dma=True)

    def barrier(self, dmas=True):
        for e in ENGS:
            waits = []
            for e2 in ENGS:
                if e2 != e and self.cnt[e2] > 0:
                    self._need(e, (e2, self.cnt[e2]), waits)
            if dmas:
                for i in range(NDMA):
                    if self.dval[i] > 0:
                        self._need(e, (("d", i), self.dval[i]), waits)
            self.ops[e].append((waits, None, None))

    def emit(self):
        nc = self.nc
        with nc.Block() as block:
            def mk(e):
                def body(g):
                    for waits, fn, inc in self.ops[e]:
                        for key, val in waits:
                            s = self.dsem[key[1]] if isinstance(key, tuple) else self.sem[key]
                            g.wait_ge(s, val)
                        if fn is not None:
                            fn(g).then_inc(inc[0], inc[1])
                return body
            block.tensor(mk("pe"))
            block.scalar(mk("act"))
            block.vector(mk("dve"))
            block.gpsimd(mk("pool"))
            block.sync(mk("sp"))


def build(T=4096, NS=16, NPG=64, NPHYS=10240, topk_p=256, topk_s=256, n_bis=15, do_sample=True, AW=36000, stop_after=None, nt_lim=None, no_lim=None):
    NT = T // 128
    NO = NT // 2
    NTILES = NT + NO + 1
    nc = bass.Bass("TRN2", target_bir_lowering=False)
    es = ExitStack()
    b = Bld(nc, es)

    def din(name, shape, dt=F32):
        return nc.dram_tensor(name, list(shape), dt, kind="ExternalInput").ap()

    def dout(name, shape, dt=F32):
        return nc.dram_tensor(name, list(shape), dt, kind="ExternalOutput").ap()

    xall = din("xall", [NTILES * 128, D])
    call = din("call", [17, D])
    w_in = din("w_in", [D, DIN])
    w_ada = din("w_ada", [D, 3 * D])
    b_ada = din("b_ada", [3 * D])
    norm_w = din("norm_w", [D])
    w_out = din("w_out", [D, D])
    qnw = din("qnw", [HD])
    knw = din("knw", [HD])
    mu = din("mu", [SHW])
    pw0 = din("w0", [512]); pa0 = din("a0", [512]); pkk = din("k_k", [512]); pka = din("k_a", [512])
    prk = din("r_k", [512]); plnw = din("ln_x_w", [512]); plnb = din("ln_x_b", [512])
    w_up = din("w_up", [64, 512]); a_up = din("a_up", [64, 512])
    identf_d = din("identf", [128, 128])
    cs_all = din("cs_all", [NTILES * 128, 16])
    parsel_d = din("parsel", [128, 2])
    qrel_d = din("qrel", [128, 1])
    ownidx_d = din("ownidx", [128, NO], I32)
    iota_d = din("iota256", [128, 256])
    maskT_d = din("maskT", [64, 256])
    maskL_d = din("maskL", [64, 64])
    reset_d = din("resetm", [64, 1024])
    sel16_d = din("sel16", [17, 128])
    ones64_d = din("ones64", [64, 64])

    swkv_d = din("swkv", [128, 4096]); sshift_d = din("sshift", [16, SHW]); ptab_d = din("ptab", [128, 8], I32)
    if do_sample:
        cache_k = din("cache_k", [NPHYS * 128, 128]); cache_v = din("cache_v", [NPHYS * 128, 128])
        cache_ki = din("cache_kidx", [NPHYS, 8192])
    rep_d = din("rep", [16, 8 * 128]); repT_d = din("repT", [128, 8 * 16]); blk_d = din("blk", [128, 128]); oh0_d = din("oh0", [128, 1])
    y_s = dout("y_s", [16, D]); k_s = dout("k_s", [16, 128]); v_s = dout("v_s", [16, 128]); ki_s = dout("ki_s", [16, 64])
    wkv_s = dout("wkv_s", [128, 4096]); shift_s = dout("shift_s", [16, SHW])
    gscr = nc.dram_tensor("gscr", [17, D], F32, kind="Internal").ap()
    scr1 = nc.dram_tensor("scr1", [16, 3072], F32, kind="Internal").ap()
    scr2 = nc.dram_tensor("scr2", [128, 64], F32, kind="Internal").ap()
    scr3 = nc.dram_tensor("scr3", [16, 512], F32, kind="Internal").ap()
    scr4 = nc.dram_tensor("scr4", [16, 64, 16], F32, kind="Internal").ap()
    y_own = dout("y_own", [NO * 128, D])
    k_nat = dout("k_nat", [T, 128]); v_nat = dout("v_nat", [T, 128]); ki_nat = dout("ki_nat", [T, 64])
    wkv_p = dout("wkv_p", [8, 64, 64]); shift_p = dout("shift_p", [SHW])
    rwscr = nc.dram_tensor("rwscr", [T, 512], F32, kind="Internal").ap()

    PTb = b.ps("PTb", [128, 1024], BF16)
    F2 = b.ps("F2", [128, 512])
    R2 = b.ps("R2", [128, 1024])
    K2 = b.ps("K2", [128, 1024])
    V2 = b.ps("V2", [128, 1024])

    identf = b.sb("identf", [128, 128]); identb = b.sb("identb", [128, 128], BF16)
    cst = b.sb("cst", [128, 4])
    kT_all = b.sb("kT_all", [64, 2, T], BF16)
    kiT_all = b.sb("kiT_all", [64, T], BF16)
    Vaug = b.sb("Vaug", [128, NT, 2, 65], BF16)
    modT = b.sb("modT", [128, 24, 17])
    g1 = b.sb("g1", [128, 8, 17])
    nwT = b.sb("nwT", [128, 8]); badaT = b.sb("badaT", [128, 24])
    lnw_bc = b.sb("lnw_bc", [64, 512]); lnb_bc = b.sb("lnb_bc", [64, 512])
    qnw_bc = b.sb("qnw_bc", [128, 64]); knw_bc = b.sb("knw_bc", [128, 64])
    sel16 = b.sb("sel16", [17, 128]); ones64 = b.sb("ones64", [64, 64])
    maskT = b.sb("maskT", [64, 256]); maskL = b.sb("maskL", [64, 64]); resetm = b.sb("resetm", [64, 1024])
    qrel = b.sb("qrel", [128, 1]); parsel = b.sb("parsel", [128, 2]); ownidx = b.sb("ownidx", [128, NO], I32)
    fp = {}
    for nm in ("w0", "a0", "kk", "ka", "rk"):
        fp[nm] = b.sb("fp_" + nm, [64, 8])
    muT = b.sb("muT", [64, 26]); wupS = b.sb("wupS", [64, 512]); aupS = b.sb("aupS", [64, 512])
    xt0 = b.sb("xt0", [128, D]); xt = [xt0, xt0]
    xn = b.sb("xn", [128, D], BF16)
    hT0 = b.sb("hT0", [128, 8, 128], BF16); hT = [hT0, hT0]
    hTs = b.sb("hTs", [128, 8, 128], BF16)
    hlast = b.sb("hlast", [128, 8, 1], BF16)
    cs_t = b.sb("cs_t", [128, 16])
    sm = b.sb("sm", [128, 64])
    ARENA = b.sb("ARENA", [128, AW])
    csT = b.sb("csT", [128, 8, 17])

    def TT(e, out, in0, in1, op, R, W):
        b.op(e, lambda g: g.tensor_tensor(out=out, in0=in0, in1=in1, op=op), R, W)

    def TS(e, out, in0, s1, s2, op0, op1, R, W, accum=None):
        if op1 is None:
            b.op(e, lambda g: g.tensor_scalar(out=out, in0=in0, scalar1=s1, scalar2=None, op0=op0), R, W)
        elif accum is None:
            b.op(e, lambda g: g.tensor_scalar(out=out, in0=in0, scalar1=s1, scalar2=s2, op0=op0, op1=op1), R, W)
        else:
            b.op(e, lambda g: g.tensor_scalar(out=out, in0=in0, scalar1=s1, scalar2=s2, op0=op0, op1=op1,
                                              accum_out=accum), R, W)

    def STT(out, in0, scalar, in1, op0, op1, R, W):
        b.op("dve", lambda g: g.scalar_tensor_tensor(out=out, in0=in0, scalar=scalar, in1=in1, op0=op0, op1=op1), R, W)

    def ACT(out, in_, func, R, W, scale=1.0, bias=None, accum=None):
        kw = {}
        if bias is not None:
            kw["bias"] = bias
        if accum is not None:
            kw["accum_out"] = accum
        b.op("act", lambda g: g.activation(out=out, in_=in_, func=func, scale=scale, **kw), R, W)

    def MM(out, lhsT, rhs, start, stop, R, W):
        b.op("pe", lambda g: g.matmul(out=out, lhsT=lhsT, rhs=rhs, start=start, stop=stop), R, W)

    def TR(out, in_, ident, R, W):
        b.op("pe", lambda g: g.transpose(out=out, in_=in_, identity=ident), R, W)

    def CP(e, out, in_, R, W):
        if e == "act":
            b.op(e, lambda g: g.copy(out=out, in_=in_), R, W)
        else:
            b.op(e, lambda g: g.tensor_copy(out=out, in_=in_), R, W)

    def RED(out, in_, op, R, W, axis=AX.X):
        b.op("dve", lambda g: g.tensor_reduce(out=out, in_=in_, axis=axis, op=op), R, W)

    def MS(e, ap, val, W):
        b.op(e, lambda g: g.memset(ap, val), (), W)

    def bc(ap, shape):
        return ap.to_broadcast(list(shape))

    ncd = nc.allow_non_contiguous_dma(reason="small parameter layouts")
    ncd.__enter__()

    b.dma("sp", identf[:], identf_d[:, :], writes=["identf"])
    CP("dve", identb[:], identf[:], ["identf"], ["identb"])
    MS("dve", cst[:, 0:1], NORM_EPS, ["cst"]); MS("dve", cst[:, 1:2], GN_EPS, ["cst"]); MS("dve", cst[:, 2:3], 1e-24, ["cst"]); MS("dve", cst[:, 3:4], -30000.0, ["cst"])
    for (t_, d_, nm) in ((sel16, sel16_d, "sel16"), (ones64, ones64_d, "ones64"), (maskT, maskT_d, "maskT"),
                         (maskL, maskL_d, "maskL"), (resetm, reset_d, "resetm"),
                         (qrel, qrel_d, "qrel"), (parsel, parsel_d, "parsel"), (ownidx, ownidx_d, "ownidx"), (wupS, w_up, "wupS"), (aupS, a_up, "aupS")):
        b.dma("sp", t_[:], d_[:, :], writes=[nm])
    for nm, src in (("w0", pw0), ("a0", pa0), ("kk", pkk), ("ka", pka), ("rk", prk)):
        b.dma("sp", fp[nm][:], src.rearrange("(h j) -> j h", j=64), writes=["fp_" + nm])
    b.dma("sp", muT[:], mu.rearrange("(c j) -> j c", j=64), writes=["muT"])
    b.dma("sp", nwT[:], norm_w.rearrange("(k p) -> p k", p=128), writes=["nwT"])
    b.dma("sp", badaT[:], b_ada.rearrange("(t p) -> p t", p=128), writes=["badaT"])
    b.dma("sp", lnw_bc[:], plnw.partition_broadcast(64), writes=["lnw_bc"])
    b.dma("sp", lnb_bc[:], plnb.partition_broadcast(64), writes=["lnb_bc"])
    b.dma("sp", qnw_bc[:], qnw.partition_broadcast(128), writes=["qnw_bc"])
    b.dma("sp", knw_bc[:], knw.partition_broadcast(128), writes=["knw_bc"])
    MS("pool", Vaug[:, :, :, 64:65], 1.0, ["Vaug"])

    def carve(off, shape, dt=F32):
        n = int(np.prod(shape[1:]))
        words = n if dt in (F32, I32) else (n + 1) // 2
        v = ARENA[0:shape[0], off:off + words]
        if dt != F32:
            v = v.bitcast(dt)
        if len(shape) == 3:
            v = v.rearrange("p (a b) -> p a b", a=shape[1])
        elif len(shape) == 4:
            v = v.rearrange("p (a b c) -> p a b c", a=shape[1], b=shape[2])
        return v, off + words

    off = 0
    Wn, off = carve(off, [128, 8, 832], BF16)
    Wm, off = carve(off, [128, 8, SHW], BF16)
    Wom, off = carve(off, [128, 8, SHW], BF16)
    W_end = off
    stg, off = carve(off, [128, 8, 512])
    mu_bc, off = carve(off, [128, SHW])
    omu_bc, off = carve(off, [128, SHW])
    gtok, off = carve(off, [17, D])
    bgate, off = carve(off, [17, D])
    csall_sil, off = carve(off, [17, D])

    b.dma("sp", mu_bc, mu.partition_broadcast(128), writes=["mu_bc"])
    b.dma("sp", bgate, b_ada[2 * D:3 * D].partition_broadcast(17), writes=["bgate"])
    TS("dve", omu_bc, mu_bc, -1.0, 1.0, ALU.mult, ALU.add, ["mu_bc"], ["omu_bc"])
    w_in_v = w_in.rearrange("(k p) c -> p k c", p=128)

    def load_cols(dst, dcol, c0, n, scale_bc=None, scale_off=0, tag=""):
        done = 0
        while done < n:
            w = min(512, n - done)
            b.dma("sp", stg[:, :, 0:w], w_in_v[:, :, c0 + done:c0 + done + w], writes=["stg"])
            if scale_bc is None:
                CP("pool", dst[:, :, dcol + done:dcol + done + w], stg[:, :, 0:w], ["stg"], [tag])
            else:
                for sname, sbcv, d2 in scale_bc:
                    TT("dve", d2[:, :, dcol + done:dcol + done + w], stg[:, :, 0:w],
                       bc(sbcv[:, scale_off + done:scale_off + done + w].unsqueeze(1), [128, 8, w]),
                       ALU.mult, ["stg", sname], [tag])
            done += w

    load_cols(Wn, 0, C_K, 256, tag="Wn")
    load_cols(Wn, 256, C_KI, 64, tag="Wn")
    load_cols(Wn, 320, C_GR, 512, tag="Wn")
    load_cols(None, 0, C_R, SHW, scale_bc=[("mu_bc", mu_bc, Wm), ("omu_bc", omu_bc, Wom)], tag="Wm")
    calt = sm
    b.dma("sp", csall_sil, call[:, :], writes=["csil"])
    ACT(csall_sil, csall_sil, AF.Silu, ["csil"], ["csil"])
    for k in range(8):
        TR(F2[:, k * 17:(k + 1) * 17], csall_sil[:, k * 128:(k + 1) * 128], identf[0:17, 0:17], ["csil", "identf"], ["F2"])
    CP("dve", csT[:], F2[:, 0:136].rearrange("p (k m) -> p k m", k=8), ["F2"], ["csT"])
    w_ada_v = w_ada.rearrange("(k p) c -> p k c", p=128)
    for ch in range(6):
        b.dma("sp", stg[:, :, :], w_ada_v[:, :, ch * 512:(ch + 1) * 512], writes=["stg"])
        for ct in range(4):
            for k in range(8):
                MM(R2[:, ct * 17:(ct + 1) * 17], stg[:, k, ct * 128:(ct + 1) * 128], csT[:, k, :], k == 0, k == 7,
                   ["stg", "csT"], ["R2"])
        TT("dve", modT[:, ch * 4:(ch + 1) * 4, :], R2[:, 0:68].rearrange("p (c m) -> p c m", c=4),
           bc(badaT[:, ch * 4:(ch + 1) * 4].unsqueeze(2), [128, 4, 17]), ALU.add, ["R2", "badaT"], ["modT"])
        if ch >= 4:
            for k in range(8):
                MM(K2[0:17, 0:512], csT[:, k, :], stg[:, k, :], k == 0, k == 7, ["stg", "csT"], ["K2"])
            TT("dve", gtok[:, (ch - 4) * 512:(ch - 3) * 512], K2[0:17, 0:512], bgate[:, (ch - 4) * 512:(ch - 3) * 512],
               ALU.add, ["K2", "bgate"], ["gtok"])
    b.dma("sp", gscr[:, :], gtok[0:17, :], reads=["gtok"], writes=["gscr"])
    STT(g1[:], modT[:, 8:16, :], 1.0, bc(nwT[:].unsqueeze(2), [128, 8, 17]), ALU.add, ALU.mult, ["modT", "nwT"], ["g1"])
    def front(ti, par, m_prompt=True, ntok=128):
        x_ = xt[par]
        h_ = hT[par]
        xr, hr = "xt0", "hT0"
        b.dma("sp", x_[:], xall[ti * 128:(ti + 1) * 128, :], writes=[xr])
        b.dma("sp", cs_t[:], cs_all[ti * 128:(ti + 1) * 128, :], writes=["cs_t"])
        ACT(xn[:], x_[:], AF.Square, [xr], ["xn", "sm"], accum=sm[:, 0:1])
        ACT(sm[:, 1:2], sm[:, 0:1], AF.Sqrt, ["sm", "cst"], ["sm"], scale=1.0 / D, bias=cst[:, 0:1])
        b.op("dve", lambda g: g.reciprocal(out=sm[:, 2:3], in_=sm[:, 1:2]), ["sm"], ["sm"])
        TS("dve", xn[:], x_[:], sm[:, 2:3], None, ALU.mult, None, [xr, "sm"], ["xn"])
        for k in range(8):
            TR(PTb[:, k * 128:(k + 1) * 128], xn[:, k * 128:(k + 1) * 128], identb[:], ["xn", "identb"], ["PTb"])
        pv = PTb[:, :].rearrange("p (k t) -> p k t", k=8)
        if m_prompt:
            TT("dve", h_[:], pv, bc(g1[:, :, 16:17], [128, 8, 128]), ALU.mult, ["PTb", "g1"], [hr])
            TT("pool", h_[:], h_[:], bc(modT[:, 0:8, 16:17], [128, 8, 128]), ALU.add, [hr, "modT"], [hr])
        else:
            TT("dve", h_[:, :, 0:ntok], pv[:, :, 0:ntok], g1[:, :, 0:ntok], ALU.mult, ["PTb", "g1"], [hr])
            TT("pool", h_[:, :, 0:ntok], h_[:, :, 0:ntok], modT[:, 0:8, 0:ntok], ALU.add, [hr, "modT"], [hr])
        return x_, h_, xr, hr

    def rope(e, buf, nh, hd_stride_view, R, nrows=128):
        x1 = buf[:, :, 0:8]
        x2 = buf[:, :, 8:16]
        cosb = bc(cs_t[0:nrows, 0:8].unsqueeze(1), [nrows, nh, 8])
        sinb = bc(cs_t[0:nrows, 8:16].unsqueeze(1), [nrows, nh, 8])
        t = ropet[0:nrows, 0:4 * nh * 8].rearrange("p (a h d) -> p a h d", a=4, h=nh)
        TT(e, t[:, 0], x1, cosb, ALU.mult, R + ["cs_t"], ["ropet"])
        TT(e, t[:, 1], x2, sinb, ALU.mult, R + ["cs_t"], ["ropet"])
        TT(e, t[:, 2], x2, cosb, ALU.mult, R + ["cs_t"], ["ropet"])
        TT(e, t[:, 3], x1, sinb, ALU.mult, R + ["cs_t"], ["ropet"])
        TT(e, x1, t[:, 0], t[:, 1], ALU.subtract, ["ropet"], R)
        TT(e, x2, t[:, 2], t[:, 3], ALU.add, ["ropet"], R)

    ropet = b.sb("ropet", [128, 256])

    def qknorm(src_ps, dst, nh, wbc, extra_scale, Rsrc, Wdst, nrows=128):
        sq = nrm_t[0:nrows, 0:nh * 64].rearrange("p (h d) -> p h d", h=nh)
        ACT(sq, src_ps, AF.Square, Rsrc, ["nrm_t"])
        RED(sm[0:nrows, 8:8 + nh], sq, ALU.add, ["nrm_t"], ["sm"])
        ACT(sm[0:nrows, 16:16 + nh], sm[0:nrows, 8:8 + nh], AF.Sqrt, ["sm", "cst"], ["sm"], scale=1.0 / 64, bias=cst[0:nrows, 0:1])
        b.op("dve", lambda g: g.reciprocal(out=sm[0:nrows, 24:24 + nh], in_=sm[0:nrows, 16:16 + nh]), ["sm"], ["sm"])
        TT("dve", dst, src_ps, bc(sm[0:nrows, 24:24 + nh].unsqueeze(2), [nrows, nh, 64]), ALU.mult, Rsrc + ["sm"], Wdst)
        STT(dst, dst, float(extra_scale), bc(wbc[0:nrows, :].unsqueeze(1), [nrows, nh, 64]), ALU.mult, ALU.mult, Wdst + ["qnw_bc", "knw_bc"], Wdst)

    nrm_t = b.sb("nrm_t", [128, 512])
    kfin = b.sb("kfin", [128, 128]); vfin = b.sb("vfin", [128, 128]); kifin = b.sb("kifin", [128, 64])
    gr_s = b.sb("gr_s", [128, 512])

    off = W_end
    rw = {}
    for nm in ("tw", "adc"):
        rw[nm], off = carve(off, [64, 128])
    for nm in ("sg", "L", "g", "t1"):
        rw[nm], off = carve(off, [64, 8, 128])
    blkA, off = carve(off, [64, 2048])
    blkB, off = carve(off, [64, 3072])
    rw["gprev"] = blkA[:, 0:1024].rearrange("p (h t) -> p h t", h=8)
    rw["ginv"] = blkA[:, 1024:2048].rearrange("p (h t) -> p h t", h=8)
    rw["asig"] = blkB[:, 0:1024].rearrange("p (h t) -> p h t", h=8)
    rw["kkn"] = blkB[:, 1024:2048].rearrange("p (h t) -> p h t", h=8)
    rw["kmod"] = blkB[:, 2048:3072].rearrange("p (h t) -> p h t", h=8)
    AMx = blkA.bitcast(BF16).rearrange("p (h x) -> p h x", h=16)
    LNPx = blkB.bitcast(BF16).rearrange("p (a h x) -> p a h x", a=6, h=16)
    rw["sg"] = rw["sg"]
    QTt, off = carve(off, [64, 8, 2, 128], BF16)
    KTt, off = carve(off, [64, 8, 2, 128], BF16)
    def alias(view64, shape):
        return view64.rearrange("p h t -> p (h t)").bitcast(BF16)
    AM = AMx
    Lm = [LNPx[:, 0], LNPx[:, 1]]
    Nm = [LNPx[:, 2], LNPx[:, 3]]
    Pm = [LNPx[:, 4], LNPx[:, 5]]
    BKtok, off = carve(off, [64, 8, 2, 64], BF16)
    Vc, off = carve(off, [64, 2, 8, 64], BF16)
    P0s, off = carve(off, [64, 8, 64], BF16)
    Us, off = carve(off, [64, 8, 64], BF16)
    H32, off = carve(off, [64, 8, 64])
    Hb, off = carve(off, [64, 8, 64], BF16)
    ych, off = carve(off, [64, 8, 64])
    yt1, off = carve(off, [64, 8, 64])
    bon, off = carve(off, [128, 8])
    rawl, off = carve(off, [64, 26])
    xmt, off = carve(off, [128, SHW])
    st8, off = carve(off, [64, 64])
    assert off <= AW, off
    identb64 = identb[0:64, 0:64]

    KR = int(os.environ.get('KR', '9'))
    KQ = int(os.environ.get('KQ', '9'))

    def rwkv_tile(ti, h_, hr):
        CP("pool", hTs[:, :, 1:128], h_[:, :, 0:127], [hr], ["hTs"])
        CP("pool", hTs[:, :, 0:1], hlast[:], ["hlast"], ["hTs"])
        CP("pool", hlast[:], h_[:, :, 127:128], [hr], ["hlast"])
        if KQ < 1:
            return
        for gi, (c0, n) in enumerate(((0, 512), (512, 512), (1024, 512), (1536, 128))):
            dst = (R2[:, 0:512], R2[:, 512:1024], K2[:, 0:512], K2[:, 512:640])[gi]
            nm = ("R2", "R2", "K2", "K2")[gi]
            for k in range(8):
                MM(dst, h_[:, k, :], Wom[:, k, c0:c0 + n], k == 0, False, ["Wm", hr], [nm])
            for k in range(8):
                MM(dst, hTs[:, k, :], Wm[:, k, c0:c0 + n], False, k == 7, ["Wm", "hTs"], [nm])
        CP("act", xmt[:, 0:1024], R2[:, :], ["R2"], ["xmt"])
        CP("dve", xmt[:, 1024:1664], K2[:, 0:640], ["K2"], ["xmt"])
        TR(F2[0:64, 0:128], xmt[:, 1536:1600], identf[:], ["xmt", "identf"], ["F2"])
        TR(F2[0:64, 128:256], xmt[:, 1600:1664], identf[:], ["xmt", "identf"], ["F2"])
        KW = int(os.environ.get('KW', '3'))
        if KW & 1:
            ACT(rw["tw"], F2[0:64, 0:128], AF.Tanh, ["F2"], ["tw"])
        if KW & 2:
            CP("dve", rw["adc"], F2[0:64, 128:256], ["F2"], ["adc"])
        if KR < 1:
            return
        R2v = R2[0:64, :].rearrange("p (h t) -> p h t", h=8)
        K2v = K2[0:64, :].rearrange("p (h t) -> p h t", h=8)
        V2v = V2[0:64, :].rearrange("p (h t) -> p h t", h=8)
        for h in range(8):
            MM(R2v[:, h, :], wupS[:, h * 64:(h + 1) * 64], rw["tw"], True, True, ["wupS", "tw"], ["R2"])
            MM(K2v[:, h, :], aupS[:, h * 64:(h + 1) * 64], rw["adc"], True, True, ["aupS", "adc"], ["K2"])
        TT("dve", rw["sg"], R2v, bc(fp["w0"][:].unsqueeze(2), [64, 8, 128]), ALU.add, ["R2", "fp_w0"], ["sg"])
        ACT(rw["sg"], rw["sg"], AF.Sigmoid, ["sg"], ["sg"])
        TT("dve", rw["asig"], K2v, bc(fp["a0"][:].unsqueeze(2), [64, 8, 128]), ALU.add, ["K2", "fp_a0"], ["asig"])
        ACT(rw["asig"], rw["asig"], AF.Sigmoid, ["asig"], ["asig"])
        TS("dve", rw["sg"], rw["sg"], -0.6065306597126334, None, ALU.mult, None, ["sg"], ["sg"])
        b.op("dve", lambda g: g.tensor_tensor_scan(out=rw["L"].rearrange("p h t -> p (h t)"), data0=resetm[:, :],
                                                   data1=rw["sg"].rearrange("p h t -> p (h t)"), initial=0.0,
                                                   op0=ALU.mult, op1=ALU.add), ["sg", "resetm"], ["L"])
        ACT(rw["g"], rw["L"], AF.Exp, ["L"], ["g"])
        ACT(rw["ginv"], rw["L"], AF.Exp, ["L"], ["ginv"], scale=-1.0)
        TT("pool", rw["gprev"], rw["L"], rw["sg"], ALU.subtract, ["L", "sg"], ["gprev"])
        ACT(rw["gprev"], rw["gprev"], AF.Exp, ["gprev"], ["gprev"])
        if KR < 2:
            return
        for h in range(8):
            TR(R2v[:, h, :], xmt[:, h * 64:(h + 1) * 64], identf[:], ["xmt", "identf"], ["R2"])
            TR(K2v[:, h, :], xmt[:, 512 + h * 64:512 + (h + 1) * 64], identf[:], ["xmt", "identf"], ["K2"])
        TT("dve", rw["L"], K2v, bc(fp["kk"][:].unsqueeze(2), [64, 8, 128]), ALU.mult, ["K2", "fp_kk"], ["L"])
        ACT(rw["t1"], rw["L"], AF.Square, ["L"], ["t1"])
        t1f = rw["t1"].rearrange("p h t -> p (h t)")
        for hh in range(2):
            MM(V2[0:64, hh * 512:(hh + 1) * 512], ones64[:, :], t1f[:, hh * 512:(hh + 1) * 512], True, True, ["ones64", "t1"], ["V2"])
        ACT(rw["t1"], V2v, AF.Sqrt, ["V2", "cst"], ["t1"], bias=cst[0:64, 2:3])
        b.op("dve", lambda g: g.reciprocal(out=rw["t1"], in_=rw["t1"]), ["t1"], ["t1"])
        TT("dve", rw["kkn"], rw["L"], rw["t1"], ALU.mult, ["L", "t1"], ["kkn"])
        STT(rw["t1"], rw["asig"], -1.0, bc(fp["ka"][:].unsqueeze(2), [64, 8, 128]), ALU.add, ALU.mult, ["asig", "fp_ka"], ["t1"])
        STT(rw["kmod"], rw["t1"], 1.0, K2v, ALU.add, ALU.mult, ["t1", "K2"], ["kmod"])
        if KR < 3:
            return
        QTv = QTt.rearrange("p h c (q t) -> p h c q t", q=2)
        KTv = KTt.rearrange("p h c (q t) -> p h c q t", q=2)

        def ch(v):
            return v.rearrange("p h (c t) -> p h c t", c=2)
        STT(QTv[:, :, :, 0, :], ch(rw["kkn"]), -1.0, ch(rw["gprev"]), ALU.mult, ALU.mult, ["kkn", "gprev"], ["QTt"])
        TT("dve", QTv[:, :, :, 1, :], ch(R2v), ch(rw["g"]), ALU.mult, ["R2", "g"], ["QTt"])
        TT("pool", rw["t1"], rw["kkn"], rw["asig"], ALU.mult, ["kkn", "asig"], ["t1"])
        TT("pool", KTv[:, :, :, 0, :], ch(rw["t1"]), ch(rw["ginv"]), ALU.mult, ["t1", "ginv"], ["KTt"])
        TT("pool", KTv[:, :, :, 1, :], ch(rw["kmod"]), ch(rw["ginv"]), ALU.mult, ["kmod", "ginv"], ["KTt"])
        TT("dve", rw["L"], R2v, bc(fp["rk"][:].unsqueeze(2), [64, 8, 128]), ALU.mult, ["R2", "fp_rk"], ["L"])
        TT("dve", rw["L"], rw["L"], rw["kmod"], ALU.mult, ["L", "kmod"], ["L"])
        for h in range(8):
            MM(F2[:, 256 + h:257 + h], rw["L"][:, h, :], ones64[:, 0:1], True, True, ["L", "ones64"], ["F2"])
        CP("dve", bon, F2[:, 256:264], ["F2"], ["bon"])
        if KR < 4:
            return
        CP("act", Vc[:, 0], xmt[0:64, 1024:1536].rearrange("p (h i) -> p h i", h=8), ["xmt"], ["Vc"])
        MM(V2[0:64, 0:512], identf[:, 64:128], xmt[:, 1024:1536], True, True, ["identf", "xmt"], ["V2"])
        CP("act", Vc[:, 1], V2[0:64, 0:512].rearrange("p (h i) -> p h i", h=8), ["V2"], ["Vc"])
        if int(os.environ.get("KLVL", "9")) < 3:
            return
        for q in range(4):
            c, hg = q // 2, q % 2
            bk, bkn = (K2, "K2") if q % 2 == 0 else (V2, "V2")
            AMp = bk[0:64, :].rearrange("p (h x) -> p h x", h=4)
            for hd in range(4):
                h = hg * 4 + hd
                MM(AMp[:, hd, 0:128], KTt[:, h, c, 0:64], QTt[:, h, c, :], True, True, ["KTt", "QTt"], [bkn])
                MM(AMp[:, hd, 128:256], KTt[:, h, c, 64:128], QTt[:, h, c, :], True, True, ["KTt", "QTt"], [bkn])
            TT("dve", AM[:, q * 4:(q + 1) * 4, :], AMp, bc(maskT[:].unsqueeze(1), [64, 4, 256]), ALU.mult, [bkn, "maskT"], ["AM"])
        Lp = R2[0:64, :].rearrange("p (h x) -> p h x", h=16)
        for q in range(4):
            c, hg = q // 2, q % 2
            for hd in range(4):
                h = hg * 4 + hd
                MM(Lp[:, q * 4 + hd, :], QTt[:, h, c, 0:64], KTt[:, h, c, 0:64], True, True, ["KTt", "QTt"], ["R2"])
        TT("dve", Lm[0], Lp, bc(maskL[:].unsqueeze(1), [64, 16, 64]), ALU.mult, ["R2", "maskL"], ["Lm0"])
        CP("act", Nm[0], AM[:, :, 0:64], ["AM"], ["Nm0"])
        TT("dve", Pm[0], AM[:, :, 0:64], bc(identb64.unsqueeze(1), [64, 16, 64]), ALU.add, ["AM", "identb"], ["Pm0"])
        cur = 0
        Np = K2[0:64, :].rearrange("p (h x) -> p h x", h=16)
        Lpp = V2[0:64, :].rearrange("p (h x) -> p h x", h=16)
        PPp = R2[0:64, :].rearrange("p (h x) -> p h x", h=16)
        for lvl in range(1, 6):
            nx = 1 - cur
            for i in range(16):
                if lvl < 5:
                    MM(Np[:, i, :], Lm[cur][:, i, :], Nm[cur][:, i, :], True, True, ["Lm%d" % cur, "Nm%d" % cur], ["K2"])
                MM(Lpp[:, i, :], Nm[cur][:, i, :], Lm[cur][:, i, :], True, True, ["Lm%d" % cur, "Nm%d" % cur], ["V2"])
            if lvl < 5:
                CP("act", Nm[nx], Np, ["K2"], ["Nm%d" % nx])
            CP("dve", Lm[nx], Lpp, ["V2"], ["Lm%d" % nx])
            for i in range(16):
                MM(PPp[:, i, :], Lm[nx][:, i, :], Pm[cur][:, i, :], True, True, ["Lm%d" % nx, "Pm%d" % cur], ["R2"])
            TT("dve", Pm[nx], PPp, Pm[cur], ALU.add, ["R2", "Pm%d" % cur], ["Pm%d" % nx])
            cur = nx
        P6 = Pm[cur]
        P6n = "Pm%d" % cur
        for c in range(2):
            BKp = PTb[0:64, :].rearrange("p (h q j) -> p h q j", h=8, q=2)
            for h in range(8):
                TR(BKp[:, h, 0, :], KTt[:, h, c, 0:64], identb64, ["KTt", "identb"], ["PTb"])
                TR(BKp[:, h, 1, :], KTt[:, h, c, 64:128], identb64, ["KTt", "identb"], ["PTb"])
            CP("act", BKtok, BKp, ["PTb"], ["BKtok"])
            P0p = F2[0:64, :].rearrange("p (h i) -> p h i", h=8)
            Up = R2[0:64, 0:512].rearrange("p (h i) -> p h i", h=8)
            Yp = K2[0:64, 0:512].rearrange("p (h i) -> p h i", h=8)
            Hp = V2[0:64, 0:512].rearrange("p (h i) -> p h i", h=8)

            def ai(h):
                return (c * 2 + h // 4) * 4 + h % 4
            for h in range(8):
                MM(P0p[:, h, :], QTt[:, h, c, 0:64], Hb[:, h, :], True, False, ["QTt", "Hb"], ["F2"])
                MM(P0p[:, h, :], AM[:, ai(h), 128:192], Vc[:, c, h, :], False, True, ["AM", "Vc"], ["F2"])
            CP("act", P0s, P0p, ["F2"], ["P0s"])
            for h in range(8):
                MM(Up[:, h, :], P6[:, ai(h), :], P0s[:, h, :], True, True, [P6n, "P0s"], ["R2"])
            CP("act", Us, Up, ["R2"], ["Us"])
            for h in range(8):
                MM(Yp[:, h, :], QTt[:, h, c, 64:128], Hb[:, h, :], True, False, ["QTt", "Hb"], ["K2"])
                MM(Yp[:, h, :], AM[:, ai(h), 64:128], Us[:, h, :], False, False, ["AM", "Us"], ["K2"])
                MM(Yp[:, h, :], AM[:, ai(h), 192:256], Vc[:, c, h, :], False, True, ["AM", "Vc"], ["K2"])
            for h in range(8):
                MM(Hp[:, h, :], BKtok[:, h, 0, :], Us[:, h, :], True, False, ["BKtok", "Us"], ["V2"])
                MM(Hp[:, h, :], BKtok[:, h, 1, :], Vc[:, c, h, :], False, True, ["BKtok", "Vc"], ["V2"])
            CP("act", ych, Yp, ["K2"], ["ych"])
            TT("dve", H32, H32, Hp, ALU.add, ["H32", "V2"], ["H32"])
            TT("dve", H32, H32, bc(rw["g"][:, :, c * 64 + 63:c * 64 + 64], [64, 8, 64]), ALU.mult, ["H32", "g"], ["H32"])
            CP("act", Hb, H32, ["H32"], ["Hb"])
            RED(st8[:, 0:8], ych, ALU.add, ["ych"], ["st8"])
            TT("dve", yt1, ych, ych, ALU.mult, ["ych"], ["yt1"])
            RED(st8[:, 8:16], yt1, ALU.add, ["yt1"], ["st8"])
            TS("dve", st8[:, 0:16], st8[:, 0:16], 1.0 / 64, None, ALU.mult, None, ["st8"], ["st8"])
            TT("dve", st8[:, 16:24], st8[:, 0:8], st8[:, 0:8], ALU.mult, ["st8"], ["st8"])
            TT("dve", st8[:, 24:32], st8[:, 8:16], st8[:, 16:24], ALU.subtract, ["st8"], ["st8"])
            ACT(st8[:, 32:40], st8[:, 24:32], AF.Sqrt, ["st8", "cst"], ["st8"], bias=cst[0:64, 1:2])
            b.op("dve", lambda g: g.reciprocal(out=st8[:, 40:48], in_=st8[:, 32:40]), ["st8"], ["st8"])
            TT("dve", yt1, ych, bc(st8[:, 0:8].unsqueeze(2), [64, 8, 64]), ALU.subtract, ["ych", "st8"], ["yt1"])
            TT("dve", yt1, yt1, bc(st8[:, 40:48].unsqueeze(2), [64, 8, 64]), ALU.mult, ["yt1", "st8"], ["yt1"])
            lnwv = lnw_bc[:].rearrange("p (h i) -> p h i", h=8)
            lnbv = lnb_bc[:].rearrange("p (h i) -> p h i", h=8)
            TT("dve", yt1, yt1, lnwv, ALU.mult, ["yt1", "lnw_bc"], ["yt1"])
            TT("pool", yt1, yt1, lnbv, ALU.add, ["yt1", "lnb_bc"], ["yt1"])
            MM(F2[0:64, 264:272], identf[:, c * 64:(c + 1) * 64], bon, True, True, ["identf", "bon"], ["F2"])
            CP("act", st8[:, 48:56], F2[0:64, 264:272], ["F2"], ["st8"])
            TT("dve", ych, Vc[:, c], bc(st8[:, 48:56].unsqueeze(2), [64, 8, 64]), ALU.mult, ["Vc", "st8"], ["ych"])
            TT("pool", yt1, yt1, ych, ALU.add, ["yt1", "ych"], ["yt1"])
            MM(R2[0:64, 0:512], identf[:, c * 64:(c + 1) * 64], gr_s[:, :], True, True, ["identf", "gr_s"], ["R2"])
            TT("dve", yt1.rearrange("p h i -> p (h i)"), yt1.rearrange("p h i -> p (h i)"), R2[0:64, 0:512], ALU.mult, ["yt1", "R2"], ["yt1"])
            b.dma("sp", rwscr[ti * 128 + c * 64: ti * 128 + (c + 1) * 64, :], yt1.rearrange("p h i -> p (h i)"), reads=["yt1"], writes=["rwscr"])
        b.barrier(dmas=False)

    if stop_after == "A":
        b.barrier(); b.emit(); ncd.__exit__(None, None, None); es.close()
        return nc
    b.barrier()
    MS("dve", H32, 0.0, ["H32"]); MS("dve", Hb, 0.0, ["Hb"]); MS("pool", hlast[:], 0.0, ["hlast"])
    KSUB = int(os.environ.get('KSUB', '9'))
    V2a = V2[:, 0:512]
    V2b = V2[:, 512:1024]
    for ti in range(NT if nt_lim is None else nt_lim):
        par = ti % 2
        x_, h_, xr, hr = front(ti, par)
        if KSUB >= 1:
            for (c0, n, dst) in ((0, 256, V2a[:, 0:256]), (256, 64, V2a[:, 256:320]), (320, 512, V2b)):
                for k in range(8):
                    MM(dst, h_[:, k, :], Wn[:, k, c0:c0 + n], k == 0, k == 7, [hr, "Wn"], ["V2"])
        if KSUB >= 2:
            kv3 = kfin[:].rearrange("p (g d) -> p g d", g=2)
            qknorm(V2a[:, 0:128].rearrange("p (g d) -> p g d", g=2), kv3, 2, knw_bc, 1.0, ["V2"], ["kfin"])
            rope("dve", kv3, 2, None, ["kfin"])
            CP("act", vfin[:], V2a[:, 128:256], ["V2"], ["vfin"])
            CP("act", Vaug[:, ti, :, 0:64], V2a[:, 128:256].rearrange("p (g d) -> p g d", g=2), ["V2"], ["Vaug"])
            CP("act", kifin[:], V2a[:, 256:320], ["V2"], ["kifin"])
            rope("pool", kifin[:].unsqueeze(1), 1, None, ["kifin"])
            ACT(gr_s[:], V2b, AF.Silu, ["V2"], ["gr_s"])
        if KSUB >= 3:
            b.dma("sp", k_nat[ti * 128:(ti + 1) * 128, :], kfin[:], reads=["kfin"])
            b.dma("sp", v_nat[ti * 128:(ti + 1) * 128, :], vfin[:], reads=["vfin"])
            b.dma("sp", ki_nat[ti * 128:(ti + 1) * 128, :], kifin[:], reads=["kifin"])
        if KSUB >= 4:
            for g_ in range(2):
                TR(F2[0:64, g_ * 128:(g_ + 1) * 128], kfin[:, g_ * 64:(g_ + 1) * 64], identf[:], ["kfin", "identf"], ["F2"])
            TR(F2[0:64, 256:384], kifin[:, :], identf[:], ["kifin", "identf"], ["F2"])
            if KSUB >= 5:
                CP("act", kT_all[:, :, ti * 128:(ti + 1) * 128], F2[0:64, 0:256].rearrange("p (g t) -> p g t", g=2), ["F2"], ["kT_all"])
            if KSUB >= 6:
                if os.environ.get("KV") == "1":
                    CP("act", nrm_t[0:64, 0:128], F2[0:64, 256:384], ["F2"], ["nrm_t"])
                elif os.environ.get("KV") == "2":
                    CP("act", kiT_all[:, ti * 128:(ti + 1) * 128], F2[0:64, 0:128], ["F2"], ["kiT_all"])
                else:
                    CP("act", kiT_all[:, ti * 128:(ti + 1) * 128], F2[0:64, 256:384], ["F2"], ["kiT_all"])

        if int(os.environ.get("KLVL", "9")) >= 2:
            rwkv_tile(ti, h_, hr)
        if ti == NT - 1:
            for gi, (c0, n) in enumerate(((0, 512), (512, 512), (1024, 512), (1536, 128))):
                dst = (R2[0:1, 0:512], R2[0:1, 512:1024], K2[0:1, 0:512], K2[0:1, 512:640])[gi]
                nm = ("R2", "R2", "K2", "K2")[gi]
                for k in range(8):
                    MM(dst, h_[:, k, 127:128], Wom[:, k, c0:c0 + n], k == 0, False, ["Wm", hr], [nm])
                for k in range(8):
                    MM(dst, h_[:, k, 127:128], Wm[:, k, c0:c0 + n], False, k == 7, ["Wm", hr], [nm])
            CP("act", xmt[0:1, 0:1024], R2[0:1, :], ["R2"], ["xmt"])
            CP("dve", xmt[0:1, 1024:1664], K2[0:1, 0:640], ["K2"], ["xmt"])
            b.dma("sp", shift_p.rearrange("(a n) -> a n", a=1), xmt[0:1, :], reads=["xmt"])
    for h in range(8):
        TR(F2[0:64, h * 64:(h + 1) * 64], H32[:, h, :], identf[0:64, 0:64], ["H32", "identf"], ["F2"])
    CP("dve", ych, F2[0:64, 0:512].rearrange("p (h j) -> p h j", h=8), ["F2"], ["ych"])
    b.dma("sp", wkv_p.rearrange("h i j -> i h j"), ych, reads=["ych"])

    if stop_after == "B":
        b.barrier(); b.emit(); ncd.__exit__(None, None, None); es.close()
        return nc
    b.barrier()
    off = 0
    Wq, off = carve(off, [128, 8, 1544], BF16)
    stg, off = carve(off, [128, 8, 512])
    score, off = carve(off, [128, T])
    selm, off = carve(off, [128, T], BF16)
    selT, off = carve(off, [128, NT, 128], BF16)
    rl, off = carve(off, [128, 512], BF16)
    rl2, off = carve(off, [128, 512], BF16)
    rlf, off = carve(off, [128, 256])
    diagw, off = carve(off, [128, 8, 128], BF16)
    qfin, off = carve(off, [128, 512])
    qifin, off = carve(off, [128, 512])
    qT, off = carve(off, [64, 8, 128], BF16)
    qiT, off = carve(off, [64, 8, 128], BF16)
    ga, off = carve(off, [128, 512])
    eT, off = carve(off, [128, 4, 128], BF16)
    pTt, off = carve(off, [128, 4, 128], BF16)
    eT2, off = carve(off, [128, 4, 128], BF16)
    pTt2, off = carve(off, [128, 4, 128], BF16)
    cat, off = carve(off, [128, D], BF16)
    catT, off = carve(off, [128, 8, 128], BF16)
    rwo, off = carve(off, [128, 512])
    rwo2, off = carve(off, [128, 512])
    att, off = carve(off, [128, 8, 64])
    ybuf, off = carve(off, [128, D])
    bs, off = carve(off, [128, 16])
    wis, off = carve(off, [128, 8])
    oacc, off = carve(off, [128, 2, 4, 65])
    Wout, off = carve(off, [128, 8, D], BF16)
    gate_bc, off = carve(off, [128, D])
    iota256, off = carve(off, [128, 256])
    assert off <= AW, off
    b.dma("sp", gate_bc, gscr[16:17, :].partition_broadcast(128) if False else gscr[16, :].partition_broadcast(128), reads=["gscr"], writes=["gate_bc"])
    b.dma("sp", iota256, iota_d[:, :], writes=["iota256"])
    load_cols(Wq, 0, C_Q, 512, tag="Wq")
    load_cols(Wq, 512, C_QI, 520, tag="Wq")
    load_cols(Wq, 1032, C_GA, 512, tag="Wq")
    w_out_v = w_out.rearrange("(k p) c -> p k c", p=128)
    for hh in range(2):
        b.dma("sp", stg[:, :, :], w_out_v[:, :, hh * 512:(hh + 1) * 512], writes=["stg"])
        CP("pool", Wout[:, :, hh * 512:(hh + 1) * 512], stg[:, :, :], ["stg"], ["Wout"])


    for j in range(NO if no_lim is None else no_lim):
        ti = NT + j
        x_, h_, xr, hr = front(ti, 0)
        NKT = 2 * (j + 1)
        NK = NKT * 128
        for (c0, n, dst, nm) in ((0, 512, R2[:, 0:512], "R2a"), (512, 512, R2[:, 512:1024], "R2b"),
                                 (1024, 8, F2[:, 0:8], "F2"), (1032, 512, K2[:, 0:512], "K2a")):
            for k in range(8):
                MM(dst, h_[:, k, :], Wq[:, k, c0:c0 + n], k == 0, k == 7, [hr, "Wq"], [nm])
        q3 = qfin.rearrange("p (h d) -> p h d", h=8)
        qknorm(R2[:, 0:512].rearrange("p (h d) -> p h d", h=8), q3, 8, qnw_bc, 0.125, ["R2a"], ["qfin"])
        rope("dve", q3, 8, None, ["qfin"])
        qi3 = qifin.rearrange("p (h d) -> p h d", h=8)
        CP("act", qifin, R2[:, 512:1024], ["R2b"], ["qifin"])
        rope("pool", qi3, 8, None, ["qifin"])
        TS("dve", wis, F2[:, 0:8], 0.044194173824159216, None, ALU.mult, None, ["F2"], ["wis"])
        ACT(ga, K2[:, 0:512], AF.Silu, ["K2a"], ["ga"])
        for (src, srcn, dstT, dn) in ((qfin, "qfin", qT, "qT"), (qifin, "qifin", qiT, "qiT")):
            pv = K2[0:64, :].rearrange("p (h t) -> p h t", h=8)
            for h in range(8):
                TR(pv[:, h, :], src[:, h * 64:(h + 1) * 64], identf[:], [srcn, "identf"], ["K2a" if h < 4 else "K2b"])
            CP("act", dstT, pv, ["K2a", "K2b"], [dn])
        TT("dve", diagw, bc(identb[:].unsqueeze(1), [128, 8, 128]), bc(wis.unsqueeze(2), [128, 8, 128]), ALU.mult, ["identb", "wis"], ["diagw"])
        nchk = (NK + 511) // 512
        ib = 0
        for kc in range(nchk):
            w = min(512, NK - kc * 512)
            pend = None
            for h in range(8):
                pb = (R2[:, 0:512], R2[:, 512:1024])[ib % 2]
                pbn = ("R2a", "R2b")[ib % 2]
                rlb = (rl, rl2)[ib % 2]
                rln = ("rl", "rl2")[ib % 2]
                ib += 1
                MM(pb[:, 0:w], qiT[:, h, :], kiT_all[:, kc * 512:kc * 512 + w], True, True, ["qiT", "kiT_all"], [pbn])
                ACT(rlb[:, 0:w], pb[:, 0:w], AF.Relu, [pbn], [rln])
                if pend is not None:
                    ph, prl, prn = pend
                    MM(F2[:, 0:w], diagw[:, ph, :], prl[:, 0:w], ph == 0, False, ["diagw", prn], ["F2"])
                pend = (h, rlb, rln)
            ph, prl, prn = pend
            MM(F2[:, 0:w], diagw[:, ph, :], prl[:, 0:w], False, True, ["diagw", prn], ["F2"])
            CP("dve", score[:, kc * 512:kc * 512 + w], F2[:, 0:w], ["F2"], ["score"])
        RED(bs[:, 0:1], score[:, 0:NK], ALU.max, ["score"], ["bs"])
        RED(bs[:, 1:2], score[:, 0:NK], ALU.min, ["score"], ["bs"])
        TS("dve", rlf, iota256, qrel[:, 0:1], -1e30, ALU.is_gt, ALU.mult, ["iota256", "qrel"], ["rlf"])
        TT("dve", score[:, NK - 256:NK], score[:, NK - 256:NK], rlf, ALU.add, ["score", "rlf"], ["score"])
        TS("dve", bs[:, 2:3], bs[:, 1:2], -1.0, None, ALU.add, None, ["bs"], ["bs"])
        STT(bs[:, 3:4], bs[:, 0:1], 2.0, bs[:, 1:2], ALU.add, ALU.subtract, ["bs"], ["bs"])
        for it in range(1, n_bis + 1):
            sc_ = float(2.0 ** (-it))
            STT(bs[:, 4:5], bs[:, 3:4], sc_, bs[:, 2:3], ALU.mult, ALU.add, ["bs"], ["bs"])
            TS("dve", selm[:, 0:NK], score[:, 0:NK], bs[:, 4:5], 0.0, ALU.is_gt, ALU.add, ["score", "bs"], ["selm", "bs"], accum=bs[:, 5:6])
            TS("dve", bs[:, 6:7], bs[:, 5:6], float(topk_p) - 0.5, bs[:, 3:4], ALU.is_gt, ALU.mult, ["bs"], ["bs"])
            STT(bs[:, 2:3], bs[:, 6:7], sc_, bs[:, 2:3], ALU.mult, ALU.add, ["bs"], ["bs"])
        TS("dve", selm[:, 0:NK], score[:, 0:NK], bs[:, 2:3], None, ALU.is_gt, None, ["score", "bs"], ["selm"])
        for kt in range(NKT):
            TR(PTb[:, (kt % 8) * 128:(kt % 8 + 1) * 128], selm[:, kt * 128:(kt + 1) * 128], identb[:], ["selm", "identb"], ["PTb"])
            if kt % 8 == 7 or kt == NKT - 1:
                k0 = (kt // 8) * 8
                n_ = kt - k0 + 1
                ACT(selT[:, k0:k0 + n_, :], PTb[:, 0:n_ * 128].rearrange("p (a t) -> p a t", a=n_), AF.Identity, ["PTb", "cst"], ["selT"],
                    scale=30000.0, bias=cst[:, 3:4])
        po = [V2[:, 0:260].rearrange("p (h e) -> p h e", h=4), V2[:, 512:772].rearrange("p (h e) -> p h e", h=4)]
        def att_front(kt, g_):
            lp = K2[:, g_ * 512:(g_ + 1) * 512]
            kn_ = ("K2a", "K2b")[g_]
            pTb = (pTt, pTt2)[g_]
            pn_ = ("pTt", "pTt2")[g_]
            MM(lp, kT_all[:, g_, kt * 128:(kt + 1) * 128], qT[:, g_ * 4:(g_ + 1) * 4, :].rearrange("p h t -> p (h t)"),
               True, False, ["kT_all", "qT"], [kn_])
            for hh in range(4):
                MM(lp[:, hh * 128:(hh + 1) * 128], identb[:], selT[:, kt, :], False, hh == 3, ["identb", "selT"], [kn_])
            ACT(pTb, lp.rearrange("p (h t) -> p h t", h=4), AF.Exp, [kn_], [pn_])

        def att_back(kt, g_):
            vn_ = ("V2a", "V2b")[g_]
            pTb = (pTt, pTt2)[g_]
            pn_ = ("pTt", "pTt2")[g_]
            on_ = ("oacc0", "oacc1")[g_]
            for hh in range(4):
                MM(po[g_][:, hh, :], pTb[:, hh, :], Vaug[:, kt, g_, :], True, True, [pn_, "Vaug"], [vn_])
            if kt == 0:
                CP("act", oacc[:, g_], po[g_], [vn_], [on_])
            else:
                TT("dve", oacc[:, g_], oacc[:, g_], po[g_], ALU.add, [vn_, on_], [on_])
        att_front(0, 0)
        for kt in range(NKT):
            att_front(kt, 1)
            att_back(kt, 0)
            if kt + 1 < NKT:
                att_front(kt + 1, 0)
            att_back(kt, 1)
        for g_ in range(2):
            b.op("dve", lambda g, g_=g_: g.reciprocal(out=bs[:, 8 + g_ * 4:12 + g_ * 4], in_=oacc[:, g_, :, 64]), ["oacc0", "oacc1"], ["bs"])
            TT("dve", att[:, g_ * 4:(g_ + 1) * 4, :], oacc[:, g_, :, 0:64], bc(bs[:, 8 + g_ * 4:12 + g_ * 4].unsqueeze(2), [128, 4, 64]),
               ALU.mult, ["oacc0", "oacc1", "bs"], ["att"])
        TT("dve", cat[:, 0:512], att.rearrange("p h d -> p (h d)"), ga, ALU.mult, ["att", "ga"], ["cat"])
        b.dma("sp", rwo, rwscr[(2 * j) * 128:(2 * j + 1) * 128, :], reads=["rwscr"], writes=["rwo"])
        b.dma("sp", rwo2, rwscr[(2 * j + 1) * 128:(2 * j + 2) * 128, :], reads=["rwscr"], writes=["rwo2"])
        TS("dve", rwo, rwo, parsel[:, 1:2], None, ALU.mult, None, ["rwo", "parsel"], ["rwo"])
        STT(rwo, rwo2, parsel[:, 0:1], rwo, ALU.mult, ALU.add, ["rwo2", "parsel", "rwo"], ["rwo"])
        CP("act", cat[:, 512:1024], rwo, ["rwo"], ["cat"])
        for k in range(8):
            TR(PTb[:, k * 128:(k + 1) * 128], cat[:, k * 128:(k + 1) * 128], identb[:], ["cat", "identb"], ["PTb"])
        CP("act", catT, PTb[:, :].rearrange("p (k t) -> p k t", k=8), ["PTb"], ["catT"])
        for hh in range(2):
            for k in range(8):
                MM(R2[:, hh * 512:(hh + 1) * 512], catT[:, k, :], Wout[:, k, hh * 512:(hh + 1) * 512], k == 0, k == 7, ["catT", "Wout"], [("R2a", "R2b")[hh]])
        TT("dve", ybuf, R2[:, :], gate_bc, ALU.mult, ["R2a", "R2b", "gate_bc"], ["ybuf"])
        TT("pool", ybuf, ybuf, x_[:], ALU.add, ["ybuf", xr], ["ybuf"])
        b.dma("sp", y_own[j * 128:(j + 1) * 128, :], ybuf, reads=["ybuf"])


    if do_sample:
        b.barrier()
        PW = 10500
        off = 0
        proj, off = carve(off, [16, DIN])
        tk = {}
        for nm in ("qs", "ga", "grs", "ta", "tb"):
            tk[nm], off = carve(off, [16, 512])
        ks_, off = carve(off, [16, 128])
        s16, off = carve(off, [16, 64])
        tokd, off = carve(off, [16, 1040])
        ysb, off = carve(off, [16, D])
        cats, off = carve(off, [16, D], BF16)
        catTs, off = carve(off, [128, 8, 16], BF16)
        assert off <= PW, off
        off = PW
        stg, off = carve(off, [128, 8, 512])
        wbf, off = carve(off, [128, 8, 512], BF16)
        sshift_t, off = carve(off, [16, SHW])
        mu16, off = carve(off, [16, SHW])
        X1 = off
        xm, off = carve(off, [16, SHW])
        prm, off = carve(off, [16, 5, 512])
        vecs, off = carve(off, [16, 8, 6, 64])
        for nm in ("dec", "asg", "kkv", "kkn", "kmod"):
            tk[nm], off = carve(off, [16, 512])
        wdt, off = carve(off, [16, 128])
        wdT, off = carve(off, [64, 32])
        assert off <= AW, off
        NPAIR = NS // 2

        x_, h_, xr, hr = front(NT + NO, 0, m_prompt=False, ntok=16)
        for ch in range(8):
            c0 = ch * 505
            b.dma("sp", stg[:, :, 0:505], w_in_v[:, :, c0:c0 + 505], writes=["stg"])
            CP("dve", wbf[:, :, 0:505], stg[:, :, 0:505], ["stg"], ["wbf"])
            for k in range(8):
                MM(R2[0:16, 0:505], h_[:, k, 0:16], wbf[:, k, 0:505], k == 0, k == 7, [hr, "wbf"], ["R2"])
            CP("act", proj[:, c0:c0 + 505], R2[0:16, 0:505], ["R2"], ["proj"])
        b.dma("sp", sshift_t, sshift_d[:, :], writes=["sshift"])
        b.dma("sp", mu16, mu.partition_broadcast(16), writes=["mu16"])
        for i_, src in enumerate((pw0, pa0, pkk, pka, prk)):
            b.dma("sp", prm[:, i_, :], src.partition_broadcast(16), writes=["prm"])
        qs3 = tk["qs"].rearrange("p (h d) -> p h d", h=8)
        qknorm(proj[:, 0:512].rearrange("p (h d) -> p h d", h=8), qs3, 8, qnw_bc, 0.125, ["proj"], ["qs"], nrows=16)
        rope("dve", qs3, 8, None, ["qs"], nrows=16)
        ks3 = ks_.rearrange("p (g d) -> p g d", g=2)
        qknorm(proj[:, 512:640].rearrange("p (g d) -> p g d", g=2), ks3, 2, knw_bc, 1.0, ["proj"], ["ks"], nrows=16)
        rope("dve", ks3, 2, None, ["ks"], nrows=16)
        b.dma("sp", k_s[:, :], ks_, reads=["ks"])
        b.dma("sp", v_s[:, :], proj[:, 640:768], reads=["proj"])
        b.dma("sp", shift_s[:, :], proj[:, C_R:C_R + SHW], reads=["proj"])
        rope("dve", proj[:, 768:1280].rearrange("p (h d) -> p h d", h=8), 8, None, ["proj"], nrows=16)
        rope("dve", proj[:, 1288:1352].unsqueeze(1), 1, None, ["proj"], nrows=16)
        b.dma("sp", ki_s[:, :], proj[:, 1288:1352], reads=["proj"])
        ACT(tk["ga"], proj[:, C_GA:C_GA + 512], AF.Silu, ["proj"], ["ga"])
        ACT(tk["grs"], proj[:, C_GR:C_GR + 512], AF.Silu, ["proj"], ["grs"])
        xs_ = proj[:, C_R:C_R + SHW]
        TT("dve", xm, sshift_t, xs_, ALU.subtract, ["sshift", "proj"], ["xm"])
        TT("dve", xm, xm, mu16, ALU.mult, ["xm", "mu16"], ["xm"])
        TT("dve", xm, xm, xs_, ALU.add, ["xm", "proj"], ["xm"])
        ACT(wdt[:, 0:64], xm[:, 1536:1600], AF.Tanh, ["xm"], ["wdt"])
        CP("dve", wdt[:, 64:128], xm[:, 1600:1664], ["xm"], ["wdt"])
        TR(F2[0:64, 0:16], wdt[:, 0:64], identf[0:16, 0:16], ["wdt", "identf"], ["F2"])
        TR(F2[0:64, 16:32], wdt[:, 64:128], identf[0:16, 0:16], ["wdt", "identf"], ["F2"])
        CP("dve", wdT, F2[0:64, 0:32], ["F2"], ["wdT"])
        MM(R2[0:16, 0:512], wdT[:, 0:16], wupS[:, :], True, True, ["wdT", "wupS"], ["R2"])
        MM(R2[0:16, 512:1024], wdT[:, 16:32], aupS[:, :], True, True, ["wdT", "aupS"], ["R2"])
        TT("dve", tk["dec"], R2[0:16, 0:512], prm[:, 0, :], ALU.add, ["R2", "prm"], ["dec"])
        ACT(tk["dec"], tk["dec"], AF.Sigmoid, ["dec"], ["dec"])
        ACT(tk["dec"], tk["dec"], AF.Exp, ["dec"], ["dec"], scale=-0.6065306597126334)
        TT("dve", tk["asg"], R2[0:16, 512:1024], prm[:, 1, :], ALU.add, ["R2", "prm"], ["asg"])
        ACT(tk["asg"], tk["asg"], AF.Sigmoid, ["asg"], ["asg"])
        xr_, xk_, xv_ = xm[:, 0:512], xm[:, 512:1024], xm[:, 1024:1536]
        TT("dve", tk["kkv"], xk_, prm[:, 2, :], ALU.mult, ["xm", "prm"], ["kkv"])
        ACT(tk["ta"], tk["kkv"], AF.Square, ["kkv"], ["ta"])
        RED(s16[:, 0:8], tk["ta"].rearrange("p (h d) -> p h d", h=8), ALU.add, ["ta"], ["s16"])
        ACT(s16[:, 8:16], s16[:, 0:8], AF.Sqrt, ["s16", "cst"], ["s16"], bias=cst[0:16, 2:3])
        b.op("dve", lambda g: g.reciprocal(out=s16[:, 16:24], in_=s16[:, 8:16]), ["s16"], ["s16"])
        TT("dve", tk["kkn"].rearrange("p (h d) -> p h d", h=8), tk["kkv"].rearrange("p (h d) -> p h d", h=8),
           bc(s16[:, 16:24].unsqueeze(2), [16, 8, 64]), ALU.mult, ["kkv", "s16"], ["kkn"])
        STT(tk["ta"], tk["asg"], -1.0, prm[:, 3, :], ALU.add, ALU.mult, ["asg", "prm"], ["ta"])
        STT(tk["kmod"], tk["ta"], 1.0, xk_, ALU.add, ALU.mult, ["ta", "xm"], ["kmod"])

        def v8(ap):
            return ap.rearrange("p (h d) -> p h d", h=8)
        CP("dve", vecs[:, :, 0, :], v8(tk["dec"]), ["dec"], ["vecs"])
        TS("dve", vecs[:, :, 1, :], v8(tk["kkn"]), -1.0, None, ALU.mult, None, ["kkn"], ["vecs"])
        TT("dve", vecs[:, :, 2, :], v8(tk["kkn"]), v8(tk["asg"]), ALU.mult, ["kkn", "asg"], ["vecs"])
        CP("dve", vecs[:, :, 3, :], v8(tk["kmod"]), ["kmod"], ["vecs"])
        CP("dve", vecs[:, :, 4, :], v8(xr_), ["xm"], ["vecs"])
        CP("dve", vecs[:, :, 5, :], v8(xv_), ["xm"], ["vecs"])
        TT("dve", tk["ta"], xr_, prm[:, 4, :], ALU.mult, ["xm", "prm"], ["ta"])
        TT("dve", tk["ta"], tk["ta"], tk["kmod"], ALU.mult, ["ta", "kmod"], ["ta"])
        RED(s16[:, 24:32], v8(tk["ta"]), ALU.add, ["ta"], ["s16"])
        b.dma("sp", scr1[:, :], vecs.rearrange("p h v j -> p (h v j)"), reads=["vecs"], writes=["scr1"])
        b.barrier()
        off = PW
        S_, off = carve(off, [128, 4096])
        tmpS, off = carve(off, [128, 4096])
        vsh, off = carve(off, [128, 384])
        ysh, off = carve(off, [128, 128])
        assert off <= X1
        b.dma("sp", S_, swkv_d[:, :], writes=["S"])
        b.dma("sp", vsh, scr1.rearrange("s (h x) -> (s h) x", h=8), reads=["scr1"], writes=["vsh"])
        S3 = S_.rearrange("p (i j) -> p i j", i=64)
        T3 = tmpS.rearrange("p (i j) -> p i j", i=64)

        def jb(vi):
            return bc(vsh[:, vi * 64:(vi + 1) * 64].unsqueeze(1), [128, 64, 64])

        def ib(ap):
            return bc(ap.unsqueeze(2), [128, 64, 64])
        TT("dve", T3, S3, jb(1), ALU.mult, ["S", "vsh"], ["tmpS"])
        RED(ysh[:, 0:64], T3, ALU.add, ["tmpS"], ["ysh"])
        TT("dve", S3, S3, jb(0), ALU.mult, ["S", "vsh"], ["S"])
        TT("dve", T3, jb(2), ib(ysh[:, 0:64]), ALU.mult, ["vsh", "ysh"], ["tmpS"])
        TT("dve", S3, S3, T3, ALU.add, ["S", "tmpS"], ["S"])
        TT("dve", T3, jb(3), ib(vsh[:, 320:384]), ALU.mult, ["vsh"], ["tmpS"])
        TT("dve", S3, S3, T3, ALU.add, ["S", "tmpS"], ["S"])
        b.dma("sp", wkv_s[:, :], S_, reads=["S"])
        TT("dve", T3, S3, jb(4), ALU.mult, ["S", "vsh"], ["tmpS"])
        RED(ysh[:, 64:128], T3, ALU.add, ["tmpS"], ["ysh"])
        b.dma("sp", scr2[:, :], ysh[:, 64:128], reads=["ysh"], writes=["scr2"])
        yS = tk["tb"]
        b.dma("sp", yS, scr2.rearrange("(s h) i -> s (h i)", h=8), reads=["scr2"], writes=["tb"])
        y3 = v8(yS)
        RED(s16[:, 32:40], y3, ALU.add, ["tb"], ["s16"])
        ACT(tk["ta"], yS, AF.Square, ["tb"], ["ta"])
        RED(s16[:, 40:48], v8(tk["ta"]), ALU.add, ["ta"], ["s16"])
        TS("dve", s16[:, 32:48], s16[:, 32:48], 1.0 / 64, None, ALU.mult, None, ["s16"], ["s16"])
        TT("dve", s16[:, 48:56], s16[:, 32:40], s16[:, 32:40], ALU.mult, ["s16"], ["s16"])
        TT("dve", s16[:, 48:56], s16[:, 40:48], s16[:, 48:56], ALU.subtract, ["s16"], ["s16"])
        ACT(s16[:, 56:64], s16[:, 48:56], AF.Sqrt, ["s16", "cst"], ["s16"], bias=cst[0:16, 1:2])
        b.op("dve", lambda g: g.reciprocal(out=s16[:, 56:64], in_=s16[:, 56:64]), ["s16"], ["s16"])
        TT("dve", y3, y3, bc(s16[:, 32:40].unsqueeze(2), [16, 8, 64]), ALU.subtract, ["tb", "s16"], ["tb"])
        TT("dve", y3, y3, bc(s16[:, 56:64].unsqueeze(2), [16, 8, 64]), ALU.mult, ["tb", "s16"], ["tb"])
        TT("dve", yS, yS, lnw_bc[0:16, :], ALU.mult, ["tb", "lnw_bc"], ["tb"])
        TT("dve", yS, yS, lnb_bc[0:16, :], ALU.add, ["tb", "lnb_bc"], ["tb"])
        TT("dve", v8(tk["ta"]), v8(xv_), bc(s16[:, 24:32].unsqueeze(2), [16, 8, 64]), ALU.mult, ["xm", "s16"], ["ta"])
        TT("dve", yS, yS, tk["ta"], ALU.add, ["tb", "ta"], ["tb"])
        TT("dve", cats[:, 512:1024], yS, tk["grs"], ALU.mult, ["tb", "grs"], ["cats"])
        b.barrier()
        NCAND = 16
        off = PW
        Gi, off = carve(off, [128, 8192], BF16)
        Gi2, off = carve(off, [128, 8192], BF16)
        tmpG, off = carve(off, [128, 64, 64])
        Kc, off = carve(off, [128, NCAND, 128])
        Vcd, off = carve(off, [128, NCAND, 128])
        tmpc, off = carve(off, [128, NCAND, 64])
        repd, off = carve(off, [128, 1040])
        opd, off = carve(off, [128, 520])
        repS, off = carve(off, [16, 1024])
        repTS, off = carve(off, [128, 128])
        blkS, off = carve(off, [128, 128])
        sc, off = carve(off, [128, 132])
        msc, off = carve(off, [128, 132])
        sh_, off = carve(off, [128, 128])
        cv, off = carve(off, [128, NCAND])
        ci, off = carve(off, [128, NCAND], I32)
        cif, off = carve(off, [128, NCAND])
        rowi, off = carve(off, [128, NCAND], I32)
        lg, off = carve(off, [128, 8, NCAND])
        b2, off = carve(off, [128, 16])
        ptab, off = carve(off, [128, 8], I32)
        ptf, off = carve(off, [128, 8])
        oh0, off = carve(off, [128, 1])
        assert off <= AW, off
        GiB = (Gi, Gi2)
        tmpGb = tmpG.rearrange("p a b -> p (a b)").bitcast(BF16)[:, 0:4096].rearrange("p (a b) -> p a b", a=64)
        qib = opd[:, 0:256].bitcast(BF16)
        for (t_, d_, nm) in ((ptab, ptab_d, "ptab"), (repS, rep_d, "repS"), (repTS, repT_d, "repTS"), (blkS, blk_d, "blkS"), (oh0, oh0_d, "oh0")):
            b.dma("sp", t_, d_[:, :], writes=[nm])
        CP("dve", tokd[:, 0:512], proj[:, 768:1280], ["proj"], ["tokd"])
        TS("dve", tokd[:, 512:520], proj[:, 1280:1288], 0.044194173824159216, None, ALU.mult, None, ["proj"], ["tokd"])
        CP("dve", tokd[:, 520:1032], tk["qs"], ["qs"], ["tokd"])
        TT("dve", v8(tk["ta"]), v8(tokd[:, 0:512]), bc(proj[:, 1288:1352].unsqueeze(1), [16, 8, 64]), ALU.mult, ["tokd", "proj"], ["ta"])
        RED(s16[:, 0:8], v8(tk["ta"]), ALU.add, ["ta"], ["s16"])
        TS("dve", s16[:, 0:8], s16[:, 0:8], 0.0, None, ALU.max, None, ["s16"], ["s16"])
        TT("dve", s16[:, 0:8], s16[:, 0:8], tokd[:, 512:520], ALU.mult, ["s16", "tokd"], ["s16"])
        RED(tokd[:, 1032:1033], s16[:, 0:8], ALU.add, ["s16"], ["tokd"])
        CP("dve", ptf, ptab, ["ptab"], ["ptf"])
        TS("dve", ptf, ptf, 128.0, None, ALU.mult, None, ["ptf"], ["ptf"])
        ck_rows = cache_k
        cv_rows = cache_v
        cvA, off = carve(off, [128, 8, NCAND])
        Kc2, off = carve(off, [128, NCAND, 128])
        Vcd2, off = carve(off, [128, NCAND, 128])
        rowiA, off = carve(off, [128, 8, NCAND], I32)
        cvalA, off = carve(off, [128, 8, NCAND])
        ciA, off = carve(off, [128, 8, NCAND], I32)
        thrA, off = carve(off, [128, 8])
        tmpGf = tmpG.rearrange("p a b -> p (a b)")
        cand16 = tmpGf[0:16, 0:1540]
        candj = tmpGf[0:16, 1540:3080]
        assert off <= AW, off
        def gi_gather(sp):
            gb = GiB[sp % 2]
            b.op("pool", lambda g: g.indirect_dma_start(out=gb, out_offset=None, in_=cache_ki[:, :],
                                                        in_offset=bass.IndirectOffsetOnAxis(ap=ptab[:, sp:sp + 1], axis=0)),
                 ["ptab"], ["Gi%d" % (sp % 2)], dma=True)
        gi_gather(0)
        for sp in range(NPAIR):
            if sp + 1 < NPAIR:
                gi_gather(sp + 1)
            Gi3 = GiB[sp % 2].rearrange("p (t d) -> p t d", t=128)
            gin = "Gi%d" % (sp % 2)
            for (c0, n) in ((0, 512), (512, 8)):
                MM(K2[:, 0:n], repS[:, sp * 128:(sp + 1) * 128], tokd[:, c0:c0 + n], True, True, ["repS", "tokd"], ["K2"])
                CP("act", repd[:, c0:c0 + n], K2[:, 0:n], ["K2"], ["repd"])
            CP("act", qib, repd[:, 0:512], ["repd"], ["qib"])
            for h in range(8):
                for hf in range(2):
                    TT("dve", tmpGb, Gi3[:, hf * 64:(hf + 1) * 64, :], bc(qib[:, h * 64:(h + 1) * 64].unsqueeze(1), [128, 64, 64]), ALU.mult, [gin, "qib"], ["tmpG"])
                    RED(sh_[:, hf * 64:(hf + 1) * 64], tmpGb, ALU.add, ["tmpG"], ["sh"])
                if h == 0:
                    TS("dve", sc[:, 0:128], sh_, 0.0, repd[:, 512:513], ALU.max, ALU.mult, ["sh", "repd"], ["sc"])
                else:
                    TS("dve", sh_, sh_, 0.0, repd[:, 512 + h:513 + h], ALU.max, ALU.mult, ["sh", "repd"], ["sh"])
                    TT("dve", sc[:, 0:128], sc[:, 0:128], sh_, ALU.add, ["sc", "sh"], ["sc"])
            for r_ in range(NCAND // 8):
                b.op("dve", lambda g, r_=r_, sp=sp: g.max(out=cvA[:, sp, r_ * 8:(r_ + 1) * 8], in_=sc[:, 0:128]), ["sc"], ["cvA"])
                b.op("dve", lambda g, r_=r_, sp=sp: g.max_index(out=ciA[:, sp, r_ * 8:(r_ + 1) * 8].bitcast(mybir.dt.uint32),
                                                                in_max=cvA[:, sp, r_ * 8:(r_ + 1) * 8], in_values=sc[:, 0:128]), ["sc", "cvA"], ["ciA"])
                if r_ < NCAND // 8 - 1:
                    b.op("dve", lambda g, r_=r_, sp=sp: g.match_replace(out=sc[:, 0:128], in_to_replace=cvA[:, sp, r_ * 8:(r_ + 1) * 8],
                                                                        in_values=sc[:, 0:128], imm_value=-3e30), ["sc", "cvA"], ["sc"])
        b.dma("sp", scr4.rearrange("(sp s2) g c -> (s2 g) sp c", s2=2), cvA, reads=["cvA"], writes=["scr4"])
        b.dma("sp", cand16[:, 0:64 * NCAND], scr4.rearrange("s g c -> s (g c)"), reads=["scr4"], writes=["tmpG"])
        CP("dve", cand16[:, 64 * NCAND:64 * NCAND + 1], tokd[:, 1032:1033], ["tokd"], ["tmpG"])
        cnd = cand16[:, 0:64 * NCAND + 1]
        RED(s16[:, 40:41], cnd, ALU.max, ["tmpG"], ["s16"])
        RED(s16[:, 41:42], cnd, ALU.min, ["tmpG"], ["s16"])
        TS("dve", s16[:, 42:43], s16[:, 41:42], -1.0, None, ALU.add, None, ["s16"], ["s16"])
        STT(s16[:, 43:44], s16[:, 40:41], 2.0, s16[:, 41:42], ALU.add, ALU.subtract, ["s16"], ["s16"])
        for it in range(1, n_bis + 2):
            sc_ = float(2.0 ** (-it))
            STT(s16[:, 44:45], s16[:, 43:44], sc_, s16[:, 42:43], ALU.mult, ALU.add, ["s16"], ["s16"])
            TS("dve", candj[:, 0:64 * NCAND + 1], cnd, s16[:, 44:45], 0.0, ALU.is_gt, ALU.add, ["tmpG", "s16"], ["tmpG", "s16"], accum=s16[:, 45:46])
            TS("dve", s16[:, 46:47], s16[:, 45:46], float(topk_s) - 0.5, s16[:, 43:44], ALU.is_gt, ALU.mult, ["s16"], ["s16"])
            STT(s16[:, 42:43], s16[:, 46:47], sc_, s16[:, 42:43], ALU.mult, ALU.add, ["s16"], ["s16"])
        TT("dve", s16[:, 32:33], tokd[:, 1032:1033], s16[:, 42:43], ALU.is_gt, ["tokd", "s16"], ["s16"])
        for sp in range(NPAIR):
            MM(F2[:, sp:sp + 1], repS[:, sp * 128:(sp + 1) * 128], s16[:, 42:43], True, True, ["repS", "s16"], ["F2"])
        CP("dve", thrA, F2[:, 0:8], ["F2"], ["thrA"])
        for sp in range(NPAIR):
            CP("dve", cif, ciA[:, sp, :], ["ciA"], ["cif"])
            TS("dve", cif, cif, ptf[:, sp:sp + 1], None, ALU.add, None, ["cif", "ptf"], ["cif"])
            CP("dve", rowiA[:, sp, :], cif, ["cif"], ["rowiA"])
            TS("dve", cvalA[:, sp, :], cvA[:, sp, :], thrA[:, sp:sp + 1], None, ALU.is_gt, None, ["cvA", "thrA"], ["cvalA"])
        KcB = (Kc, Kc2)
        VcB = (Vcd, Vcd2)

        def gathers(sp):
            kb, vb = KcB[sp % 2], VcB[sp % 2]
            kn, vn = "Kc%d" % (sp % 2), "Vcd%d" % (sp % 2)
            for c_ in range(NCAND):
                b.op("pool", lambda g, c_=c_: g.indirect_dma_start(out=kb[:, c_, :], out_offset=None, in_=ck_rows[:, :],
                                                                  in_offset=bass.IndirectOffsetOnAxis(ap=rowiA[:, sp, c_:c_ + 1], axis=0)),
                     ["rowiA"], [kn], dma=True)
                b.op("pool", lambda g, c_=c_: g.indirect_dma_start(out=vb[:, c_, :], out_offset=None, in_=cv_rows[:, :],
                                                                  in_offset=bass.IndirectOffsetOnAxis(ap=rowiA[:, sp, c_:c_ + 1], axis=0)),
                     ["rowiA"], [vn], dma=True)
        gathers(0)
        for sp in range(NPAIR):
            if sp + 1 < NPAIR:
                gathers(sp + 1)
            kn, vn = "Kc%d" % (sp % 2), "Vcd%d" % (sp % 2)
            MM(K2[:, 0:512], repS[:, sp * 128:(sp + 1) * 128], tokd[:, 520:1032], True, True, ["repS", "tokd"], ["K2"])
            CP("act", repd[:, 520:1032], K2[:, 0:512], ["K2"], ["repd"])
            Kc4 = KcB[sp % 2].rearrange("p c (g d) -> p c g d", g=2)
            Vc4 = VcB[sp % 2].rearrange("p c (g d) -> p c g d", g=2)
            cvv = cvalA[:, sp, :]
            for h in range(8):
                TT("dve", tmpc, Kc4[:, :, h // 4, :], bc(repd[:, 520 + h * 64:520 + (h + 1) * 64].unsqueeze(1), [128, NCAND, 64]), ALU.mult, [kn, "repd"], ["tmpc"])
                RED(lg[:, h, :], tmpc, ALU.add, ["tmpc"], ["lg"])
            ACT(lg, lg, AF.Exp, ["lg"], ["lg"])
            TT("dve", lg, lg, bc(cvv.unsqueeze(1), [128, 8, NCAND]), ALU.mult, ["lg", "cvalA"], ["lg"])
            RED(opd[:, 512:520], lg, ALU.add, ["lg"], ["opd"])
            for h in range(8):
                TT("dve", tmpc, Vc4[:, :, h // 4, :], bc(lg[:, h, :].unsqueeze(2), [128, NCAND, 64]), ALU.mult, [vn, "lg"], ["tmpc"])
                RED(opd[:, h * 64:(h + 1) * 64], tmpc.rearrange("p c d -> p d c"), ALU.add, ["tmpc"], ["opd"])
            MM(V2[0:16, 0:512], repTS[:, sp * 16:(sp + 1) * 16], opd[:, 0:512], sp == 0, sp == NPAIR - 1, ["repTS", "opd"], ["V2"])
            MM(V2[0:16, 512:520], repTS[:, sp * 16:(sp + 1) * 16], opd[:, 512:520], sp == 0, sp == NPAIR - 1, ["repTS", "opd"], ["V2"])
        qv = v8(tk["qs"])
        for g_ in range(2):
            TT("dve", v8(tk["ta"])[:, g_ * 4:(g_ + 1) * 4, :], qv[:, g_ * 4:(g_ + 1) * 4, :],
               bc(ks_[:, g_ * 64:(g_ + 1) * 64].unsqueeze(1), [16, 4, 64]), ALU.mult, ["qs", "ks"], ["ta"])
        RED(s16[:, 0:8], v8(tk["ta"]), ALU.add, ["ta"], ["s16"])
        ACT(s16[:, 0:8], s16[:, 0:8], AF.Exp, ["s16"], ["s16"])
        TS("dve", s16[:, 0:8], s16[:, 0:8], s16[:, 32:33], None, ALU.mult, None, ["s16"], ["s16"])
        TT("dve", s16[:, 8:16], V2[0:16, 512:520], s16[:, 0:8], ALU.add, ["V2", "s16"], ["s16"])
        b.op("dve", lambda g: g.reciprocal(out=s16[:, 8:16], in_=s16[:, 8:16]), ["s16"], ["s16"])
        for g_ in range(2):
            TT("dve", v8(tk["ta"])[:, g_ * 4:(g_ + 1) * 4, :], bc(proj[:, 640 + g_ * 64:640 + (g_ + 1) * 64].unsqueeze(1), [16, 4, 64]),
               bc(s16[:, g_ * 4:(g_ + 1) * 4].unsqueeze(2), [16, 4, 64]), ALU.mult, ["proj", "s16"], ["ta"])
        TT("dve", tk["ta"], tk["ta"], V2[0:16, 0:512], ALU.add, ["ta", "V2"], ["ta"])
        TT("dve", v8(tk["ta"]), v8(tk["ta"]), bc(s16[:, 8:16].unsqueeze(2), [16, 8, 64]), ALU.mult, ["ta", "s16"], ["ta"])
        TT("dve", cats[:, 0:512], tk["ta"], tk["ga"], ALU.mult, ["ta", "ga"], ["cats"])
        for k in range(8):
            TR(PTb[:, k * 16:(k + 1) * 16], cats[:, k * 128:(k + 1) * 128], identb[0:16, 0:16], ["cats", "identb"], ["PTb"])
        CP("act", catTs, PTb[:, 0:128].rearrange("p (k t) -> p k t", k=8), ["PTb"], ["catTs"])
        b.barrier()
        off = PW
        stg2, off = carve(off, [128, 8, 512])
        wbf2, off = carve(off, [128, 8, 512], BF16)
        b.dma("sp", ysb, gscr[0:16, :], reads=["gscr"], writes=["ysb"])
        w_out_v2 = w_out.rearrange("(k p) c -> p k c", p=128)
        for hh in range(2):
            b.dma("sp", stg2, w_out_v2[:, :, hh * 512:(hh + 1) * 512], writes=["stg2"])
            CP("dve", wbf2, stg2, ["stg2"], ["wbf2"])
            for k in range(8):
                MM(R2[0:16, hh * 512:(hh + 1) * 512], catTs[:, k, :], wbf2[:, k, :], k == 0, k == 7, ["catTs", "wbf2"], ["R2"])
        TT("dve", ysb, ysb, R2[0:16, :], ALU.mult, ["ysb", "R2"], ["ysb"])
        TT("dve", ysb, ysb, x_[0:16, :], ALU.add, ["ysb", xr], ["ysb"])
        b.dma("sp", y_s[:, :], ysb, reads=["ysb"])

    b.barrier()
    b.emit()
    ncd.__exit__(None, None, None)
    es.close()
    return nc


def _consts(T):
    NT = T // 128
    NO = NT // 2
    cst = {}
    cst["identf"] = np.eye(128, dtype=np.float32)
    cst["iota256"] = np.tile(np.arange(256, dtype=np.float32)[None, :], (128, 1))
    s = np.arange(64)[:, None]
    t = np.arange(64)[None, :]
    lt = (s < t).astype(np.float32)
    le = (s <= t).astype(np.float32)
    cst["maskT"] = np.concatenate([lt, le, lt, le], axis=1)
    cst["maskL"] = (np.arange(64)[None, :] < np.arange(64)[:, None]).astype(np.float32)
    r = np.ones((64, 1024), np.float32)
    r[:, ::64] = 0.0
    cst["resetm"] = r
    sel = np.zeros((17, 128), np.float32)
    sel[16, :] = 1.0
    cst["sel16"] = sel
    cst["ones64"] = np.ones((64, 64), np.float32)
    return cst


def _rope_table(pos):
    half = 8
    inv = np.power(np.float32(ROPE_THETA), -np.arange(half, dtype=np.float32) / np.float32(half)).astype(np.float32)
    ang = pos.astype(np.float32)[:, None] * inv[None, :]
    return np.concatenate([np.cos(ang), np.sin(ang)], axis=1).astype(np.float32)


def _core_inputs(inp, c, T, NS, past_len):
    NT = T // 128
    NO = NT // 2
    bi, par = c // 2, c % 2
    xp = np.asarray(inp["x_prompt"][bi], np.float32)
    own_tiles = [2 * j + par for j in range(NO)]
    own_rows = np.concatenate([np.arange(t * 128, (t + 1) * 128) for t in own_tiles])
    xs = np.zeros((128, D), np.float32)
    xs[:NS] = np.asarray(inp["x_sample"][c * NS:(c + 1) * NS, 0], np.float32)
    m = {}
    m["xall"] = np.ascontiguousarray(np.concatenate([xp, xp[own_rows], xs], axis=0))
    m["call"] = np.ascontiguousarray(np.concatenate([inp["c_sample"][c * NS:(c + 1) * NS], inp["c_prompt"][bi:bi + 1]], axis=0).astype(np.float32))
    pos = np.concatenate([np.arange(T), own_rows, np.full(128, past_len)])
    m["cs_all"] = _rope_table(pos)
    m["parsel"] = np.tile(np.array([[par, 1 - par]], np.float32), (128, 1))
    m["qrel"] = (par * 128 + np.arange(128, dtype=np.float32)).reshape(128, 1)
    m["ownidx"] = np.ascontiguousarray(own_rows.reshape(NO, 128).T.astype(np.int32))
    for k_, v_ in (("w_in", "w_in"), ("w_ada", "w_ada"), ("b_ada", "b_ada"), ("norm_w", "norm_w"), ("w_out", "w_out"),
                   ("qnw", "q_norm_w"), ("knw", "k_norm_w"), ("mu", "mu_shift"), ("w0", "w0"), ("a0", "a0"),
                   ("k_k", "k_k"), ("k_a", "k_a"), ("ln_x_w", "ln_x_w"), ("ln_x_b", "ln_x_b"), ("w_up", "w_up"), ("a_up", "a_up")):
        m[k_] = np.ascontiguousarray(np.asarray(inp[v_], np.float32))
    m["r_k"] = np.ascontiguousarray(np.asarray(inp["r_k"], np.float32).reshape(512))
    m["swkv"] = np.ascontiguousarray(np.asarray(inp["state_wkv"][c * NS:(c + 1) * NS], np.float32).reshape(NS * 8, 4096))
    m["sshift"] = np.ascontiguousarray(np.asarray(inp["state_shift"][c * NS:(c + 1) * NS, 0], np.float32))
    pt = np.asarray(inp["page_table"][c * NS:(c + 1) * NS], np.int32)
    m["ptab"] = np.ascontiguousarray(pt.reshape(NS // 2, 128).T)
    nphys = inp["cache_k"].shape[0]
    m["cache_k"] = np.asarray(inp["cache_k"], np.float32).reshape(nphys * 128, 128)
    m["cache_v"] = np.asarray(inp["cache_v"], np.float32).reshape(nphys * 128, 128)
    m["cache_kidx"] = np.asarray(inp["cache_kidx"], np.float32).reshape(nphys, 8192)
    rep = np.zeros((16, 8, 128), np.float32)
    for sp in range(8):
        for p in range(128):
            rep[2 * sp + p // 64, sp, p] = 1.0
    m["rep"] = rep.reshape(16, 1024)
    m["repT"] = np.ascontiguousarray(rep.transpose(2, 1, 0).reshape(128, 128))
    blk = np.zeros((128, 128), np.float32)
    blk[:64, :64] = 1.0
    blk[64:, 64:] = 1.0
    m["blk"] = blk
    oh = np.zeros((128, 1), np.float32)
    oh[0, 0] = 1.0
    oh[64, 0] = 1.0
    m["oh0"] = oh
    m.update(_consts(T))
    return m


_NC_CACHE = {}


def kernel(**inp):
    T = 4096
    NS = 16
    past_len = 8192
    inp = {k: np.asarray(v) for k, v in inp.items()}
    if "nc" not in _NC_CACHE:
        _NC_CACHE["nc"] = build(T=T, NPHYS=int(inp["cache_k"].shape[0]))
    nc = _NC_CACHE["nc"]
    in_maps = [_core_inputs(inp, c, T, NS, past_len) for c in range(8)]
    res = run_bass_kernel_spmd(nc, in_maps, core_ids=list(range(8)))
    outs = res.results
    B = 4
    NO = T // 256
    y_p = np.zeros((B, T, D), np.float32)
    for c in range(8):
        bi, par = c // 2, c % 2
        yo = np.asarray(outs[c]["y_own"]).reshape(NO, 128, D)
        y_p[bi].reshape(T // 256, 2, 128, D)[:, par] = yo
    k_p = np.stack([np.asarray(outs[2 * bi]["k_nat"]).reshape(T, 2, 64) for bi in range(B)])
    v_p = np.stack([np.asarray(outs[2 * bi]["v_nat"]).reshape(T, 2, 64) for bi in range(B)])
    ki_p = np.stack([np.asarray(outs[2 * bi]["ki_nat"]).reshape(T, 64) for bi in range(B)])
    wkv_pp = np.stack([np.asarray(outs[2 * bi]["wkv_p"]).reshape(8, 64, 64) for bi in range(B)])
    sh_p = np.stack([np.asarray(outs[2 * bi]["shift_p"]).reshape(1, SHW) for bi in range(B)])
    y_s = np.concatenate([np.asarray(outs[c]["y_s"]) for c in range(8)]).reshape(128, 1, D)
    k_s = np.concatenate([np.asarray(outs[c]["k_s"]) for c in range(8)]).reshape(128, 1, 2, 64)
    v_s = np.concatenate([np.asarray(outs[c]["v_s"]) for c in range(8)]).reshape(128, 1, 2, 64)
    ki_s = np.concatenate([np.asarray(outs[c]["ki_s"]) for c in range(8)]).reshape(128, 1, 64)
    wkv_s = np.concatenate([np.asarray(outs[c]["wkv_s"]) for c in range(8)]).reshape(128, 8, 64, 64)
    sh_s = np.concatenate([np.asarray(outs[c]["shift_s"]) for c in range(8)]).reshape(128, 1, SHW)
    f = lambda a: np.ascontiguousarray(a, dtype=np.float32)
    return (f(y_p), f(y_s), f(k_p), f(v_p), f(ki_p), f(wkv_pp), f(sh_p), f(k_s), f(v_s), f(ki_s), f(wkv_s), f(sh_s))
```
